# Optimizing a Trainium2 kernel written in Bass

```python
import functools
import jax
import jax.numpy as jnp
from jax import lax
import numpy as np

D_MODEL = 1024
BATCH = 16
SEQ = 256
DEPTH = 2
DEC_BATCH = 2
DEC_SEQ = 2048
PAST_LEN = 512

GRID_W = 64
ROPE_BASE = 10000.0
NORM_EPS = 1e-6
Q_BLOCK = 128
N_MOD = 9
D_FF = 2816

MLA_HEADS = 8
MLA_Q_LORA = 384
MLA_KV_LORA = 256
MLA_NOPE = 64
MLA_ROPE = 32
MLA_V = 64

SWA_HEADS = 8
SWA_KV_HEADS = 2
SWA_GROUP = SWA_HEADS // SWA_KV_HEADS
SWA_HD = 64
SWA_WINDOW = 128

GDN_HEADS = 4
GDN_DK = 128
GDN_DV = 128
GDN_CONV = 5
GDN_CHUNK = 64
GDN_QK_DIM = GDN_HEADS * GDN_DK
GDN_V_DIM = GDN_HEADS * GDN_DV
GDN_CONV_CH = 2 * GDN_QK_DIM + GDN_V_DIM

IN_SPLIT_SIZES = (MLA_Q_LORA, MLA_KV_LORA, MLA_ROPE,
                  SWA_HEADS * SWA_HD, SWA_KV_HEADS * SWA_HD, SWA_KV_HEADS * SWA_HD,
                  GDN_QK_DIM, GDN_QK_DIM, GDN_V_DIM, GDN_V_DIM, 4 * GDN_HEADS)
IN_DIM = (MLA_Q_LORA + MLA_KV_LORA + MLA_ROPE + SWA_HEADS * SWA_HD + 2 * SWA_KV_HEADS * SWA_HD
          + 2 * GDN_QK_DIM + 2 * GDN_V_DIM + 4 * GDN_HEADS)
D_MIX = MLA_HEADS * MLA_V + SWA_HEADS * SWA_HD + GDN_HEADS * GDN_DV

kernel_name = 'hybrid_mla_swa_gdn_macaron_diffusion_step'


def rms_norm(x, gain):
    xf = x.astype(jnp.float32)
    y = xf * lax.rsqrt(jnp.mean(xf * xf, axis=-1, keepdims=True) + NORM_EPS)
    return (y * gain.astype(jnp.float32)).astype(x.dtype)


def l2_normalize(x):
    xf = x.astype(jnp.float32)
    return (xf * lax.rsqrt(jnp.sum(xf * xf, axis=-1, keepdims=True) + 1e-6)).astype(x.dtype)


def modulate(x, shift, scale):
    return x * (1 + scale) + shift


def adaln_params(cond, w_ada, b_ada):
    m = jax.nn.silu(cond) @ w_ada + b_ada
    return jnp.split(m[:, None, :], N_MOD, axis=-1)


def swiglu(h, w1, w2):
    gate, up = jnp.split(h @ w1, 2, axis=-1)
    return (jax.nn.silu(gate) * up) @ w2


def axial_rope_tables(n_tokens, dim):
    rows = n_tokens // GRID_W
    row = jnp.repeat(jnp.arange(rows, dtype=jnp.float32), GRID_W)
    col = jnp.tile(jnp.arange(GRID_W, dtype=jnp.float32), rows)
    axis_dim = dim // 2
    inv_freq = 1.0 / (ROPE_BASE ** (jnp.arange(0, axis_dim, 2, dtype=jnp.float32) / axis_dim))
    ang_r = row[:, None] * inv_freq[None, :]
    ang_c = col[:, None] * inv_freq[None, :]
    ang = jnp.concatenate([ang_r, ang_r, ang_c, ang_c], axis=-1)
    return jnp.cos(ang), jnp.sin(ang)


def rotate_half(z):
    z1, z2 = jnp.split(z, 2, axis=-1)
    return jnp.concatenate([-z2, z1], axis=-1)


def apply_rope(x, cos, sin):
    shape = (cos.shape[0],) + (1,) * (x.ndim - 3) + (cos.shape[1],)
    xf = x.astype(jnp.float32)
    half = x.shape[-1] // 2
    rot = jnp.concatenate([rotate_half(xf[..., :half]), rotate_half(xf[..., half:])], axis=-1)
    return (xf * cos.reshape(shape) + rot * sin.reshape(shape)).astype(x.dtype)


def centred_depthwise_conv(x, w):
    k = w.shape[0]
    return lax.conv_general_dilated(x, w[:, None, :], window_strides=(1,), padding=[(k // 2, k // 2)],
                                    dimension_numbers=('NWC', 'WIO', 'NWC'),
                                    feature_group_count=x.shape[-1])


def project_mixers(h, lp):
    b, t, _ = h.shape
    offsets = np.cumsum(IN_SPLIT_SIZES)[:-1].tolist()
    u = h @ lp['w_in']
    cq, ckv, krope, sq, sk, sv, gq, gk, gv, gz, gates = jnp.split(u, offsets, axis=-1)
    q = (rms_norm(cq, lp['mla_q_norm']) @ lp['mla_w_qb']).reshape(b, t, MLA_HEADS, MLA_NOPE + MLA_ROPE)
    qkv = jax.nn.silu(centred_depthwise_conv(jnp.concatenate([gq, gk, gv], axis=-1), lp['gdn_conv_w']))
    gq, gk, gv = jnp.split(qkv, [GDN_QK_DIM, 2 * GDN_QK_DIM], axis=-1)
    gates = gates.reshape(b, t, 2, 2, GDN_HEADS)
    return {
        'q_nope': q[..., :MLA_NOPE], 'q_rope': q[..., MLA_NOPE:],
        'ckv': rms_norm(ckv, lp['mla_kv_norm']), 'krope': krope,
        'sq': sq.reshape(b, t, SWA_KV_HEADS, SWA_GROUP, SWA_HD),
        'sk': sk.reshape(b, t, SWA_KV_HEADS, SWA_HD),
        'sv': sv.reshape(b, t, SWA_KV_HEADS, SWA_HD),
        'gq': l2_normalize(gq.reshape(b, t, GDN_HEADS, GDN_DK)),
        'gk': l2_normalize(gk.reshape(b, t, GDN_HEADS, GDN_DK)),
        'gv': gv.reshape(b, t, GDN_HEADS, GDN_DV),
        'gz': gz.reshape(b, t, GDN_HEADS, GDN_DV),
        'ga': gates[:, :, 0], 'gb': gates[:, :, 1],
    }


def mla_expand(ckv, w_kvb):
    b, s, _ = ckv.shape
    kv = (ckv @ w_kvb).reshape(b, s, MLA_HEADS, MLA_NOPE + MLA_V)
    return kv[..., :MLA_NOPE], kv[..., MLA_NOPE:]


def mla_attend(q_nope, q_rope, k_nope, k_rope, v):
    b, t, h, _ = q_nope.shape
    nb = t // Q_BLOCK
    scale = (MLA_NOPE + MLA_ROPE) ** -0.5

    def blocks(z):
        return jnp.swapaxes(z.reshape((b, nb, Q_BLOCK) + z.shape[2:]), 0, 1)

    def one(qs):
        qn, qr = qs
        s = (jnp.einsum('bqhd,bshd->bhqs', qn, k_nope)
             + jnp.einsum('bqhr,bsr->bhqs', qr, k_rope)).astype(jnp.float32) * scale
        p = jax.nn.softmax(s, axis=-1).astype(v.dtype)
        return jnp.einsum('bhqs,bshd->bqhd', p, v)

    o = lax.map(one, (blocks(q_nope), blocks(q_rope)))
    return jnp.swapaxes(o, 0, 1).reshape(b, t, h * MLA_V)


def gqa_sink_attend(q, k, v, sink):
    b, t, kvh, g, d = q.shape
    nb = t // Q_BLOCK
    scale = d ** -0.5
    snk = sink.reshape(kvh, g)[None, :, :, None, None].astype(jnp.float32)
    qb = jnp.swapaxes(q.reshape(b, nb, Q_BLOCK, kvh, g, d), 0, 1)

    def one(q_blk):
        s = jnp.einsum('bqkgd,bskd->bkgqs', q_blk, k).astype(jnp.float32) * scale
        s = jnp.concatenate([s, jnp.broadcast_to(snk, s.shape[:-1] + (1,))], axis=-1)
        p = jax.nn.softmax(s, axis=-1)[..., :-1].astype(v.dtype)
        return jnp.einsum('bkgqs,bskd->bqkgd', p, v)

    o = lax.map(one, qb)
    return jnp.swapaxes(o, 0, 1).reshape(b, t, kvh * g * d)


def swa_band_attend(q, k, v, k_ctx, v_ctx, sink):
    b, t, kvh, g, d = q.shape
    w = SWA_WINDOW
    nb = t // w
    scale = d ** -0.5
    qb = q.reshape(b, nb, w, kvh, g, d)

    def band(z):
        zp = jnp.pad(z, ((0, 0), (w, w), (0, 0), (0, 0))).reshape(b, nb + 2, w, kvh, d)
        return jnp.concatenate([zp[:, :-2], zp[:, 1:-1], zp[:, 2:]], axis=2)

    kb, vb = band(k), band(v)
    qi = jnp.arange(nb)[:, None] * w + jnp.arange(w)[None, :]
    kj = (jnp.arange(nb)[:, None] - 1) * w + jnp.arange(3 * w)[None, :]
    valid = ((jnp.abs(qi[:, :, None] - kj[:, None, :]) <= w)
             & (kj[:, None, :] >= 0) & (kj[:, None, :] < t))
    s_loc = jnp.einsum('bnqkgd,bnskd->bnkgqs', qb, kb).astype(jnp.float32) * scale
    s_loc = jnp.where(valid[None, :, None, None], s_loc, -jnp.inf)
    s_ctx = jnp.einsum('bnqkgd,blkd->bnkgql', qb, k_ctx).astype(jnp.float32) * scale
    snk = jnp.broadcast_to(sink.reshape(kvh, g)[None, None, :, :, None, None].astype(jnp.float32),
                           s_loc.shape[:-1] + (1,))
    p = jax.nn.softmax(jnp.concatenate([s_loc, s_ctx, snk], axis=-1), axis=-1)
    p_loc = p[..., :3 * w].astype(v.dtype)
    p_ctx = p[..., 3 * w:-1].astype(v.dtype)
    o = (jnp.einsum('bnkgqs,bnskd->bnqkgd', p_loc, vb)
         + jnp.einsum('bnkgql,blkd->bnqkgd', p_ctx, v_ctx))
    return o.reshape(b, t, kvh * g * d)


def gated_delta_chunked(q, k, v, g, beta, s0):
    b, t, h, dk = q.shape
    dv = v.shape[-1]
    c = GDN_CHUNK
    n = t // c

    def chunks(z):
        return z.astype(jnp.float32).reshape(b, n, c, h, -1).transpose(0, 3, 1, 2, 4)

    qc = chunks(q) * dk ** -0.5
    kc = chunks(k)
    vc = chunks(v)
    gc = jnp.cumsum(g.astype(jnp.float32).reshape(b, n, c, h).transpose(0, 3, 1, 2), axis=-1)
    bc = beta.astype(jnp.float32).reshape(b, n, c, h).transpose(0, 3, 1, 2)
    causal = jnp.tril(jnp.ones((c, c), dtype=bool))
    strict = jnp.tril(jnp.ones((c, c), dtype=bool), -1)
    decay = jnp.exp(jnp.where(causal, gc[..., :, None] - gc[..., None, :], -jnp.inf))
    kb = kc * bc[..., None]
    lower = jnp.where(strict, jnp.einsum('bhnid,bhnjd->bhnij', kb, kc) * decay, 0.0)
    eye = jnp.eye(c, dtype=jnp.float32)
    tinv = lax.linalg.triangular_solve(eye + lower, jnp.broadcast_to(eye, lower.shape),
                                       left_side=True, lower=True)
    u = tinv @ (vc * bc[..., None])
    wk = tinv @ (kb * jnp.exp(gc)[..., None])
    a_intra = jnp.where(causal, jnp.einsum('bhnid,bhnjd->bhnij', qc, kc) * decay, 0.0)
    q_dec = qc * jnp.exp(gc)[..., None]
    k_dec = kc * jnp.exp(gc[..., -1:] - gc)[..., None]
    g_tot = jnp.exp(gc[..., -1])

    def step(s, xs):
        w_i, u_i, qd_i, kd_i, a_i, gt_i = xs
        v_new = u_i - w_i @ s
        o_i = qd_i @ s + a_i @ v_new
        s = s * gt_i[..., None, None] + jnp.swapaxes(kd_i, -1, -2) @ v_new
        return s, o_i

    xs = tuple(jnp.moveaxis(z, 2, 0) for z in (wk, u, q_dec, k_dec, a_intra, g_tot))
    s_fin, o = lax.scan(step, s0.astype(jnp.float32), xs)
    o = o.transpose(1, 0, 3, 2, 4).reshape(b, t, h, dv)
    return o.astype(v.dtype), s_fin


def gdn_bidirectional(pr, lp, s_fwd0, s_bwd0):
    b, t = pr['gq'].shape[:2]
    g = -jnp.exp(lp['gdn_a_log'].astype(jnp.float32)) * jax.nn.softplus(
        pr['ga'].astype(jnp.float32) + lp['gdn_dt_bias'].astype(jnp.float32))
    beta = jax.nn.sigmoid(pr['gb'].astype(jnp.float32))
    q, k, v = pr['gq'], pr['gk'], pr['gv']
    o_f, s_f = gated_delta_chunked(q, k, v, g[:, :, 0], beta[:, :, 0], s_fwd0)

    def rev(z):
        return jnp.flip(z, axis=1)

    o_b, s_b = gated_delta_chunked(rev(q), rev(k), rev(v), rev(g[:, :, 1]), rev(beta[:, :, 1]), s_bwd0)
    o = rms_norm(o_f + rev(o_b), lp['gdn_norm']) * jax.nn.silu(pr['gz'])
    return o.reshape(b, t, GDN_HEADS * GDN_DV), jnp.stack([s_f, s_b], axis=1)


def mixer_context(h, lp):
    b = h.shape[0]
    pr = project_mixers(h, lp)
    k_nope, v = mla_expand(pr['ckv'], lp['mla_w_kvb'])
    o_mla = mla_attend(pr['q_nope'], pr['q_rope'], k_nope, pr['krope'], v)
    o_swa = gqa_sink_attend(pr['sq'], pr['sk'], pr['sv'], lp['swa_sink'])
    zero = jnp.zeros((b, GDN_HEADS, GDN_DK, GDN_DV), jnp.float32)
    o_gdn, st = gdn_bidirectional(pr, lp, zero, zero)
    out = jnp.concatenate([o_mla, o_swa, o_gdn], axis=-1) @ lp['w_out']
    return out, (pr['ckv'], pr['krope'], pr['sk'], pr['sv'], st)


def mixer_latent(h, lp, ctx):
    ckv_c, krope_c, sk_c, sv_c, st_c = ctx
    t = h.shape[1]
    pr = project_mixers(h, lp)
    cos_m, sin_m = axial_rope_tables(t, MLA_ROPE)
    cos_s, sin_s = axial_rope_tables(t, SWA_HD)
    kn_l, v_l = mla_expand(pr['ckv'], lp['mla_w_kvb'])
    kn_c, v_c = mla_expand(ckv_c, lp['mla_w_kvb'])
    o_mla = mla_attend(pr['q_nope'], apply_rope(pr['q_rope'], cos_m, sin_m),
                       jnp.concatenate([kn_l, kn_c], axis=1),
                       jnp.concatenate([apply_rope(pr['krope'], cos_m, sin_m), krope_c], axis=1),
                       jnp.concatenate([v_l, v_c], axis=1))
    o_swa = swa_band_attend(apply_rope(pr['sq'], cos_s, sin_s), apply_rope(pr['sk'], cos_s, sin_s),
                            pr['sv'], sk_c, sv_c, lp['swa_sink'])
    o_gdn, _ = gdn_bidirectional(pr, lp, st_c[:, 0], st_c[:, 1])
    out = jnp.concatenate([o_mla, o_swa, o_gdn], axis=-1) @ lp['w_out']
    return out, None


def macaron_layer(x, mod, lp, mix_fn):
    sh1, sc1, g1, sh2, sc2, g2, sh3, sc3, g3 = mod
    x = x + g1 * (0.5 * swiglu(modulate(rms_norm(x, lp['norm_ffn1']), sh1, sc1), lp['ffn1_w1'], lp['ffn1_w2']))
    m, aux = mix_fn(modulate(rms_norm(x, lp['norm_mix']), sh2, sc2))
    x = x + g2 * m
    x = x + g3 * (0.5 * swiglu(modulate(rms_norm(x, lp['norm_ffn2']), sh3, sc3), lp['ffn2_w1'], lp['ffn2_w2']))
    return x, aux


def setup_inputs(seed: int = 0) -> dict:
    key = jax.random.key(seed)
    ks = list(jax.random.split(key, 32))

    def nrm(i, shape, scale):
        return jax.random.normal(ks[i], shape, jnp.float32) * scale

    L = DEPTH
    dt = jax.random.uniform(ks[20], (L, 2, GDN_HEADS), jnp.float32, minval=1e-3, maxval=1e-1)
    return {
        'x_prompt': nrm(0, (BATCH, SEQ, D_MODEL), 1.0),
        'x_sample': nrm(1, (DEC_BATCH, DEC_SEQ, D_MODEL), 1.0),
        'cache_mla_ckv': nrm(2, (DEC_BATCH, DEPTH, PAST_LEN, MLA_KV_LORA), 1.0),
        'cache_mla_krope': nrm(3, (DEC_BATCH, DEPTH, PAST_LEN, MLA_ROPE), 1.0),
        'cache_swa_k': nrm(4, (DEC_BATCH, DEPTH, PAST_LEN, SWA_KV_HEADS, SWA_HD), 1.0),
        'cache_swa_v': nrm(5, (DEC_BATCH, DEPTH, PAST_LEN, SWA_KV_HEADS, SWA_HD), 1.0),
        'state_gdn': nrm(6, (DEC_BATCH, DEPTH, 2, GDN_HEADS, GDN_DK, GDN_DV), 0.1),
        'c': nrm(7, (DEC_BATCH, D_MODEL), 1.0),
        'c_ctx': nrm(8, (D_MODEL,), 1.0),
        'w_ada': nrm(9, (L, D_MODEL, N_MOD * D_MODEL), 0.5 * D_MODEL ** -0.5),
        'b_ada': nrm(10, (L, N_MOD * D_MODEL), 0.02),
        'norm_ffn1': 1.0 + nrm(11, (L, D_MODEL), 0.05),
        'ffn1_w1': nrm(12, (L, D_MODEL, 2 * D_FF), D_MODEL ** -0.5),
        'ffn1_w2': nrm(13, (L, D_FF, D_MODEL), D_FF ** -0.5),
        'norm_mix': 1.0 + nrm(14, (L, D_MODEL), 0.05),
        'w_in': nrm(15, (L, D_MODEL, IN_DIM), D_MODEL ** -0.5),
        'mla_q_norm': 1.0 + nrm(16, (L, MLA_Q_LORA), 0.05),
        'mla_w_qb': nrm(17, (L, MLA_Q_LORA, MLA_HEADS * (MLA_NOPE + MLA_ROPE)), MLA_Q_LORA ** -0.5),
        'mla_kv_norm': 1.0 + nrm(18, (L, MLA_KV_LORA), 0.05),
        'mla_w_kvb': nrm(19, (L, MLA_KV_LORA, MLA_HEADS * (MLA_NOPE + MLA_V)), MLA_KV_LORA ** -0.5),
        'swa_sink': nrm(21, (L, SWA_HEADS), 0.5),
        'gdn_conv_w': nrm(22, (L, GDN_CONV, GDN_CONV_CH), GDN_CONV ** -0.5),
        'gdn_a_log': jnp.log(jax.random.uniform(ks[23], (L, 2, GDN_HEADS), jnp.float32, minval=1.0, maxval=8.0)),
        'gdn_dt_bias': dt + jnp.log(-jnp.expm1(-dt)),
        'gdn_norm': 1.0 + nrm(24, (L, GDN_DV), 0.05),
        'w_out': nrm(25, (L, D_MIX, D_MODEL), D_MIX ** -0.5),
        'norm_ffn2': 1.0 + nrm(26, (L, D_MODEL), 0.05),
        'ffn2_w1': nrm(27, (L, D_MODEL, 2 * D_FF), D_MODEL ** -0.5),
        'ffn2_w2': nrm(28, (L, D_FF, D_MODEL), D_FF ** -0.5),
        'final_norm': 1.0 + nrm(29, (D_MODEL,), 0.05),
    }


def reference(x_prompt, x_sample, cache_mla_ckv, cache_mla_krope, cache_swa_k, cache_swa_v, state_gdn, c,
              c_ctx, w_ada, b_ada, norm_ffn1, ffn1_w1, ffn1_w2, norm_mix, w_in, mla_q_norm, mla_w_qb,
              mla_kv_norm, mla_w_kvb, swa_sink, gdn_conv_w, gdn_a_log, gdn_dt_bias, gdn_norm, w_out,
              norm_ffn2, ffn2_w1, ffn2_w2, final_norm):
    xp, xs = x_prompt, x_sample
    ctx_ckv, ctx_krope, ctx_sk, ctx_sv, ctx_st = [], [], [], [], []
    for l in range(DEPTH):
        lp = {
            'norm_ffn1': norm_ffn1[l], 'ffn1_w1': ffn1_w1[l], 'ffn1_w2': ffn1_w2[l],
            'norm_mix': norm_mix[l], 'w_in': w_in[l],
            'mla_q_norm': mla_q_norm[l], 'mla_w_qb': mla_w_qb[l],
            'mla_kv_norm': mla_kv_norm[l], 'mla_w_kvb': mla_w_kvb[l],
            'swa_sink': swa_sink[l], 'gdn_conv_w': gdn_conv_w[l],
            'gdn_a_log': gdn_a_log[l], 'gdn_dt_bias': gdn_dt_bias[l], 'gdn_norm': gdn_norm[l],
            'w_out': w_out[l], 'norm_ffn2': norm_ffn2[l], 'ffn2_w1': ffn2_w1[l], 'ffn2_w2': ffn2_w2[l],
        }
        mod_ctx = adaln_params(c_ctx[None, :], w_ada[l], b_ada[l])
        mod_lat = adaln_params(c, w_ada[l], b_ada[l])
        xp, (ckv_l, krope_l, sk_l, sv_l, st_l) = macaron_layer(
            xp, mod_ctx, lp, functools.partial(mixer_context, lp=lp))
        ctx_ckv.append(ckv_l)
        ctx_krope.append(krope_l)
        ctx_sk.append(sk_l)
        ctx_sv.append(sv_l)
        ctx_st.append(st_l)
        cache_l = (cache_mla_ckv[:, l], cache_mla_krope[:, l], cache_swa_k[:, l], cache_swa_v[:, l],
                   state_gdn[:, l])
        xs, _ = macaron_layer(xs, mod_lat, lp, functools.partial(mixer_latent, lp=lp, ctx=cache_l))
    y_prompt = rms_norm(xp, final_norm)
    y_sample = rms_norm(xs, final_norm)
    new_mla_ckv = jnp.stack(ctx_ckv, axis=1)
    new_mla_krope = jnp.stack(ctx_krope, axis=1)
    new_swa_k = jnp.stack(ctx_sk, axis=1)
    new_swa_v = jnp.stack(ctx_sv, axis=1)
    new_state_gdn = jnp.stack(ctx_st, axis=1).astype(x_prompt.dtype)
    return (y_prompt, y_sample, new_mla_ckv, new_mla_krope, new_swa_k, new_swa_v, new_state_gdn)
```

```python
import contextlib
import numpy as np
import concourse.bass as bass
import concourse.mybir as mybir
from concourse.bass_utils import run_bass_kernel_spmd

F32 = mybir.dt.float32
BF16 = mybir.dt.bfloat16
AF = mybir.ActivationFunctionType
ALU = mybir.AluOpType

D = 1024
DFF = 2816
L = 2
NPT = 512
NST = 2048
PAST = 512
EPS = 1e-6
DEBUG = []
STOP = None
SKIP = None
GSTOP = 99
FUSE_WAIT = True


class Buf:
    __slots__ = ("name", "w", "r", "parent", "kids")

    def __init__(self, name, parent=None):
        self.name = name
        self.w = None
        self.r = []
        self.parent = parent
        self.kids = {}

    def sub(self, key):
        if key not in self.kids:
            self.kids[key] = Buf(f"{self.name}.{key}", self)
        return self.kids[key]

    def family(self):
        out = [self]
        p = self.parent
        while p is not None:
            out.append(p)
            p = p.parent
        st = list(self.kids.values())
        while st:
            q = st.pop()
            out.append(q)
            st.extend(q.kids.values())
        return out


class T:
    def __init__(self, t, name):
        self.t = t
        self.b = Buf(name)

    def __getitem__(self, key):
        return self.t[key]

    def sub(self, key):
        return self.b.sub(key)


def _b(x):
    return x.b if isinstance(x, T) else x


class K:
    SAME_ENGINE_SYNC = True

    def __init__(self, n_dma_sems=32):
        self.nc = bass.Bass("TRN2", target_bir_lowering=False)
        nc = self.nc
        self.es = contextlib.ExitStack()
        self.engs = {"pe": nc.tensor, "act": nc.scalar, "dve": nc.vector, "pool": nc.gpsimd, "sp": nc.sync}
        self.sems = {}
        self.cnt = {}
        for e in self.engs:
            self.sems[e] = self.es.enter_context(nc.semaphore(f"prog_{e}"))
            self.cnt[e] = 0
        self.dsems = [self.es.enter_context(nc.semaphore(f"dma_{i}")) for i in range(n_dma_sems)]
        self.dval = [0] * n_dma_sems
        self.dnext = 0
        self.dnext2 = 0
        self.semobj = dict(self.sems)
        for i, s in enumerate(self.dsems):
            self.semobj[("d", i)] = s
        self.waited = {}
        self.n_wait = 0
        self.n_ins = 0
        self.phase_es = None
        self.psums = []
        self.psnext = 0
        self.ps_rot = list(range(8))

    def sb(self, name, shape, dtype=F32, glob=False):
        es = self.es if (glob or self.phase_es is None) else self.phase_es
        self.n_alloc = getattr(self, "n_alloc", 0) + 1
        name = f"sb{self.n_alloc}_{name}"
        t = es.enter_context(self.nc.sbuf_tensor(name, list(shape), dtype))
        return T(t, name)

    def dram(self, name, shape, dtype=F32, kind="Internal"):
        return T(self.nc.dram_tensor(name, list(shape), dtype, kind=kind), name)

    def init_psum(self):
        for i in range(8):
            t = self.es.enter_context(self.nc.psum_tensor(f"psb{i}", [128, 512], F32))
            self.psums.append(T(t, f"psb{i}"))

    def psum(self):
        rot = self.ps_rot
        self.psnext = (self.psnext + 1) % len(rot)
        return self.psums[rot[self.psnext]]

    def psum_hold(self, n):
        held = self.ps_rot[-n:]
        self.ps_rot = self.ps_rot[:-n]
        self.psnext = 0
        return [self.psums[i] for i in held]

    def psum_release(self, banks):
        self.ps_rot = self.ps_rot + [self.psums.index(b) for b in banks]

    def begin_phase(self):
        assert self.phase_es is None
        self.phase_es = contextlib.ExitStack()

    def end_phase(self):
        self.barrier()
        self.phase_es.close()
        self.phase_es = None

    def _deps(self, reads, writes, e=None):
        deps = {}

        def add(d):
            if d is not None and deps.get(d[0], 0) < d[1]:
                deps[d[0]] = d[1]
        for b in reads:
            bb = _b(b)
            for f in bb.family():
                add(f.w)
            if bb.name.startswith("psb"):
                for f in bb.family():
                    for r in f.r:
                        if r[0] != e:
                            add(r)
        for b in writes:
            for f in _b(b).family():
                add(f.w)
                for r in f.r:
                    add(r)
        return deps

    def _emit_waits(self, e, deps, keep_one=False):
        eng = self.engs[e]
        need = []
        for kk, v in deps.items():
            if kk == e and (e == "pe" or not self.SAME_ENGINE_SYNC):
                continue
            if self.waited.get((e, kk), 0) >= v:
                continue
            need.append((kk, v))
            self.waited[(e, kk)] = v
        fused = None
        if keep_one and FUSE_WAIT and need:
            fused = need.pop()
        for kk, v in need:
            eng.wait_ge(self.semobj[kk], v)
            self.n_wait += 1
        return fused

    def _record(self, tag, reads, writes):
        for b in reads:
            _b(b).r.append(tag)
        for b in writes:
            bb = _b(b)
            bb.w = tag
            bb.r = []

    def op(self, e, fn, reads=(), writes=()):
        deps = self._deps(reads, writes, e)
        fused = self._emit_waits(e, deps, keep_one=True)
        ins = fn(self.engs[e])
        if fused is not None:
            ins._wait_ge(self.semobj[fused[0]], fused[1])
        self.cnt[e] += 1
        ins.then_inc(self.sems[e], 1)
        self._record((e, self.cnt[e]), reads, writes)
        self.n_ins += 1
        return ins

    def dma(self, q, out, in_, reads=(), writes=(), **kw):
        deps = self._deps(reads, writes)
        half = len(self.dsems) // 2
        if q == "sp":
            i = self.dnext % half
            self.dnext += 1
        else:
            i = half + self.dnext2 % half
            self.dnext2 += 1
        key = ("d", i)
        if self.dval[i] > 0:
            deps[key] = max(deps.get(key, 0), self.dval[i])
        self._emit_waits(q, deps)
        ins = self.engs[q].dma_start(out=out, in_=in_, **kw)
        self.dval[i] += 16
        ins.then_inc(self.dsems[i], 16)
        self._record((key, self.dval[i]), reads, writes)
        self.n_ins += 1
        return ins

    def barrier(self):
        alld = {e: self.cnt[e] for e in self.engs if self.cnt[e] > 0}
        for i, v in enumerate(self.dval):
            if v > 0:
                alld[("d", i)] = v
        for e in self.engs:
            self._emit_waits(e, dict(alld))

    def mm(self, ps, out, lhsT, rhs, start, stop, reads):
        self.op("pe", lambda e: e.matmul(out, lhsT=lhsT, rhs=rhs, start=start, stop=stop), reads=reads, writes=[ps])

    def tr(self, ps, out, in_, ident, reads):
        self.op("pe", lambda e: e.transpose(out, in_, ident), reads=reads, writes=[ps])


def _rope_tables(n_tokens, dim):
    rows = n_tokens // 64
    row = np.repeat(np.arange(rows, dtype=np.float32), 64)
    col = np.tile(np.arange(64, dtype=np.float32), rows)
    axis_dim = dim // 2
    inv = (1.0 / (10000.0 ** (np.arange(0, axis_dim, 2, dtype=np.float32) / axis_dim))).astype(np.float32)
    ang_r = row[:, None] * inv[None, :]
    ang_c = col[:, None] * inv[None, :]
    ang = np.concatenate([ang_r, ang_r, ang_c, ang_c], axis=-1)
    return np.cos(ang).astype(np.float32), np.sin(ang).astype(np.float32)


def host_consts():
    c = {}
    p = np.arange(128)[:, None]
    f = np.arange(128)[None, :]
    masks = np.zeros((128, 5, 128), np.float32)
    masks[:, 0, :] = (p == f)
    masks[:, 1, :] = (p <= f)
    masks[:, 2, :] = (p < f)
    masks[:, 3, :] = (p >= f)
    masks[:, 4, :] = (p > f)
    c["masks"] = masks
    cm, sm = _rope_tables(NST, 32)
    cosm = np.ones((96, NST), np.float32)
    sinm = np.zeros((96, NST), np.float32)
    cosm[64:96] = cm.T
    sinm[64:96] = sm.T
    cs, ss = _rope_tables(NST, 64)
    c["rope_m"] = np.stack([cosm, sinm], 1).copy()
    c["rope_s"] = np.stack([cs.T, ss.T], 1).copy()
    return c


def n_is_k(ap, self):
    return ap.tensor.name == self.sm_k.t.name


class Builder:
    def __init__(self):
        self.k = K()
        k = self.k
        self.nc = k.nc
        self.ins = {}
        self.outs = {}
        k.init_psum()

    def din(self, name, shape):
        t = self.k.dram(name, shape, F32, kind="ExternalInput")
        self.ins[name] = t
        return t

    def dout(self, name, shape):
        t = self.k.dram(name, shape, F32, kind="ExternalOutput")
        self.outs[name] = t
        return t

    def build(self):
        k = self.k
        W = {}
        for nm, shp in [("xp", [NPT, D]), ("xs", [NST, D]), ("c_ckv", [L, PAST, 256]), ("c_kr", [L, PAST, 32]),
                        ("c_sk", [L, PAST, 128]), ("c_sv", [L, PAST, 128]), ("c_st", [L, 2, 4, 128, 128]),
                        ("vecA", [82, 128]), ("b_ada", [L, 72, 128]), ("convw", [120, 128]), ("rowv", [1, 304 * L]),
                        ("w_ada", [L, D, 9 * D]), ("ffn1_w1", [L, D, 2 * DFF]), ("ffn1_w2", [L, DFF, D]),
                        ("w_in", [L, D, 3504]), ("mla_w_qb", [L, 384, 768]), ("mla_w_kvb", [L, 256, 1024]),
                        ("w_out", [L, 1536, D]), ("ffn2_w1", [L, D, 2 * DFF]), ("ffn2_w2", [L, DFF, D]),
                        ("masks", [128, 5, 128]), ("rope_m", [96, 2, NST]), ("rope_s", [64, 2, NST])]:
            W[nm] = self.din(nm, shp)
        self.W = W
        O = {}
        O["yp"] = self.dout("yp", [NPT, D])
        O["ys"] = self.dout("ys", [NST, D])
        O["n_ckv"] = self.dout("n_ckv", [2, L, 256, 256])
        O["n_kr"] = self.dout("n_kr", [2, L, 256, 32])
        O["n_sk"] = self.dout("n_sk", [2, L, 256, 128])
        O["n_sv"] = self.dout("n_sv", [2, L, 256, 128])
        O["n_st"] = self.dout("n_st", [2, L, 2, 4, 128, 128])
        self.O = O
        self.dbg = {}

        self.xT = {"p": k.dram("xTp", [128, 8, NPT]), "s": k.dram("xTs", [128, 8, NST])}
        self.ntok = {"p": NPT, "s": NST}
        self.blocks = [("p", 0)] + [("s", i * 512) for i in range(4)]

        self.masks = k.sb("masks", [128, 5, 128], F32, glob=True)
        k.dma("sp", self.masks[:], W["masks"][:], writes=[self.masks])
        self.ident = self.masks.t[:, 0, :]
        self.ones_f = k.sb("ones_f", [128, 128], F32, glob=True)
        self.ones_b = k.sb("ones_b", [128, 128], BF16, glob=True)
        k.op("dve", lambda e: e.memset(self.ones_f[:], 1.0), writes=[self.ones_f])
        k.op("dve", lambda e: e.memset(self.ones_b[:], 1.0), writes=[self.ones_b])
        self.epsc = k.sb("epsc", [128, 1], F32, glob=True)
        k.op("dve", lambda e: e.memset(self.epsc[:], EPS), writes=[self.epsc])
        self.masks_b = k.sb("masks_b", [128, 5, 128], BF16, glob=True)
        k.op("dve", lambda e: e.tensor_copy(out=self.masks_b[:], in_=self.masks[:]), reads=[self.masks], writes=[self.masks_b])
        self.vecT = k.sb("vecT", [128, 82], F32, glob=True)
        self.badaT = k.sb("badaT", [128, L, 72], F32, glob=True)
        self.convT = k.sb("convT", [128, 120], F32, glob=True)
        self.rowbc = k.sb("rowbc", [128, 304 * L], F32, glob=True)
        self.modT = k.sb("modT", [128, L, 72, 2], F32, glob=True)
        self.modA = k.sb("modA", [128, L, 2, 3, 8], F32, glob=True)
        self.modG = k.sb("modG", [128, L, 2, 3, 8], F32, glob=True)
        self.hT = {"p": k.sb("hTp", [128, 8, NPT], BF16, glob=True), "s": k.sb("hTs", [128, 8, NST], BF16, glob=True)}
        stages = [("setup", self.setup_small), ("adaln", self.adaln), ("load", self.load_inputs)]
        for l in range(L):
            stages.append((f"ffn1_{l}", lambda l=l: self.ffn(l, 0)))
            for path in ("p", "s"):
                stages.append((f"mla_{l}{path}", lambda l=l, path=path: self.mla(l, path)))
                stages.append((f"swa_{l}{path}", lambda l=l, path=path: self.swa(l, path)))
                stages.append((f"gdn_{l}{path}", lambda l=l, path=path: self.gdn(l, path)))
            stages.append((f"ffn2_{l}", lambda l=l: self.ffn(l, 2)))
        stages.append(("final", self.final))
        for nm, fn in stages:
            if SKIP and nm in SKIP:
                continue
            fn()
            if STOP and nm == STOP:
                break
        if DEBUG:
            k.barrier()
            for nm in DEBUG:
                if nm in ("xTp", "xTs"):
                    src = self.xT[nm[-1]]
                    o = self.dout("dbg_" + nm, [128, 8, self.ntok[nm[-1]]])
                    for t0 in range(0, self.ntok[nm[-1]], 512):
                        k.dma("sp", o[:, :, t0:t0 + 512], src[:, :, t0:t0 + 512], reads=[src], writes=[o])
                elif nm in ("hTp", "hTs"):
                    src = self.hT[nm[-1]]
                    o = T(self.nc.dram_tensor("dbg_" + nm, [128, 8, self.ntok[nm[-1]]], BF16, kind="ExternalOutput"), "dbg_" + nm)
                    k.dma("sp", o[:, :, :], src[:, :, :], reads=[src], writes=[o])
                elif nm == "modT":
                    o = self.dout("dbg_modT", [128, L * 72 * 2])
                    k.dma("sp", o[:, :], self.modT[:, :, :, :].rearrange("p l c r -> p (l c r)"), reads=[self.modT], writes=[o])
        k.barrier()
        return self.nc

    def setup_small(self):
        k = self.k
        W = self.W
        k.begin_phase()
        stage = k.sb("stage", [128, 128], F32)
        for (src, nrow, dst) in [(W["vecA"][:, :], 82, self.vecT[:, :]), (W["b_ada"][0], 72, self.badaT[:, 0, :]),
                                 (W["b_ada"][1], 72, self.badaT[:, 1, :]), (W["convw"][:, :], 120, self.convT[:, :])]:
            k.dma("sp", stage[0:nrow, :], src, writes=[stage])
            ps = k.psum()
            k.tr(ps, ps[:, 0:nrow], stage[0:nrow, :], self.ident[0:nrow, 0:nrow], reads=[stage, self.masks])
            k.op("dve", lambda e, dst=dst, ps=ps, nrow=nrow: e.tensor_copy(out=dst, in_=ps[:, 0:nrow]), reads=[ps],
                 writes=[self.vecT, self.badaT, self.convT])
        row = k.sb("rowstage", [1, 304 * L], F32)
        k.dma("sp", row[:], W["rowv"][:, :], writes=[row])
        for i in range(0, 304 * L, 512):
            n = min(512, 304 * L - i)
            ps = k.psum()
            k.mm(ps, ps[:, 0:n], self.ones_f[0:1, :], row[0:1, i:i + n], True, True, reads=[self.ones_f, row])
            k.op("dve", lambda e, ps=ps, i=i, n=n: e.tensor_copy(out=self.rowbc[:, i:i + n], in_=ps[:, 0:n]), reads=[ps], writes=[self.rowbc])
        k.end_phase()

    def rowv(self, l, what):
        o = {"sink": (0, 8), "alog": (8, 16), "dtb": (16, 24), "gnorm": (24, 152)}[what]
        return self.rowbc.t[:, l * 304 + o[0]: l * 304 + o[1]]

    def adaln(self):
        k = self.k
        W = self.W
        k.begin_phase()
        condT = k.sb("condT", [128, 8, 2], BF16)
        tmp = k.sb("condtmp", [128, 16], F32)
        k.op("act", lambda e: e.activation(out=tmp[:], in_=self.vecT[:, 56:72], func=AF.Silu), reads=[self.vecT], writes=[tmp])
        for r in range(2):
            k.op("dve", lambda e, r=r: e.tensor_copy(out=condT[:, :, r], in_=tmp[:, r * 8:(r + 1) * 8]), reads=[tmp], writes=[condT])
        wb = [k.sb(f"wada{i}", [128, 8, 512], BF16) for i in range(2)]
        for l in range(L):
            psm = k.psum()
            for g in range(18):
                w = wb[g % 2]
                k.dma("pool", w[:], W["w_ada"][l].rearrange("(kc p) n -> p kc n", p=128)[:, :, g * 512:(g + 1) * 512], writes=[w])
                for j in range(4):
                    cc = g * 4 + j
                    for kc in range(8):
                        k.mm(psm, psm[:, cc * 2:cc * 2 + 2], w[:, kc, j * 128:(j + 1) * 128], condT[:, kc, :], kc == 0, kc == 7,
                             reads=[w, condT])
            for r in range(2):
                k.op("dve", lambda e, l=l, r=r, psm=psm: e.tensor_tensor(
                    out=self.modT[:, l, :, r], in0=psm[:, 0:144].rearrange("p (c r) -> p c r", r=2)[:, :, r],
                    in1=self.badaT[:, l, :], op=ALU.add), reads=[psm, self.badaT], writes=[self.modT])
        for l in range(L):
            for r in range(2):
                for s in range(3):
                    gain = self.vecT[:, [0, 16, 32][s] + l * 8:[0, 16, 32][s] + l * 8 + 8]
                    sc = self.modT[:, l, (3 * s + 1) * 8:(3 * s + 2) * 8, r]
                    k.op("dve", lambda e, l=l, r=r, s=s, gain=gain, sc=sc: e.scalar_tensor_tensor(
                        out=self.modA[:, l, r, s, :], in0=sc, scalar=1.0, in1=gain, op0=ALU.add, op1=ALU.mult),
                        reads=[self.modT, self.vecT], writes=[self.modA])
                    gt = self.modT[:, l, (3 * s + 2) * 8:(3 * s + 3) * 8, r]
                    k.op("dve", lambda e, l=l, r=r, s=s, gt=gt: e.tensor_scalar(
                        out=self.modG[:, l, r, s, :], in0=gt, scalar1=(1.0 if s == 1 else 0.5), scalar2=None, op0=ALU.mult),
                        reads=[self.modT], writes=[self.modG])
        k.end_phase()

    def mod_shift(self, l, r, s, kc):
        c = (3 * s) * 8 + kc
        return self.modT.t[:, l, c:c + 1, r]

    def load_inputs(self):
        k = self.k
        k.begin_phase()
        xin = [k.sb(f"xin{i}", [128, D], F32) for i in range(2)]
        xblk = [k.sb(f"xblk{i}", [128, 8, 512], F32) for i in range(2)]
        bi = 0
        for (path, t0) in self.blocks:
            src = self.W["xp"] if path == "p" else self.W["xs"]
            xb = xblk[bi % 2]
            for tt in range(4):
                xi = xin[tt % 2]
                k.dma("sp", xi[:], src[t0 + tt * 128:t0 + (tt + 1) * 128, :], writes=[xi])
                for half in range(2):
                    ps = k.psum()
                    for j in range(4):
                        kc = half * 4 + j
                        k.tr(ps, ps[:, j * 128:(j + 1) * 128], xi[:, kc * 128:(kc + 1) * 128], self.ident, reads=[xi, self.masks])
                    eng = "dve" if half == 0 else "act"
                    if eng == "dve":
                        k.op("dve", lambda e, xb=xb, ps=ps, half=half, tt=tt: e.tensor_copy(
                            out=xb[:, half * 4:half * 4 + 4, tt * 128:(tt + 1) * 128],
                            in_=ps[:, :].rearrange("p (j t) -> p j t", j=4)), reads=[ps], writes=[xb])
                    else:
                        k.op("act", lambda e, xb=xb, ps=ps, half=half, tt=tt: e.activation(
                            out=xb[:, half * 4:half * 4 + 4, tt * 128:(tt + 1) * 128],
                            in_=ps[:, :].rearrange("p (j t) -> p j t", j=4), func=AF.Copy), reads=[ps], writes=[xb])
            k.dma("sp", self.xT[path][:, :, t0:t0 + 512], xb[:], reads=[xb], writes=[self.xT[path].sub(t0)])
            bi += 1
        k.end_phase()

    def rmsnorm(self, x, n, Acol, Bcol, out, sq, rstd, nfeat_chunks=8, nfeat=D, tmp=None):
        k = self.k
        xs, xr = x
        outf, outw = out
        for kc in range(nfeat_chunks):
            k.op("act", lambda e, kc=kc: e.activation(out=sq[:, kc, 0:n], in_=xs(kc), func=AF.Square), reads=xr, writes=[sq])
        ps = k.psum()
        for kc in range(nfeat_chunks):
            k.mm(ps, ps[:, 0:n], self.ones_b[:, :], sq[:, kc, 0:n], kc == 0, kc == nfeat_chunks - 1, reads=[self.ones_b, sq])
        k.op("act", lambda e: e.activation(out=rstd[:, 0:n], in_=ps[:, 0:n], func=AF.Sqrt, bias=self.epsc[:, 0:1], scale=1.0 / nfeat),
             reads=[ps, self.epsc], writes=[rstd])
        k.op("dve", lambda e: e.reciprocal(out=rstd[:, 0:n], in_=rstd[:, 0:n]), reads=[rstd], writes=[rstd])
        for kc in range(nfeat_chunks):
            if Bcol is None:
                k.op("dve", lambda e, kc=kc: e.scalar_tensor_tensor(out=outf(kc), in0=xs(kc), scalar=Acol(kc), in1=rstd[:, 0:n],
                                                                     op0=ALU.mult, op1=ALU.mult), reads=list(xr) + [rstd], writes=outw)
            else:
                k.op("dve", lambda e, kc=kc: e.scalar_tensor_tensor(out=tmp[:, kc, 0:n], in0=xs(kc), scalar=Acol(kc), in1=rstd[:, 0:n],
                                                                     op0=ALU.mult, op1=ALU.mult), reads=list(xr) + [rstd], writes=[tmp])
                k.op("act", lambda e, kc=kc: e.activation(out=outf(kc), in_=tmp[:, kc, 0:n], func=AF.Identity, bias=Bcol(kc), scale=1.0),
                     reads=[tmp], writes=outw)

    def ffn(self, l, s):
        k = self.k
        W = self.W
        w1 = W["ffn1_w1" if s == 0 else "ffn2_w1"][l].rearrange("(kc p) n -> p kc n", p=128)
        w2 = W["ffn1_w2" if s == 0 else "ffn2_w2"][l].rearrange("(j p) n -> p j n", p=128)
        k.begin_phase()
        xb = k.sb("f_x", [128, 8, 512], F32)
        hb = k.sb("f_h", [128, 8, 512], BF16)
        tmp = k.sb("f_tmp", [128, 8, 512], F32)
        act = k.sb("f_act", [128, 22, 512], BF16)
        sq = k.sb("f_sq", [128, 8, 512], BF16)
        rstd = k.sb("f_rstd", [128, 512], F32)
        sg = [k.sb(f"f_sg{i}", [128, 512], F32) for i in range(2)]
        w1b = [k.sb(f"f_w1{i}", [128, 8, 512], BF16) for i in range(2)]
        w2b = [k.sb(f"f_w2{i}", [128, 22, 512], BF16) for i in range(2)]
        nw1 = 0
        nw2 = 0
        w1c = k.dram(f"w1c_{l}_{s}", [128, 11, 8, 512], BF16)
        w2c = k.dram(f"w2c_{l}_{s}", [128, 2, 22, 512], BF16)
        for bi, (path, t0) in enumerate(self.blocks):
            r = 0 if path == "p" else 1
            xsub = self.xT[path].sub(t0)
            k.dma("sp", xb[:], self.xT[path][:, :, t0:t0 + 512], reads=[xsub], writes=[xb])
            self.rmsnorm((lambda kc: xb[:, kc, :], [xb]), 512,
                         lambda kc: self.modA[:, l, r, s, kc:kc + 1], lambda kc: self.mod_shift(l, r, s, kc),
                         (lambda kc: hb[:, kc, :], [hb]), sq, rstd, tmp=tmp)
            for jg in range(11):
                wt = w1b[nw1 % 2]
                nw1 += 1
                if bi == 0:
                    k.dma("pool", wt[:, :, 0:256], w1[:, :, jg * 256:(jg + 1) * 256], writes=[wt])
                    k.dma("pool", wt[:, :, 256:512], w1[:, :, DFF + jg * 256:DFF + (jg + 1) * 256], writes=[wt])
                    k.dma("sp", w1c[:, jg], wt[:], reads=[wt], writes=[w1c.sub(jg)])
                else:
                    k.dma("sp", wt[:], w1c[:, jg], reads=[w1c.sub(jg)], writes=[wt])
                for jj in range(2):
                    j = jg * 2 + jj
                    pg = k.psum()
                    pu = k.psum()
                    for kc in range(8):
                        k.mm(pg, pg[:, :], wt[:, kc, jj * 128:(jj + 1) * 128], hb[:, kc, :], kc == 0, kc == 7, reads=[wt, hb])
                    for kc in range(8):
                        k.mm(pu, pu[:, :], wt[:, kc, 256 + jj * 128:256 + (jj + 1) * 128], hb[:, kc, :], kc == 0, kc == 7, reads=[wt, hb])
                    sgt = sg[j % 2]
                    k.op("act", lambda e, sgt=sgt, pg=pg: e.activation(out=sgt[:], in_=pg[:, :], func=AF.Silu), reads=[pg], writes=[sgt])
                    k.op("dve", lambda e, sgt=sgt, pu=pu, j=j: e.tensor_tensor(out=act[:, j, :], in0=pu[:, :], in1=sgt[:], op=ALU.mult),
                         reads=[pu, sgt], writes=[act.sub(j)])
            for half in range(2):
                wt = w2b[nw2 % 2]
                nw2 += 1
                if bi == 0:
                    for q4 in range(2):
                        k.dma("pool", wt[:, q4 * 11:(q4 + 1) * 11, :], w2[:, q4 * 11:(q4 + 1) * 11, half * 512:(half + 1) * 512], writes=[wt])
                    k.dma("sp", w2c[:, half], wt[:], reads=[wt], writes=[w2c.sub(half)])
                else:
                    k.dma("sp", wt[:], w2c[:, half], reads=[w2c.sub(half)], writes=[wt])
                for f4 in range(4):
                    fc = half * 4 + f4
                    po = k.psum()
                    for j in range(22):
                        k.mm(po, po[:, :], wt[:, j, f4 * 128:(f4 + 1) * 128], act[:, j, :], j == 0, j == 21, reads=[wt, act.sub(j)])
                    k.op("dve", lambda e, po=po, fc=fc: e.scalar_tensor_tensor(
                        out=xb[:, fc, :], in0=po[:, :], scalar=self.modG[:, l, r, s, fc:fc + 1], in1=xb[:, fc, :],
                        op0=ALU.mult, op1=ALU.add), reads=[po, xb, self.modG], writes=[xb])
            k.dma("sp", self.xT[path][:, :, t0:t0 + 512], xb[:], reads=[xb], writes=[xsub])
            if s == 0:
                hT = self.hT[path]
                self.rmsnorm((lambda kc: xb[:, kc, :], [xb]), 512,
                             lambda kc: self.modA[:, l, r, 1, kc:kc + 1], lambda kc: self.mod_shift(l, r, 1, kc),
                             (lambda kc: hT[:, kc, t0:t0 + 512], [hT.sub(t0)]), sq, rstd, tmp=tmp)
        k.end_phase()

    def final(self):
        k = self.k
        k.begin_phase()
        xb = k.sb("o_x", [128, 8, 512], F32)
        yb = k.sb("o_y", [128, 8, 512], F32)
        sq = k.sb("o_sq", [128, 8, 512], BF16)
        rstd = k.sb("o_rstd", [128, 512], F32)
        yo = [k.sb(f"o_yo{i}", [128, D], F32) for i in range(2)]
        n = 0
        for (path, t0) in self.blocks:
            k.dma("sp", xb[:], self.xT[path][:, :, t0:t0 + 512], reads=[self.xT[path].sub(t0)], writes=[xb])
            self.rmsnorm((lambda kc: xb[:, kc, :], [xb]), 512, lambda kc: self.vecT[:, 48 + kc:49 + kc], None,
                         (lambda kc: yb[:, kc, :], [yb]), sq, rstd)
            dst = self.O["yp"] if path == "p" else self.O["ys"]
            for tt in range(4):
                y = yo[n % 2]
                n += 1
                for half in range(2):
                    ps = k.psum()
                    for j in range(4):
                        kc = half * 4 + j
                        k.tr(ps, ps[:, j * 128:(j + 1) * 128], yb[:, kc, tt * 128:(tt + 1) * 128], self.ident, reads=[yb, self.masks])
                    if half == 0:
                        k.op("dve", lambda e, y=y, ps=ps: e.tensor_copy(out=y[:, 0:512], in_=ps[:, :]), reads=[ps], writes=[y])
                    else:
                        k.op("act", lambda e, y=y, ps=ps: e.activation(out=y[:, 512:1024], in_=ps[:, :], func=AF.Copy), reads=[ps], writes=[y])
                k.dma("sp", dst[t0 + tt * 128:t0 + (tt + 1) * 128, :], y[:], reads=[y], writes=[dst])
        k.end_phase()

    ao_k = 128

    def apply_out(self, l, path, t0, n, mixT, nchunks, wout_t, chunk0):
        k = self.k
        r = 0 if path == "p" else 1
        mixf, mixr = mixT
        xb = self.ao_x
        xsub = self.xT[path].sub((t0 // 512) * 512)
        k.dma("sp", xb[:, :, 0:n], self.xT[path][:, :, t0:t0 + n], reads=[xsub], writes=[xb])
        for fc in range(8):
            po = k.psum()
            for c in range(nchunks):
                k.mm(po, po[:, 0:n], wout_t[0:self.ao_k, chunk0 + c, fc * 128:(fc + 1) * 128], mixf(c), c == 0, c == nchunks - 1,
                     reads=[wout_t] + list(mixr))
            k.op("dve", lambda e, po=po, fc=fc: e.scalar_tensor_tensor(
                out=xb[:, fc, 0:n], in0=po[:, 0:n], scalar=self.modG[:, l, r, 1, fc:fc + 1], in1=xb[:, fc, 0:n],
                op0=ALU.mult, op1=ALU.add), reads=[po, xb, self.modG], writes=[xb])
        k.dma("sp", self.xT[path][:, :, t0:t0 + n], xb[:, :, 0:n], reads=[xb], writes=[xsub])

    def store_tok(self, srcf, srcr, nfeat, ntok, dst_fn, f0=0):
        k = self.k
        for tt in range(ntok // 128):
            ps = k.psum()
            k.tr(ps, ps[:, 0:nfeat], srcf(tt), self.ident[0:nfeat, 0:nfeat], reads=list(srcr) + [self.masks])
            st = self.st_buf[self.st_n % 2]
            self.st_n += 1
            k.op("act", lambda e, st=st, ps=ps: e.activation(out=st[:, 0:nfeat - f0], in_=ps[:, f0:nfeat], func=AF.Copy), reads=[ps], writes=[st])
            k.dma("sp", dst_fn(tt), st[:, 0:nfeat - f0], reads=[st], writes=[])

    def mixer(self, l, path):
        self.mla(l, path)
        self.swa(l, path)
        self.gdn(l, path)

    def seqs(self, path):
        return [(0, 256), (256, 256)] if path == "p" else [(0, NST)]

    def softmax_bound(self, q2max, k2max, scale, negc):
        k = self.k
        t = self.sm_t
        k.op("dve", lambda e: e.tensor_tensor(out=t[0:1, 0:1], in0=q2max, in1=k2max, op=ALU.add), reads=[self.sm_q, self.sm_k], writes=[t])
        k.op("dve", lambda e: e.tensor_scalar(out=t[0:1, 0:1], in0=t[0:1, 0:1], scalar1=-0.5 * scale, scalar2=None, op0=ALU.mult),
             reads=[t], writes=[t])
        ps = k.psum()
        k.mm(ps, ps[:, 0:1], self.ones_f[0:1, :], t[0:1, 0:1], True, True, reads=[self.ones_f, t])
        k.op("dve", lambda e: e.tensor_copy(out=negc[:, 0:1], in_=ps[:, 0:1]), reads=[ps], writes=[negc])

    def sq_max(self, src_ap, nrow, n, dst, first, reads, dstT=None):
        k = self.k
        sqt = self.sm_sq
        k.op("pool", lambda e: e.tensor_tensor(out=sqt[0:nrow, 0:n], in0=src_ap, in1=src_ap, op=ALU.mult), reads=reads, writes=[sqt])
        ps = k.psum()
        k.mm(ps, ps[0:1, 0:n], self.ones_b[0:nrow, 0:1], sqt[0:nrow, 0:n], True, True, reads=[self.ones_b, sqt])
        tm = self.sm_m
        if dstT is None:
            dstT = self.sm_k if n_is_k(dst, self) else self.sm_q
        if first:
            k.op("dve", lambda e: e.reduce_max(out=dst, in_=ps[0:1, 0:n], axis=mybir.AxisListType.X), reads=[ps], writes=[dstT])
        else:
            k.op("dve", lambda e: e.reduce_max(out=tm[0:1, 0:1], in_=ps[0:1, 0:n], axis=mybir.AxisListType.X), reads=[ps], writes=[tm])
            k.op("dve", lambda e: e.tensor_tensor(out=dst, in0=dst, in1=tm[0:1, 0:1], op=ALU.max), reads=[tm, dstT], writes=[dstT])

    def mla(self, l, path):
        k = self.k
        W = self.W
        smp = path == "s"
        hT = self.hT[path]
        scale = 96 ** -0.5
        k.begin_phase()
        self.ao_x = k.sb("ao_x", [128, 8, 512], F32)
        self.st_buf = [k.sb(f"st{i}", [128, 256], F32) for i in range(2)]
        self.st_n = 0
        self.sm_t = k.sb("sm_t", [1, 1], F32)
        self.sm_q = k.sb("sm_q", [1, 1], F32)
        self.sm_k = k.sb("sm_k", [1, 8], F32)
        self.sm_m = k.sb("sm_m", [1, 1], F32)
        self.sm_sq = k.sb("sm_sq", [96, 512], BF16)
        negc = k.sb("negc", [128, 1], F32)
        win = k.sb("m_win", [128, 8, 672], BF16)
        k.dma("pool", win[:], W["w_in"][l].rearrange("(kc p) n -> p kc n", p=128)[:, :, 0:672], writes=[win])
        wqb = k.sb("m_wqb", [128, 3, 768], BF16)
        k.dma("pool", wqb[:], W["mla_w_qb"][l].rearrange("(kc p) n -> p kc n", p=128), writes=[wqb])
        wkvb = k.sb("m_wkvb", [128, 2, 1024], BF16)
        k.dma("pool", wkvb[:], W["mla_w_kvb"][l].rearrange("(kc p) n -> p kc n", p=128), writes=[wkvb])
        wout = k.sb("m_wout", [64, 4, D], BF16)
        wkr = k.sb("m_wkr", [128, 8, 96], BF16)
        k.op("pool", lambda e: e.memset(wkr[:], 0.0), writes=[wkr])
        k.op("pool", lambda e: e.tensor_copy(out=wkr[:, :, 64:96], in_=win[:, :, 640:672]), reads=[win], writes=[wkr])
        if smp:
            rope = k.sb("m_rope", [96, 2, NST], F32)
            k.dma("sp", rope[:], W["rope_m"][:, :, :], writes=[rope])
            wkr_r = k.sb("m_wkr_r", [128, 8, 96], BF16)
            k.op("pool", lambda e: e.memset(wkr_r[:], 0.0), writes=[wkr_r])
            wqb_r = k.sb("m_wqb_r", [128, 3, 768], BF16)
            k.op("pool", lambda e: e.memset(wqb_r[:], 0.0), writes=[wqb_r])
            for a in range(2):
                b0 = 64 + a * 16
                k.op("pool", lambda e, b0=b0: e.tensor_scalar(out=wkr_r[:, :, b0:b0 + 8], in0=wkr[:, :, b0 + 8:b0 + 16], scalar1=-1.0,
                                                               scalar2=None, op0=ALU.mult), reads=[wkr], writes=[wkr_r])
                k.op("pool", lambda e, b0=b0: e.tensor_copy(out=wkr_r[:, :, b0 + 8:b0 + 16], in_=wkr[:, :, b0:b0 + 8]), reads=[wkr], writes=[wkr_r])
                for h in range(8):
                    c0 = h * 96 + b0
                    k.op("pool", lambda e, c0=c0: e.tensor_scalar(out=wqb_r[:, :, c0:c0 + 8], in0=wqb[:, :, c0 + 8:c0 + 16], scalar1=-1.0,
                                                                   scalar2=None, op0=ALU.mult), reads=[wqb], writes=[wqb_r])
                    k.op("pool", lambda e, c0=c0: e.tensor_copy(out=wqb_r[:, :, c0 + 8:c0 + 16], in_=wqb[:, :, c0:c0 + 8]), reads=[wqb], writes=[wqb_r])
        sq = k.sb("m_sq", [128, 3, 512], BF16)
        rstd = k.sb("m_rstd", [128, 512], F32)
        cf = k.sb("m_cf", [128, 3, 512], F32)
        krf = k.sb("m_krf", [96, 512], F32)
        krb = k.sb("m_krb", [96, 512], F32)
        qtmp, qtmp2 = krf, krb
        qTs = [k.sb(f"m_qT{i}", [96, 512], BF16) for i in range(4)]
        negcs = [k.sb(f"m_negc{i}", [128, 1], F32) for i in range(4)]
        smqs = [k.sb(f"m_smq{i}", [1, 1], F32) for i in range(4)]
        smts = [k.sb(f"m_smt{i}", [1, 1], F32) for i in range(4)]
        qT = qTs[0]
        PT = [k.sb(f"m_PT{i}", [128, 512], BF16) for i in range(3)]
        mixH = k.sb("m_mixH", [64, 4, 512], BF16)
        rrow = k.sb("m_rrow", [65, 512], F32)
        rbc = k.sb("m_rbc", [64, 512], F32)
        for (s0, T_) in self.seqs(path):
            nk = T_ + (PAST if smp else 0)
            nkt = nk // 128
            cqn = k.sb(f"m_cqn{s0}", [128, 3, T_], BF16)
            ckvn = k.sb(f"m_ckvn{s0}", [128, 2, nk], BF16)
            Kt = k.sb(f"m_Kt{s0}", [96, 4, nk], BF16)
            Vt = k.sb(f"m_Vt{s0}", [128, nkt, 4, 65], BF16)
            krall = k.sb(f"m_krall{s0}", [96, nk], BF16)
            k.op("pool", lambda e: e.memset(Vt[:], 1.0), writes=[Vt])
            groups = [(g0, min(512, T_ - g0)) for g0 in range(0, T_, 512)]
            seq_i = s0 // 256
            for (g0, n) in groups:
                a0 = s0 + g0
                for (c0, nch, gcol, dst, nfeat) in [(0, 3, 72, cqn, 384), (384, 2, 78, ckvn, 256)]:
                    pss = []
                    for c in range(nch):
                        ps = k.psum()
                        pss.append(ps)
                        for kc in range(8):
                            k.mm(ps, ps[:, 0:n], win[:, kc, c0 + c * 128:c0 + (c + 1) * 128], hT[:, kc, a0:a0 + n], kc == 0, kc == 7,
                                 reads=[win, hT.sub((a0 // 512) * 512)])
                    for c in range(nch):
                        k.op("act", lambda e, c=c, ps=pss[c]: e.activation(out=cf[:, c, 0:n], in_=ps[:, 0:n], func=AF.Copy), reads=[pss[c]], writes=[cf])
                    gcl = gcol + l * nch
                    self.rmsnorm((lambda c: cf[:, c, 0:n], [cf]), n, lambda c: self.vecT[:, gcl + c:gcl + c + 1], None,
                                 (lambda c: cf[:, c, 0:n], [cf]), sq, rstd, nfeat_chunks=nch, nfeat=nfeat)
                    for c in range(nch):
                        k.op("pool", lambda e, c=c, dst=dst: e.tensor_copy(out=dst[:, c, g0:g0 + n], in_=cf[:, c, 0:n]), reads=[cf], writes=[dst])
                    if dst is ckvn and not smp:
                        for c in range(2):
                            self.store_tok(lambda tt, c=c: cf[:, c, tt * 128:(tt + 1) * 128], [cf], 128, n,
                                           lambda tt, c=c: self.O["n_ckv"][seq_i, l, g0 + tt * 128:g0 + (tt + 1) * 128, c * 128:(c + 1) * 128])
                pa = k.psum()
                for kc in range(8):
                    k.mm(pa, pa[0:96, 0:n], wkr[:, kc, :], hT[:, kc, a0:a0 + n], kc == 0, kc == 7, reads=[wkr, hT.sub((a0 // 512) * 512)])
                if smp:
                    pb = k.psum()
                    for kc in range(8):
                        k.mm(pb, pb[0:96, 0:n], wkr_r[:, kc, :], hT[:, kc, a0:a0 + n], kc == 0, kc == 7, reads=[wkr_r, hT.sub((a0 // 512) * 512)])
                    k.op("dve", lambda e: e.tensor_tensor(out=krf[64:96, 0:n], in0=pa[64:96, 0:n], in1=rope[64:96, 0, g0:g0 + n], op=ALU.mult),
                         reads=[pa, rope], writes=[krf])
                    k.op("dve", lambda e: e.tensor_tensor(out=krb[64:96, 0:n], in0=pb[64:96, 0:n], in1=rope[64:96, 1, g0:g0 + n], op=ALU.mult),
                         reads=[pb, rope], writes=[krb])
                    k.op("pool", lambda e: e.tensor_tensor(out=krall[64:96, g0:g0 + n], in0=krf[64:96, 0:n], in1=krb[64:96, 0:n], op=ALU.add),
                         reads=[krf, krb], writes=[krall])
                else:
                    k.op("dve", lambda e: e.tensor_copy(out=krf[0:96, 0:n], in_=pa[0:96, 0:n]), reads=[pa], writes=[krf])
                    k.op("pool", lambda e: e.tensor_copy(out=krall[64:96, g0:g0 + n], in_=krf[64:96, 0:n]), reads=[krf], writes=[krall])
                    self.store_tok(lambda tt: krf[0:96, tt * 128:(tt + 1) * 128], [krf], 96, n,
                                   lambda tt: self.O["n_kr"][seq_i, l, g0 + tt * 128:g0 + (tt + 1) * 128, :], f0=64)
            if smp:
                kin = k.sb("m_kin", [128, 96], F32)
                k.op("pool", lambda e: e.memset(kin[:], 0.0), writes=[kin])
                for i in range(4):
                    ci = cf.t[:, i % 2, 0:256]
                    k.dma("sp", ci, W["c_ckv"][l, i * 128:(i + 1) * 128, :], writes=[cf])
                    ps = k.psum()
                    for c in range(2):
                        k.tr(ps, ps[:, c * 128:(c + 1) * 128], ci[:, c * 128:(c + 1) * 128], self.ident, reads=[cf, self.masks])
                    k.op("act", lambda e, ps=ps, i=i: e.activation(out=ckvn[:, :, T_ + i * 128:T_ + (i + 1) * 128],
                                                                   in_=ps[:, 0:256].rearrange("p (c t) -> p c t", c=2), func=AF.Copy),
                         reads=[ps], writes=[ckvn])
                    k.dma("sp", kin[:, 64:96], W["c_kr"][l, i * 128:(i + 1) * 128, :], writes=[kin])
                    ps2 = k.psum()
                    k.tr(ps2, ps2[0:96, 0:128], kin[:, :], self.ident, reads=[kin, self.masks])
                    k.op("dve", lambda e, ps2=ps2, i=i: e.tensor_copy(out=krall[64:96, T_ + i * 128:T_ + (i + 1) * 128], in_=ps2[64:96, 0:128]),
                         reads=[ps2], writes=[krall])
            kgroups = [(g0, min(512, nk - g0)) for g0 in range(0, nk, 512)]
            for hg in range(2):
                k.dma("pool", wout[:], W["w_out"][l][hg * 256:(hg + 1) * 256, :].rearrange("(h p) n -> p h n", p=64), writes=[wout])
                for (g0, n) in kgroups:
                    for hh in range(4):
                        h = hg * 4 + hh
                        ps = k.psum()
                        for c in range(2):
                            k.mm(ps, ps[0:64, 0:n], wkvb[:, c, h * 128:h * 128 + 64], ckvn[:, c, g0:g0 + n], c == 0, c == 1, reads=[wkvb, ckvn])
                        eng = "act" if hh % 2 == 0 else "dve"
                        if eng == "act":
                            k.op("act", lambda e, ps=ps, hh=hh: e.activation(out=Kt[0:64, hh, g0:g0 + n], in_=ps[0:64, 0:n], func=AF.Copy),
                                 reads=[ps], writes=[Kt.sub(hh)])
                        else:
                            k.op("dve", lambda e, ps=ps, hh=hh: e.tensor_copy(out=Kt[0:64, hh, g0:g0 + n], in_=ps[0:64, 0:n]), reads=[ps], writes=[Kt.sub(hh)])
                        if hg == 0:
                            pass
                        k.op("pool", lambda e, hh=hh: e.tensor_copy(out=Kt[64:96, hh, g0:g0 + n], in_=krall[64:96, g0:g0 + n]),
                             reads=[krall], writes=[Kt.sub(hh)])
                        self.sq_max(Kt[0:96, hh, g0:g0 + n], 96, n, self.sm_k[0:1, hh:hh + 1], g0 == 0, [Kt.sub(hh)])
                    for kt in range(g0 // 128, (g0 + n) // 128):
                        ps = k.psum()
                        for c in range(2):
                            k.mm(ps, ps[:, 0:256].rearrange("p (h d) -> p h d", h=4),
                                 ckvn[:, c, kt * 128:(kt + 1) * 128],
                                 wkvb[:, c, hg * 512:(hg + 1) * 512].rearrange("p (h d) -> p h d", h=4)[:, :, 64:128], c == 0, c == 1,
                                 reads=[wkvb, ckvn])
                        k.op("act", lambda e, ps=ps, kt=kt: e.activation(out=Vt[:, kt, :, 0:64], in_=ps[:, 0:256].rearrange("p (h d) -> p h d", h=4),
                                                                         func=AF.Copy), reads=[ps], writes=[Vt])
                npt = 0
                for (g0, n) in groups:
                    nqt = n // 128
                    for hh in range(4):
                        h = hg * 4 + hh
                        pa = k.psum()
                        for c in range(3):
                            k.mm(pa, pa[0:96, 0:n], wqb[:, c, h * 96:(h + 1) * 96], cqn[:, c, g0:g0 + n], c == 0, c == 2, reads=[wqb, cqn])
                        if smp:
                            pb = k.psum()
                            for c in range(3):
                                k.mm(pb, pb[0:96, 0:n], wqb_r[:, c, h * 96:(h + 1) * 96], cqn[:, c, g0:g0 + n], c == 0, c == 2, reads=[wqb_r, cqn])
                            k.op("dve", lambda e: e.tensor_tensor(out=qtmp[:, 0:n], in0=pa[0:96, 0:n], in1=rope[:, 0, g0:g0 + n], op=ALU.mult),
                                 reads=[pa, rope], writes=[qtmp])
                            k.op("dve", lambda e: e.tensor_tensor(out=qtmp2[:, 0:n], in0=pb[0:96, 0:n], in1=rope[:, 1, g0:g0 + n], op=ALU.mult),
                                 reads=[pb, rope], writes=[qtmp2])
                            k.op("pool", lambda e: e.tensor_tensor(out=qTs[hh][:, 0:n], in0=qtmp[:, 0:n], in1=qtmp2[:, 0:n], op=ALU.add),
                                 reads=[qtmp, qtmp2], writes=[qTs[hh]])
                        else:
                            k.op("act", lambda e: e.activation(out=qTs[hh][:, 0:n], in_=pa[0:96, 0:n], func=AF.Copy), reads=[pa], writes=[qTs[hh]])
                        self.sm_q = smqs[hh]; self.sm_t = smts[hh]
                        self.sq_max(qTs[hh][0:96, 0:n], 96, n, self.sm_q[0:1, 0:1], True, [qTs[hh]], dstT=self.sm_q)
                        self.softmax_bound(self.sm_q[0:1, 0:1], self.sm_k[0:1, hh:hh + 1], scale, negcs[hh])
                    for hh in range(4):
                        h = hg * 4 + hh
                        qT = qTs[hh]
                        negc = negcs[hh]
                        pos = k.psum_hold(1)
                        po = pos[0]
                        def qk(kt):
                            ps = k.psum()
                            k.mm(ps, ps[:, 0:n], Kt[:, hh, kt * 128:(kt + 1) * 128], qT[:, 0:n], True, True, reads=[Kt.sub(hh), qT])
                            return ps
                        pend = qk(0)
                        for kt in range(nkt):
                            ps = pend
                            if kt + 1 < nkt:
                                pend = qk(kt + 1)
                            pt = PT[npt % 3]
                            npt += 1
                            k.op("act", lambda e, pt=pt, ps=ps: e.activation(out=pt[:, 0:n], in_=ps[:, 0:n], func=AF.Exp, bias=negc[:, 0:1], scale=scale),
                                 reads=[ps, negc], writes=[pt])
                            k.mm(po, po[0:65, 0:n], Vt[:, kt, hh, :], pt[:, 0:n], kt == 0, kt == nkt - 1, reads=[pt, Vt])
                        k.op("act", lambda e, po=po: e.activation(out=rrow[64:65, 0:n], in_=po[64:65, 0:n], func=AF.Copy), reads=[po], writes=[rrow])
                        pbc = k.psum()
                        k.mm(pbc, pbc[0:64, 0:n], self.ones_f[64:65, 0:64], rrow[64:65, 0:n], True, True, reads=[self.ones_f, rrow])
                        k.op("dve", lambda e, pbc=pbc: e.reciprocal(out=rbc[:, 0:n], in_=pbc[0:64, 0:n]), reads=[pbc], writes=[rbc])
                        k.op("dve", lambda e, po=po, hh=hh: e.tensor_tensor(out=mixH[:, hh, 0:n], in0=po[0:64, 0:n], in1=rbc[:, 0:n], op=ALU.mult),
                             reads=[po, rbc], writes=[mixH])
                        k.psum_release(pos)
                    self.ao_k = 64
                    self.apply_out(l, path, s0 + g0, n, (lambda c: mixH[:, c, 0:n], [mixH]), 4, wout, 0)
                    self.ao_k = 128
        k.end_phase()

    def swa(self, l, path):
        k = self.k
        W = self.W
        smp = path == "s"
        hT = self.hT[path]
        scale = 64 ** -0.5
        k.begin_phase()
        self.ao_x = k.sb("ao_x", [128, 8, 512], F32)
        self.st_buf = [k.sb(f"st{i}", [128, 256], F32) for i in range(2)]
        self.st_n = 0
        self.sm_t = k.sb("sm_t", [1, 1], F32)
        self.sm_q = k.sb("sm_q", [1, 1], F32)
        self.sm_k = k.sb("sm_k", [1, 8], F32)
        self.sm_m = k.sb("sm_m", [1, 1], F32)
        self.sm_sq = k.sb("sm_sq", [96, 512], BF16)
        negc = k.sb("negc", [128, 1], F32)
        psink = k.sb("psink", [128, 8], F32)
        wsw = k.sb("s_w", [128, 8, 768], BF16)
        k.dma("pool", wsw[:], W["w_in"][l].rearrange("(kc p) n -> p kc n", p=128)[:, :, 672:1440], writes=[wsw])
        wout = k.sb("s_wout", [64, 8, D], BF16)
        k.dma("pool", wout[:], W["w_out"][l][512:1024, :].rearrange("(h p) n -> p h n", p=64), writes=[wout])
        if smp:
            rope = k.sb("s_rope", [64, 2, NST], F32)
            k.dma("sp", rope[:], W["rope_s"][:, :, :], writes=[rope])
            wr = k.sb("s_wr", [128, 8, 640], BF16)
            wv = wsw[:, :, 0:640].rearrange("p k (h a z d) -> p k h a z d", h=10, a=2, z=2)
            wrv = wr[:, :, :].rearrange("p k (h a z d) -> p k h a z d", h=10, a=2, z=2)
            for kc in range(8):
                k.op("pool", lambda e, kc=kc: e.tensor_scalar(out=wrv[:, kc, :, :, 0, :], in0=wv[:, kc, :, :, 1, :], scalar1=-1.0, scalar2=None, op0=ALU.mult),
                     reads=[wsw], writes=[wr])
                k.op("pool", lambda e, kc=kc: e.tensor_copy(out=wrv[:, kc, :, :, 1, :], in_=wv[:, kc, :, :, 0, :]), reads=[wsw], writes=[wr])
        kf = k.sb("s_kf", [64, 512], F32)
        kb_ = k.sb("s_kb", [64, 512], F32)

        def proj_rope(col0, a0, n, g0, dst_ap, dst_w, fp_out=None):
            pa = k.psum()
            for kc in range(8):
                k.mm(pa, pa[0:64, 0:n], wsw[:, kc, col0:col0 + 64], hT[:, kc, a0:a0 + n], kc == 0, kc == 7, reads=[wsw, hT.sub((a0 // 512) * 512)])
            if smp:
                pb = k.psum()
                for kc in range(8):
                    k.mm(pb, pb[0:64, 0:n], wr[:, kc, col0:col0 + 64], hT[:, kc, a0:a0 + n], kc == 0, kc == 7, reads=[wr, hT.sub((a0 // 512) * 512)])
                k.op("dve", lambda e: e.tensor_tensor(out=kf[:, 0:n], in0=pa[0:64, 0:n], in1=rope[:, 0, g0:g0 + n], op=ALU.mult), reads=[pa, rope], writes=[kf])
                k.op("dve", lambda e: e.tensor_tensor(out=kb_[:, 0:n], in0=pb[0:64, 0:n], in1=rope[:, 1, g0:g0 + n], op=ALU.mult), reads=[pb, rope], writes=[kb_])
                k.op("pool", lambda e: e.tensor_tensor(out=dst_ap, in0=kf[:, 0:n], in1=kb_[:, 0:n], op=ALU.add), reads=[kf, kb_], writes=dst_w)
            else:
                if fp_out is not None:
                    k.op("dve", lambda e: e.tensor_copy(out=kf[:, 0:n], in_=pa[0:64, 0:n]), reads=[pa], writes=[kf])
                    k.op("pool", lambda e: e.tensor_copy(out=dst_ap, in_=kf[:, 0:n]), reads=[kf], writes=dst_w)
                else:
                    k.op("act", lambda e: e.activation(out=dst_ap, in_=pa[0:64, 0:n], func=AF.Copy), reads=[pa], writes=dst_w)

        qs = k.sb("s_q", [64, 4, 512], BF16)
        PT = [k.sb(f"s_PT{i}", [128, 4, 128], BF16) for i in range(3)]
        mixH = k.sb("s_mixH", [64, 8, 512], BF16)
        rrow = k.sb("s_rrow", [65, 512], F32)
        rbc = k.sb("s_rbc", [64, 512], F32)
        for (s0, T_) in self.seqs(path):
            nk = T_ + (PAST if smp else 0)
            nkt = nk // 128
            nlt = T_ // 128
            seq_i = s0 // 256
            Ks = k.sb(f"s_K{s0}", [64, 2, nk], BF16)
            Vs = k.sb(f"s_V{s0}", [128, nkt, 2, 65], BF16)
            k.op("pool", lambda e: e.memset(Vs[:], 1.0), writes=[Vs])
            vst = [k.sb(f"s_vst{s0}_{i}", [128, 128], F32) for i in range(2)]
            groups = [(g0, min(512, T_ - g0)) for g0 in range(0, T_, 512)]
            for (g0, n) in groups:
                a0 = s0 + g0
                for kvh in range(2):
                    proj_rope(512 + kvh * 64, a0, n, g0, Ks[:, kvh, g0:g0 + n], [Ks.sub(kvh)], fp_out=(None if smp else True))
                    if not smp:
                        self.store_tok(lambda tt: kf[0:64, tt * 128:(tt + 1) * 128], [kf], 64, n,
                                       lambda tt, kvh=kvh: self.O["n_sk"][seq_i, l, g0 + tt * 128:g0 + (tt + 1) * 128, kvh * 64:(kvh + 1) * 64])
                    self.sq_max(Ks[0:64, kvh, g0:g0 + n], 64, n, self.sm_k[0:1, kvh:kvh + 1], g0 == 0, [Ks.sub(kvh)])
                for tt in range(n // 128):
                    kt = (g0 // 128) + tt
                    ps = k.psum()
                    for kc in range(8):
                        k.mm(ps, ps[:, 0:128], hT[:, kc, a0 + tt * 128:a0 + (tt + 1) * 128], wsw[:, kc, 640:768], kc == 0, kc == 7,
                             reads=[wsw, hT.sub((a0 // 512) * 512)])
                    k.op("act", lambda e, ps=ps, kt=kt: e.activation(out=Vs[:, kt, :, 0:64], in_=ps[:, 0:128].rearrange("p (h d) -> p h d", h=2), func=AF.Copy),
                         reads=[ps], writes=[Vs])
                    if not smp:
                        vt = vst[tt % 2]
                        k.op("dve", lambda e, ps=ps, vt=vt: e.tensor_copy(out=vt[:], in_=ps[:, 0:128]), reads=[ps], writes=[vt])
                        k.dma("sp", self.O["n_sv"][seq_i, l, g0 + tt * 128:g0 + (tt + 1) * 128, :], vt[:], reads=[vt], writes=[])
            if smp:
                cin = [k.sb(f"s_cin{i}", [128, 128], F32) for i in range(2)]
                cvn = [k.sb(f"s_cvn{i}", [128, 128], F32) for i in range(2)]
                for i in range(4):
                    ci = cin[i % 2]
                    k.dma("sp", ci[:], W["c_sk"][l, i * 128:(i + 1) * 128, :], writes=[ci])
                    for kvh in range(2):
                        ps = k.psum()
                        k.tr(ps, ps[0:64, 0:128], ci[:, kvh * 64:(kvh + 1) * 64], self.ident, reads=[ci, self.masks])
                        k.op("act", lambda e, ps=ps, kvh=kvh, i=i: e.activation(out=Ks[:, kvh, T_ + i * 128:T_ + (i + 1) * 128], in_=ps[0:64, 0:128], func=AF.Copy),
                             reads=[ps], writes=[Ks.sub(kvh)])
                    cv = cvn[i % 2]
                    k.dma("sp", cv[:], W["c_sv"][l, i * 128:(i + 1) * 128, :], writes=[cv])
                    k.op("pool", lambda e, cv=cv, i=i: e.tensor_copy(out=Vs[:, nlt + i, :, 0:64], in_=cv[:, :].rearrange("p (h d) -> p h d", h=2)),
                         reads=[cv], writes=[Vs])
                for kvh in range(2):
                    for g0 in range(T_, nk, 512):
                        self.sq_max(Ks[0:64, kvh, g0:g0 + 512], 64, 512, self.sm_k[0:1, kvh:kvh + 1], False, [Ks.sub(kvh)])
            npt = 0
            for (g0, n) in groups:
                a0 = s0 + g0
                nqt = n // 128
                for kvh in range(2):
                    for g in range(4):
                        proj_rope((kvh * 4 + g) * 64, a0, n, g0, qs[:, g, 0:n], [qs])
                        self.sq_max(qs[0:64, g, 0:n], 64, n, self.sm_q[0:1, 0:1], g == 0, [qs])
                    self.softmax_bound(self.sm_q[0:1, 0:1], self.sm_k[0:1, kvh:kvh + 1], scale, negc)
                    k.op("act", lambda e, kvh=kvh: e.activation(out=psink[:, kvh * 4:(kvh + 1) * 4], in_=self.rowv(l, "sink")[:, kvh * 4:(kvh + 1) * 4],
                                                                func=AF.Exp, bias=negc[:, 0:1], scale=1.0), reads=[self.rowbc, negc], writes=[psink])
                    for qt in range(nqt):
                        gq = g0 // 128 + qt
                        if smp:
                            kts = [(t, t - gq) for t in (gq - 1, gq, gq + 1) if 0 <= t < nlt] + [(nlt + i, 0) for i in range(4)]
                        else:
                            kts = [(t, 0) for t in range(nlt)]
                        pos = k.psum_hold(1)
                        po = pos[0]
                        def qk(kt):
                            ps = k.psum()
                            k.mm(ps, ps[:, :].rearrange("p (g q) -> p g q", g=4), Ks[:, kvh, kt * 128:(kt + 1) * 128], qs[:, :, qt * 128:(qt + 1) * 128],
                                 True, True, reads=[Ks.sub(kvh), qs])
                            return ps
                        pend = qk(kts[0][0])
                        for ki, (kt, rel) in enumerate(kts):
                            ps = pend
                            if ki + 1 < len(kts):
                                pend = qk(kts[ki + 1][0])
                            pt = PT[npt % 3]
                            npt += 1
                            k.op("act", lambda e, pt=pt, ps=ps: e.activation(out=pt[:, :, :], in_=ps[:, :].rearrange("p (g q) -> p g q", g=4), func=AF.Exp,
                                                                           bias=negc[:, 0:1], scale=scale), reads=[ps, negc], writes=[pt])
                            if rel != 0:
                                mi = 3 if rel == -1 else 1
                                for g in range(4):
                                    k.op("pool" if g % 2 else "dve", lambda e, pt=pt, g=g, mi=mi: e.tensor_tensor(out=pt[:, g, :], in0=pt[:, g, :], in1=self.masks_b[:, mi, :], op=ALU.mult),
                                         reads=[pt, self.masks_b], writes=[pt])
                            k.mm(po, po[0:65, :].rearrange("p (g q) -> p g q", g=4), Vs[:, kt, kvh, :], pt[:, :, :], ki == 0, ki == len(kts) - 1, reads=[pt, Vs])
                        for g in range(4):
                            k.op("dve", lambda e, po=po, g=g, kvh=kvh: e.tensor_scalar(out=rrow[64:65, g * 128:(g + 1) * 128], in0=po[64:65, g * 128:(g + 1) * 128],
                                                                                         scalar1=psink[64:65, kvh * 4 + g:kvh * 4 + g + 1], scalar2=None, op0=ALU.add),
                                 reads=[po, psink], writes=[rrow])
                        pbc = k.psum()
                        k.mm(pbc, pbc[0:64, :], self.ones_f[64:65, 0:64], rrow[64:65, :], True, True, reads=[self.ones_f, rrow])
                        k.op("dve", lambda e, pbc=pbc: e.reciprocal(out=rbc[:, :], in_=pbc[0:64, :]), reads=[pbc], writes=[rbc])
                        k.op("dve", lambda e, po=po, kvh=kvh, qt=qt: e.tensor_tensor(
                            out=mixH[:, kvh * 4:(kvh + 1) * 4, qt * 128:(qt + 1) * 128], in0=po[0:64, :].rearrange("p (g q) -> p g q", g=4),
                            in1=rbc[:, :].rearrange("p (g q) -> p g q", g=4), op=ALU.mult), reads=[po, rbc], writes=[mixH])
                        k.psum_release(pos)
                self.ao_k = 64
                self.apply_out(l, path, s0 + g0, n, (lambda c: mixH[:, c, 0:n], [mixH]), 8, wout, 0)
                self.ao_k = 128
        k.end_phase()

    def gdn(self, l, path):
        k = self.k
        W = self.W
        smp = path == "s"
        hT = self.hT[path]
        M = self.masks
        k.begin_phase()
        self.ao_x = k.sb("ao_x", [128, 8, 512], F32)
        win_all = W["w_in"][l].rearrange("(kc p) n -> p kc n", p=128)
        wg = k.sb("g_wg", [128, 8, 16], BF16)
        k.dma("pool", wg[:], win_all[:, :, 3488:3504], writes=[wg])
        wout = k.sb("g_wout", [128, 4, D], BF16)
        k.dma("pool", wout[:], W["w_out"][l].rearrange("(c p) n -> p c n", p=128)[:, 8:12, :], writes=[wout])
        wh = [k.sb(f"g_wh{i}", [128, 8, 512], BF16) for i in range(2)]
        nalog = k.sb("g_nalog", [128, 8], F32)
        k.op("act", lambda e: e.activation(out=nalog[:], in_=self.rowv(l, "alog"), func=AF.Exp), reads=[self.rowbc], writes=[nalog])
        k.op("dve", lambda e: e.tensor_scalar(out=nalog[:], in0=nalog[:], scalar1=-1.0, scalar2=None, op0=ALU.mult), reads=[nalog], writes=[nalog])
        nwh = 0
        for (s0, T_) in self.seqs(path):
            nch = T_ // 128
            seq_i = s0 // 256
            gg = k.sb(f"g_g{s0}", [128, nch, 8], F32)
            bt = k.sb(f"g_b{s0}", [128, nch, 8], F32)
            for ci in range(nch):
                a0 = s0 + ci * 128
                ps = k.psum()
                for kc in range(8):
                    k.mm(ps, ps[:, 0:16], hT[:, kc, a0:a0 + 128], wg[:, kc, :], kc == 0, kc == 7, reads=[wg, hT.sub((a0 // 512) * 512)])
                k.op("dve", lambda e, ps=ps, ci=ci: e.tensor_tensor(out=gg[:, ci, :], in0=ps[:, 0:8], in1=self.rowv(l, "dtb"), op=ALU.add),
                     reads=[ps, self.rowbc], writes=[gg])
                k.op("act", lambda e, ps=ps, ci=ci: e.activation(out=bt[:, ci, :], in_=ps[:, 8:16], func=AF.Sigmoid), reads=[ps], writes=[bt])
            k.op("dve", lambda e: e.tensor_scalar(out=gg[:], in0=gg[:], scalar1=60.0, scalar2=None, op0=ALU.min), reads=[gg], writes=[gg])
            k.op("act", lambda e: e.activation(out=gg[:], in_=gg[:], func=AF.Exp), reads=[gg], writes=[gg])
            k.op("act", lambda e: e.activation(out=gg[:], in_=gg[:], func=AF.Ln, bias=1.0, scale=1.0), reads=[gg], writes=[gg])
            for ci in range(nch):
                k.op("dve", lambda e, ci=ci: e.tensor_tensor(out=gg[:, ci, :], in0=gg[:, ci, :], in1=nalog[:], op=ALU.mult), reads=[gg, nalog], writes=[gg])
            if GSTOP <= 1:
                continue
            pre = k.sb(f"g_pre{s0}", [128, 3, T_ + 4], F32)
            acc = k.sb(f"g_acc{s0}", [128, 3, T_], F32)
            osum = k.sb(f"g_osum{s0}", [128, nch, 128], F32)
            mixT = k.sb(f"g_mixT{s0}", [128, T_], BF16)
            sqf = k.sb(f"g_sqf{s0}", [128, 512], F32)
            rst = k.sb(f"g_rst{s0}", [128, 512], F32)
            names = ["gcb", "X", "DT", "Dm", "N", "NT", "Bm", "BmT", "B2", "B2T", "Wm", "W2", "AT", "kbg", "kdec", "vb",
                     "u", "wkT", "qdec", "vnew", "E", "gones", "gz", "on"]
            tqs = [{nm: k.sb(f"g_{nm}{s0}_{d}", [128, 128], F32) for nm in names} for d in range(2)]
            scs = [k.sb(f"g_sc{s0}_{d}", [128, 8], F32) for d in range(2)]
            Ss = [k.sb(f"g_S{s0}_{d}", [128, 128], F32) for d in range(2)]
            osums = [osum, k.sb(f"g_osumb{s0}", [128, nch, 128], F32)]
            tq, sc = tqs[0], scs[0]
            k.op("pool", lambda e: e.memset(pre[:], 0.0), writes=[pre])
            groups = [(g0, min(512, T_ - g0)) for g0 in range(0, T_, 512)]
            for h in range(4):
                w = wh[nwh % 2]
                nwh += 1
                for wi, c0 in enumerate([1440, 1952, 2464, 2976]):
                    k.dma("pool", w[:, :, wi * 128:(wi + 1) * 128], win_all[:, :, c0 + h * 128:c0 + (h + 1) * 128], writes=[w])
                for (g0, n) in groups:
                    a0 = s0 + g0
                    for wi in range(3):
                        ps = k.psum()
                        for kc in range(8):
                            k.mm(ps, ps[:, 0:n], w[:, kc, wi * 128:(wi + 1) * 128], hT[:, kc, a0:a0 + n], kc == 0, kc == 7, reads=[w, hT.sub((a0 // 512) * 512)])
                        k.op("act", lambda e, ps=ps, wi=wi: e.activation(out=pre[:, wi, 2 + g0:2 + g0 + n], in_=ps[:, 0:n], func=AF.Copy), reads=[ps], writes=[pre.sub(wi)])
                for wi in range(3):
                    eng = "dve"
                    cch = wi * 4 + h
                    for tap in range(5):
                        wcol = self.convT[:, l * 60 + tap * 12 + cch:l * 60 + tap * 12 + cch + 1]
                        if tap == 0:
                            k.op(eng, lambda e, wi=wi, wcol=wcol: e.tensor_scalar(out=acc[:, wi, :], in0=pre[:, wi, 0:T_], scalar1=wcol, scalar2=None, op0=ALU.mult),
                                 reads=[pre.sub(wi), self.convT], writes=[acc.sub(wi)])
                        else:
                            k.op(eng, lambda e, wi=wi, wcol=wcol, tap=tap: e.scalar_tensor_tensor(out=acc[:, wi, :], in0=pre[:, wi, tap:tap + T_], scalar=wcol,
                                                                                                 in1=acc[:, wi, :], op0=ALU.mult, op1=ALU.add),
                                 reads=[pre.sub(wi), self.convT, acc.sub(wi)], writes=[acc.sub(wi)])
                    k.op("act", lambda e, wi=wi: e.activation(out=acc[:, wi, :], in_=acc[:, wi, :], func=AF.Silu), reads=[acc.sub(wi)], writes=[acc.sub(wi)])
                if GSTOP <= 2:
                    continue
                for wi in range(2):
                    for (g0, n) in groups:
                        k.op("act", lambda e, wi=wi: e.activation(out=sqf[:, 0:n], in_=acc[:, wi, g0:g0 + n], func=AF.Square), reads=[acc.sub(wi)], writes=[sqf])
                        ps = k.psum()
                        k.mm(ps, ps[:, 0:n], self.ones_f[:, :], sqf[:, 0:n], True, True, reads=[self.ones_f, sqf])
                        k.op("act", lambda e, ps=ps: e.activation(out=rst[:, 0:n], in_=ps[:, 0:n], func=AF.Sqrt, bias=self.epsc[:, 0:1], scale=1.0),
                             reads=[ps, self.epsc], writes=[rst])
                        k.op("dve", lambda e: e.reciprocal(out=rst[:, 0:n], in_=rst[:, 0:n]), reads=[rst], writes=[rst])
                        if wi == 0:
                            k.op("dve", lambda e, wi=wi: e.scalar_tensor_tensor(out=acc[:, wi, g0:g0 + n], in0=acc[:, wi, g0:g0 + n], scalar=128 ** -0.5, in1=rst[:, 0:n],
                                                                                 op0=ALU.mult, op1=ALU.mult), reads=[acc.sub(wi), rst], writes=[acc.sub(wi)])
                        else:
                            k.op("pool", lambda e, wi=wi: e.tensor_tensor(out=acc[:, wi, g0:g0 + n], in0=acc[:, wi, g0:g0 + n], in1=rst[:, 0:n], op=ALU.mult),
                                 reads=[acc.sub(wi), rst], writes=[acc.sub(wi)])
                if GSTOP <= 3:
                    continue
                def chain(d, tq, sc, S, osum):
                    col = d * 4 + h
                    if smp:
                        k.dma("sp", S[:], W["c_st"][l, d, h], writes=[S])
                        yield
                    else:
                        k.op("pool", lambda e: e.memset(S[:], 0.0), writes=[S])
                        yield
                    order = range(nch) if d == 0 else range(nch - 1, -1, -1)
                    mcs = 1 if d == 0 else 3
                    m_dt_incl = 1 if d == 0 else 3
                    m_d_strict = 4 if d == 0 else 2
                    last = 127 if d == 0 else 0
                    for ci in order:
                        c0 = ci * 128
                        kT_c = acc[:, 1, c0:c0 + 128]
                        qT_c = acc[:, 0, c0:c0 + 128]
                        vT_c = acc[:, 2, c0:c0 + 128]
                        gcol = gg[:, ci, col:col + 1]
                        bcol = bt[:, ci, col:col + 1]
                        rd = [acc.sub(0), acc.sub(1), acc.sub(2)]
                        t = tq
                        yield
                        p1 = k.psum()
                        k.mm(p1, p1[:, 0:128], kT_c, kT_c, True, True, reads=rd)
                        yield
                        k.mm(p1, p1[:, 128:256], kT_c, qT_c, True, True, reads=rd)
                        yield
                        k.tr(p1, p1[:, 256:384], kT_c, self.ident, reads=rd + [M])
                        yield
                        k.tr(p1, p1[:, 384:512], vT_c, self.ident, reads=rd + [M])
                        yield
                        yield
                        k.op("dve", lambda e: e.tensor_scalar(out=t["gones"][:], in0=self.ones_f[:], scalar1=gcol, scalar2=None, op0=ALU.mult),
                             reads=[gg, self.ones_f], writes=[t["gones"]])
                        yield
                        p2 = k.psum()
                        k.mm(p2, p2[:, 0:128], t["gones"][:], M[:, mcs, :], True, True, reads=[t["gones"], M])
                        yield
                        k.mm(p2, p2[:, 128:129], M[:, mcs, :], gcol, True, True, reads=[M, gg])
                        yield
                        k.op("act", lambda e: e.activation(out=t["gcb"][:], in_=p2[:, 0:128], func=AF.Copy), reads=[p2], writes=[t["gcb"]])
                        yield
                        k.op("dve", lambda e: e.tensor_copy(out=sc[:, 0:1], in_=p2[:, 128:129]), reads=[p2], writes=[sc])
                        yield
                        yield
                        k.op("dve", lambda e: e.tensor_scalar(out=t["X"][:], in0=t["gcb"][:], scalar1=sc[:, 0:1], scalar2=None, op0=ALU.subtract),
                             reads=[t["gcb"], sc], writes=[t["X"]])
                        yield
                        k.op("dve", lambda e: e.tensor_scalar(out=t["DT"][:], in0=t["X"][:], scalar1=0.0, scalar2=None, op0=ALU.min), reads=[t["X"]], writes=[t["DT"]])
                        yield
                        k.op("act", lambda e: e.activation(out=t["DT"][:], in_=t["DT"][:], func=AF.Exp), reads=[t["DT"]], writes=[t["DT"]])
                        yield
                        k.op("pool", lambda e: e.tensor_scalar(out=t["Dm"][:], in0=t["X"][:], scalar1=-1.0, scalar2=0.0, op0=ALU.mult, op1=ALU.min),
                             reads=[t["X"]], writes=[t["Dm"]])
                        yield
                        k.op("act", lambda e: e.activation(out=t["Dm"][:], in_=t["Dm"][:], func=AF.Exp), reads=[t["Dm"]], writes=[t["Dm"]])
                        yield
                        k.op("pool", lambda e: e.tensor_tensor(out=t["DT"][:], in0=t["DT"][:], in1=M[:, m_dt_incl, :], op=ALU.mult), reads=[t["DT"], M], writes=[t["DT"]])
                        yield
                        k.op("pool", lambda e: e.tensor_tensor(out=t["Dm"][:], in0=t["Dm"][:], in1=M[:, m_d_strict, :], op=ALU.mult), reads=[t["Dm"], M], writes=[t["Dm"]])
                        yield
                        yield
                        k.op("act", lambda e: e.activation(out=t["E"][:], in_=t["gcb"][:], func=AF.Exp), reads=[t["gcb"]], writes=[t["E"]])
                        yield
                        k.op("act", lambda e: e.activation(out=sc[:, 1:2], in_=sc[:, 0:1], func=AF.Exp), reads=[sc], writes=[sc])
                        yield
                        k.op("dve", lambda e: e.tensor_tensor(out=sc[:, 1:2], in0=sc[:, 1:2], in1=bcol, op=ALU.mult), reads=[sc, bt], writes=[sc])
                        yield
                        k.op("dve", lambda e: e.tensor_tensor(out=sc[:, 2:3], in0=t["gcb"][:, last:last + 1], in1=sc[:, 0:1], op=ALU.subtract),
                             reads=[t["gcb"], sc], writes=[sc])
                        yield
                        k.op("act", lambda e: e.activation(out=sc[:, 2:3], in_=sc[:, 2:3], func=AF.Exp), reads=[sc], writes=[sc])
                        yield
                        yield
                        k.op("dve", lambda e: e.scalar_tensor_tensor(out=t["N"][:], in0=p1[:, 0:128], scalar=bcol, in1=t["Dm"][:], op0=ALU.mult, op1=ALU.mult),
                             reads=[p1, bt, t["Dm"]], writes=[t["N"]])
                        yield
                        k.op("pool", lambda e: e.tensor_scalar(out=t["N"][:], in0=t["N"][:], scalar1=-1.0, scalar2=None, op0=ALU.mult), reads=[t["N"]], writes=[t["N"]])
                        yield
                        k.op("dve", lambda e: e.tensor_tensor(out=t["AT"][:], in0=p1[:, 128:256], in1=t["DT"][:], op=ALU.mult), reads=[p1, t["DT"]], writes=[t["AT"]])
                        yield
                        yield
                        k.op("dve", lambda e: e.tensor_scalar(out=t["kbg"][:], in0=p1[:, 256:384], scalar1=sc[:, 1:2], scalar2=None, op0=ALU.mult), reads=[p1, sc], writes=[t["kbg"]])
                        yield
                        k.op("dve", lambda e: e.tensor_scalar(out=t["kdec"][:], in0=p1[:, 256:384], scalar1=sc[:, 2:3], scalar2=None, op0=ALU.mult), reads=[p1, sc], writes=[t["kdec"]])
                        yield
                        k.op("dve", lambda e: e.tensor_scalar(out=t["vb"][:], in0=p1[:, 384:512], scalar1=bcol, scalar2=None, op0=ALU.mult), reads=[p1, bt], writes=[t["vb"]])
                        yield
                        k.op("pool", lambda e: e.tensor_tensor(out=t["qdec"][:], in0=qT_c, in1=t["E"][:], op=ALU.mult), reads=[acc.sub(0), t["E"]], writes=[t["qdec"]])
                        yield
                        if GSTOP <= 4:
                            continue
                        yield
                        p3 = k.psum()
                        k.tr(p3, p3[:, 0:128], t["N"][:], self.ident, reads=[t["N"], M])
                        yield
                        k.op("act", lambda e: e.activation(out=t["NT"][:], in_=p3[:, 0:128], func=AF.Copy), reads=[p3], writes=[t["NT"]])
                        yield
                        yield
                        k.op("dve", lambda e: e.tensor_tensor(out=t["Wm"][:], in0=t["NT"][:], in1=self.ident, op=ALU.add), reads=[t["NT"], M], writes=[t["Wm"]])
                        yield
                        Bc, BcT, Wc = t["NT"], t["N"], t["Wm"]
                        nxt = [(t["Bm"], t["BmT"], t["W2"]), (t["B2"], t["B2T"], t["Wm"])]
                        for lev in range(1, 7):
                            Bn, BnT, Wn = nxt[(lev - 1) % 2]
                            pq = k.psum()
                            k.mm(pq, pq[:, 0:128], Bc[:], BcT[:], True, True, reads=[Bc, BcT])
                            yield
                            if lev < 6:
                                k.mm(pq, pq[:, 128:256], BcT[:], Bc[:], True, True, reads=[Bc, BcT])
                                yield
                            k.op("act", lambda e, pq=pq, BnT=BnT: e.activation(out=BnT[:], in_=pq[:, 0:128], func=AF.Copy), reads=[pq], writes=[BnT])
                            yield
                            if lev < 6:
                                k.op("dve", lambda e, pq=pq, Bn=Bn: e.tensor_copy(out=Bn[:], in_=pq[:, 128:256]), reads=[pq], writes=[Bn])
                                yield
                            yield
                            pw = k.psum()
                            k.mm(pw, pw[:, 0:128], BnT[:], Wc[:], True, True, reads=[BnT, Wc])
                            yield
                            k.op("dve", lambda e, pw=pw, Wn=Wn, Wc=Wc: e.tensor_tensor(out=Wn[:], in0=pw[:, 0:128], in1=Wc[:], op=ALU.add), reads=[pw, Wc], writes=[Wn])
                            yield
                            Bc, BcT, Wc = Bn, BnT, Wn
                            yield
                        TT = Wc
                        yield
                        p4 = k.psum()
                        k.mm(p4, p4[:, 0:128], TT[:], t["vb"][:], True, True, reads=[TT, t["vb"]])
                        yield
                        k.mm(p4, p4[:, 128:256], t["kbg"][:], TT[:], True, True, reads=[TT, t["kbg"]])
                        yield
                        k.op("act", lambda e: e.activation(out=t["u"][:], in_=p4[:, 0:128], func=AF.Copy), reads=[p4], writes=[t["u"]])
                        yield
                        k.op("dve", lambda e: e.tensor_copy(out=t["wkT"][:], in_=p4[:, 128:256]), reads=[p4], writes=[t["wkT"]])
                        yield
                        if GSTOP <= 5:
                            continue
                        yield
                        p5 = k.psum()
                        k.mm(p5, p5[:, 0:128], t["wkT"][:], S[:], True, True, reads=[t["wkT"], S])
                        yield
                        k.op("dve", lambda e: e.tensor_tensor(out=t["vnew"][:], in0=t["u"][:], in1=p5[:, 0:128], op=ALU.subtract), reads=[p5, t["u"]], writes=[t["vnew"]])
                        yield
                        k.mm(p5, p5[:, 128:256], t["qdec"][:], S[:], True, False, reads=[t["qdec"], S])
                        yield
                        k.mm(p5, p5[:, 128:256], t["AT"][:], t["vnew"][:], False, True, reads=[t["AT"], t["vnew"]])
                        yield
                        k.mm(p5, p5[:, 256:384], t["kdec"][:], t["vnew"][:], True, True, reads=[t["kdec"], t["vnew"]])
                        yield
                        k.op("act", lambda e, ci=ci: e.activation(out=osum[:, ci, :], in_=p5[:, 128:256], func=AF.Copy), reads=[p5], writes=[osum.sub(ci)])
                        yield
                        k.op("dve", lambda e: e.scalar_tensor_tensor(out=S[:], in0=S[:], scalar=t["E"][:, last:last + 1], in1=p5[:, 256:384], op0=ALU.mult, op1=ALU.add),
                             reads=[p5, S, t["E"]], writes=[S])
                        yield
                    if not smp:
                        k.dma("sp", self.O["n_st"][seq_i, l, d, h], S[:], reads=[S], writes=[])
                        yield
                chains = []
                for d in range(2):
                    chains.append([chain(d, tqs[d], scs[d], Ss[d], osums[d]), [4 * d, 4 * d + 1, 4 * d + 2, 4 * d + 3], 0])
                save_rot, save_next = k.ps_rot, k.psnext
                while chains:
                    for cobj in list(chains):
                        k.ps_rot, k.psnext = cobj[1], cobj[2]
                        try:
                            next(cobj[0])
                        except StopIteration:
                            chains.remove(cobj)
                        cobj[2] = k.psnext
                k.ps_rot, k.psnext = save_rot, save_next
                if GSTOP <= 6:
                    continue
                for ci in range(nch):
                    a0 = s0 + ci * 128
                    ps = k.psum()
                    for kc in range(8):
                        k.mm(ps, ps[:, 0:128], hT[:, kc, a0:a0 + 128], w[:, kc, 384:512], kc == 0, kc == 7, reads=[w, hT.sub((a0 // 512) * 512)])
                    t = tqs[ci % 2]
                    sc = scs[ci % 2]
                    k.op("act", lambda e, ps=ps: e.activation(out=t["gz"][:], in_=ps[:, 0:128], func=AF.Silu), reads=[ps], writes=[t["gz"]])
                    k.op("pool", lambda e, ci=ci: e.tensor_tensor(out=osum[:, ci, :], in0=osum[:, ci, :], in1=osums[1][:, ci, :], op=ALU.add),
                         reads=[osum.sub(ci), osums[1].sub(ci)], writes=[osum.sub(ci)])
                    k.op("act", lambda e, ci=ci: e.activation(out=t["on"][:], in_=osum[:, ci, :], func=AF.Square, accum_out=sc[:, 3:4]), reads=[osum.sub(ci)], writes=[t["on"], sc])
                    k.op("act", lambda e: e.activation(out=sc[:, 4:5], in_=sc[:, 3:4], func=AF.Sqrt, bias=self.epsc[:, 0:1], scale=1.0 / 128), reads=[sc, self.epsc], writes=[sc])
                    k.op("dve", lambda e: e.reciprocal(out=sc[:, 4:5], in_=sc[:, 4:5]), reads=[sc], writes=[sc])
                    k.op("dve", lambda e, ci=ci: e.scalar_tensor_tensor(out=t["on"][:], in0=osum[:, ci, :], scalar=sc[:, 4:5], in1=self.rowv(l, "gnorm"), op0=ALU.mult, op1=ALU.mult),
                         reads=[osum.sub(ci), sc, self.rowbc], writes=[t["on"]])
                    k.op("pool", lambda e: e.tensor_tensor(out=t["on"][:], in0=t["on"][:], in1=t["gz"][:], op=ALU.mult), reads=[t["on"], t["gz"]], writes=[t["on"]])
                    p6 = k.psum()
                    k.tr(p6, p6[:, 0:128], t["on"][:], self.ident, reads=[t["on"], M])
                    k.op("act", lambda e, p6=p6, ci=ci: e.activation(out=mixT[:, ci * 128:(ci + 1) * 128], in_=p6[:, 0:128], func=AF.Copy), reads=[p6], writes=[mixT])
                for (g0, n) in groups:
                    self.apply_out(l, path, s0 + g0, n, (lambda c: mixT[:, g0:g0 + n], [mixT]), 1, wout, h)
        k.end_phase()


_CACHE = {}


def _get_program():
    if "nc" not in _CACHE:
        b = Builder()
        _CACHE["nc"] = b.build()
        _CACHE["b"] = b
    return _CACHE["nc"], _CACHE["b"]


def make_in_maps(inp):
    f = lambda a: np.ascontiguousarray(np.asarray(a, dtype=np.float32))
    hc = host_consts()
    shared = {}
    for nm in ["w_ada", "ffn1_w1", "ffn1_w2", "w_in", "mla_w_qb", "mla_w_kvb", "w_out", "ffn2_w1", "ffn2_w2"]:
        shared[nm] = f(inp[nm])
    shared["b_ada"] = f(inp["b_ada"]).reshape(L, 72, 128)
    shared["convw"] = f(inp["gdn_conv_w"]).reshape(L * 5 * 12, 128)
    rowv = np.zeros((L, 304), np.float32)
    rowv[:, 0:8] = f(inp["swa_sink"])
    rowv[:, 8:16] = f(inp["gdn_a_log"]).reshape(L, 8)
    rowv[:, 16:24] = f(inp["gdn_dt_bias"]).reshape(L, 8)
    rowv[:, 24:152] = f(inp["gdn_norm"])
    shared["rowv"] = rowv.reshape(1, L * 304)
    shared.update(hc)
    maps = []
    for c in range(8):
        b = c // 4
        m = dict(shared)
        m["xp"] = f(inp["x_prompt"][2 * c:2 * c + 2]).reshape(NPT, D)
        m["xs"] = f(inp["x_sample"][b])
        m["c_ckv"] = f(inp["cache_mla_ckv"][b])
        m["c_kr"] = f(inp["cache_mla_krope"][b])
        m["c_sk"] = f(inp["cache_swa_k"][b]).reshape(L, PAST, 128)
        m["c_sv"] = f(inp["cache_swa_v"][b]).reshape(L, PAST, 128)
        m["c_st"] = f(inp["state_gdn"][b])
        vecA = np.zeros((82, 128), np.float32)
        vecA[0:16] = f(inp["norm_ffn1"]).reshape(16, 128)
        vecA[16:32] = f(inp["norm_mix"]).reshape(16, 128)
        vecA[32:48] = f(inp["norm_ffn2"]).reshape(16, 128)
        vecA[48:56] = f(inp["final_norm"]).reshape(8, 128)
        vecA[56:64] = f(inp["c_ctx"]).reshape(8, 128)
        vecA[64:72] = f(inp["c"][b]).reshape(8, 128)
        vecA[72:78] = f(inp["mla_q_norm"]).reshape(6, 128)
        vecA[78:82] = f(inp["mla_kv_norm"]).reshape(4, 128)
        m["vecA"] = vecA
        maps.append(m)
    return maps


def kernel(**inp):
    nc, b = _get_program()
    maps = make_in_maps(inp)
    res = run_bass_kernel_spmd(nc, maps, core_ids=list(range(8)))
    R = res.results
    yp = np.concatenate([R[c]["yp"].reshape(2, 256, D) for c in range(8)], 0)
    ys = np.stack([R[0]["ys"], R[4]["ys"]], 0)
    n_ckv = np.concatenate([R[c]["n_ckv"] for c in range(8)], 0)
    n_kr = np.concatenate([R[c]["n_kr"] for c in range(8)], 0)
    n_sk = np.concatenate([R[c]["n_sk"] for c in range(8)], 0).reshape(16, L, 256, 2, 64)
    n_sv = np.concatenate([R[c]["n_sv"] for c in range(8)], 0).reshape(16, L, 256, 2, 64)
    n_st = np.concatenate([R[c]["n_st"] for c in range(8)], 0)
    return (yp.astype(np.float32), ys.astype(np.float32), n_ckv.astype(np.float32), n_kr.astype(np.float32),
            n_sk.astype(np.float32), n_sv.astype(np.float32), n_st.astype(np.float32))
```

```python
import contextlib
import numpy as np
import concourse.bass as bass
import concourse.mybir as mybir
from concourse.bass_utils import run_bass_kernel_spmd

F32 = mybir.dt.float32
BF16 = mybir.dt.bfloat16
AF = mybir.ActivationFunctionType
ALU = mybir.AluOpType

D = 1024
DFF = 2816
L = 2
NPT = 512
NST = 2048
PAST = 512
EPS = 1e-6
DEBUG = []
STOP = None
SKIP = None
GSTOP = 99
FUSE_WAIT = True


class Buf:
    __slots__ = ("name", "w", "r", "parent", "kids")

    def __init__(self, name, parent=None):
        self.name = name
        self.w = None
        self.r = []
        self.parent = parent
        self.kids = {}

    def sub(self, key):
        if key not in self.kids:
            self.kids[key] = Buf(f"{self.name}.{key}", self)
        return self.kids[key]

    def family(self):
        out = [self]
        p = self.parent
        while p is not None:
            out.append(p)
            p = p.parent
        st = list(self.kids.values())
        while st:
            q = st.pop()
            out.append(q)
            st.extend(q.kids.values())
        return out


class T:
    def __init__(self, t, name):
        self.t = t
        self.b = Buf(name)

    def __getitem__(self, key):
        return self.t[key]

    def sub(self, key):
        return self.b.sub(key)


def _b(x):
    return x.b if isinstance(x, T) else x


class K:
    SAME_ENGINE_SYNC = True

    def __init__(self, n_dma_sems=64):
        self.nc = bass.Bass("TRN2", target_bir_lowering=False)
        nc = self.nc
        self.es = contextlib.ExitStack()
        self.engs = {"pe": nc.tensor, "act": nc.scalar, "dve": nc.vector, "pool": nc.gpsimd, "sp": nc.sync}
        self.sems = {}
        self.cnt = {}
        for e in self.engs:
            self.sems[e] = self.es.enter_context(nc.semaphore(f"prog_{e}"))
            self.cnt[e] = 0
        self.dsems = [self.es.enter_context(nc.semaphore(f"dma_{i}")) for i in range(n_dma_sems)]
        self.dval = [0] * n_dma_sems
        self.dnext = 0
        self.dnext2 = 0
        self.semobj = dict(self.sems)
        for i, s in enumerate(self.dsems):
            self.semobj[("d", i)] = s
        self.waited = {}
        self.n_wait = 0
        self.n_ins = 0
        self.phase_es = None
        self.psums = []
        self.psnext = 0
        self.ps_rot = list(range(8))

    def sb(self, name, shape, dtype=F32, glob=False):
        es = self.es if (glob or self.phase_es is None) else self.phase_es
        self.n_alloc = getattr(self, "n_alloc", 0) + 1
        name = f"sb{self.n_alloc}_{name}"
        t = es.enter_context(self.nc.sbuf_tensor(name, list(shape), dtype))
        return T(t, name)

    def dram(self, name, shape, dtype=F32, kind="Internal"):
        return T(self.nc.dram_tensor(name, list(shape), dtype, kind=kind), name)

    def init_psum(self):
        for i in range(8):
            t = self.es.enter_context(self.nc.psum_tensor(f"psb{i}", [128, 512], F32))
            self.psums.append(T(t, f"psb{i}"))

    def psum(self):
        rot = self.ps_rot
        self.psnext = (self.psnext + 1) % len(rot)
        return self.psums[rot[self.psnext]]

    def psum_hold(self, n):
        held = self.ps_rot[-n:]
        self.ps_rot = self.ps_rot[:-n]
        self.psnext = 0
        return [self.psums[i] for i in held]

    def psum_release(self, banks):
        self.ps_rot = self.ps_rot + [self.psums.index(b) for b in banks]

    def begin_phase(self):
        assert self.phase_es is None
        self.phase_es = contextlib.ExitStack()

    def end_phase(self):
        self.barrier()
        self.phase_es.close()
        self.phase_es = None

    def _deps(self, reads, writes, e=None):
        deps = {}

        def add(d):
            if d is not None and deps.get(d[0], 0) < d[1]:
                deps[d[0]] = d[1]
        for b in reads:
            bb = _b(b)
            for f in bb.family():
                add(f.w)
            if bb.name.startswith("psb"):
                for f in bb.family():
                    for r in f.r:
                        if r[0] != e:
                            add(r)
        for b in writes:
            for f in _b(b).family():
                add(f.w)
                for r in f.r:
                    add(r)
        return deps

    def _emit_waits(self, e, deps, keep_one=False):
        eng = self.engs[e]
        need = []
        for kk, v in deps.items():
            if kk == e and (e == "pe" or not self.SAME_ENGINE_SYNC):
                continue
            if self.waited.get((e, kk), 0) >= v:
                continue
            need.append((kk, v))
            self.waited[(e, kk)] = v
        fused = None
        if keep_one and FUSE_WAIT and need:
            fused = need.pop()
        for kk, v in need:
            eng.wait_ge(self.semobj[kk], v)
            self.n_wait += 1
        return fused

    def _record(self, tag, reads, writes):
        for b in reads:
            _b(b).r.append(tag)
        for b in writes:
            bb = _b(b)
            bb.w = tag
            bb.r = []

    def op(self, e, fn, reads=(), writes=()):
        deps = self._deps(reads, writes, e)
        fused = self._emit_waits(e, deps, keep_one=True)
        ins = fn(self.engs[e])
        if fused is not None:
            ins._wait_ge(self.semobj[fused[0]], fused[1])
        self.cnt[e] += 1
        ins.then_inc(self.sems[e], 1)
        self._record((e, self.cnt[e]), reads, writes)
        self.n_ins += 1
        return ins

    def dma(self, q, out, in_, reads=(), writes=(), **kw):
        deps = self._deps(reads, writes)
        half = len(self.dsems) // 2
        if q == "sp":
            i = self.dnext % half
            self.dnext += 1
        else:
            i = half + self.dnext2 % half
            self.dnext2 += 1
        key = ("d", i)
        if self.dval[i] > 0:
            deps[key] = max(deps.get(key, 0), self.dval[i])
        self._emit_waits(q, deps)
        ins = self.engs[q].dma_start(out=out, in_=in_, **kw)
        self.dval[i] += 16
        ins.then_inc(self.dsems[i], 16)
        self._record((key, self.dval[i]), reads, writes)
        self.n_ins += 1
        return ins

    def barrier(self):
        alld = {e: self.cnt[e] for e in self.engs if self.cnt[e] > 0}
        for i, v in enumerate(self.dval):
            if v > 0:
                alld[("d", i)] = v
        for e in self.engs:
            self._emit_waits(e, dict(alld))

    def mm(self, ps, out, lhsT, rhs, start, stop, reads):
        self.op("pe", lambda e: e.matmul(out, lhsT=lhsT, rhs=rhs, start=start, stop=stop), reads=reads, writes=[ps])

    def tr(self, ps, out, in_, ident, reads):
        self.op("pe", lambda e: e.transpose(out, in_, ident), reads=reads, writes=[ps])


def _rope_tables(n_tokens, dim):
    rows = n_tokens // 64
    row = np.repeat(np.arange(rows, dtype=np.float32), 64)
    col = np.tile(np.arange(64, dtype=np.float32), rows)
    axis_dim = dim // 2
    inv = (1.0 / (10000.0 ** (np.arange(0, axis_dim, 2, dtype=np.float32) / axis_dim))).astype(np.float32)
    ang_r = row[:, None] * inv[None, :]
    ang_c = col[:, None] * inv[None, :]
    ang = np.concatenate([ang_r, ang_r, ang_c, ang_c], axis=-1)
    return np.cos(ang).astype(np.float32), np.sin(ang).astype(np.float32)


def host_consts():
    c = {}
    p = np.arange(128)[:, None]
    f = np.arange(128)[None, :]
    masks = np.zeros((128, 5, 128), np.float32)
    masks[:, 0, :] = (p == f)
    masks[:, 1, :] = (p <= f)
    masks[:, 2, :] = (p < f)
    masks[:, 3, :] = (p >= f)
    masks[:, 4, :] = (p > f)
    c["masks"] = masks
    cm, sm = _rope_tables(NST, 32)
    cosm = np.ones((96, NST), np.float32)
    sinm = np.zeros((96, NST), np.float32)
    cosm[64:96] = cm.T
    sinm[64:96] = sm.T
    cs, ss = _rope_tables(NST, 64)
    c["rope_m"] = np.stack([cosm, sinm], 1).copy()
    c["rope_s"] = np.stack([cs.T, ss.T], 1).copy()
    return c


def n_is_k(ap, self):
    return ap.tensor.name == self.sm_k.t.name


class Builder:
    def __init__(self):
        self.k = K()
        k = self.k
        self.nc = k.nc
        self.ins = {}
        self.outs = {}
        k.init_psum()

    def din(self, name, shape):
        t = self.k.dram(name, shape, F32, kind="ExternalInput")
        self.ins[name] = t
        return t

    def dout(self, name, shape):
        t = self.k.dram(name, shape, F32, kind="ExternalOutput")
        self.outs[name] = t
        return t

    def build(self):
        k = self.k
        W = {}
        for nm, shp in [("xp", [NPT, D]), ("xs", [NST, D]), ("c_ckv", [L, PAST, 256]), ("c_kr", [L, PAST, 32]),
                        ("c_sk", [L, PAST, 128]), ("c_sv", [L, PAST, 128]), ("c_st", [L, 2, 4, 128, 128]),
                        ("vecA", [82, 128]), ("b_ada", [L, 72, 128]), ("convw", [120, 128]), ("rowv", [1, 304 * L]),
                        ("w_ada", [L, D, 9 * D]), ("ffn1_w1", [L, D, 2 * DFF]), ("ffn1_w2", [L, DFF, D]),
                        ("w_in", [L, D, 3504]), ("mla_w_qb", [L, 384, 768]), ("mla_w_kvb", [L, 256, 1024]),
                        ("w_out", [L, 1536, D]), ("ffn2_w1", [L, D, 2 * DFF]), ("ffn2_w2", [L, DFF, D]),
                        ("masks", [128, 5, 128]), ("rope_m", [96, 2, NST]), ("rope_s", [64, 2, NST])]:
            W[nm] = self.din(nm, shp)
        self.W = W
        O = {}
        O["yp"] = self.dout("yp", [NPT, D])
        O["ys"] = self.dout("ys", [NST, D])
        O["n_ckv"] = self.dout("n_ckv", [2, L, 256, 256])
        O["n_kr"] = self.dout("n_kr", [2, L, 256, 32])
        O["n_sk"] = self.dout("n_sk", [2, L, 256, 128])
        O["n_sv"] = self.dout("n_sv", [2, L, 256, 128])
        O["n_st"] = self.dout("n_st", [2, L, 2, 4, 128, 128])
        self.O = O
        self.dbg = {}

        self.xT = {"p": k.dram("xTp", [128, 8, NPT]), "s": k.dram("xTs", [128, 8, NST])}
        self.ntok = {"p": NPT, "s": NST}
        self.blocks = [("p", 0)] + [("s", i * 512) for i in range(4)]

        self.masks = k.sb("masks", [128, 5, 128], F32, glob=True)
        k.dma("sp", self.masks[:], W["masks"][:], writes=[self.masks])
        self.ident = self.masks.t[:, 0, :]
        self.ones_f = k.sb("ones_f", [128, 128], F32, glob=True)
        self.ones_b = k.sb("ones_b", [128, 128], BF16, glob=True)
        k.op("dve", lambda e: e.memset(self.ones_f[:], 1.0), writes=[self.ones_f])
        k.op("dve", lambda e: e.memset(self.ones_b[:], 1.0), writes=[self.ones_b])
        self.epsc = k.sb("epsc", [128, 1], F32, glob=True)
        k.op("dve", lambda e: e.memset(self.epsc[:], EPS), writes=[self.epsc])
        self.masks_b = k.sb("masks_b", [128, 5, 128], BF16, glob=True)
        k.op("dve", lambda e: e.tensor_copy(out=self.masks_b[:], in_=self.masks[:]), reads=[self.masks], writes=[self.masks_b])
        self.vecT = k.sb("vecT", [128, 82], F32, glob=True)
        self.badaT = k.sb("badaT", [128, L, 72], F32, glob=True)
        self.convT = k.sb("convT", [128, 120], F32, glob=True)
        self.rowbc = k.sb("rowbc", [128, 304 * L], F32, glob=True)
        self.modT = k.sb("modT", [128, L, 72, 2], F32, glob=True)
        self.modA = k.sb("modA", [128, L, 2, 3, 8], F32, glob=True)
        self.modG = k.sb("modG", [128, L, 2, 3, 8], F32, glob=True)
        self.hT = {"p": k.sb("hTp", [128, 8, NPT], BF16, glob=True), "s": k.sb("hTs", [128, 8, NST], BF16, glob=True)}
        stages = [("setup", self.setup_small), ("adaln", self.adaln), ("load", self.load_inputs)]
        for l in range(L):
            stages.append((f"ffn1_{l}", lambda l=l: self.ffn(l, 0)))
            for path in ("p", "s"):
                stages.append((f"mla_{l}{path}", lambda l=l, path=path: self.mla(l, path)))
                stages.append((f"swa_{l}{path}", lambda l=l, path=path: self.swa(l, path)))
                stages.append((f"gdn_{l}{path}", lambda l=l, path=path: self.gdn(l, path)))
            stages.append((f"ffn2_{l}", lambda l=l: self.ffn(l, 2)))
        stages.append(("final", self.final))
        for nm, fn in stages:
            if SKIP and nm in SKIP:
                continue
            fn()
            if STOP and nm == STOP:
                break
        if DEBUG:
            k.barrier()
            for nm in DEBUG:
                if nm in ("xTp", "xTs"):
                    src = self.xT[nm[-1]]
                    o = self.dout("dbg_" + nm, [128, 8, self.ntok[nm[-1]]])
                    for t0 in range(0, self.ntok[nm[-1]], 512):
                        k.dma("sp", o[:, :, t0:t0 + 512], src[:, :, t0:t0 + 512], reads=[src], writes=[o])
                elif nm in ("hTp", "hTs"):
                    src = self.hT[nm[-1]]
                    o = T(self.nc.dram_tensor("dbg_" + nm, [128, 8, self.ntok[nm[-1]]], BF16, kind="ExternalOutput"), "dbg_" + nm)
                    k.dma("sp", o[:, :, :], src[:, :, :], reads=[src], writes=[o])
                elif nm == "modT":
                    o = self.dout("dbg_modT", [128, L * 72 * 2])
                    k.dma("sp", o[:, :], self.modT[:, :, :, :].rearrange("p l c r -> p (l c r)"), reads=[self.modT], writes=[o])
        k.barrier()
        return self.nc

    def setup_small(self):
        k = self.k
        W = self.W
        k.begin_phase()
        stage = k.sb("stage", [128, 128], F32)
        for (src, nrow, dst) in [(W["vecA"][:, :], 82, self.vecT[:, :]), (W["b_ada"][0], 72, self.badaT[:, 0, :]),
                                 (W["b_ada"][1], 72, self.badaT[:, 1, :]), (W["convw"][:, :], 120, self.convT[:, :])]:
            k.dma("sp", stage[0:nrow, :], src, writes=[stage])
            ps = k.psum()
            k.tr(ps, ps[:, 0:nrow], stage[0:nrow, :], self.ident[0:nrow, 0:nrow], reads=[stage, self.masks])
            k.op("dve", lambda e, dst=dst, ps=ps, nrow=nrow: e.tensor_copy(out=dst, in_=ps[:, 0:nrow]), reads=[ps],
                 writes=[self.vecT, self.badaT, self.convT])
        row = k.sb("rowstage", [1, 304 * L], F32)
        k.dma("sp", row[:], W["rowv"][:, :], writes=[row])
        for i in range(0, 304 * L, 512):
            n = min(512, 304 * L - i)
            ps = k.psum()
            k.mm(ps, ps[:, 0:n], self.ones_f[0:1, :], row[0:1, i:i + n], True, True, reads=[self.ones_f, row])
            k.op("dve", lambda e, ps=ps, i=i, n=n: e.tensor_copy(out=self.rowbc[:, i:i + n], in_=ps[:, 0:n]), reads=[ps], writes=[self.rowbc])
        k.end_phase()

    def rowv(self, l, what):
        o = {"sink": (0, 8), "alog": (8, 16), "dtb": (16, 24), "gnorm": (24, 152)}[what]
        return self.rowbc.t[:, l * 304 + o[0]: l * 304 + o[1]]

    def adaln(self):
        k = self.k
        W = self.W
        k.begin_phase()
        condT = k.sb("condT", [128, 8, 2], BF16)
        tmp = k.sb("condtmp", [128, 16], F32)
        k.op("act", lambda e: e.activation(out=tmp[:], in_=self.vecT[:, 56:72], func=AF.Silu), reads=[self.vecT], writes=[tmp])
        for r in range(2):
            k.op("dve", lambda e, r=r: e.tensor_copy(out=condT[:, :, r], in_=tmp[:, r * 8:(r + 1) * 8]), reads=[tmp], writes=[condT])
        wb = [k.sb(f"wada{i}", [128, 8, 512], BF16) for i in range(2)]
        for l in range(L):
            psm = k.psum()
            for g in range(18):
                w = wb[g % 2]
                k.dma("pool", w[:], W["w_ada"][l].rearrange("(kc p) n -> p kc n", p=128)[:, :, g * 512:(g + 1) * 512], writes=[w])
                for j in range(4):
                    cc = g * 4 + j
                    for kc in range(8):
                        k.mm(psm, psm[:, cc * 2:cc * 2 + 2], w[:, kc, j * 128:(j + 1) * 128], condT[:, kc, :], kc == 0, kc == 7,
                             reads=[w, condT])
            for r in range(2):
                k.op("dve", lambda e, l=l, r=r, psm=psm: e.tensor_tensor(
                    out=self.modT[:, l, :, r], in0=psm[:, 0:144].rearrange("p (c r) -> p c r", r=2)[:, :, r],
                    in1=self.badaT[:, l, :], op=ALU.add), reads=[psm, self.badaT], writes=[self.modT])
        for l in range(L):
            for r in range(2):
                for s in range(3):
                    gain = self.vecT[:, [0, 16, 32][s] + l * 8:[0, 16, 32][s] + l * 8 + 8]
                    sc = self.modT[:, l, (3 * s + 1) * 8:(3 * s + 2) * 8, r]
                    k.op("dve", lambda e, l=l, r=r, s=s, gain=gain, sc=sc: e.scalar_tensor_tensor(
                        out=self.modA[:, l, r, s, :], in0=sc, scalar=1.0, in1=gain, op0=ALU.add, op1=ALU.mult),
                        reads=[self.modT, self.vecT], writes=[self.modA])
                    gt = self.modT[:, l, (3 * s + 2) * 8:(3 * s + 3) * 8, r]
                    k.op("dve", lambda e, l=l, r=r, s=s, gt=gt: e.tensor_scalar(
                        out=self.modG[:, l, r, s, :], in0=gt, scalar1=(1.0 if s == 1 else 0.5), scalar2=None, op0=ALU.mult),
                        reads=[self.modT], writes=[self.modG])
        k.end_phase()

    def mod_shift(self, l, r, s, kc):
        c = (3 * s) * 8 + kc
        return self.modT.t[:, l, c:c + 1, r]

    def load_inputs(self):
        k = self.k
        k.begin_phase()
        xin = [k.sb(f"xin{i}", [128, D], F32) for i in range(2)]
        xblk = [k.sb(f"xblk{i}", [128, 8, 512], F32) for i in range(2)]
        bi = 0
        for (path, t0) in self.blocks:
            src = self.W["xp"] if path == "p" else self.W["xs"]
            xb = xblk[bi % 2]
            for tt in range(4):
                xi = xin[tt % 2]
                k.dma("sp", xi[:], src[t0 + tt * 128:t0 + (tt + 1) * 128, :], writes=[xi])
                for half in range(2):
                    ps = k.psum()
                    for j in range(4):
                        kc = half * 4 + j
                        k.tr(ps, ps[:, j * 128:(j + 1) * 128], xi[:, kc * 128:(kc + 1) * 128], self.ident, reads=[xi, self.masks])
                    eng = "dve" if half == 0 else "act"
                    if eng == "dve":
                        k.op("dve", lambda e, xb=xb, ps=ps, half=half, tt=tt: e.tensor_copy(
                            out=xb[:, half * 4:half * 4 + 4, tt * 128:(tt + 1) * 128],
                            in_=ps[:, :].rearrange("p (j t) -> p j t", j=4)), reads=[ps], writes=[xb])
                    else:
                        k.op("act", lambda e, xb=xb, ps=ps, half=half, tt=tt: e.activation(
                            out=xb[:, half * 4:half * 4 + 4, tt * 128:(tt + 1) * 128],
                            in_=ps[:, :].rearrange("p (j t) -> p j t", j=4), func=AF.Copy), reads=[ps], writes=[xb])
            k.dma("sp", self.xT[path][:, :, t0:t0 + 512], xb[:], reads=[xb], writes=[self.xT[path].sub(t0)])
            bi += 1
        k.end_phase()

    def rmsnorm(self, x, n, Acol, Bcol, out, sq, rstd, nfeat_chunks=8, nfeat=D, tmp=None):
        k = self.k
        xs, xr = x
        outf, outw = out
        for kc in range(nfeat_chunks):
            k.op("act", lambda e, kc=kc: e.activation(out=sq[:, kc, 0:n], in_=xs(kc), func=AF.Square), reads=xr, writes=[sq])
        ps = k.psum()
        for kc in range(nfeat_chunks):
            k.mm(ps, ps[:, 0:n], self.ones_b[:, :], sq[:, kc, 0:n], kc == 0, kc == nfeat_chunks - 1, reads=[self.ones_b, sq])
        k.op("act", lambda e: e.activation(out=rstd[:, 0:n], in_=ps[:, 0:n], func=AF.Sqrt, bias=self.epsc[:, 0:1], scale=1.0 / nfeat),
             reads=[ps, self.epsc], writes=[rstd])
        k.op("dve", lambda e: e.reciprocal(out=rstd[:, 0:n], in_=rstd[:, 0:n]), reads=[rstd], writes=[rstd])
        for kc in range(nfeat_chunks):
            if Bcol is None:
                k.op("dve", lambda e, kc=kc: e.scalar_tensor_tensor(out=outf(kc), in0=xs(kc), scalar=Acol(kc), in1=rstd[:, 0:n],
                                                                     op0=ALU.mult, op1=ALU.mult), reads=list(xr) + [rstd], writes=outw)
            else:
                k.op("dve", lambda e, kc=kc: e.scalar_tensor_tensor(out=tmp[:, kc, 0:n], in0=xs(kc), scalar=Acol(kc), in1=rstd[:, 0:n],
                                                                     op0=ALU.mult, op1=ALU.mult), reads=list(xr) + [rstd], writes=[tmp])
                k.op("act", lambda e, kc=kc: e.activation(out=outf(kc), in_=tmp[:, kc, 0:n], func=AF.Identity, bias=Bcol(kc), scale=1.0),
                     reads=[tmp], writes=outw)

    def ffn(self, l, s):
        k = self.k
        W = self.W
        w1 = W["ffn1_w1" if s == 0 else "ffn2_w1"][l].rearrange("(kc p) n -> p kc n", p=128)
        w2 = W["ffn1_w2" if s == 0 else "ffn2_w2"][l].rearrange("(j p) n -> p j n", p=128)
        k.begin_phase()
        xb = k.sb("f_x", [128, 8, 512], F32)
        hb = k.sb("f_h", [128, 8, 512], BF16)
        tmp = k.sb("f_tmp", [128, 8, 512], F32)
        act = k.sb("f_act", [128, 22, 512], BF16)
        sq = k.sb("f_sq", [128, 8, 512], BF16)
        rstd = k.sb("f_rstd", [128, 512], F32)
        sg = [k.sb(f"f_sg{i}", [128, 512], F32) for i in range(2)]
        w1b = [k.sb(f"f_w1{i}", [128, 8, 512], BF16) for i in range(2)]
        w2b = [k.sb(f"f_w2{i}", [128, 22, 512], BF16) for i in range(2)]
        nw1 = 0
        nw2 = 0
        w1c = k.dram(f"w1c_{l}_{s}", [128, 11, 8, 512], BF16)
        w2c = k.dram(f"w2c_{l}_{s}", [128, 2, 22, 512], BF16)
        for bi, (path, t0) in enumerate(self.blocks):
            r = 0 if path == "p" else 1
            xsub = self.xT[path].sub(t0)
            k.dma("sp", xb[:], self.xT[path][:, :, t0:t0 + 512], reads=[xsub], writes=[xb])
            self.rmsnorm((lambda kc: xb[:, kc, :], [xb]), 512,
                         lambda kc: self.modA[:, l, r, s, kc:kc + 1], lambda kc: self.mod_shift(l, r, s, kc),
                         (lambda kc: hb[:, kc, :], [hb]), sq, rstd, tmp=tmp)
            for jg in range(11):
                wt = w1b[nw1 % 2]
                nw1 += 1
                if bi == 0:
                    k.dma("pool", wt[:, :, 0:256], w1[:, :, jg * 256:(jg + 1) * 256], writes=[wt])
                    k.dma("pool", wt[:, :, 256:512], w1[:, :, DFF + jg * 256:DFF + (jg + 1) * 256], writes=[wt])
                    k.dma("sp", w1c[:, jg], wt[:], reads=[wt], writes=[w1c.sub(jg)])
                else:
                    k.dma("sp", wt[:], w1c[:, jg], reads=[w1c.sub(jg)], writes=[wt])
                for jj in range(2):
                    j = jg * 2 + jj
                    pg = k.psum()
                    pu = k.psum()
                    for kc in range(8):
                        k.mm(pg, pg[:, :], wt[:, kc, jj * 128:(jj + 1) * 128], hb[:, kc, :], kc == 0, kc == 7, reads=[wt, hb])
                    for kc in range(8):
                        k.mm(pu, pu[:, :], wt[:, kc, 256 + jj * 128:256 + (jj + 1) * 128], hb[:, kc, :], kc == 0, kc == 7, reads=[wt, hb])
                    sgt = sg[j % 2]
                    k.op("act", lambda e, sgt=sgt, pg=pg: e.activation(out=sgt[:], in_=pg[:, :], func=AF.Silu), reads=[pg], writes=[sgt])
                    k.op("dve", lambda e, sgt=sgt, pu=pu, j=j: e.tensor_tensor(out=act[:, j, :], in0=pu[:, :], in1=sgt[:], op=ALU.mult),
                         reads=[pu, sgt], writes=[act.sub(j)])
            for half in range(2):
                wt = w2b[nw2 % 2]
                nw2 += 1
                if bi == 0:
                    for q4 in range(2):
                        k.dma("pool", wt[:, q4 * 11:(q4 + 1) * 11, :], w2[:, q4 * 11:(q4 + 1) * 11, half * 512:(half + 1) * 512], writes=[wt])
                    k.dma("sp", w2c[:, half], wt[:], reads=[wt], writes=[w2c.sub(half)])
                else:
                    k.dma("sp", wt[:], w2c[:, half], reads=[w2c.sub(half)], writes=[wt])
                for f4 in range(4):
                    fc = half * 4 + f4
                    po = k.psum()
                    for j in range(22):
                        k.mm(po, po[:, :], wt[:, j, f4 * 128:(f4 + 1) * 128], act[:, j, :], j == 0, j == 21, reads=[wt, act.sub(j)])
                    k.op("dve", lambda e, po=po, fc=fc: e.scalar_tensor_tensor(
                        out=xb[:, fc, :], in0=po[:, :], scalar=self.modG[:, l, r, s, fc:fc + 1], in1=xb[:, fc, :],
                        op0=ALU.mult, op1=ALU.add), reads=[po, xb, self.modG], writes=[xb])
            k.dma("sp", self.xT[path][:, :, t0:t0 + 512], xb[:], reads=[xb], writes=[xsub])
            if s == 0:
                hT = self.hT[path]
                self.rmsnorm((lambda kc: xb[:, kc, :], [xb]), 512,
                             lambda kc: self.modA[:, l, r, 1, kc:kc + 1], lambda kc: self.mod_shift(l, r, 1, kc),
                             (lambda kc: hT[:, kc, t0:t0 + 512], [hT.sub(t0)]), sq, rstd, tmp=tmp)
        k.end_phase()

    def final(self):
        k = self.k
        k.begin_phase()
        xb = k.sb("o_x", [128, 8, 512], F32)
        yb = k.sb("o_y", [128, 8, 512], F32)
        sq = k.sb("o_sq", [128, 8, 512], BF16)
        rstd = k.sb("o_rstd", [128, 512], F32)
        yo = [k.sb(f"o_yo{i}", [128, D], F32) for i in range(2)]
        n = 0
        for (path, t0) in self.blocks:
            k.dma("sp", xb[:], self.xT[path][:, :, t0:t0 + 512], reads=[self.xT[path].sub(t0)], writes=[xb])
            self.rmsnorm((lambda kc: xb[:, kc, :], [xb]), 512, lambda kc: self.vecT[:, 48 + kc:49 + kc], None,
                         (lambda kc: yb[:, kc, :], [yb]), sq, rstd)
            dst = self.O["yp"] if path == "p" else self.O["ys"]
            for tt in range(4):
                y = yo[n % 2]
                n += 1
                for half in range(2):
                    ps = k.psum()
                    for j in range(4):
                        kc = half * 4 + j
                        k.tr(ps, ps[:, j * 128:(j + 1) * 128], yb[:, kc, tt * 128:(tt + 1) * 128], self.ident, reads=[yb, self.masks])
                    if half == 0:
                        k.op("dve", lambda e, y=y, ps=ps: e.tensor_copy(out=y[:, 0:512], in_=ps[:, :]), reads=[ps], writes=[y])
                    else:
                        k.op("act", lambda e, y=y, ps=ps: e.activation(out=y[:, 512:1024], in_=ps[:, :], func=AF.Copy), reads=[ps], writes=[y])
                k.dma("sp", dst[t0 + tt * 128:t0 + (tt + 1) * 128, :], y[:], reads=[y], writes=[dst])
        k.end_phase()

    ao_k = 128

    def apply_out(self, l, path, t0, n, mixT, nchunks, wout_t, chunk0):
        k = self.k
        r = 0 if path == "p" else 1
        mixf, mixr = mixT
        xb = self.ao_x
        xsub = self.xT[path].sub((t0 // 512) * 512)
        k.dma("sp", xb[:, :, 0:n], self.xT[path][:, :, t0:t0 + n], reads=[xsub], writes=[xb])
        for fc in range(8):
            po = k.psum()
            for c in range(nchunks):
                k.mm(po, po[:, 0:n], wout_t[0:self.ao_k, chunk0 + c, fc * 128:(fc + 1) * 128], mixf(c), c == 0, c == nchunks - 1,
                     reads=[wout_t] + list(mixr))
            k.op("dve", lambda e, po=po, fc=fc: e.scalar_tensor_tensor(
                out=xb[:, fc, 0:n], in0=po[:, 0:n], scalar=self.modG[:, l, r, 1, fc:fc + 1], in1=xb[:, fc, 0:n],
                op0=ALU.mult, op1=ALU.add), reads=[po, xb, self.modG], writes=[xb])
        k.dma("sp", self.xT[path][:, :, t0:t0 + n], xb[:, :, 0:n], reads=[xb], writes=[xsub])

    def store_tok(self, srcf, srcr, nfeat, ntok, dst_fn, f0=0):
        k = self.k
        for tt in range(ntok // 128):
            ps = k.psum()
            k.tr(ps, ps[:, 0:nfeat], srcf(tt), self.ident[0:nfeat, 0:nfeat], reads=list(srcr) + [self.masks])
            st = self.st_buf[self.st_n % 2]
            self.st_n += 1
            k.op("act", lambda e, st=st, ps=ps: e.activation(out=st[:, 0:nfeat - f0], in_=ps[:, f0:nfeat], func=AF.Copy), reads=[ps], writes=[st])
            k.dma("sp", dst_fn(tt), st[:, 0:nfeat - f0], reads=[st], writes=[])

    def mixer(self, l, path):
        self.mla(l, path)
        self.swa(l, path)
        self.gdn(l, path)

    def seqs(self, path):
        return [(0, 256), (256, 256)] if path == "p" else [(0, NST)]

    def softmax_bound(self, q2max, k2max, scale, negc):
        k = self.k
        t = self.sm_t
        k.op("dve", lambda e: e.tensor_tensor(out=t[0:1, 0:1], in0=q2max, in1=k2max, op=ALU.add), reads=[self.sm_q, self.sm_k], writes=[t])
        k.op("dve", lambda e: e.tensor_scalar(out=t[0:1, 0:1], in0=t[0:1, 0:1], scalar1=-0.5 * scale, scalar2=None, op0=ALU.mult),
             reads=[t], writes=[t])
        ps = k.psum()
        k.mm(ps, ps[:, 0:1], self.ones_f[0:1, :], t[0:1, 0:1], True, True, reads=[self.ones_f, t])
        k.op("dve", lambda e: e.tensor_copy(out=negc[:, 0:1], in_=ps[:, 0:1]), reads=[ps], writes=[negc])

    def sq_max(self, src_ap, nrow, n, dst, first, reads, dstT=None):
        k = self.k
        sqt = self.sm_sq
        k.op("pool", lambda e: e.tensor_tensor(out=sqt[0:nrow, 0:n], in0=src_ap, in1=src_ap, op=ALU.mult), reads=reads, writes=[sqt])
        ps = k.psum()
        k.mm(ps, ps[0:1, 0:n], self.ones_b[0:nrow, 0:1], sqt[0:nrow, 0:n], True, True, reads=[self.ones_b, sqt])
        tm = self.sm_m
        if dstT is None:
            dstT = self.sm_k if n_is_k(dst, self) else self.sm_q
        if first:
            k.op("dve", lambda e: e.reduce_max(out=dst, in_=ps[0:1, 0:n], axis=mybir.AxisListType.X), reads=[ps], writes=[dstT])
        else:
            k.op("dve", lambda e: e.reduce_max(out=tm[0:1, 0:1], in_=ps[0:1, 0:n], axis=mybir.AxisListType.X), reads=[ps], writes=[tm])
            k.op("dve", lambda e: e.tensor_tensor(out=dst, in0=dst, in1=tm[0:1, 0:1], op=ALU.max), reads=[tm, dstT], writes=[dstT])

    def mla(self, l, path):
        k = self.k
        W = self.W
        smp = path == "s"
        hT = self.hT[path]
        scale = 96 ** -0.5
        k.begin_phase()
        self.ao_x = k.sb("ao_x", [128, 8, 512], F32)
        self.st_buf = [k.sb(f"st{i}", [128, 256], F32) for i in range(2)]
        self.st_n = 0
        self.sm_t = k.sb("sm_t", [1, 1], F32)
        self.sm_q = k.sb("sm_q", [1, 1], F32)
        self.sm_k = k.sb("sm_k", [1, 8], F32)
        self.sm_m = k.sb("sm_m", [1, 1], F32)
        self.sm_sq = k.sb("sm_sq", [96, 512], BF16)
        negc = k.sb("negc", [128, 1], F32)
        win = k.sb("m_win", [128, 8, 672], BF16)
        k.dma("pool", win[:], W["w_in"][l].rearrange("(kc p) n -> p kc n", p=128)[:, :, 0:672], writes=[win])
        wqb = k.sb("m_wqb", [128, 3, 768], BF16)
        k.dma("pool", wqb[:], W["mla_w_qb"][l].rearrange("(kc p) n -> p kc n", p=128), writes=[wqb])
        wkvb = k.sb("m_wkvb", [128, 2, 1024], BF16)
        k.dma("pool", wkvb[:], W["mla_w_kvb"][l].rearrange("(kc p) n -> p kc n", p=128), writes=[wkvb])
        wout = k.sb("m_wout", [64, 4, D], BF16)
        wkr = k.sb("m_wkr", [128, 8, 96], BF16)
        k.op("pool", lambda e: e.memset(wkr[:], 0.0), writes=[wkr])
        k.op("pool", lambda e: e.tensor_copy(out=wkr[:, :, 64:96], in_=win[:, :, 640:672]), reads=[win], writes=[wkr])
        if smp:
            rope = k.sb("m_rope", [96, 2, NST], F32)
            k.dma("sp", rope[:], W["rope_m"][:, :, :], writes=[rope])
            wkr_r = k.sb("m_wkr_r", [128, 8, 96], BF16)
            k.op("pool", lambda e: e.memset(wkr_r[:], 0.0), writes=[wkr_r])
            wqb_r = k.sb("m_wqb_r", [128, 3, 768], BF16)
            k.op("pool", lambda e: e.memset(wqb_r[:], 0.0), writes=[wqb_r])
            for a in range(2):
                b0 = 64 + a * 16
                k.op("pool", lambda e, b0=b0: e.tensor_scalar(out=wkr_r[:, :, b0:b0 + 8], in0=wkr[:, :, b0 + 8:b0 + 16], scalar1=-1.0,
                                                               scalar2=None, op0=ALU.mult), reads=[wkr], writes=[wkr_r])
                k.op("pool", lambda e, b0=b0: e.tensor_copy(out=wkr_r[:, :, b0 + 8:b0 + 16], in_=wkr[:, :, b0:b0 + 8]), reads=[wkr], writes=[wkr_r])
                for h in range(8):
                    c0 = h * 96 + b0
                    k.op("pool", lambda e, c0=c0: e.tensor_scalar(out=wqb_r[:, :, c0:c0 + 8], in0=wqb[:, :, c0 + 8:c0 + 16], scalar1=-1.0,
                                                                   scalar2=None, op0=ALU.mult), reads=[wqb], writes=[wqb_r])
                    k.op("pool", lambda e, c0=c0: e.tensor_copy(out=wqb_r[:, :, c0 + 8:c0 + 16], in_=wqb[:, :, c0:c0 + 8]), reads=[wqb], writes=[wqb_r])
        sq = k.sb("m_sq", [128, 3, 512], BF16)
        rstd = k.sb("m_rstd", [128, 512], F32)
        cf = k.sb("m_cf", [128, 3, 512], F32)
        krf = k.sb("m_krf", [96, 512], F32)
        krb = k.sb("m_krb", [96, 512], F32)
        qtmp, qtmp2 = krf, krb
        qTs = [k.sb(f"m_qT{i}", [96, 512], BF16) for i in range(4)]
        negcs = [k.sb(f"m_negc{i}", [128, 1], F32) for i in range(4)]
        smqs = [k.sb(f"m_smq{i}", [1, 1], F32) for i in range(4)]
        smts = [k.sb(f"m_smt{i}", [1, 1], F32) for i in range(4)]
        qT = qTs[0]
        PT = [k.sb(f"m_PT{i}", [128, 512], BF16) for i in range(3)]
        mixH = k.sb("m_mixH", [64, 4, 512], BF16)
        rrow = k.sb("m_rrow", [65, 512], F32)
        rbc = k.sb("m_rbc", [64, 512], F32)
        for (s0, T_) in self.seqs(path):
            nk = T_ + (PAST if smp else 0)
            nkt = nk // 128
            cqn = k.sb(f"m_cqn{s0}", [128, 3, T_], BF16)
            ckvn = k.sb(f"m_ckvn{s0}", [128, 2, nk], BF16)
            Kt = k.sb(f"m_Kt{s0}", [96, 4, nk], BF16)
            Vt = k.sb(f"m_Vt{s0}", [128, nkt, 4, 65], BF16)
            krall = k.sb(f"m_krall{s0}", [96, nk], BF16)
            k.op("pool", lambda e: e.memset(Vt[:], 1.0), writes=[Vt])
            groups = [(g0, min(512, T_ - g0)) for g0 in range(0, T_, 512)]
            seq_i = s0 // 256
            for (g0, n) in groups:
                a0 = s0 + g0
                for (c0, nch, gcol, dst, nfeat) in [(0, 3, 72, cqn, 384), (384, 2, 78, ckvn, 256)]:
                    pss = []
                    for c in range(nch):
                        ps = k.psum()
                        pss.append(ps)
                        for kc in range(8):
                            k.mm(ps, ps[:, 0:n], win[:, kc, c0 + c * 128:c0 + (c + 1) * 128], hT[:, kc, a0:a0 + n], kc == 0, kc == 7,
                                 reads=[win, hT.sub((a0 // 512) * 512)])
                    for c in range(nch):
                        k.op("act", lambda e, c=c, ps=pss[c]: e.activation(out=cf[:, c, 0:n], in_=ps[:, 0:n], func=AF.Copy), reads=[pss[c]], writes=[cf])
                    gcl = gcol + l * nch
                    self.rmsnorm((lambda c: cf[:, c, 0:n], [cf]), n, lambda c: self.vecT[:, gcl + c:gcl + c + 1], None,
                                 (lambda c: cf[:, c, 0:n], [cf]), sq, rstd, nfeat_chunks=nch, nfeat=nfeat)
                    for c in range(nch):
                        k.op("pool", lambda e, c=c, dst=dst: e.tensor_copy(out=dst[:, c, g0:g0 + n], in_=cf[:, c, 0:n]), reads=[cf], writes=[dst])
                    if dst is ckvn and not smp:
                        for c in range(2):
                            self.store_tok(lambda tt, c=c: cf[:, c, tt * 128:(tt + 1) * 128], [cf], 128, n,
                                           lambda tt, c=c: self.O["n_ckv"][seq_i, l, g0 + tt * 128:g0 + (tt + 1) * 128, c * 128:(c + 1) * 128])
                pa = k.psum()
                for kc in range(8):
                    k.mm(pa, pa[0:96, 0:n], wkr[:, kc, :], hT[:, kc, a0:a0 + n], kc == 0, kc == 7, reads=[wkr, hT.sub((a0 // 512) * 512)])
                if smp:
                    pb = k.psum()
                    for kc in range(8):
                        k.mm(pb, pb[0:96, 0:n], wkr_r[:, kc, :], hT[:, kc, a0:a0 + n], kc == 0, kc == 7, reads=[wkr_r, hT.sub((a0 // 512) * 512)])
                    k.op("dve", lambda e: e.tensor_tensor(out=krf[64:96, 0:n], in0=pa[64:96, 0:n], in1=rope[64:96, 0, g0:g0 + n], op=ALU.mult),
                         reads=[pa, rope], writes=[krf])
                    k.op("dve", lambda e: e.tensor_tensor(out=krb[64:96, 0:n], in0=pb[64:96, 0:n], in1=rope[64:96, 1, g0:g0 + n], op=ALU.mult),
                         reads=[pb, rope], writes=[krb])
                    k.op("pool", lambda e: e.tensor_tensor(out=krall[64:96, g0:g0 + n], in0=krf[64:96, 0:n], in1=krb[64:96, 0:n], op=ALU.add),
                         reads=[krf, krb], writes=[krall])
                else:
                    k.op("dve", lambda e: e.tensor_copy(out=krf[0:96, 0:n], in_=pa[0:96, 0:n]), reads=[pa], writes=[krf])
                    k.op("pool", lambda e: e.tensor_copy(out=krall[64:96, g0:g0 + n], in_=krf[64:96, 0:n]), reads=[krf], writes=[krall])
                    self.store_tok(lambda tt: krf[0:96, tt * 128:(tt + 1) * 128], [krf], 96, n,
                                   lambda tt: self.O["n_kr"][seq_i, l, g0 + tt * 128:g0 + (tt + 1) * 128, :], f0=64)
            if smp:
                kin = k.sb("m_kin", [128, 96], F32)
                k.op("pool", lambda e: e.memset(kin[:], 0.0), writes=[kin])
                for i in range(4):
                    ci = cf.t[:, i % 2, 0:256]
                    k.dma("sp", ci, W["c_ckv"][l, i * 128:(i + 1) * 128, :], writes=[cf])
                    ps = k.psum()
                    for c in range(2):
                        k.tr(ps, ps[:, c * 128:(c + 1) * 128], ci[:, c * 128:(c + 1) * 128], self.ident, reads=[cf, self.masks])
                    k.op("act", lambda e, ps=ps, i=i: e.activation(out=ckvn[:, :, T_ + i * 128:T_ + (i + 1) * 128],
                                                                   in_=ps[:, 0:256].rearrange("p (c t) -> p c t", c=2), func=AF.Copy),
                         reads=[ps], writes=[ckvn])
                    k.dma("sp", kin[:, 64:96], W["c_kr"][l, i * 128:(i + 1) * 128, :], writes=[kin])
                    ps2 = k.psum()
                    k.tr(ps2, ps2[0:96, 0:128], kin[:, :], self.ident, reads=[kin, self.masks])
                    k.op("dve", lambda e, ps2=ps2, i=i: e.tensor_copy(out=krall[64:96, T_ + i * 128:T_ + (i + 1) * 128], in_=ps2[64:96, 0:128]),
                         reads=[ps2], writes=[krall])
            kgroups = [(g0, min(512, nk - g0)) for g0 in range(0, nk, 512)]
            for hg in range(2):
                k.dma("pool", wout[:], W["w_out"][l][hg * 256:(hg + 1) * 256, :].rearrange("(h p) n -> p h n", p=64), writes=[wout])
                for (g0, n) in kgroups:
                    for hh in range(4):
                        h = hg * 4 + hh
                        ps = k.psum()
                        for c in range(2):
                            k.mm(ps, ps[0:64, 0:n], wkvb[:, c, h * 128:h * 128 + 64], ckvn[:, c, g0:g0 + n], c == 0, c == 1, reads=[wkvb, ckvn])
                        eng = "act" if hh % 2 == 0 else "dve"
                        if eng == "act":
                            k.op("act", lambda e, ps=ps, hh=hh: e.activation(out=Kt[0:64, hh, g0:g0 + n], in_=ps[0:64, 0:n], func=AF.Copy),
                                 reads=[ps], writes=[Kt.sub(hh)])
                        else:
                            k.op("dve", lambda e, ps=ps, hh=hh: e.tensor_copy(out=Kt[0:64, hh, g0:g0 + n], in_=ps[0:64, 0:n]), reads=[ps], writes=[Kt.sub(hh)])
                        if hg == 0:
                            pass
                        k.op("pool", lambda e, hh=hh: e.tensor_copy(out=Kt[64:96, hh, g0:g0 + n], in_=krall[64:96, g0:g0 + n]),
                             reads=[krall], writes=[Kt.sub(hh)])
                        self.sq_max(Kt[0:96, hh, g0:g0 + n], 96, n, self.sm_k[0:1, hh:hh + 1], g0 == 0, [Kt.sub(hh)])
                    for kt in range(g0 // 128, (g0 + n) // 128):
                        ps = k.psum()
                        for c in range(2):
                            k.mm(ps, ps[:, 0:256].rearrange("p (h d) -> p h d", h=4),
                                 ckvn[:, c, kt * 128:(kt + 1) * 128],
                                 wkvb[:, c, hg * 512:(hg + 1) * 512].rearrange("p (h d) -> p h d", h=4)[:, :, 64:128], c == 0, c == 1,
                                 reads=[wkvb, ckvn])
                        k.op("act", lambda e, ps=ps, kt=kt: e.activation(out=Vt[:, kt, :, 0:64], in_=ps[:, 0:256].rearrange("p (h d) -> p h d", h=4),
                                                                         func=AF.Copy), reads=[ps], writes=[Vt])
                npt = 0
                for (g0, n) in groups:
                    nqt = n // 128
                    for hh in range(4):
                        h = hg * 4 + hh
                        pa = k.psum()
                        for c in range(3):
                            k.mm(pa, pa[0:96, 0:n], wqb[:, c, h * 96:(h + 1) * 96], cqn[:, c, g0:g0 + n], c == 0, c == 2, reads=[wqb, cqn])
                        if smp:
                            pb = k.psum()
                            for c in range(3):
                                k.mm(pb, pb[0:96, 0:n], wqb_r[:, c, h * 96:(h + 1) * 96], cqn[:, c, g0:g0 + n], c == 0, c == 2, reads=[wqb_r, cqn])
                            k.op("dve", lambda e: e.tensor_tensor(out=qtmp[:, 0:n], in0=pa[0:96, 0:n], in1=rope[:, 0, g0:g0 + n], op=ALU.mult),
                                 reads=[pa, rope], writes=[qtmp])
                            k.op("dve", lambda e: e.tensor_tensor(out=qtmp2[:, 0:n], in0=pb[0:96, 0:n], in1=rope[:, 1, g0:g0 + n], op=ALU.mult),
                                 reads=[pb, rope], writes=[qtmp2])
                            k.op("pool", lambda e: e.tensor_tensor(out=qTs[hh][:, 0:n], in0=qtmp[:, 0:n], in1=qtmp2[:, 0:n], op=ALU.add),
                                 reads=[qtmp, qtmp2], writes=[qTs[hh]])
                        else:
                            k.op("act", lambda e: e.activation(out=qTs[hh][:, 0:n], in_=pa[0:96, 0:n], func=AF.Copy), reads=[pa], writes=[qTs[hh]])
                        self.sm_q = smqs[hh]; self.sm_t = smts[hh]
                        self.sq_max(qTs[hh][0:96, 0:n], 96, n, self.sm_q[0:1, 0:1], True, [qTs[hh]], dstT=self.sm_q)
                        self.softmax_bound(self.sm_q[0:1, 0:1], self.sm_k[0:1, hh:hh + 1], scale, negcs[hh])
                    for hh in range(4):
                        h = hg * 4 + hh
                        qT = qTs[hh]
                        negc = negcs[hh]
                        pos = k.psum_hold(1)
                        po = pos[0]
                        def qk(kt):
                            ps = k.psum()
                            k.mm(ps, ps[:, 0:n], Kt[:, hh, kt * 128:(kt + 1) * 128], qT[:, 0:n], True, True, reads=[Kt.sub(hh), qT])
                            return ps
                        pend = qk(0)
                        for kt in range(nkt):
                            ps = pend
                            if kt + 1 < nkt:
                                pend = qk(kt + 1)
                            pt = PT[npt % 3]
                            npt += 1
                            k.op("act", lambda e, pt=pt, ps=ps: e.activation(out=pt[:, 0:n], in_=ps[:, 0:n], func=AF.Exp, bias=negc[:, 0:1], scale=scale),
                                 reads=[ps, negc], writes=[pt])
                            k.mm(po, po[0:65, 0:n], Vt[:, kt, hh, :], pt[:, 0:n], kt == 0, kt == nkt - 1, reads=[pt, Vt])
                        k.op("act", lambda e, po=po: e.activation(out=rrow[64:65, 0:n], in_=po[64:65, 0:n], func=AF.Copy), reads=[po], writes=[rrow])
                        pbc = k.psum()
                        k.mm(pbc, pbc[0:64, 0:n], self.ones_f[64:65, 0:64], rrow[64:65, 0:n], True, True, reads=[self.ones_f, rrow])
                        k.op("dve", lambda e, pbc=pbc: e.reciprocal(out=rbc[:, 0:n], in_=pbc[0:64, 0:n]), reads=[pbc], writes=[rbc])
                        k.op("dve", lambda e, po=po, hh=hh: e.tensor_tensor(out=mixH[:, hh, 0:n], in0=po[0:64, 0:n], in1=rbc[:, 0:n], op=ALU.mult),
                             reads=[po, rbc], writes=[mixH])
                        k.psum_release(pos)
                    self.ao_k = 64
                    self.apply_out(l, path, s0 + g0, n, (lambda c: mixH[:, c, 0:n], [mixH]), 4, wout, 0)
                    self.ao_k = 128
        k.end_phase()

    def swa(self, l, path):
        k = self.k
        W = self.W
        smp = path == "s"
        hT = self.hT[path]
        scale = 64 ** -0.5
        k.begin_phase()
        self.ao_x = k.sb("ao_x", [128, 8, 512], F32)
        self.st_buf = [k.sb(f"st{i}", [128, 256], F32) for i in range(2)]
        self.st_n = 0
        self.sm_t = k.sb("sm_t", [1, 1], F32)
        self.sm_q = k.sb("sm_q", [1, 1], F32)
        self.sm_k = k.sb("sm_k", [1, 8], F32)
        self.sm_m = k.sb("sm_m", [1, 1], F32)
        self.sm_sq = k.sb("sm_sq", [96, 512], BF16)
        negc = k.sb("negc", [128, 1], F32)
        psink = k.sb("psink", [128, 8], F32)
        wsw = k.sb("s_w", [128, 8, 768], BF16)
        k.dma("pool", wsw[:], W["w_in"][l].rearrange("(kc p) n -> p kc n", p=128)[:, :, 672:1440], writes=[wsw])
        wout = k.sb("s_wout", [64, 8, D], BF16)
        k.dma("pool", wout[:], W["w_out"][l][512:1024, :].rearrange("(h p) n -> p h n", p=64), writes=[wout])
        if smp:
            rope = k.sb("s_rope", [64, 2, NST], F32)
            k.dma("sp", rope[:], W["rope_s"][:, :, :], writes=[rope])
            wr = k.sb("s_wr", [128, 8, 640], BF16)
            wv = wsw[:, :, 0:640].rearrange("p k (h a z d) -> p k h a z d", h=10, a=2, z=2)
            wrv = wr[:, :, :].rearrange("p k (h a z d) -> p k h a z d", h=10, a=2, z=2)
            for kc in range(8):
                k.op("pool", lambda e, kc=kc: e.tensor_scalar(out=wrv[:, kc, :, :, 0, :], in0=wv[:, kc, :, :, 1, :], scalar1=-1.0, scalar2=None, op0=ALU.mult),
                     reads=[wsw], writes=[wr])
                k.op("pool", lambda e, kc=kc: e.tensor_copy(out=wrv[:, kc, :, :, 1, :], in_=wv[:, kc, :, :, 0, :]), reads=[wsw], writes=[wr])
        kf = k.sb("s_kf", [64, 512], F32)
        kb_ = k.sb("s_kb", [64, 512], F32)

        def proj_rope(col0, a0, n, g0, dst_ap, dst_w, fp_out=None):
            pa = k.psum()
            for kc in range(8):
                k.mm(pa, pa[0:64, 0:n], wsw[:, kc, col0:col0 + 64], hT[:, kc, a0:a0 + n], kc == 0, kc == 7, reads=[wsw, hT.sub((a0 // 512) * 512)])
            if smp:
                pb = k.psum()
                for kc in range(8):
                    k.mm(pb, pb[0:64, 0:n], wr[:, kc, col0:col0 + 64], hT[:, kc, a0:a0 + n], kc == 0, kc == 7, reads=[wr, hT.sub((a0 // 512) * 512)])
                k.op("dve", lambda e: e.tensor_tensor(out=kf[:, 0:n], in0=pa[0:64, 0:n], in1=rope[:, 0, g0:g0 + n], op=ALU.mult), reads=[pa, rope], writes=[kf])
                k.op("dve", lambda e: e.tensor_tensor(out=kb_[:, 0:n], in0=pb[0:64, 0:n], in1=rope[:, 1, g0:g0 + n], op=ALU.mult), reads=[pb, rope], writes=[kb_])
                k.op("pool", lambda e: e.tensor_tensor(out=dst_ap, in0=kf[:, 0:n], in1=kb_[:, 0:n], op=ALU.add), reads=[kf, kb_], writes=dst_w)
            else:
                if fp_out is not None:
                    k.op("dve", lambda e: e.tensor_copy(out=kf[:, 0:n], in_=pa[0:64, 0:n]), reads=[pa], writes=[kf])
                    k.op("pool", lambda e: e.tensor_copy(out=dst_ap, in_=kf[:, 0:n]), reads=[kf], writes=dst_w)
                else:
                    k.op("act", lambda e: e.activation(out=dst_ap, in_=pa[0:64, 0:n], func=AF.Copy), reads=[pa], writes=dst_w)

        qs = k.sb("s_q", [64, 4, 512], BF16)
        PT = [k.sb(f"s_PT{i}", [128, 4, 128], BF16) for i in range(3)]
        mixH = k.sb("s_mixH", [64, 8, 512], BF16)
        rrow = k.sb("s_rrow", [65, 512], F32)
        rbc = k.sb("s_rbc", [64, 512], F32)
        for (s0, T_) in self.seqs(path):
            nk = T_ + (PAST if smp else 0)
            nkt = nk // 128
            nlt = T_ // 128
            seq_i = s0 // 256
            Ks = k.sb(f"s_K{s0}", [64, 2, nk], BF16)
            Vs = k.sb(f"s_V{s0}", [128, nkt, 2, 65], BF16)
            k.op("pool", lambda e: e.memset(Vs[:], 1.0), writes=[Vs])
            vst = [k.sb(f"s_vst{s0}_{i}", [128, 128], F32) for i in range(2)]
            groups = [(g0, min(512, T_ - g0)) for g0 in range(0, T_, 512)]
            for (g0, n) in groups:
                a0 = s0 + g0
                for kvh in range(2):
                    proj_rope(512 + kvh * 64, a0, n, g0, Ks[:, kvh, g0:g0 + n], [Ks.sub(kvh)], fp_out=(None if smp else True))
                    if not smp:
                        self.store_tok(lambda tt: kf[0:64, tt * 128:(tt + 1) * 128], [kf], 64, n,
                                       lambda tt, kvh=kvh: self.O["n_sk"][seq_i, l, g0 + tt * 128:g0 + (tt + 1) * 128, kvh * 64:(kvh + 1) * 64])
                    self.sq_max(Ks[0:64, kvh, g0:g0 + n], 64, n, self.sm_k[0:1, kvh:kvh + 1], g0 == 0, [Ks.sub(kvh)])
                for tt in range(n // 128):
                    kt = (g0 // 128) + tt
                    ps = k.psum()
                    for kc in range(8):
                        k.mm(ps, ps[:, 0:128], hT[:, kc, a0 + tt * 128:a0 + (tt + 1) * 128], wsw[:, kc, 640:768], kc == 0, kc == 7,
                             reads=[wsw, hT.sub((a0 // 512) * 512)])
                    k.op("act", lambda e, ps=ps, kt=kt: e.activation(out=Vs[:, kt, :, 0:64], in_=ps[:, 0:128].rearrange("p (h d) -> p h d", h=2), func=AF.Copy),
                         reads=[ps], writes=[Vs])
                    if not smp:
                        vt = vst[tt % 2]
                        k.op("dve", lambda e, ps=ps, vt=vt: e.tensor_copy(out=vt[:], in_=ps[:, 0:128]), reads=[ps], writes=[vt])
                        k.dma("sp", self.O["n_sv"][seq_i, l, g0 + tt * 128:g0 + (tt + 1) * 128, :], vt[:], reads=[vt], writes=[])
            if smp:
                cin = [k.sb(f"s_cin{i}", [128, 128], F32) for i in range(2)]
                cvn = [k.sb(f"s_cvn{i}", [128, 128], F32) for i in range(2)]
                for i in range(4):
                    ci = cin[i % 2]
                    k.dma("sp", ci[:], W["c_sk"][l, i * 128:(i + 1) * 128, :], writes=[ci])
                    for kvh in range(2):
                        ps = k.psum()
                        k.tr(ps, ps[0:64, 0:128], ci[:, kvh * 64:(kvh + 1) * 64], self.ident, reads=[ci, self.masks])
                        k.op("act", lambda e, ps=ps, kvh=kvh, i=i: e.activation(out=Ks[:, kvh, T_ + i * 128:T_ + (i + 1) * 128], in_=ps[0:64, 0:128], func=AF.Copy),
                             reads=[ps], writes=[Ks.sub(kvh)])
                    cv = cvn[i % 2]
                    k.dma("sp", cv[:], W["c_sv"][l, i * 128:(i + 1) * 128, :], writes=[cv])
                    k.op("pool", lambda e, cv=cv, i=i: e.tensor_copy(out=Vs[:, nlt + i, :, 0:64], in_=cv[:, :].rearrange("p (h d) -> p h d", h=2)),
                         reads=[cv], writes=[Vs])
                for kvh in range(2):
                    for g0 in range(T_, nk, 512):
                        self.sq_max(Ks[0:64, kvh, g0:g0 + 512], 64, 512, self.sm_k[0:1, kvh:kvh + 1], False, [Ks.sub(kvh)])
            npt = 0
            for (g0, n) in groups:
                a0 = s0 + g0
                nqt = n // 128
                for kvh in range(2):
                    for g in range(4):
                        proj_rope((kvh * 4 + g) * 64, a0, n, g0, qs[:, g, 0:n], [qs])
                        self.sq_max(qs[0:64, g, 0:n], 64, n, self.sm_q[0:1, 0:1], g == 0, [qs])
                    self.softmax_bound(self.sm_q[0:1, 0:1], self.sm_k[0:1, kvh:kvh + 1], scale, negc)
                    k.op("act", lambda e, kvh=kvh: e.activation(out=psink[:, kvh * 4:(kvh + 1) * 4], in_=self.rowv(l, "sink")[:, kvh * 4:(kvh + 1) * 4],
                                                                func=AF.Exp, bias=negc[:, 0:1], scale=1.0), reads=[self.rowbc, negc], writes=[psink])
                    for qt in range(nqt):
                        gq = g0 // 128 + qt
                        if smp:
                            kts = [(t, t - gq) for t in (gq - 1, gq, gq + 1) if 0 <= t < nlt] + [(nlt + i, 0) for i in range(4)]
                        else:
                            kts = [(t, 0) for t in range(nlt)]
                        pos = k.psum_hold(1)
                        po = pos[0]
                        def qk(kt):
                            ps = k.psum()
                            k.mm(ps, ps[:, :].rearrange("p (g q) -> p g q", g=4), Ks[:, kvh, kt * 128:(kt + 1) * 128], qs[:, :, qt * 128:(qt + 1) * 128],
                                 True, True, reads=[Ks.sub(kvh), qs])
                            return ps
                        pend = qk(kts[0][0])
                        for ki, (kt, rel) in enumerate(kts):
                            ps = pend
                            if ki + 1 < len(kts):
                                pend = qk(kts[ki + 1][0])
                            pt = PT[npt % 3]
                            npt += 1
                            k.op("act", lambda e, pt=pt, ps=ps: e.activation(out=pt[:, :, :], in_=ps[:, :].rearrange("p (g q) -> p g q", g=4), func=AF.Exp,
                                                                           bias=negc[:, 0:1], scale=scale), reads=[ps, negc], writes=[pt])
                            if rel != 0:
                                mi = 3 if rel == -1 else 1
                                for g in range(4):
                                    k.op("pool" if g % 2 else "dve", lambda e, pt=pt, g=g, mi=mi: e.tensor_tensor(out=pt[:, g, :], in0=pt[:, g, :], in1=self.masks_b[:, mi, :], op=ALU.mult),
                                         reads=[pt, self.masks_b], writes=[pt])
                            k.mm(po, po[0:65, :].rearrange("p (g q) -> p g q", g=4), Vs[:, kt, kvh, :], pt[:, :, :], ki == 0, ki == len(kts) - 1, reads=[pt, Vs])
                        for g in range(4):
                            k.op("dve", lambda e, po=po, g=g, kvh=kvh: e.tensor_scalar(out=rrow[64:65, g * 128:(g + 1) * 128], in0=po[64:65, g * 128:(g + 1) * 128],
                                                                                         scalar1=psink[64:65, kvh * 4 + g:kvh * 4 + g + 1], scalar2=None, op0=ALU.add),
                                 reads=[po, psink], writes=[rrow])
                        pbc = k.psum()
                        k.mm(pbc, pbc[0:64, :], self.ones_f[64:65, 0:64], rrow[64:65, :], True, True, reads=[self.ones_f, rrow])
                        k.op("dve", lambda e, pbc=pbc: e.reciprocal(out=rbc[:, :], in_=pbc[0:64, :]), reads=[pbc], writes=[rbc])
                        k.op("dve", lambda e, po=po, kvh=kvh, qt=qt: e.tensor_tensor(
                            out=mixH[:, kvh * 4:(kvh + 1) * 4, qt * 128:(qt + 1) * 128], in0=po[0:64, :].rearrange("p (g q) -> p g q", g=4),
                            in1=rbc[:, :].rearrange("p (g q) -> p g q", g=4), op=ALU.mult), reads=[po, rbc], writes=[mixH])
                        k.psum_release(pos)
                self.ao_k = 64
                self.apply_out(l, path, s0 + g0, n, (lambda c: mixH[:, c, 0:n], [mixH]), 8, wout, 0)
                self.ao_k = 128
        k.end_phase()

    def gdn(self, l, path):
        k = self.k
        W = self.W
        smp = path == "s"
        hT = self.hT[path]
        M = self.masks
        k.begin_phase()
        self.ao_x = k.sb("ao_x", [128, 8, 512], F32)
        win_all = W["w_in"][l].rearrange("(kc p) n -> p kc n", p=128)
        wg = k.sb("g_wg", [128, 8, 16], BF16)
        k.dma("pool", wg[:], win_all[:, :, 3488:3504], writes=[wg])
        wout = k.sb("g_wout", [128, 4, D], BF16)
        k.dma("pool", wout[:], W["w_out"][l].rearrange("(c p) n -> p c n", p=128)[:, 8:12, :], writes=[wout])
        wh = [k.sb(f"g_wh{i}", [128, 8, 512], BF16) for i in range(2)]
        nalog = k.sb("g_nalog", [128, 8], F32)
        k.op("act", lambda e: e.activation(out=nalog[:], in_=self.rowv(l, "alog"), func=AF.Exp), reads=[self.rowbc], writes=[nalog])
        k.op("dve", lambda e: e.tensor_scalar(out=nalog[:], in0=nalog[:], scalar1=-1.0, scalar2=None, op0=ALU.mult), reads=[nalog], writes=[nalog])
        nwh = 0
        for (s0, T_) in self.seqs(path):
            nch = T_ // 128
            seq_i = s0 // 256
            gg = k.sb(f"g_g{s0}", [128, nch, 8], F32)
            bt = k.sb(f"g_b{s0}", [128, nch, 8], F32)
            for ci in range(nch):
                a0 = s0 + ci * 128
                ps = k.psum()
                for kc in range(8):
                    k.mm(ps, ps[:, 0:16], hT[:, kc, a0:a0 + 128], wg[:, kc, :], kc == 0, kc == 7, reads=[wg, hT.sub((a0 // 512) * 512)])
                k.op("dve", lambda e, ps=ps, ci=ci: e.tensor_tensor(out=gg[:, ci, :], in0=ps[:, 0:8], in1=self.rowv(l, "dtb"), op=ALU.add),
                     reads=[ps, self.rowbc], writes=[gg])
                k.op("act", lambda e, ps=ps, ci=ci: e.activation(out=bt[:, ci, :], in_=ps[:, 8:16], func=AF.Sigmoid), reads=[ps], writes=[bt])
            k.op("dve", lambda e: e.tensor_scalar(out=gg[:], in0=gg[:], scalar1=60.0, scalar2=None, op0=ALU.min), reads=[gg], writes=[gg])
            k.op("act", lambda e: e.activation(out=gg[:], in_=gg[:], func=AF.Exp), reads=[gg], writes=[gg])
            k.op("act", lambda e: e.activation(out=gg[:], in_=gg[:], func=AF.Ln, bias=1.0, scale=1.0), reads=[gg], writes=[gg])
            for ci in range(nch):
                k.op("dve", lambda e, ci=ci: e.tensor_tensor(out=gg[:, ci, :], in0=gg[:, ci, :], in1=nalog[:], op=ALU.mult), reads=[gg, nalog], writes=[gg])
            if GSTOP <= 1:
                continue
            pre = k.sb(f"g_pre{s0}", [128, 3, T_ + 4], F32)
            acc = k.sb(f"g_acc{s0}", [128, 3, T_], F32)
            osum = k.sb(f"g_osum{s0}", [128, nch, 128], F32)
            mixT = k.sb(f"g_mixT{s0}", [128, T_], BF16)
            sqf = k.sb(f"g_sqf{s0}", [128, 512], F32)
            rst = k.sb(f"g_rst{s0}", [128, 512], F32)
            names = ["gcb", "X", "DT", "Dm", "N", "NT", "Bm", "BmT", "B2", "B2T", "Wm", "W2", "kbg", "vb",
                     "u", "wkT", "E", "gones", "gz", "on"]
            names_b = ["AT", "kdec", "qdec", "vnew", "Sb"]
            tqs = [{nm: k.sb(f"g_{nm}{s0}_{d}", [128, 128], F32) for nm in names} for d in range(2)]
            for d in range(2):
                tqs[d].update({nm: k.sb(f"g_{nm}{s0}_{d}", [128, 128], BF16) for nm in names_b})
            accb = k.sb(f"g_accb{s0}", [128, 2, T_], BF16)
            scs = [k.sb(f"g_sc{s0}_{d}", [128, 8], F32) for d in range(2)]
            Ss = [k.sb(f"g_S{s0}_{d}", [128, 128], F32) for d in range(2)]
            osums = [osum, k.sb(f"g_osumb{s0}", [128, nch, 128], F32)]
            tq, sc = tqs[0], scs[0]
            k.op("pool", lambda e: e.memset(pre[:], 0.0), writes=[pre])
            groups = [(g0, min(512, T_ - g0)) for g0 in range(0, T_, 512)]
            for h in range(4):
                w = wh[nwh % 2]
                nwh += 1
                for wi, c0 in enumerate([1440, 1952, 2464, 2976]):
                    k.dma("pool", w[:, :, wi * 128:(wi + 1) * 128], win_all[:, :, c0 + h * 128:c0 + (h + 1) * 128], writes=[w])
                for (g0, n) in groups:
                    a0 = s0 + g0
                    for wi in range(3):
                        ps = k.psum()
                        for kc in range(8):
                            k.mm(ps, ps[:, 0:n], w[:, kc, wi * 128:(wi + 1) * 128], hT[:, kc, a0:a0 + n], kc == 0, kc == 7, reads=[w, hT.sub((a0 // 512) * 512)])
                        k.op("act", lambda e, ps=ps, wi=wi: e.activation(out=pre[:, wi, 2 + g0:2 + g0 + n], in_=ps[:, 0:n], func=AF.Copy), reads=[ps], writes=[pre.sub(wi)])
                for wi in range(3):
                    eng = "dve"
                    cch = wi * 4 + h
                    for tap in range(5):
                        wcol = self.convT[:, l * 60 + tap * 12 + cch:l * 60 + tap * 12 + cch + 1]
                        if tap == 0:
                            k.op(eng, lambda e, wi=wi, wcol=wcol: e.tensor_scalar(out=acc[:, wi, :], in0=pre[:, wi, 0:T_], scalar1=wcol, scalar2=None, op0=ALU.mult),
                                 reads=[pre.sub(wi), self.convT], writes=[acc.sub(wi)])
                        else:
                            k.op(eng, lambda e, wi=wi, wcol=wcol, tap=tap: e.scalar_tensor_tensor(out=acc[:, wi, :], in0=pre[:, wi, tap:tap + T_], scalar=wcol,
                                                                                                 in1=acc[:, wi, :], op0=ALU.mult, op1=ALU.add),
                                 reads=[pre.sub(wi), self.convT, acc.sub(wi)], writes=[acc.sub(wi)])
                    k.op("act", lambda e, wi=wi: e.activation(out=acc[:, wi, :], in_=acc[:, wi, :], func=AF.Silu), reads=[acc.sub(wi)], writes=[acc.sub(wi)])
                if GSTOP <= 2:
                    continue
                for wi in range(2):
                    for (g0, n) in groups:
                        k.op("act", lambda e, wi=wi: e.activation(out=sqf[:, 0:n], in_=acc[:, wi, g0:g0 + n], func=AF.Square), reads=[acc.sub(wi)], writes=[sqf])
                        ps = k.psum()
                        k.mm(ps, ps[:, 0:n], self.ones_f[:, :], sqf[:, 0:n], True, True, reads=[self.ones_f, sqf])
                        k.op("act", lambda e, ps=ps: e.activation(out=rst[:, 0:n], in_=ps[:, 0:n], func=AF.Sqrt, bias=self.epsc[:, 0:1], scale=1.0),
                             reads=[ps, self.epsc], writes=[rst])
                        k.op("dve", lambda e: e.reciprocal(out=rst[:, 0:n], in_=rst[:, 0:n]), reads=[rst], writes=[rst])
                        if wi == 0:
                            k.op("dve", lambda e, wi=wi: e.scalar_tensor_tensor(out=acc[:, wi, g0:g0 + n], in0=acc[:, wi, g0:g0 + n], scalar=128 ** -0.5, in1=rst[:, 0:n],
                                                                                 op0=ALU.mult, op1=ALU.mult), reads=[acc.sub(wi), rst], writes=[acc.sub(wi)])
                        else:
                            k.op("pool", lambda e, wi=wi: e.tensor_tensor(out=acc[:, wi, g0:g0 + n], in0=acc[:, wi, g0:g0 + n], in1=rst[:, 0:n], op=ALU.mult),
                                 reads=[acc.sub(wi), rst], writes=[acc.sub(wi)])
                k.op("act", lambda e: e.activation(out=accb[:, 0, :], in_=acc[:, 0, :], func=AF.Copy), reads=[acc.sub(0)], writes=[accb])
                k.op("pool", lambda e: e.tensor_copy(out=accb[:, 1, :], in_=acc[:, 1, :]), reads=[acc.sub(1)], writes=[accb])
                if GSTOP <= 3:
                    continue
                def chain(d, tq, sc, S, osum):
                    col = d * 4 + h
                    if smp:
                        k.dma("sp", S[:], W["c_st"][l, d, h], writes=[S])
                        yield
                    else:
                        k.op("pool", lambda e: e.memset(S[:], 0.0), writes=[S])
                        yield
                    k.op("act", lambda e: e.activation(out=tq["Sb"][:], in_=S[:], func=AF.Copy), reads=[S], writes=[tq["Sb"]])
                    yield
                    order = range(nch) if d == 0 else range(nch - 1, -1, -1)
                    mcs = 1 if d == 0 else 3
                    m_dt_incl = 1 if d == 0 else 3
                    m_d_strict = 4 if d == 0 else 2
                    last = 127 if d == 0 else 0
                    for ci in order:
                        c0 = ci * 128
                        kT_c = acc[:, 1, c0:c0 + 128]
                        qT_c = acc[:, 0, c0:c0 + 128]
                        vT_c = acc[:, 2, c0:c0 + 128]
                        gcol = gg[:, ci, col:col + 1]
                        bcol = bt[:, ci, col:col + 1]
                        rd = [acc.sub(0), acc.sub(1), acc.sub(2)]
                        t = tq
                        yield
                        p1 = k.psum()
                        k.mm(p1, p1[:, 0:128], kT_c, kT_c, True, True, reads=rd)
                        yield
                        k.mm(p1, p1[:, 128:256], accb[:, 1, c0:c0 + 128], accb[:, 0, c0:c0 + 128], True, True, reads=[accb])
                        yield
                        k.tr(p1, p1[:, 256:384], kT_c, self.ident, reads=rd + [M])
                        yield
                        k.tr(p1, p1[:, 384:512], vT_c, self.ident, reads=rd + [M])
                        yield
                        yield
                        k.op("dve", lambda e: e.tensor_scalar(out=t["gones"][:], in0=self.ones_f[:], scalar1=gcol, scalar2=None, op0=ALU.mult),
                             reads=[gg, self.ones_f], writes=[t["gones"]])
                        yield
                        p2 = k.psum()
                        k.mm(p2, p2[:, 0:128], t["gones"][:], M[:, mcs, :], True, True, reads=[t["gones"], M])
                        yield
                        k.mm(p2, p2[:, 128:129], M[:, mcs, :], gcol, True, True, reads=[M, gg])
                        yield
                        k.op("act", lambda e: e.activation(out=t["gcb"][:], in_=p2[:, 0:128], func=AF.Copy), reads=[p2], writes=[t["gcb"]])
                        yield
                        k.op("dve", lambda e: e.tensor_copy(out=sc[:, 0:1], in_=p2[:, 128:129]), reads=[p2], writes=[sc])
                        yield
                        yield
                        k.op("dve", lambda e: e.tensor_scalar(out=t["X"][:], in0=t["gcb"][:], scalar1=sc[:, 0:1], scalar2=None, op0=ALU.subtract),
                             reads=[t["gcb"], sc], writes=[t["X"]])
                        yield
                        k.op("dve", lambda e: e.tensor_scalar(out=t["DT"][:], in0=t["X"][:], scalar1=0.0, scalar2=None, op0=ALU.min), reads=[t["X"]], writes=[t["DT"]])
                        yield
                        k.op("act", lambda e: e.activation(out=t["DT"][:], in_=t["DT"][:], func=AF.Exp), reads=[t["DT"]], writes=[t["DT"]])
                        yield
                        k.op("pool", lambda e: e.tensor_scalar(out=t["Dm"][:], in0=t["X"][:], scalar1=-1.0, scalar2=0.0, op0=ALU.mult, op1=ALU.min),
                             reads=[t["X"]], writes=[t["Dm"]])
                        yield
                        k.op("act", lambda e: e.activation(out=t["Dm"][:], in_=t["Dm"][:], func=AF.Exp), reads=[t["Dm"]], writes=[t["Dm"]])
                        yield
                        k.op("pool", lambda e: e.tensor_tensor(out=t["DT"][:], in0=t["DT"][:], in1=M[:, m_dt_incl, :], op=ALU.mult), reads=[t["DT"], M], writes=[t["DT"]])
                        yield
                        k.op("pool", lambda e: e.tensor_tensor(out=t["Dm"][:], in0=t["Dm"][:], in1=M[:, m_d_strict, :], op=ALU.mult), reads=[t["Dm"], M], writes=[t["Dm"]])
                        yield
                        yield
                        k.op("act", lambda e: e.activation(out=t["E"][:], in_=t["gcb"][:], func=AF.Exp), reads=[t["gcb"]], writes=[t["E"]])
                        yield
                        k.op("act", lambda e: e.activation(out=sc[:, 1:2], in_=sc[:, 0:1], func=AF.Exp), reads=[sc], writes=[sc])
                        yield
                        k.op("dve", lambda e: e.tensor_tensor(out=sc[:, 1:2], in0=sc[:, 1:2], in1=bcol, op=ALU.mult), reads=[sc, bt], writes=[sc])
                        yield
                        k.op("dve", lambda e: e.tensor_tensor(out=sc[:, 2:3], in0=t["gcb"][:, last:last + 1], in1=sc[:, 0:1], op=ALU.subtract),
                             reads=[t["gcb"], sc], writes=[sc])
                        yield
                        k.op("act", lambda e: e.activation(out=sc[:, 2:3], in_=sc[:, 2:3], func=AF.Exp), reads=[sc], writes=[sc])
                        yield
                        yield
                        k.op("dve", lambda e: e.scalar_tensor_tensor(out=t["N"][:], in0=p1[:, 0:128], scalar=bcol, in1=t["Dm"][:], op0=ALU.mult, op1=ALU.mult),
                             reads=[p1, bt, t["Dm"]], writes=[t["N"]])
                        yield
                        k.op("pool", lambda e: e.tensor_scalar(out=t["N"][:], in0=t["N"][:], scalar1=-1.0, scalar2=None, op0=ALU.mult), reads=[t["N"]], writes=[t["N"]])
                        yield
                        k.op("dve", lambda e: e.tensor_tensor(out=t["AT"][:], in0=p1[:, 128:256], in1=t["DT"][:], op=ALU.mult), reads=[p1, t["DT"]], writes=[t["AT"]])
                        yield
                        yield
                        k.op("dve", lambda e: e.tensor_scalar(out=t["kbg"][:], in0=p1[:, 256:384], scalar1=sc[:, 1:2], scalar2=None, op0=ALU.mult), reads=[p1, sc], writes=[t["kbg"]])
                        yield
                        k.op("dve", lambda e: e.tensor_scalar(out=t["kdec"][:], in0=p1[:, 256:384], scalar1=sc[:, 2:3], scalar2=None, op0=ALU.mult), reads=[p1, sc], writes=[t["kdec"]])
                        yield
                        k.op("dve", lambda e: e.tensor_scalar(out=t["vb"][:], in0=p1[:, 384:512], scalar1=bcol, scalar2=None, op0=ALU.mult), reads=[p1, bt], writes=[t["vb"]])
                        yield
                        k.op("pool", lambda e: e.tensor_tensor(out=t["qdec"][:], in0=qT_c, in1=t["E"][:], op=ALU.mult), reads=[acc.sub(0), t["E"]], writes=[t["qdec"]])
                        yield
                        if GSTOP <= 4:
                            continue
                        yield
                        p3 = k.psum()
                        k.tr(p3, p3[:, 0:128], t["N"][:], self.ident, reads=[t["N"], M])
                        yield
                        k.op("act", lambda e: e.activation(out=t["NT"][:], in_=p3[:, 0:128], func=AF.Copy), reads=[p3], writes=[t["NT"]])
                        yield
                        yield
                        k.op("dve", lambda e: e.tensor_tensor(out=t["Wm"][:], in0=t["NT"][:], in1=self.ident, op=ALU.add), reads=[t["NT"], M], writes=[t["Wm"]])
                        yield
                        Bc, BcT, Wc = t["NT"], t["N"], t["Wm"]
                        nxt = [(t["Bm"], t["BmT"], t["W2"]), (t["B2"], t["B2T"], t["Wm"])]
                        for lev in range(1, 7):
                            Bn, BnT, Wn = nxt[(lev - 1) % 2]
                            pq = k.psum()
                            k.mm(pq, pq[:, 0:128], Bc[:], BcT[:], True, True, reads=[Bc, BcT])
                            yield
                            if lev < 6:
                                k.mm(pq, pq[:, 128:256], BcT[:], Bc[:], True, True, reads=[Bc, BcT])
                                yield
                            k.op("act", lambda e, pq=pq, BnT=BnT: e.activation(out=BnT[:], in_=pq[:, 0:128], func=AF.Copy), reads=[pq], writes=[BnT])
                            yield
                            if lev < 6:
                                k.op("dve", lambda e, pq=pq, Bn=Bn: e.tensor_copy(out=Bn[:], in_=pq[:, 128:256]), reads=[pq], writes=[Bn])
                                yield
                            yield
                            pw = k.psum()
                            k.mm(pw, pw[:, 0:128], BnT[:], Wc[:], True, True, reads=[BnT, Wc])
                            yield
                            k.op("dve", lambda e, pw=pw, Wn=Wn, Wc=Wc: e.tensor_tensor(out=Wn[:], in0=pw[:, 0:128], in1=Wc[:], op=ALU.add), reads=[pw, Wc], writes=[Wn])
                            yield
                            Bc, BcT, Wc = Bn, BnT, Wn
                            yield
                        TT = Wc
                        yield
                        p4 = k.psum()
                        k.mm(p4, p4[:, 0:128], TT[:], t["vb"][:], True, True, reads=[TT, t["vb"]])
                        yield
                        k.mm(p4, p4[:, 128:256], t["kbg"][:], TT[:], True, True, reads=[TT, t["kbg"]])
                        yield
                        k.op("act", lambda e: e.activation(out=t["u"][:], in_=p4[:, 0:128], func=AF.Copy), reads=[p4], writes=[t["u"]])
                        yield
                        k.op("dve", lambda e: e.tensor_copy(out=t["wkT"][:], in_=p4[:, 128:256]), reads=[p4], writes=[t["wkT"]])
                        yield
                        if GSTOP <= 5:
                            continue
                        yield
                        p5 = k.psum()
                        k.mm(p5, p5[:, 0:128], t["wkT"][:], S[:], True, True, reads=[t["wkT"], S])
                        yield
                        k.op("dve", lambda e: e.tensor_tensor(out=t["vnew"][:], in0=t["u"][:], in1=p5[:, 0:128], op=ALU.subtract), reads=[p5, t["u"]], writes=[t["vnew"]])
                        yield
                        k.mm(p5, p5[:, 128:256], t["qdec"][:], t["Sb"][:], True, False, reads=[t["qdec"], t["Sb"]])
                        yield
                        k.mm(p5, p5[:, 128:256], t["AT"][:], t["vnew"][:], False, True, reads=[t["AT"], t["vnew"]])
                        yield
                        k.mm(p5, p5[:, 256:384], t["kdec"][:], t["vnew"][:], True, True, reads=[t["kdec"], t["vnew"]])
                        yield
                        k.op("act", lambda e, ci=ci: e.activation(out=osum[:, ci, :], in_=p5[:, 128:256], func=AF.Copy), reads=[p5], writes=[osum.sub(ci)])
                        yield
                        k.op("dve", lambda e: e.scalar_tensor_tensor(out=S[:], in0=S[:], scalar=t["E"][:, last:last + 1], in1=p5[:, 256:384], op0=ALU.mult, op1=ALU.add),
                             reads=[p5, S, t["E"]], writes=[S])
                        yield
                        k.op("act", lambda e: e.activation(out=t["Sb"][:], in_=S[:], func=AF.Copy), reads=[S], writes=[t["Sb"]])
                        yield
                    if not smp:
                        k.dma("sp", self.O["n_st"][seq_i, l, d, h], S[:], reads=[S], writes=[])
                        yield
                chains = []
                for d in range(2):
                    chains.append([chain(d, tqs[d], scs[d], Ss[d], osums[d]), [4 * d, 4 * d + 1, 4 * d + 2, 4 * d + 3], 0])
                save_rot, save_next = k.ps_rot, k.psnext
                while chains:
                    for cobj in list(chains):
                        k.ps_rot, k.psnext = cobj[1], cobj[2]
                        try:
                            next(cobj[0])
                        except StopIteration:
                            chains.remove(cobj)
                        cobj[2] = k.psnext
                k.ps_rot, k.psnext = save_rot, save_next
                if GSTOP <= 6:
                    continue
                for ci in range(nch):
                    a0 = s0 + ci * 128
                    ps = k.psum()
                    for kc in range(8):
                        k.mm(ps, ps[:, 0:128], hT[:, kc, a0:a0 + 128], w[:, kc, 384:512], kc == 0, kc == 7, reads=[w, hT.sub((a0 // 512) * 512)])
                    t = tqs[ci % 2]
                    sc = scs[ci % 2]
                    k.op("act", lambda e, ps=ps: e.activation(out=t["gz"][:], in_=ps[:, 0:128], func=AF.Silu), reads=[ps], writes=[t["gz"]])
                    k.op("pool", lambda e, ci=ci: e.tensor_tensor(out=osum[:, ci, :], in0=osum[:, ci, :], in1=osums[1][:, ci, :], op=ALU.add),
                         reads=[osum.sub(ci), osums[1].sub(ci)], writes=[osum.sub(ci)])
                    k.op("act", lambda e, ci=ci: e.activation(out=t["on"][:], in_=osum[:, ci, :], func=AF.Square, accum_out=sc[:, 3:4]), reads=[osum.sub(ci)], writes=[t["on"], sc])
                    k.op("act", lambda e: e.activation(out=sc[:, 4:5], in_=sc[:, 3:4], func=AF.Sqrt, bias=self.epsc[:, 0:1], scale=1.0 / 128), reads=[sc, self.epsc], writes=[sc])
                    k.op("dve", lambda e: e.reciprocal(out=sc[:, 4:5], in_=sc[:, 4:5]), reads=[sc], writes=[sc])
                    k.op("dve", lambda e, ci=ci: e.scalar_tensor_tensor(out=t["on"][:], in0=osum[:, ci, :], scalar=sc[:, 4:5], in1=self.rowv(l, "gnorm"), op0=ALU.mult, op1=ALU.mult),
                         reads=[osum.sub(ci), sc, self.rowbc], writes=[t["on"]])
                    k.op("pool", lambda e: e.tensor_tensor(out=t["on"][:], in0=t["on"][:], in1=t["gz"][:], op=ALU.mult), reads=[t["on"], t["gz"]], writes=[t["on"]])
                    p6 = k.psum()
                    k.tr(p6, p6[:, 0:128], t["on"][:], self.ident, reads=[t["on"], M])
                    k.op("act", lambda e, p6=p6, ci=ci: e.activation(out=mixT[:, ci * 128:(ci + 1) * 128], in_=p6[:, 0:128], func=AF.Copy), reads=[p6], writes=[mixT])
                for (g0, n) in groups:
                    self.apply_out(l, path, s0 + g0, n, (lambda c: mixT[:, g0:g0 + n], [mixT]), 1, wout, h)
        k.end_phase()


_CACHE = {}


def _get_program():
    if "nc" not in _CACHE:
        b = Builder()
        _CACHE["nc"] = b.build()
        _CACHE["b"] = b
    return _CACHE["nc"], _CACHE["b"]


def make_in_maps(inp):
    f = lambda a: np.ascontiguousarray(np.asarray(a, dtype=np.float32))
    hc = host_consts()
    shared = {}
    for nm in ["w_ada", "ffn1_w1", "ffn1_w2", "w_in", "mla_w_qb", "mla_w_kvb", "w_out", "ffn2_w1", "ffn2_w2"]:
        shared[nm] = f(inp[nm])
    shared["b_ada"] = f(inp["b_ada"]).reshape(L, 72, 128)
    shared["convw"] = f(inp["gdn_conv_w"]).reshape(L * 5 * 12, 128)
    rowv = np.zeros((L, 304), np.float32)
    rowv[:, 0:8] = f(inp["swa_sink"])
    rowv[:, 8:16] = f(inp["gdn_a_log"]).reshape(L, 8)
    rowv[:, 16:24] = f(inp["gdn_dt_bias"]).reshape(L, 8)
    rowv[:, 24:152] = f(inp["gdn_norm"])
    shared["rowv"] = rowv.reshape(1, L * 304)
    shared.update(hc)
    maps = []
    for c in range(8):
        b = c // 4
        m = dict(shared)
        m["xp"] = f(inp["x_prompt"][2 * c:2 * c + 2]).reshape(NPT, D)
        m["xs"] = f(inp["x_sample"][b])
        m["c_ckv"] = f(inp["cache_mla_ckv"][b])
        m["c_kr"] = f(inp["cache_mla_krope"][b])
        m["c_sk"] = f(inp["cache_swa_k"][b]).reshape(L, PAST, 128)
        m["c_sv"] = f(inp["cache_swa_v"][b]).reshape(L, PAST, 128)
        m["c_st"] = f(inp["state_gdn"][b])
        vecA = np.zeros((82, 128), np.float32)
        vecA[0:16] = f(inp["norm_ffn1"]).reshape(16, 128)
        vecA[16:32] = f(inp["norm_mix"]).reshape(16, 128)
        vecA[32:48] = f(inp["norm_ffn2"]).reshape(16, 128)
        vecA[48:56] = f(inp["final_norm"]).reshape(8, 128)
        vecA[56:64] = f(inp["c_ctx"]).reshape(8, 128)
        vecA[64:72] = f(inp["c"][b]).reshape(8, 128)
        vecA[72:78] = f(inp["mla_q_norm"]).reshape(6, 128)
        vecA[78:82] = f(inp["mla_kv_norm"]).reshape(4, 128)
        m["vecA"] = vecA
        maps.append(m)
    return maps


def kernel(**inp):
    nc, b = _get_program()
    maps = make_in_maps(inp)
    res = run_bass_kernel_spmd(nc, maps, core_ids=list(range(8)))
    R = res.results
    yp = np.concatenate([R[c]["yp"].reshape(2, 256, D) for c in range(8)], 0)
    ys = np.stack([R[0]["ys"], R[4]["ys"]], 0)
    n_ckv = np.concatenate([R[c]["n_ckv"] for c in range(8)], 0)
    n_kr = np.concatenate([R[c]["n_kr"] for c in range(8)], 0)
    n_sk = np.concatenate([R[c]["n_sk"] for c in range(8)], 0).reshape(16, L, 256, 2, 64)
    n_sv = np.concatenate([R[c]["n_sv"] for c in range(8)], 0).reshape(16, L, 256, 2, 64)
    n_st = np.concatenate([R[c]["n_st"] for c in range(8)], 0)
    return (yp.astype(np.float32), ys.astype(np.float32), n_ckv.astype(np.float32), n_kr.astype(np.float32),
            n_sk.astype(np.float32), n_sv.astype(np.float32), n_st.astype(np.float32))
```

```python
import contextlib
import numpy as np
import concourse.bass as bass
import concourse.mybir as mybir
from concourse.bass_utils import run_bass_kernel_spmd

F32 = mybir.dt.float32
BF16 = mybir.dt.bfloat16
AF = mybir.ActivationFunctionType
ALU = mybir.AluOpType

D = 1024
DFF = 2816
L = 2
NPT = 512
NST = 2048
PAST = 512
EPS = 1e-6
DEBUG = []
STOP = None
SKIP = None
GSTOP = 99
FUSE_WAIT = True


class Buf:
    __slots__ = ("name", "w", "r", "parent", "kids")

    def __init__(self, name, parent=None):
        self.name = name
        self.w = None
        self.r = []
        self.parent = parent
        self.kids = {}

    def sub(self, key):
        if key not in self.kids:
            self.kids[key] = Buf(f"{self.name}.{key}", self)
        return self.kids[key]

    def family(self):
        out = [self]
        p = self.parent
        while p is not None:
            out.append(p)
            p = p.parent
        st = list(self.kids.values())
        while st:
            q = st.pop()
            out.append(q)
            st.extend(q.kids.values())
        return out


class T:
    def __init__(self, t, name):
        self.t = t
        self.b = Buf(name)

    def __getitem__(self, key):
        return self.t[key]

    def sub(self, key):
        return self.b.sub(key)


def _b(x):
    return x.b if isinstance(x, T) else x


class K:
    SAME_ENGINE_SYNC = True

    def __init__(self, n_dma_sems=64):
        self.nc = bass.Bass("TRN2", target_bir_lowering=False)
        nc = self.nc
        self.es = contextlib.ExitStack()
        self.engs = {"pe": nc.tensor, "act": nc.scalar, "dve": nc.vector, "pool": nc.gpsimd, "sp": nc.sync}
        self.sems = {}
        self.cnt = {}
        for e in self.engs:
            self.sems[e] = self.es.enter_context(nc.semaphore(f"prog_{e}"))
            self.cnt[e] = 0
        self.dsems = [self.es.enter_context(nc.semaphore(f"dma_{i}")) for i in range(n_dma_sems)]
        self.dval = [0] * n_dma_sems
        self.dnext = 0
        self.dnext2 = 0
        self.semobj = dict(self.sems)
        for i, s in enumerate(self.dsems):
            self.semobj[("d", i)] = s
        self.waited = {}
        self.n_wait = 0
        self.n_ins = 0
        self.phase_es = None
        self.psums = []
        self.psnext = 0
        self.ps_rot = list(range(8))

    def sb(self, name, shape, dtype=F32, glob=False):
        es = self.es if (glob or self.phase_es is None) else self.phase_es
        self.n_alloc = getattr(self, "n_alloc", 0) + 1
        name = f"sb{self.n_alloc}_{name}"
        t = es.enter_context(self.nc.sbuf_tensor(name, list(shape), dtype))
        return T(t, name)

    def dram(self, name, shape, dtype=F32, kind="Internal"):
        return T(self.nc.dram_tensor(name, list(shape), dtype, kind=kind), name)

    def init_psum(self):
        for i in range(8):
            t = self.es.enter_context(self.nc.psum_tensor(f"psb{i}", [128, 512], F32))
            self.psums.append(T(t, f"psb{i}"))

    def psum(self):
        rot = self.ps_rot
        self.psnext = (self.psnext + 1) % len(rot)
        return self.psums[rot[self.psnext]]

    def psum_hold(self, n):
        held = self.ps_rot[-n:]
        self.ps_rot = self.ps_rot[:-n]
        self.psnext = 0
        return [self.psums[i] for i in held]

    def psum_release(self, banks):
        self.ps_rot = self.ps_rot + [self.psums.index(b) for b in banks]

    def begin_phase(self):
        assert self.phase_es is None
        self.phase_es = contextlib.ExitStack()

    def end_phase(self):
        self.barrier()
        self.phase_es.close()
        self.phase_es = None

    def _deps(self, reads, writes, e=None):
        deps = {}

        def add(d):
            if d is not None and deps.get(d[0], 0) < d[1]:
                deps[d[0]] = d[1]
        for b in reads:
            bb = _b(b)
            for f in bb.family():
                add(f.w)
            if bb.name.startswith("psb"):
                for f in bb.family():
                    for r in f.r:
                        if r[0] != e:
                            add(r)
        for b in writes:
            for f in _b(b).family():
                add(f.w)
                for r in f.r:
                    add(r)
        return deps

    def _emit_waits(self, e, deps, keep_one=False):
        eng = self.engs[e]
        need = []
        for kk, v in deps.items():
            if kk == e and (e == "pe" or not self.SAME_ENGINE_SYNC):
                continue
            if self.waited.get((e, kk), 0) >= v:
                continue
            need.append((kk, v))
            self.waited[(e, kk)] = v
        fused = None
        if keep_one and FUSE_WAIT and need:
            fused = need.pop()
        for kk, v in need:
            eng.wait_ge(self.semobj[kk], v)
            self.n_wait += 1
        return fused

    def _record(self, tag, reads, writes):
        for b in reads:
            _b(b).r.append(tag)
        for b in writes:
            bb = _b(b)
            bb.w = tag
            bb.r = []

    def op(self, e, fn, reads=(), writes=()):
        deps = self._deps(reads, writes, e)
        fused = self._emit_waits(e, deps, keep_one=True)
        ins = fn(self.engs[e])
        if fused is not None:
            ins._wait_ge(self.semobj[fused[0]], fused[1])
        self.cnt[e] += 1
        ins.then_inc(self.sems[e], 1)
        self._record((e, self.cnt[e]), reads, writes)
        self.n_ins += 1
        return ins

    def dma(self, q, out, in_, reads=(), writes=(), **kw):
        deps = self._deps(reads, writes)
        half = len(self.dsems) // 2
        if q == "sp":
            i = self.dnext % half
            self.dnext += 1
        else:
            i = half + self.dnext2 % half
            self.dnext2 += 1
        key = ("d", i)
        if self.dval[i] > 0:
            deps[key] = max(deps.get(key, 0), self.dval[i])
        self._emit_waits(q, deps)
        ins = self.engs[q].dma_start(out=out, in_=in_, **kw)
        self.dval[i] += 16
        ins.then_inc(self.dsems[i], 16)
        self._record((key, self.dval[i]), reads, writes)
        self.n_ins += 1
        return ins

    def barrier(self):
        alld = {e: self.cnt[e] for e in self.engs if self.cnt[e] > 0}
        for i, v in enumerate(self.dval):
            if v > 0:
                alld[("d", i)] = v
        for e in self.engs:
            self._emit_waits(e, dict(alld))

    def mm(self, ps, out, lhsT, rhs, start, stop, reads):
        self.op("pe", lambda e: e.matmul(out, lhsT=lhsT, rhs=rhs, start=start, stop=stop), reads=reads, writes=[ps])

    def tr(self, ps, out, in_, ident, reads):
        self.op("pe", lambda e: e.transpose(out, in_, ident), reads=reads, writes=[ps])


def _rope_tables(n_tokens, dim):
    rows = n_tokens // 64
    row = np.repeat(np.arange(rows, dtype=np.float32), 64)
    col = np.tile(np.arange(64, dtype=np.float32), rows)
    axis_dim = dim // 2
    inv = (1.0 / (10000.0 ** (np.arange(0, axis_dim, 2, dtype=np.float32) / axis_dim))).astype(np.float32)
    ang_r = row[:, None] * inv[None, :]
    ang_c = col[:, None] * inv[None, :]
    ang = np.concatenate([ang_r, ang_r, ang_c, ang_c], axis=-1)
    return np.cos(ang).astype(np.float32), np.sin(ang).astype(np.float32)


def host_consts():
    c = {}
    p = np.arange(128)[:, None]
    f = np.arange(128)[None, :]
    masks = np.zeros((128, 5, 128), np.float32)
    masks[:, 0, :] = (p == f)
    masks[:, 1, :] = (p <= f)
    masks[:, 2, :] = (p < f)
    masks[:, 3, :] = (p >= f)
    masks[:, 4, :] = (p > f)
    c["masks"] = masks
    cm, sm = _rope_tables(NST, 32)
    cosm = np.ones((96, NST), np.float32)
    sinm = np.zeros((96, NST), np.float32)
    cosm[64:96] = cm.T
    sinm[64:96] = sm.T
    cs, ss = _rope_tables(NST, 64)
    c["rope_m"] = np.stack([cosm, sinm], 1).copy()
    c["rope_s"] = np.stack([cs.T, ss.T], 1).copy()
    return c


def n_is_k(ap, self):
    return ap.tensor.name == self.sm_k.t.name


class Builder:
    def __init__(self):
        self.k = K()
        k = self.k
        self.nc = k.nc
        self.ins = {}
        self.outs = {}
        k.init_psum()

    def din(self, name, shape):
        t = self.k.dram(name, shape, F32, kind="ExternalInput")
        self.ins[name] = t
        return t

    def dout(self, name, shape):
        t = self.k.dram(name, shape, F32, kind="ExternalOutput")
        self.outs[name] = t
        return t

    def build(self):
        k = self.k
        W = {}
        for nm, shp in [("xp", [NPT, D]), ("xs", [NST, D]), ("c_ckv", [L, PAST, 256]), ("c_kr", [L, PAST, 32]),
                        ("c_sk", [L, PAST, 128]), ("c_sv", [L, PAST, 128]), ("c_st", [L, 2, 4, 128, 128]),
                        ("vecA", [82, 128]), ("b_ada", [L, 72, 128]), ("convw", [120, 128]), ("rowv", [1, 304 * L]),
                        ("w_ada", [L, D, 9 * D]), ("ffn1_w1", [L, D, 2 * DFF]), ("ffn1_w2", [L, DFF, D]),
                        ("w_in", [L, D, 3504]), ("mla_w_qb", [L, 384, 768]), ("mla_w_kvb", [L, 256, 1024]),
                        ("w_out", [L, 1536, D]), ("ffn2_w1", [L, D, 2 * DFF]), ("ffn2_w2", [L, DFF, D]),
                        ("masks", [128, 5, 128]), ("rope_m", [96, 2, NST]), ("rope_s", [64, 2, NST])]:
            W[nm] = self.din(nm, shp)
        self.W = W
        O = {}
        O["yp"] = self.dout("yp", [NPT, D])
        O["ys"] = self.dout("ys", [NST, D])
        O["n_ckv"] = self.dout("n_ckv", [2, L, 256, 256])
        O["n_kr"] = self.dout("n_kr", [2, L, 256, 32])
        O["n_sk"] = self.dout("n_sk", [2, L, 256, 128])
        O["n_sv"] = self.dout("n_sv", [2, L, 256, 128])
        O["n_st"] = self.dout("n_st", [2, L, 2, 4, 128, 128])
        self.O = O
        self.dbg = {}

        self.xT = {"p": k.dram("xTp", [128, 8, NPT]), "s": k.dram("xTs", [128, 8, NST])}
        self.ntok = {"p": NPT, "s": NST}
        self.blocks = [("p", 0)] + [("s", i * 512) for i in range(4)]

        self.masks = k.sb("masks", [128, 5, 128], F32, glob=True)
        k.dma("sp", self.masks[:], W["masks"][:], writes=[self.masks])
        self.ident = self.masks.t[:, 0, :]
        self.ones_f = k.sb("ones_f", [128, 128], F32, glob=True)
        self.ones_b = k.sb("ones_b", [128, 128], BF16, glob=True)
        k.op("dve", lambda e: e.memset(self.ones_f[:], 1.0), writes=[self.ones_f])
        k.op("dve", lambda e: e.memset(self.ones_b[:], 1.0), writes=[self.ones_b])
        self.epsc = k.sb("epsc", [128, 1], F32, glob=True)
        k.op("dve", lambda e: e.memset(self.epsc[:], EPS), writes=[self.epsc])
        self.masks_b = k.sb("masks_b", [128, 5, 128], BF16, glob=True)
        k.op("dve", lambda e: e.tensor_copy(out=self.masks_b[:], in_=self.masks[:]), reads=[self.masks], writes=[self.masks_b])
        self.vecT = k.sb("vecT", [128, 82], F32, glob=True)
        self.badaT = k.sb("badaT", [128, L, 72], F32, glob=True)
        self.convT = k.sb("convT", [128, 120], F32, glob=True)
        self.rowbc = k.sb("rowbc", [128, 304 * L], F32, glob=True)
        self.modT = k.sb("modT", [128, L, 72, 2], F32, glob=True)
        self.modA = k.sb("modA", [128, L, 2, 3, 8], F32, glob=True)
        self.modG = k.sb("modG", [128, L, 2, 3, 8], F32, glob=True)
        self.hT = {"p": k.sb("hTp", [128, 8, NPT], BF16, glob=True), "s": k.sb("hTs", [128, 8, NST], BF16, glob=True)}
        stages = [("setup", self.setup_small), ("adaln", self.adaln), ("load", self.load_inputs)]
        for l in range(L):
            stages.append((f"ffn1_{l}", lambda l=l: self.ffn(l, 0)))
            for path in ("p", "s"):
                stages.append((f"mla_{l}{path}", lambda l=l, path=path: self.mla(l, path)))
                stages.append((f"swa_{l}{path}", lambda l=l, path=path: self.swa(l, path)))
                stages.append((f"gdn_{l}{path}", lambda l=l, path=path: self.gdn(l, path)))
            stages.append((f"ffn2_{l}", lambda l=l: self.ffn(l, 2)))
        stages.append(("final", self.final))
        for nm, fn in stages:
            if SKIP and nm in SKIP:
                continue
            fn()
            if STOP and nm == STOP:
                break
        if DEBUG:
            k.barrier()
            for nm in DEBUG:
                if nm in ("xTp", "xTs"):
                    src = self.xT[nm[-1]]
                    o = self.dout("dbg_" + nm, [128, 8, self.ntok[nm[-1]]])
                    for t0 in range(0, self.ntok[nm[-1]], 512):
                        k.dma("sp", o[:, :, t0:t0 + 512], src[:, :, t0:t0 + 512], reads=[src], writes=[o])
                elif nm in ("hTp", "hTs"):
                    src = self.hT[nm[-1]]
                    o = T(self.nc.dram_tensor("dbg_" + nm, [128, 8, self.ntok[nm[-1]]], BF16, kind="ExternalOutput"), "dbg_" + nm)
                    k.dma("sp", o[:, :, :], src[:, :, :], reads=[src], writes=[o])
                elif nm == "modT":
                    o = self.dout("dbg_modT", [128, L * 72 * 2])
                    k.dma("sp", o[:, :], self.modT[:, :, :, :].rearrange("p l c r -> p (l c r)"), reads=[self.modT], writes=[o])
        k.barrier()
        return self.nc

    def setup_small(self):
        k = self.k
        W = self.W
        k.begin_phase()
        stage = k.sb("stage", [128, 128], F32)
        for (src, nrow, dst) in [(W["vecA"][:, :], 82, self.vecT[:, :]), (W["b_ada"][0], 72, self.badaT[:, 0, :]),
                                 (W["b_ada"][1], 72, self.badaT[:, 1, :]), (W["convw"][:, :], 120, self.convT[:, :])]:
            k.dma("sp", stage[0:nrow, :], src, writes=[stage])
            ps = k.psum()
            k.tr(ps, ps[:, 0:nrow], stage[0:nrow, :], self.ident[0:nrow, 0:nrow], reads=[stage, self.masks])
            k.op("dve", lambda e, dst=dst, ps=ps, nrow=nrow: e.tensor_copy(out=dst, in_=ps[:, 0:nrow]), reads=[ps],
                 writes=[self.vecT, self.badaT, self.convT])
        row = k.sb("rowstage", [1, 304 * L], F32)
        k.dma("sp", row[:], W["rowv"][:, :], writes=[row])
        for i in range(0, 304 * L, 512):
            n = min(512, 304 * L - i)
            ps = k.psum()
            k.mm(ps, ps[:, 0:n], self.ones_f[0:1, :], row[0:1, i:i + n], True, True, reads=[self.ones_f, row])
            k.op("dve", lambda e, ps=ps, i=i, n=n: e.tensor_copy(out=self.rowbc[:, i:i + n], in_=ps[:, 0:n]), reads=[ps], writes=[self.rowbc])
        k.end_phase()

    def rowv(self, l, what):
        o = {"sink": (0, 8), "alog": (8, 16), "dtb": (16, 24), "gnorm": (24, 152)}[what]
        return self.rowbc.t[:, l * 304 + o[0]: l * 304 + o[1]]

    def adaln(self):
        k = self.k
        W = self.W
        k.begin_phase()
        condT = k.sb("condT", [128, 8, 2], BF16)
        tmp = k.sb("condtmp", [128, 16], F32)
        k.op("act", lambda e: e.activation(out=tmp[:], in_=self.vecT[:, 56:72], func=AF.Silu), reads=[self.vecT], writes=[tmp])
        for r in range(2):
            k.op("dve", lambda e, r=r: e.tensor_copy(out=condT[:, :, r], in_=tmp[:, r * 8:(r + 1) * 8]), reads=[tmp], writes=[condT])
        wb = [k.sb(f"wada{i}", [128, 8, 512], BF16) for i in range(2)]
        for l in range(L):
            psm = k.psum()
            for g in range(18):
                w = wb[g % 2]
                k.dma("pool", w[:], W["w_ada"][l].rearrange("(kc p) n -> p kc n", p=128)[:, :, g * 512:(g + 1) * 512], writes=[w])
                for j in range(4):
                    cc = g * 4 + j
                    for kc in range(8):
                        k.mm(psm, psm[:, cc * 2:cc * 2 + 2], w[:, kc, j * 128:(j + 1) * 128], condT[:, kc, :], kc == 0, kc == 7,
                             reads=[w, condT])
            for r in range(2):
                k.op("dve", lambda e, l=l, r=r, psm=psm: e.tensor_tensor(
                    out=self.modT[:, l, :, r], in0=psm[:, 0:144].rearrange("p (c r) -> p c r", r=2)[:, :, r],
                    in1=self.badaT[:, l, :], op=ALU.add), reads=[psm, self.badaT], writes=[self.modT])
        for l in range(L):
            for r in range(2):
                for s in range(3):
                    gain = self.vecT[:, [0, 16, 32][s] + l * 8:[0, 16, 32][s] + l * 8 + 8]
                    sc = self.modT[:, l, (3 * s + 1) * 8:(3 * s + 2) * 8, r]
                    k.op("dve", lambda e, l=l, r=r, s=s, gain=gain, sc=sc: e.scalar_tensor_tensor(
                        out=self.modA[:, l, r, s, :], in0=sc, scalar=1.0, in1=gain, op0=ALU.add, op1=ALU.mult),
                        reads=[self.modT, self.vecT], writes=[self.modA])
                    gt = self.modT[:, l, (3 * s + 2) * 8:(3 * s + 3) * 8, r]
                    k.op("dve", lambda e, l=l, r=r, s=s, gt=gt: e.tensor_scalar(
                        out=self.modG[:, l, r, s, :], in0=gt, scalar1=(1.0 if s == 1 else 0.5), scalar2=None, op0=ALU.mult),
                        reads=[self.modT], writes=[self.modG])
        k.end_phase()

    def mod_shift(self, l, r, s, kc):
        c = (3 * s) * 8 + kc
        return self.modT.t[:, l, c:c + 1, r]

    def load_inputs(self):
        k = self.k
        k.begin_phase()
        xin = [k.sb(f"xin{i}", [128, D], F32) for i in range(2)]
        xblk = [k.sb(f"xblk{i}", [128, 8, 512], F32) for i in range(2)]
        bi = 0
        for (path, t0) in self.blocks:
            src = self.W["xp"] if path == "p" else self.W["xs"]
            xb = xblk[bi % 2]
            for tt in range(4):
                xi = xin[tt % 2]
                k.dma("sp", xi[:], src[t0 + tt * 128:t0 + (tt + 1) * 128, :], writes=[xi])
                for half in range(2):
                    ps = k.psum()
                    for j in range(4):
                        kc = half * 4 + j
                        k.tr(ps, ps[:, j * 128:(j + 1) * 128], xi[:, kc * 128:(kc + 1) * 128], self.ident, reads=[xi, self.masks])
                    eng = "dve" if half == 0 else "act"
                    if eng == "dve":
                        k.op("dve", lambda e, xb=xb, ps=ps, half=half, tt=tt: e.tensor_copy(
                            out=xb[:, half * 4:half * 4 + 4, tt * 128:(tt + 1) * 128],
                            in_=ps[:, :].rearrange("p (j t) -> p j t", j=4)), reads=[ps], writes=[xb])
                    else:
                        k.op("act", lambda e, xb=xb, ps=ps, half=half, tt=tt: e.activation(
                            out=xb[:, half * 4:half * 4 + 4, tt * 128:(tt + 1) * 128],
                            in_=ps[:, :].rearrange("p (j t) -> p j t", j=4), func=AF.Copy), reads=[ps], writes=[xb])
            k.dma("sp", self.xT[path][:, :, t0:t0 + 512], xb[:], reads=[xb], writes=[self.xT[path].sub(t0)])
            bi += 1
        k.end_phase()

    def rmsnorm(self, x, n, Acol, Bcol, out, sq, rstd, nfeat_chunks=8, nfeat=D, tmp=None):
        k = self.k
        xs, xr = x
        outf, outw = out
        for kc in range(nfeat_chunks):
            k.op("act", lambda e, kc=kc: e.activation(out=sq[:, kc, 0:n], in_=xs(kc), func=AF.Square), reads=xr, writes=[sq])
        ps = k.psum()
        for kc in range(nfeat_chunks):
            k.mm(ps, ps[:, 0:n], self.ones_b[:, :], sq[:, kc, 0:n], kc == 0, kc == nfeat_chunks - 1, reads=[self.ones_b, sq])
        k.op("act", lambda e: e.activation(out=rstd[:, 0:n], in_=ps[:, 0:n], func=AF.Sqrt, bias=self.epsc[:, 0:1], scale=1.0 / nfeat),
             reads=[ps, self.epsc], writes=[rstd])
        k.op("dve", lambda e: e.reciprocal(out=rstd[:, 0:n], in_=rstd[:, 0:n]), reads=[rstd], writes=[rstd])
        for kc in range(nfeat_chunks):
            if Bcol is None:
                k.op("dve", lambda e, kc=kc: e.scalar_tensor_tensor(out=outf(kc), in0=xs(kc), scalar=Acol(kc), in1=rstd[:, 0:n],
                                                                     op0=ALU.mult, op1=ALU.mult), reads=list(xr) + [rstd], writes=outw)
            else:
                k.op("dve", lambda e, kc=kc: e.scalar_tensor_tensor(out=tmp[:, kc, 0:n], in0=xs(kc), scalar=Acol(kc), in1=rstd[:, 0:n],
                                                                     op0=ALU.mult, op1=ALU.mult), reads=list(xr) + [rstd], writes=[tmp])
                k.op("act", lambda e, kc=kc: e.activation(out=outf(kc), in_=tmp[:, kc, 0:n], func=AF.Identity, bias=Bcol(kc), scale=1.0),
                     reads=[tmp], writes=outw)

    def ffn(self, l, s):
        k = self.k
        W = self.W
        w1 = W["ffn1_w1" if s == 0 else "ffn2_w1"][l].rearrange("(kc p) n -> p kc n", p=128)
        w2 = W["ffn1_w2" if s == 0 else "ffn2_w2"][l].rearrange("(j p) n -> p j n", p=128)
        k.begin_phase()
        xb = k.sb("f_x", [128, 8, 512], F32)
        hb = k.sb("f_h", [128, 8, 512], BF16)
        tmp = k.sb("f_tmp", [128, 8, 512], F32)
        act = k.sb("f_act", [128, 22, 512], BF16)
        sq = k.sb("f_sq", [128, 8, 512], BF16)
        rstd = k.sb("f_rstd", [128, 512], F32)
        sg = [k.sb(f"f_sg{i}", [128, 512], F32) for i in range(2)]
        w1b = [k.sb(f"f_w1{i}", [128, 8, 512], BF16) for i in range(2)]
        w2b = [k.sb(f"f_w2{i}", [128, 22, 512], BF16) for i in range(2)]
        nw1 = 0
        nw2 = 0
        w1c = k.dram(f"w1c_{l}_{s}", [128, 11, 8, 512], BF16)
        w2c = k.dram(f"w2c_{l}_{s}", [128, 2, 22, 512], BF16)
        for bi, (path, t0) in enumerate(self.blocks):
            r = 0 if path == "p" else 1
            xsub = self.xT[path].sub(t0)
            k.dma("sp", xb[:], self.xT[path][:, :, t0:t0 + 512], reads=[xsub], writes=[xb])
            self.rmsnorm((lambda kc: xb[:, kc, :], [xb]), 512,
                         lambda kc: self.modA[:, l, r, s, kc:kc + 1], lambda kc: self.mod_shift(l, r, s, kc),
                         (lambda kc: hb[:, kc, :], [hb]), sq, rstd, tmp=tmp)
            for jg in range(11):
                wt = w1b[nw1 % 2]
                nw1 += 1
                if bi == 0:
                    k.dma("pool", wt[:, :, 0:256], w1[:, :, jg * 256:(jg + 1) * 256], writes=[wt])
                    k.dma("pool", wt[:, :, 256:512], w1[:, :, DFF + jg * 256:DFF + (jg + 1) * 256], writes=[wt])
                    k.dma("sp", w1c[:, jg], wt[:], reads=[wt], writes=[w1c.sub(jg)])
                else:
                    k.dma("sp", wt[:], w1c[:, jg], reads=[w1c.sub(jg)], writes=[wt])
                for jj in range(2):
                    j = jg * 2 + jj
                    pg = k.psum()
                    pu = k.psum()
                    for kc in range(8):
                        k.mm(pg, pg[:, :], wt[:, kc, jj * 128:(jj + 1) * 128], hb[:, kc, :], kc == 0, kc == 7, reads=[wt, hb])
                    for kc in range(8):
                        k.mm(pu, pu[:, :], wt[:, kc, 256 + jj * 128:256 + (jj + 1) * 128], hb[:, kc, :], kc == 0, kc == 7, reads=[wt, hb])
                    sgt = sg[j % 2]
                    k.op("act", lambda e, sgt=sgt, pg=pg: e.activation(out=sgt[:], in_=pg[:, :], func=AF.Silu), reads=[pg], writes=[sgt])
                    k.op("dve", lambda e, sgt=sgt, pu=pu, j=j: e.tensor_tensor(out=act[:, j, :], in0=pu[:, :], in1=sgt[:], op=ALU.mult),
                         reads=[pu, sgt], writes=[act.sub(j)])
            for half in range(2):
                wt = w2b[nw2 % 2]
                nw2 += 1
                if bi == 0:
                    for q4 in range(2):
                        k.dma("pool", wt[:, q4 * 11:(q4 + 1) * 11, :], w2[:, q4 * 11:(q4 + 1) * 11, half * 512:(half + 1) * 512], writes=[wt])
                    k.dma("sp", w2c[:, half], wt[:], reads=[wt], writes=[w2c.sub(half)])
                else:
                    k.dma("sp", wt[:], w2c[:, half], reads=[w2c.sub(half)], writes=[wt])
                for f4 in range(4):
                    fc = half * 4 + f4
                    po = k.psum()
                    for j in range(22):
                        k.mm(po, po[:, :], wt[:, j, f4 * 128:(f4 + 1) * 128], act[:, j, :], j == 0, j == 21, reads=[wt, act.sub(j)])
                    k.op("dve", lambda e, po=po, fc=fc: e.scalar_tensor_tensor(
                        out=xb[:, fc, :], in0=po[:, :], scalar=self.modG[:, l, r, s, fc:fc + 1], in1=xb[:, fc, :],
                        op0=ALU.mult, op1=ALU.add), reads=[po, xb, self.modG], writes=[xb])
            k.dma("sp", self.xT[path][:, :, t0:t0 + 512], xb[:], reads=[xb], writes=[xsub])
            if s == 0:
                hT = self.hT[path]
                self.rmsnorm((lambda kc: xb[:, kc, :], [xb]), 512,
                             lambda kc: self.modA[:, l, r, 1, kc:kc + 1], lambda kc: self.mod_shift(l, r, 1, kc),
                             (lambda kc: hT[:, kc, t0:t0 + 512], [hT.sub(t0)]), sq, rstd, tmp=tmp)
        k.end_phase()

    def final(self):
        k = self.k
        k.begin_phase()
        xb = k.sb("o_x", [128, 8, 512], F32)
        yb = k.sb("o_y", [128, 8, 512], F32)
        sq = k.sb("o_sq", [128, 8, 512], BF16)
        rstd = k.sb("o_rstd", [128, 512], F32)
        yo = [k.sb(f"o_yo{i}", [128, D], F32) for i in range(2)]
        n = 0
        for (path, t0) in self.blocks:
            k.dma("sp", xb[:], self.xT[path][:, :, t0:t0 + 512], reads=[self.xT[path].sub(t0)], writes=[xb])
            self.rmsnorm((lambda kc: xb[:, kc, :], [xb]), 512, lambda kc: self.vecT[:, 48 + kc:49 + kc], None,
                         (lambda kc: yb[:, kc, :], [yb]), sq, rstd)
            dst = self.O["yp"] if path == "p" else self.O["ys"]
            for tt in range(4):
                y = yo[n % 2]
                n += 1
                for half in range(2):
                    ps = k.psum()
                    for j in range(4):
                        kc = half * 4 + j
                        k.tr(ps, ps[:, j * 128:(j + 1) * 128], yb[:, kc, tt * 128:(tt + 1) * 128], self.ident, reads=[yb, self.masks])
                    if half == 0:
                        k.op("dve", lambda e, y=y, ps=ps: e.tensor_copy(out=y[:, 0:512], in_=ps[:, :]), reads=[ps], writes=[y])
                    else:
                        k.op("act", lambda e, y=y, ps=ps: e.activation(out=y[:, 512:1024], in_=ps[:, :], func=AF.Copy), reads=[ps], writes=[y])
                k.dma("sp", dst[t0 + tt * 128:t0 + (tt + 1) * 128, :], y[:], reads=[y], writes=[dst])
        k.end_phase()

    ao_k = 128

    def apply_out(self, l, path, t0, n, mixT, nchunks, wout_t, chunk0):
        k = self.k
        r = 0 if path == "p" else 1
        mixf, mixr = mixT
        xb = self.ao_x
        xsub = self.xT[path].sub((t0 // 512) * 512)
        k.dma("sp", xb[:, :, 0:n], self.xT[path][:, :, t0:t0 + n], reads=[xsub], writes=[xb])
        for fc in range(8):
            po = k.psum()
            for c in range(nchunks):
                k.mm(po, po[:, 0:n], wout_t[0:self.ao_k, chunk0 + c, fc * 128:(fc + 1) * 128], mixf(c), c == 0, c == nchunks - 1,
                     reads=[wout_t] + list(mixr))
            k.op("dve", lambda e, po=po, fc=fc: e.scalar_tensor_tensor(
                out=xb[:, fc, 0:n], in0=po[:, 0:n], scalar=self.modG[:, l, r, 1, fc:fc + 1], in1=xb[:, fc, 0:n],
                op0=ALU.mult, op1=ALU.add), reads=[po, xb, self.modG], writes=[xb])
        k.dma("sp", self.xT[path][:, :, t0:t0 + n], xb[:, :, 0:n], reads=[xb], writes=[xsub])

    def store_tok(self, srcf, srcr, nfeat, ntok, dst_fn, f0=0):
        k = self.k
        for tt in range(ntok // 128):
            ps = k.psum()
            k.tr(ps, ps[:, 0:nfeat], srcf(tt), self.ident[0:nfeat, 0:nfeat], reads=list(srcr) + [self.masks])
            st = self.st_buf[self.st_n % 2]
            self.st_n += 1
            k.op("act", lambda e, st=st, ps=ps: e.activation(out=st[:, 0:nfeat - f0], in_=ps[:, f0:nfeat], func=AF.Copy), reads=[ps], writes=[st])
            k.dma("sp", dst_fn(tt), st[:, 0:nfeat - f0], reads=[st], writes=[])

    def mixer(self, l, path):
        self.mla(l, path)
        self.swa(l, path)
        self.gdn(l, path)

    def seqs(self, path):
        return [(0, 256), (256, 256)] if path == "p" else [(0, NST)]

    def softmax_bound(self, q2max, k2max, scale, negc):
        k = self.k
        t = self.sm_t
        k.op("dve", lambda e: e.tensor_tensor(out=t[0:1, 0:1], in0=q2max, in1=k2max, op=ALU.add), reads=[self.sm_q, self.sm_k], writes=[t])
        k.op("dve", lambda e: e.tensor_scalar(out=t[0:1, 0:1], in0=t[0:1, 0:1], scalar1=-0.5 * scale, scalar2=None, op0=ALU.mult),
             reads=[t], writes=[t])
        ps = k.psum()
        k.mm(ps, ps[:, 0:1], self.ones_f[0:1, :], t[0:1, 0:1], True, True, reads=[self.ones_f, t])
        k.op("dve", lambda e: e.tensor_copy(out=negc[:, 0:1], in_=ps[:, 0:1]), reads=[ps], writes=[negc])

    def sq_max(self, src_ap, nrow, n, dst, first, reads, dstT=None):
        k = self.k
        sqt = self.sm_sq
        k.op("pool", lambda e: e.tensor_tensor(out=sqt[0:nrow, 0:n], in0=src_ap, in1=src_ap, op=ALU.mult), reads=reads, writes=[sqt])
        ps = k.psum()
        k.mm(ps, ps[0:1, 0:n], self.ones_b[0:nrow, 0:1], sqt[0:nrow, 0:n], True, True, reads=[self.ones_b, sqt])
        tm = self.sm_m
        if dstT is None:
            dstT = self.sm_k if n_is_k(dst, self) else self.sm_q
        if first:
            k.op("dve", lambda e: e.reduce_max(out=dst, in_=ps[0:1, 0:n], axis=mybir.AxisListType.X), reads=[ps], writes=[dstT])
        else:
            k.op("dve", lambda e: e.reduce_max(out=tm[0:1, 0:1], in_=ps[0:1, 0:n], axis=mybir.AxisListType.X), reads=[ps], writes=[tm])
            k.op("dve", lambda e: e.tensor_tensor(out=dst, in0=dst, in1=tm[0:1, 0:1], op=ALU.max), reads=[tm, dstT], writes=[dstT])

    def mla(self, l, path):
        k = self.k
        W = self.W
        smp = path == "s"
        hT = self.hT[path]
        scale = 96 ** -0.5
        k.begin_phase()
        self.ao_x = k.sb("ao_x", [128, 8, 512], F32)
        self.st_buf = [k.sb(f"st{i}", [128, 256], F32) for i in range(2)]
        self.st_n = 0
        self.sm_t = k.sb("sm_t", [1, 1], F32)
        self.sm_q = k.sb("sm_q", [1, 1], F32)
        self.sm_k = k.sb("sm_k", [1, 8], F32)
        self.sm_m = k.sb("sm_m", [1, 1], F32)
        self.sm_sq = k.sb("sm_sq", [96, 512], BF16)
        negc = k.sb("negc", [128, 1], F32)
        win = k.sb("m_win", [128, 8, 672], BF16)
        k.dma("pool", win[:], W["w_in"][l].rearrange("(kc p) n -> p kc n", p=128)[:, :, 0:672], writes=[win])
        wqb = k.sb("m_wqb", [128, 3, 768], BF16)
        k.dma("pool", wqb[:], W["mla_w_qb"][l].rearrange("(kc p) n -> p kc n", p=128), writes=[wqb])
        wkvb = k.sb("m_wkvb", [128, 2, 1024], BF16)
        k.dma("pool", wkvb[:], W["mla_w_kvb"][l].rearrange("(kc p) n -> p kc n", p=128), writes=[wkvb])
        wout = k.sb("m_wout", [64, 4, D], BF16)
        wkr = k.sb("m_wkr", [128, 8, 96], BF16)
        k.op("pool", lambda e: e.memset(wkr[:], 0.0), writes=[wkr])
        k.op("pool", lambda e: e.tensor_copy(out=wkr[:, :, 64:96], in_=win[:, :, 640:672]), reads=[win], writes=[wkr])
        if smp:
            rope = k.sb("m_rope", [96, 2, NST], F32)
            k.dma("sp", rope[:], W["rope_m"][:, :, :], writes=[rope])
            wkr_r = k.sb("m_wkr_r", [128, 8, 96], BF16)
            k.op("pool", lambda e: e.memset(wkr_r[:], 0.0), writes=[wkr_r])
            wqb_r = k.sb("m_wqb_r", [128, 3, 768], BF16)
            k.op("pool", lambda e: e.memset(wqb_r[:], 0.0), writes=[wqb_r])
            for a in range(2):
                b0 = 64 + a * 16
                k.op("pool", lambda e, b0=b0: e.tensor_scalar(out=wkr_r[:, :, b0:b0 + 8], in0=wkr[:, :, b0 + 8:b0 + 16], scalar1=-1.0,
                                                               scalar2=None, op0=ALU.mult), reads=[wkr], writes=[wkr_r])
                k.op("pool", lambda e, b0=b0: e.tensor_copy(out=wkr_r[:, :, b0 + 8:b0 + 16], in_=wkr[:, :, b0:b0 + 8]), reads=[wkr], writes=[wkr_r])
                for h in range(8):
                    c0 = h * 96 + b0
                    k.op("pool", lambda e, c0=c0: e.tensor_scalar(out=wqb_r[:, :, c0:c0 + 8], in0=wqb[:, :, c0 + 8:c0 + 16], scalar1=-1.0,
                                                                   scalar2=None, op0=ALU.mult), reads=[wqb], writes=[wqb_r])
                    k.op("pool", lambda e, c0=c0: e.tensor_copy(out=wqb_r[:, :, c0 + 8:c0 + 16], in_=wqb[:, :, c0:c0 + 8]), reads=[wqb], writes=[wqb_r])
        sq = k.sb("m_sq", [128, 3, 512], BF16)
        rstd = k.sb("m_rstd", [128, 512], F32)
        cf = k.sb("m_cf", [128, 3, 512], F32)
        krf = k.sb("m_krf", [96, 512], F32)
        krb = k.sb("m_krb", [96, 512], F32)
        qtmp, qtmp2 = krf, krb
        qTs = [k.sb(f"m_qT{i}", [96, 512], BF16) for i in range(4)]
        negcs = [k.sb(f"m_negc{i}", [128, 1], F32) for i in range(4)]
        smqs = [k.sb(f"m_smq{i}", [1, 1], F32) for i in range(4)]
        smts = [k.sb(f"m_smt{i}", [1, 1], F32) for i in range(4)]
        qT = qTs[0]
        PT = [k.sb(f"m_PT{i}", [128, 512], BF16) for i in range(3)]
        mixH = k.sb("m_mixH", [64, 4, 512], BF16)
        rrow = k.sb("m_rrow", [65, 512], F32)
        rbc = k.sb("m_rbc", [64, 512], F32)
        for (s0, T_) in self.seqs(path):
            nk = T_ + (PAST if smp else 0)
            nkt = nk // 128
            cqn = k.sb(f"m_cqn{s0}", [128, 3, T_], BF16)
            ckvn = k.sb(f"m_ckvn{s0}", [128, 2, nk], BF16)
            Kt = k.sb(f"m_Kt{s0}", [96, 4, nk], BF16)
            Vt = k.sb(f"m_Vt{s0}", [128, nkt, 4, 65], BF16)
            krall = k.sb(f"m_krall{s0}", [96, nk], BF16)
            k.op("pool", lambda e: e.memset(Vt[:], 1.0), writes=[Vt])
            groups = [(g0, min(512, T_ - g0)) for g0 in range(0, T_, 512)]
            seq_i = s0 // 256
            for (g0, n) in groups:
                a0 = s0 + g0
                for (c0, nch, gcol, dst, nfeat) in [(0, 3, 72, cqn, 384), (384, 2, 78, ckvn, 256)]:
                    pss = []
                    for c in range(nch):
                        ps = k.psum()
                        pss.append(ps)
                        for kc in range(8):
                            k.mm(ps, ps[:, 0:n], win[:, kc, c0 + c * 128:c0 + (c + 1) * 128], hT[:, kc, a0:a0 + n], kc == 0, kc == 7,
                                 reads=[win, hT.sub((a0 // 512) * 512)])
                    for c in range(nch):
                        k.op("act", lambda e, c=c, ps=pss[c]: e.activation(out=cf[:, c, 0:n], in_=ps[:, 0:n], func=AF.Copy), reads=[pss[c]], writes=[cf])
                    gcl = gcol + l * nch
                    self.rmsnorm((lambda c: cf[:, c, 0:n], [cf]), n, lambda c: self.vecT[:, gcl + c:gcl + c + 1], None,
                                 (lambda c: cf[:, c, 0:n], [cf]), sq, rstd, nfeat_chunks=nch, nfeat=nfeat)
                    for c in range(nch):
                        k.op("pool", lambda e, c=c, dst=dst: e.tensor_copy(out=dst[:, c, g0:g0 + n], in_=cf[:, c, 0:n]), reads=[cf], writes=[dst])
                    if dst is ckvn and not smp:
                        for c in range(2):
                            self.store_tok(lambda tt, c=c: cf[:, c, tt * 128:(tt + 1) * 128], [cf], 128, n,
                                           lambda tt, c=c: self.O["n_ckv"][seq_i, l, g0 + tt * 128:g0 + (tt + 1) * 128, c * 128:(c + 1) * 128])
                pa = k.psum()
                for kc in range(8):
                    k.mm(pa, pa[0:96, 0:n], wkr[:, kc, :], hT[:, kc, a0:a0 + n], kc == 0, kc == 7, reads=[wkr, hT.sub((a0 // 512) * 512)])
                if smp:
                    pb = k.psum()
                    for kc in range(8):
                        k.mm(pb, pb[0:96, 0:n], wkr_r[:, kc, :], hT[:, kc, a0:a0 + n], kc == 0, kc == 7, reads=[wkr_r, hT.sub((a0 // 512) * 512)])
                    k.op("dve", lambda e: e.tensor_tensor(out=krf[64:96, 0:n], in0=pa[64:96, 0:n], in1=rope[64:96, 0, g0:g0 + n], op=ALU.mult),
                         reads=[pa, rope], writes=[krf])
                    k.op("dve", lambda e: e.tensor_tensor(out=krb[64:96, 0:n], in0=pb[64:96, 0:n], in1=rope[64:96, 1, g0:g0 + n], op=ALU.mult),
                         reads=[pb, rope], writes=[krb])
                    k.op("pool", lambda e: e.tensor_tensor(out=krall[64:96, g0:g0 + n], in0=krf[64:96, 0:n], in1=krb[64:96, 0:n], op=ALU.add),
                         reads=[krf, krb], writes=[krall])
                else:
                    k.op("dve", lambda e: e.tensor_copy(out=krf[0:96, 0:n], in_=pa[0:96, 0:n]), reads=[pa], writes=[krf])
                    k.op("pool", lambda e: e.tensor_copy(out=krall[64:96, g0:g0 + n], in_=krf[64:96, 0:n]), reads=[krf], writes=[krall])
                    self.store_tok(lambda tt: krf[0:96, tt * 128:(tt + 1) * 128], [krf], 96, n,
                                   lambda tt: self.O["n_kr"][seq_i, l, g0 + tt * 128:g0 + (tt + 1) * 128, :], f0=64)
            if smp:
                kin = k.sb("m_kin", [128, 96], F32)
                k.op("pool", lambda e: e.memset(kin[:], 0.0), writes=[kin])
                for i in range(4):
                    ci = cf.t[:, i % 2, 0:256]
                    k.dma("sp", ci, W["c_ckv"][l, i * 128:(i + 1) * 128, :], writes=[cf])
                    ps = k.psum()
                    for c in range(2):
                        k.tr(ps, ps[:, c * 128:(c + 1) * 128], ci[:, c * 128:(c + 1) * 128], self.ident, reads=[cf, self.masks])
                    k.op("act", lambda e, ps=ps, i=i: e.activation(out=ckvn[:, :, T_ + i * 128:T_ + (i + 1) * 128],
                                                                   in_=ps[:, 0:256].rearrange("p (c t) -> p c t", c=2), func=AF.Copy),
                         reads=[ps], writes=[ckvn])
                    k.dma("sp", kin[:, 64:96], W["c_kr"][l, i * 128:(i + 1) * 128, :], writes=[kin])
                    ps2 = k.psum()
                    k.tr(ps2, ps2[0:96, 0:128], kin[:, :], self.ident, reads=[kin, self.masks])
                    k.op("dve", lambda e, ps2=ps2, i=i: e.tensor_copy(out=krall[64:96, T_ + i * 128:T_ + (i + 1) * 128], in_=ps2[64:96, 0:128]),
                         reads=[ps2], writes=[krall])
            kgroups = [(g0, min(512, nk - g0)) for g0 in range(0, nk, 512)]
            for hg in range(2):
                k.dma("pool", wout[:], W["w_out"][l][hg * 256:(hg + 1) * 256, :].rearrange("(h p) n -> p h n", p=64), writes=[wout])
                for (g0, n) in kgroups:
                    for hh in range(4):
                        h = hg * 4 + hh
                        ps = k.psum()
                        for c in range(2):
                            k.mm(ps, ps[0:64, 0:n], wkvb[:, c, h * 128:h * 128 + 64], ckvn[:, c, g0:g0 + n], c == 0, c == 1, reads=[wkvb, ckvn])
                        eng = "act" if hh % 2 == 0 else "dve"
                        if eng == "act":
                            k.op("act", lambda e, ps=ps, hh=hh: e.activation(out=Kt[0:64, hh, g0:g0 + n], in_=ps[0:64, 0:n], func=AF.Copy),
                                 reads=[ps], writes=[Kt.sub(hh)])
                        else:
                            k.op("dve", lambda e, ps=ps, hh=hh: e.tensor_copy(out=Kt[0:64, hh, g0:g0 + n], in_=ps[0:64, 0:n]), reads=[ps], writes=[Kt.sub(hh)])
                        if hg == 0:
                            pass
                        k.op("pool", lambda e, hh=hh: e.tensor_copy(out=Kt[64:96, hh, g0:g0 + n], in_=krall[64:96, g0:g0 + n]),
                             reads=[krall], writes=[Kt.sub(hh)])
                        self.sq_max(Kt[0:96, hh, g0:g0 + n], 96, n, self.sm_k[0:1, hh:hh + 1], g0 == 0, [Kt.sub(hh)])
                    for kt in range(g0 // 128, (g0 + n) // 128):
                        ps = k.psum()
                        for c in range(2):
                            k.mm(ps, ps[:, 0:256].rearrange("p (h d) -> p h d", h=4),
                                 ckvn[:, c, kt * 128:(kt + 1) * 128],
                                 wkvb[:, c, hg * 512:(hg + 1) * 512].rearrange("p (h d) -> p h d", h=4)[:, :, 64:128], c == 0, c == 1,
                                 reads=[wkvb, ckvn])
                        k.op("act", lambda e, ps=ps, kt=kt: e.activation(out=Vt[:, kt, :, 0:64], in_=ps[:, 0:256].rearrange("p (h d) -> p h d", h=4),
                                                                         func=AF.Copy), reads=[ps], writes=[Vt])
                npt = 0
                for (g0, n) in groups:
                    nqt = n // 128
                    for hh in range(4):
                        h = hg * 4 + hh
                        pa = k.psum()
                        for c in range(3):
                            k.mm(pa, pa[0:96, 0:n], wqb[:, c, h * 96:(h + 1) * 96], cqn[:, c, g0:g0 + n], c == 0, c == 2, reads=[wqb, cqn])
                        if smp:
                            pb = k.psum()
                            for c in range(3):
                                k.mm(pb, pb[0:96, 0:n], wqb_r[:, c, h * 96:(h + 1) * 96], cqn[:, c, g0:g0 + n], c == 0, c == 2, reads=[wqb_r, cqn])
                            k.op("dve", lambda e: e.tensor_tensor(out=qtmp[:, 0:n], in0=pa[0:96, 0:n], in1=rope[:, 0, g0:g0 + n], op=ALU.mult),
                                 reads=[pa, rope], writes=[qtmp])
                            k.op("dve", lambda e: e.tensor_tensor(out=qtmp2[:, 0:n], in0=pb[0:96, 0:n], in1=rope[:, 1, g0:g0 + n], op=ALU.mult),
                                 reads=[pb, rope], writes=[qtmp2])
                            k.op("pool", lambda e: e.tensor_tensor(out=qTs[hh][:, 0:n], in0=qtmp[:, 0:n], in1=qtmp2[:, 0:n], op=ALU.add),
                                 reads=[qtmp, qtmp2], writes=[qTs[hh]])
                        else:
                            k.op("act", lambda e: e.activation(out=qTs[hh][:, 0:n], in_=pa[0:96, 0:n], func=AF.Copy), reads=[pa], writes=[qTs[hh]])
                        self.sm_q = smqs[hh]; self.sm_t = smts[hh]
                        self.sq_max(qTs[hh][0:96, 0:n], 96, n, self.sm_q[0:1, 0:1], True, [qTs[hh]], dstT=self.sm_q)
                        self.softmax_bound(self.sm_q[0:1, 0:1], self.sm_k[0:1, hh:hh + 1], scale, negcs[hh])
                    for hh in range(4):
                        h = hg * 4 + hh
                        qT = qTs[hh]
                        negc = negcs[hh]
                        pos = k.psum_hold(1)
                        po = pos[0]
                        def qk(kt):
                            ps = k.psum()
                            k.mm(ps, ps[:, 0:n], Kt[:, hh, kt * 128:(kt + 1) * 128], qT[:, 0:n], True, True, reads=[Kt.sub(hh), qT])
                            return ps
                        pend = qk(0)
                        for kt in range(nkt):
                            ps = pend
                            if kt + 1 < nkt:
                                pend = qk(kt + 1)
                            pt = PT[npt % 3]
                            npt += 1
                            k.op("act", lambda e, pt=pt, ps=ps: e.activation(out=pt[:, 0:n], in_=ps[:, 0:n], func=AF.Exp, bias=negc[:, 0:1], scale=scale),
                                 reads=[ps, negc], writes=[pt])
                            k.mm(po, po[0:65, 0:n], Vt[:, kt, hh, :], pt[:, 0:n], kt == 0, kt == nkt - 1, reads=[pt, Vt])
                        k.op("act", lambda e, po=po: e.activation(out=rrow[64:65, 0:n], in_=po[64:65, 0:n], func=AF.Copy), reads=[po], writes=[rrow])
                        pbc = k.psum()
                        k.mm(pbc, pbc[0:64, 0:n], self.ones_f[64:65, 0:64], rrow[64:65, 0:n], True, True, reads=[self.ones_f, rrow])
                        k.op("dve", lambda e, pbc=pbc: e.reciprocal(out=rbc[:, 0:n], in_=pbc[0:64, 0:n]), reads=[pbc], writes=[rbc])
                        k.op("dve", lambda e, po=po, hh=hh: e.tensor_tensor(out=mixH[:, hh, 0:n], in0=po[0:64, 0:n], in1=rbc[:, 0:n], op=ALU.mult),
                             reads=[po, rbc], writes=[mixH])
                        k.psum_release(pos)
                    self.ao_k = 64
                    self.apply_out(l, path, s0 + g0, n, (lambda c: mixH[:, c, 0:n], [mixH]), 4, wout, 0)
                    self.ao_k = 128
        k.end_phase()

    def swa(self, l, path):
        k = self.k
        W = self.W
        smp = path == "s"
        hT = self.hT[path]
        scale = 64 ** -0.5
        k.begin_phase()
        self.ao_x = k.sb("ao_x", [128, 8, 512], F32)
        self.st_buf = [k.sb(f"st{i}", [128, 256], F32) for i in range(2)]
        self.st_n = 0
        self.sm_t = k.sb("sm_t", [1, 1], F32)
        self.sm_q = k.sb("sm_q", [1, 1], F32)
        self.sm_k = k.sb("sm_k", [1, 8], F32)
        self.sm_m = k.sb("sm_m", [1, 1], F32)
        self.sm_sq = k.sb("sm_sq", [96, 512], BF16)
        negc = k.sb("negc", [128, 1], F32)
        psink = k.sb("psink", [128, 8], F32)
        wsw = k.sb("s_w", [128, 8, 768], BF16)
        k.dma("pool", wsw[:], W["w_in"][l].rearrange("(kc p) n -> p kc n", p=128)[:, :, 672:1440], writes=[wsw])
        wout = k.sb("s_wout", [64, 8, D], BF16)
        k.dma("pool", wout[:], W["w_out"][l][512:1024, :].rearrange("(h p) n -> p h n", p=64), writes=[wout])
        if smp:
            rope = k.sb("s_rope", [64, 2, NST], F32)
            k.dma("sp", rope[:], W["rope_s"][:, :, :], writes=[rope])
            wr = k.sb("s_wr", [128, 8, 640], BF16)
            wv = wsw[:, :, 0:640].rearrange("p k (h a z d) -> p k h a z d", h=10, a=2, z=2)
            wrv = wr[:, :, :].rearrange("p k (h a z d) -> p k h a z d", h=10, a=2, z=2)
            for kc in range(8):
                k.op("pool", lambda e, kc=kc: e.tensor_scalar(out=wrv[:, kc, :, :, 0, :], in0=wv[:, kc, :, :, 1, :], scalar1=-1.0, scalar2=None, op0=ALU.mult),
                     reads=[wsw], writes=[wr])
                k.op("pool", lambda e, kc=kc: e.tensor_copy(out=wrv[:, kc, :, :, 1, :], in_=wv[:, kc, :, :, 0, :]), reads=[wsw], writes=[wr])
        kf = k.sb("s_kf", [64, 512], F32)
        kb_ = k.sb("s_kb", [64, 512], F32)

        def proj_rope(col0, a0, n, g0, dst_ap, dst_w, fp_out=None):
            pa = k.psum()
            for kc in range(8):
                k.mm(pa, pa[0:64, 0:n], wsw[:, kc, col0:col0 + 64], hT[:, kc, a0:a0 + n], kc == 0, kc == 7, reads=[wsw, hT.sub((a0 // 512) * 512)])
            if smp:
                pb = k.psum()
                for kc in range(8):
                    k.mm(pb, pb[0:64, 0:n], wr[:, kc, col0:col0 + 64], hT[:, kc, a0:a0 + n], kc == 0, kc == 7, reads=[wr, hT.sub((a0 // 512) * 512)])
                k.op("dve", lambda e: e.tensor_tensor(out=kf[:, 0:n], in0=pa[0:64, 0:n], in1=rope[:, 0, g0:g0 + n], op=ALU.mult), reads=[pa, rope], writes=[kf])
                k.op("dve", lambda e: e.tensor_tensor(out=kb_[:, 0:n], in0=pb[0:64, 0:n], in1=rope[:, 1, g0:g0 + n], op=ALU.mult), reads=[pb, rope], writes=[kb_])
                k.op("pool", lambda e: e.tensor_tensor(out=dst_ap, in0=kf[:, 0:n], in1=kb_[:, 0:n], op=ALU.add), reads=[kf, kb_], writes=dst_w)
            else:
                if fp_out is not None:
                    k.op("dve", lambda e: e.tensor_copy(out=kf[:, 0:n], in_=pa[0:64, 0:n]), reads=[pa], writes=[kf])
                    k.op("pool", lambda e: e.tensor_copy(out=dst_ap, in_=kf[:, 0:n]), reads=[kf], writes=dst_w)
                else:
                    k.op("act", lambda e: e.activation(out=dst_ap, in_=pa[0:64, 0:n], func=AF.Copy), reads=[pa], writes=dst_w)

        qs = k.sb("s_q", [64, 4, 512], BF16)
        PT = [k.sb(f"s_PT{i}", [128, 4, 128], BF16) for i in range(3)]
        mixH = k.sb("s_mixH", [64, 8, 512], BF16)
        rrow = k.sb("s_rrow", [65, 512], F32)
        rbc = k.sb("s_rbc", [64, 512], F32)
        for (s0, T_) in self.seqs(path):
            nk = T_ + (PAST if smp else 0)
            nkt = nk // 128
            nlt = T_ // 128
            seq_i = s0 // 256
            Ks = k.sb(f"s_K{s0}", [64, 2, nk], BF16)
            Vs = k.sb(f"s_V{s0}", [128, nkt, 2, 65], BF16)
            k.op("pool", lambda e: e.memset(Vs[:], 1.0), writes=[Vs])
            vst = [k.sb(f"s_vst{s0}_{i}", [128, 128], F32) for i in range(2)]
            groups = [(g0, min(512, T_ - g0)) for g0 in range(0, T_, 512)]
            for (g0, n) in groups:
                a0 = s0 + g0
                for kvh in range(2):
                    proj_rope(512 + kvh * 64, a0, n, g0, Ks[:, kvh, g0:g0 + n], [Ks.sub(kvh)], fp_out=(None if smp else True))
                    if not smp:
                        self.store_tok(lambda tt: kf[0:64, tt * 128:(tt + 1) * 128], [kf], 64, n,
                                       lambda tt, kvh=kvh: self.O["n_sk"][seq_i, l, g0 + tt * 128:g0 + (tt + 1) * 128, kvh * 64:(kvh + 1) * 64])
                    self.sq_max(Ks[0:64, kvh, g0:g0 + n], 64, n, self.sm_k[0:1, kvh:kvh + 1], g0 == 0, [Ks.sub(kvh)])
                for tt in range(n // 128):
                    kt = (g0 // 128) + tt
                    ps = k.psum()
                    for kc in range(8):
                        k.mm(ps, ps[:, 0:128], hT[:, kc, a0 + tt * 128:a0 + (tt + 1) * 128], wsw[:, kc, 640:768], kc == 0, kc == 7,
                             reads=[wsw, hT.sub((a0 // 512) * 512)])
                    k.op("act", lambda e, ps=ps, kt=kt: e.activation(out=Vs[:, kt, :, 0:64], in_=ps[:, 0:128].rearrange("p (h d) -> p h d", h=2), func=AF.Copy),
                         reads=[ps], writes=[Vs])
                    if not smp:
                        vt = vst[tt % 2]
                        k.op("dve", lambda e, ps=ps, vt=vt: e.tensor_copy(out=vt[:], in_=ps[:, 0:128]), reads=[ps], writes=[vt])
                        k.dma("sp", self.O["n_sv"][seq_i, l, g0 + tt * 128:g0 + (tt + 1) * 128, :], vt[:], reads=[vt], writes=[])
            if smp:
                cin = [k.sb(f"s_cin{i}", [128, 128], F32) for i in range(2)]
                cvn = [k.sb(f"s_cvn{i}", [128, 128], F32) for i in range(2)]
                for i in range(4):
                    ci = cin[i % 2]
                    k.dma("sp", ci[:], W["c_sk"][l, i * 128:(i + 1) * 128, :], writes=[ci])
                    for kvh in range(2):
                        ps = k.psum()
                        k.tr(ps, ps[0:64, 0:128], ci[:, kvh * 64:(kvh + 1) * 64], self.ident, reads=[ci, self.masks])
                        k.op("act", lambda e, ps=ps, kvh=kvh, i=i: e.activation(out=Ks[:, kvh, T_ + i * 128:T_ + (i + 1) * 128], in_=ps[0:64, 0:128], func=AF.Copy),
                             reads=[ps], writes=[Ks.sub(kvh)])
                    cv = cvn[i % 2]
                    k.dma("sp", cv[:], W["c_sv"][l, i * 128:(i + 1) * 128, :], writes=[cv])
                    k.op("pool", lambda e, cv=cv, i=i: e.tensor_copy(out=Vs[:, nlt + i, :, 0:64], in_=cv[:, :].rearrange("p (h d) -> p h d", h=2)),
                         reads=[cv], writes=[Vs])
                for kvh in range(2):
                    for g0 in range(T_, nk, 512):
                        self.sq_max(Ks[0:64, kvh, g0:g0 + 512], 64, 512, self.sm_k[0:1, kvh:kvh + 1], False, [Ks.sub(kvh)])
            npt = 0
            for (g0, n) in groups:
                a0 = s0 + g0
                nqt = n // 128
                for kvh in range(2):
                    for g in range(4):
                        proj_rope((kvh * 4 + g) * 64, a0, n, g0, qs[:, g, 0:n], [qs])
                        self.sq_max(qs[0:64, g, 0:n], 64, n, self.sm_q[0:1, 0:1], g == 0, [qs])
                    self.softmax_bound(self.sm_q[0:1, 0:1], self.sm_k[0:1, kvh:kvh + 1], scale, negc)
                    k.op("act", lambda e, kvh=kvh: e.activation(out=psink[:, kvh * 4:(kvh + 1) * 4], in_=self.rowv(l, "sink")[:, kvh * 4:(kvh + 1) * 4],
                                                                func=AF.Exp, bias=negc[:, 0:1], scale=1.0), reads=[self.rowbc, negc], writes=[psink])
                    for qt in range(nqt):
                        gq = g0 // 128 + qt
                        if smp:
                            kts = [(t, t - gq) for t in (gq - 1, gq, gq + 1) if 0 <= t < nlt] + [(nlt + i, 0) for i in range(4)]
                        else:
                            kts = [(t, 0) for t in range(nlt)]
                        pos = k.psum_hold(1)
                        po = pos[0]
                        def qk(kt):
                            ps = k.psum()
                            k.mm(ps, ps[:, :].rearrange("p (g q) -> p g q", g=4), Ks[:, kvh, kt * 128:(kt + 1) * 128], qs[:, :, qt * 128:(qt + 1) * 128],
                                 True, True, reads=[Ks.sub(kvh), qs])
                            return ps
                        pend = qk(kts[0][0])
                        for ki, (kt, rel) in enumerate(kts):
                            ps = pend
                            if ki + 1 < len(kts):
                                pend = qk(kts[ki + 1][0])
                            pt = PT[npt % 3]
                            npt += 1
                            k.op("act", lambda e, pt=pt, ps=ps: e.activation(out=pt[:, :, :], in_=ps[:, :].rearrange("p (g q) -> p g q", g=4), func=AF.Exp,
                                                                           bias=negc[:, 0:1], scale=scale), reads=[ps, negc], writes=[pt])
                            if rel != 0:
                                mi = 3 if rel == -1 else 1
                                for g in range(4):
                                    k.op("pool" if g % 2 else "dve", lambda e, pt=pt, g=g, mi=mi: e.tensor_tensor(out=pt[:, g, :], in0=pt[:, g, :], in1=self.masks_b[:, mi, :], op=ALU.mult),
                                         reads=[pt, self.masks_b], writes=[pt])
                            k.mm(po, po[0:65, :].rearrange("p (g q) -> p g q", g=4), Vs[:, kt, kvh, :], pt[:, :, :], ki == 0, ki == len(kts) - 1, reads=[pt, Vs])
                        for g in range(4):
                            k.op("dve", lambda e, po=po, g=g, kvh=kvh: e.tensor_scalar(out=rrow[64:65, g * 128:(g + 1) * 128], in0=po[64:65, g * 128:(g + 1) * 128],
                                                                                         scalar1=psink[64:65, kvh * 4 + g:kvh * 4 + g + 1], scalar2=None, op0=ALU.add),
                                 reads=[po, psink], writes=[rrow])
                        pbc = k.psum()
                        k.mm(pbc, pbc[0:64, :], self.ones_f[64:65, 0:64], rrow[64:65, :], True, True, reads=[self.ones_f, rrow])
                        k.op("dve", lambda e, pbc=pbc: e.reciprocal(out=rbc[:, :], in_=pbc[0:64, :]), reads=[pbc], writes=[rbc])
                        k.op("dve", lambda e, po=po, kvh=kvh, qt=qt: e.tensor_tensor(
                            out=mixH[:, kvh * 4:(kvh + 1) * 4, qt * 128:(qt + 1) * 128], in0=po[0:64, :].rearrange("p (g q) -> p g q", g=4),
                            in1=rbc[:, :].rearrange("p (g q) -> p g q", g=4), op=ALU.mult), reads=[po, rbc], writes=[mixH])
                        k.psum_release(pos)
                self.ao_k = 64
                self.apply_out(l, path, s0 + g0, n, (lambda c: mixH[:, c, 0:n], [mixH]), 8, wout, 0)
                self.ao_k = 128
        k.end_phase()

    def gdn(self, l, path):
        k = self.k
        W = self.W
        smp = path == "s"
        hT = self.hT[path]
        M = self.masks
        k.begin_phase()
        self.ao_x = k.sb("ao_x", [128, 8, 512], F32)
        win_all = W["w_in"][l].rearrange("(kc p) n -> p kc n", p=128)
        wg = k.sb("g_wg", [128, 8, 16], BF16)
        k.dma("pool", wg[:], win_all[:, :, 3488:3504], writes=[wg])
        wout = k.sb("g_wout", [128, 4, D], BF16)
        k.dma("pool", wout[:], W["w_out"][l].rearrange("(c p) n -> p c n", p=128)[:, 8:12, :], writes=[wout])
        wh = [k.sb(f"g_wh{i}", [128, 8, 512], BF16) for i in range(2)]
        nalog = k.sb("g_nalog", [128, 8], F32)
        k.op("act", lambda e: e.activation(out=nalog[:], in_=self.rowv(l, "alog"), func=AF.Exp), reads=[self.rowbc], writes=[nalog])
        k.op("dve", lambda e: e.tensor_scalar(out=nalog[:], in0=nalog[:], scalar1=-1.0, scalar2=None, op0=ALU.mult), reads=[nalog], writes=[nalog])
        nwh = 0
        for (s0, T_) in self.seqs(path):
            nch = T_ // 128
            seq_i = s0 // 256
            gg = k.sb(f"g_g{s0}", [128, nch, 8], F32)
            bt = k.sb(f"g_b{s0}", [128, nch, 8], F32)
            for ci in range(nch):
                a0 = s0 + ci * 128
                ps = k.psum()
                for kc in range(8):
                    k.mm(ps, ps[:, 0:16], hT[:, kc, a0:a0 + 128], wg[:, kc, :], kc == 0, kc == 7, reads=[wg, hT.sub((a0 // 512) * 512)])
                k.op("dve", lambda e, ps=ps, ci=ci: e.tensor_tensor(out=gg[:, ci, :], in0=ps[:, 0:8], in1=self.rowv(l, "dtb"), op=ALU.add),
                     reads=[ps, self.rowbc], writes=[gg])
                k.op("act", lambda e, ps=ps, ci=ci: e.activation(out=bt[:, ci, :], in_=ps[:, 8:16], func=AF.Sigmoid), reads=[ps], writes=[bt])
            k.op("dve", lambda e: e.tensor_scalar(out=gg[:], in0=gg[:], scalar1=60.0, scalar2=None, op0=ALU.min), reads=[gg], writes=[gg])
            k.op("act", lambda e: e.activation(out=gg[:], in_=gg[:], func=AF.Exp), reads=[gg], writes=[gg])
            k.op("act", lambda e: e.activation(out=gg[:], in_=gg[:], func=AF.Ln, bias=1.0, scale=1.0), reads=[gg], writes=[gg])
            for ci in range(nch):
                k.op("dve", lambda e, ci=ci: e.tensor_tensor(out=gg[:, ci, :], in0=gg[:, ci, :], in1=nalog[:], op=ALU.mult), reads=[gg, nalog], writes=[gg])
            if GSTOP <= 1:
                continue
            pre = k.sb(f"g_pre{s0}", [128, 3, T_ + 4], F32)
            acc = k.sb(f"g_acc{s0}", [128, 3, T_], F32)
            osum = k.sb(f"g_osum{s0}", [128, nch, 128], F32)
            mixT = k.sb(f"g_mixT{s0}", [128, 4, T_], BF16)
            sqf = k.sb(f"g_sqf{s0}", [128, 512], F32)
            rst = k.sb(f"g_rst{s0}", [128, 512], F32)
            names = ["gcb", "X", "DT", "Dm", "N", "NT", "Bm", "BmT", "B2", "B2T", "Wm", "W2", "kbg", "vb",
                     "u", "wkT", "E", "gones", "gz", "on"]
            names_b = ["AT", "kdec", "qdec", "vnew", "Sb"]
            tqs = [{nm: k.sb(f"g_{nm}{s0}_{d}", [128, 128], F32) for nm in names} for d in range(2)]
            for d in range(2):
                tqs[d].update({nm: k.sb(f"g_{nm}{s0}_{d}", [128, 128], BF16) for nm in names_b})
            accb = k.sb(f"g_accb{s0}", [128, 2, T_], BF16)
            scs = [k.sb(f"g_sc{s0}_{d}", [128, 8], F32) for d in range(2)]
            Ss = [k.sb(f"g_S{s0}_{d}", [128, 128], F32) for d in range(2)]
            osums = [osum, k.sb(f"g_osumb{s0}", [128, nch, 128], F32)]
            tq, sc = tqs[0], scs[0]
            k.op("pool", lambda e: e.memset(pre[:], 0.0), writes=[pre])
            groups = [(g0, min(512, T_ - g0)) for g0 in range(0, T_, 512)]
            for h in range(4):
                w = wh[nwh % 2]
                nwh += 1
                for wi, c0 in enumerate([1440, 1952, 2464, 2976]):
                    k.dma("pool", w[:, :, wi * 128:(wi + 1) * 128], win_all[:, :, c0 + h * 128:c0 + (h + 1) * 128], writes=[w])
                for (g0, n) in groups:
                    a0 = s0 + g0
                    for wi in range(3):
                        ps = k.psum()
                        for kc in range(8):
                            k.mm(ps, ps[:, 0:n], w[:, kc, wi * 128:(wi + 1) * 128], hT[:, kc, a0:a0 + n], kc == 0, kc == 7, reads=[w, hT.sub((a0 // 512) * 512)])
                        k.op("act", lambda e, ps=ps, wi=wi: e.activation(out=pre[:, wi, 2 + g0:2 + g0 + n], in_=ps[:, 0:n], func=AF.Copy), reads=[ps], writes=[pre.sub(wi)])
                for wi in range(3):
                    eng = "dve"
                    cch = wi * 4 + h
                    for tap in range(5):
                        wcol = self.convT[:, l * 60 + tap * 12 + cch:l * 60 + tap * 12 + cch + 1]
                        if tap == 0:
                            k.op(eng, lambda e, wi=wi, wcol=wcol: e.tensor_scalar(out=acc[:, wi, :], in0=pre[:, wi, 0:T_], scalar1=wcol, scalar2=None, op0=ALU.mult),
                                 reads=[pre.sub(wi), self.convT], writes=[acc.sub(wi)])
                        else:
                            k.op(eng, lambda e, wi=wi, wcol=wcol, tap=tap: e.scalar_tensor_tensor(out=acc[:, wi, :], in0=pre[:, wi, tap:tap + T_], scalar=wcol,
                                                                                                 in1=acc[:, wi, :], op0=ALU.mult, op1=ALU.add),
                                 reads=[pre.sub(wi), self.convT, acc.sub(wi)], writes=[acc.sub(wi)])
                    k.op("act", lambda e, wi=wi: e.activation(out=acc[:, wi, :], in_=acc[:, wi, :], func=AF.Silu), reads=[acc.sub(wi)], writes=[acc.sub(wi)])
                if GSTOP <= 2:
                    continue
                for wi in range(2):
                    for (g0, n) in groups:
                        k.op("act", lambda e, wi=wi: e.activation(out=sqf[:, 0:n], in_=acc[:, wi, g0:g0 + n], func=AF.Square), reads=[acc.sub(wi)], writes=[sqf])
                        ps = k.psum()
                        k.mm(ps, ps[:, 0:n], self.ones_f[:, :], sqf[:, 0:n], True, True, reads=[self.ones_f, sqf])
                        k.op("act", lambda e, ps=ps: e.activation(out=rst[:, 0:n], in_=ps[:, 0:n], func=AF.Sqrt, bias=self.epsc[:, 0:1], scale=1.0),
                             reads=[ps, self.epsc], writes=[rst])
                        k.op("dve", lambda e: e.reciprocal(out=rst[:, 0:n], in_=rst[:, 0:n]), reads=[rst], writes=[rst])
                        if wi == 0:
                            k.op("dve", lambda e, wi=wi: e.scalar_tensor_tensor(out=acc[:, wi, g0:g0 + n], in0=acc[:, wi, g0:g0 + n], scalar=128 ** -0.5, in1=rst[:, 0:n],
                                                                                 op0=ALU.mult, op1=ALU.mult), reads=[acc.sub(wi), rst], writes=[acc.sub(wi)])
                        else:
                            k.op("pool", lambda e, wi=wi: e.tensor_tensor(out=acc[:, wi, g0:g0 + n], in0=acc[:, wi, g0:g0 + n], in1=rst[:, 0:n], op=ALU.mult),
                                 reads=[acc.sub(wi), rst], writes=[acc.sub(wi)])
                k.op("act", lambda e: e.activation(out=accb[:, 0, :], in_=acc[:, 0, :], func=AF.Copy), reads=[acc.sub(0)], writes=[accb])
                k.op("pool", lambda e: e.tensor_copy(out=accb[:, 1, :], in_=acc[:, 1, :]), reads=[acc.sub(1)], writes=[accb])
                if GSTOP <= 3:
                    continue
                def chain(d, tq, sc, S, osum):
                    col = d * 4 + h
                    if smp:
                        k.dma("sp", S[:], W["c_st"][l, d, h], writes=[S])
                        yield
                    else:
                        k.op("pool", lambda e: e.memset(S[:], 0.0), writes=[S])
                        yield
                    k.op("act", lambda e: e.activation(out=tq["Sb"][:], in_=S[:], func=AF.Copy), reads=[S], writes=[tq["Sb"]])
                    yield
                    order = range(nch) if d == 0 else range(nch - 1, -1, -1)
                    mcs = 1 if d == 0 else 3
                    m_dt_incl = 1 if d == 0 else 3
                    m_d_strict = 4 if d == 0 else 2
                    last = 127 if d == 0 else 0
                    for ci in order:
                        c0 = ci * 128
                        kT_c = acc[:, 1, c0:c0 + 128]
                        qT_c = acc[:, 0, c0:c0 + 128]
                        vT_c = acc[:, 2, c0:c0 + 128]
                        gcol = gg[:, ci, col:col + 1]
                        bcol = bt[:, ci, col:col + 1]
                        rd = [acc.sub(0), acc.sub(1), acc.sub(2)]
                        t = tq
                        yield
                        p1 = k.psum()
                        k.mm(p1, p1[:, 0:128], kT_c, kT_c, True, True, reads=rd)
                        yield
                        k.mm(p1, p1[:, 128:256], accb[:, 1, c0:c0 + 128], accb[:, 0, c0:c0 + 128], True, True, reads=[accb])
                        yield
                        k.tr(p1, p1[:, 256:384], kT_c, self.ident, reads=rd + [M])
                        yield
                        k.tr(p1, p1[:, 384:512], vT_c, self.ident, reads=rd + [M])
                        yield
                        yield
                        k.op("dve", lambda e: e.tensor_scalar(out=t["gones"][:], in0=self.ones_f[:], scalar1=gcol, scalar2=None, op0=ALU.mult),
                             reads=[gg, self.ones_f], writes=[t["gones"]])
                        yield
                        p2 = k.psum()
                        k.mm(p2, p2[:, 0:128], t["gones"][:], M[:, mcs, :], True, True, reads=[t["gones"], M])
                        yield
                        k.mm(p2, p2[:, 128:129], M[:, mcs, :], gcol, True, True, reads=[M, gg])
                        yield
                        k.op("act", lambda e: e.activation(out=t["gcb"][:], in_=p2[:, 0:128], func=AF.Copy), reads=[p2], writes=[t["gcb"]])
                        yield
                        k.op("dve", lambda e: e.tensor_copy(out=sc[:, 0:1], in_=p2[:, 128:129]), reads=[p2], writes=[sc])
                        yield
                        yield
                        k.op("dve", lambda e: e.tensor_scalar(out=t["X"][:], in0=t["gcb"][:], scalar1=sc[:, 0:1], scalar2=None, op0=ALU.subtract),
                             reads=[t["gcb"], sc], writes=[t["X"]])
                        yield
                        k.op("dve", lambda e: e.tensor_scalar(out=t["DT"][:], in0=t["X"][:], scalar1=0.0, scalar2=None, op0=ALU.min), reads=[t["X"]], writes=[t["DT"]])
                        yield
                        k.op("act", lambda e: e.activation(out=t["DT"][:], in_=t["DT"][:], func=AF.Exp), reads=[t["DT"]], writes=[t["DT"]])
                        yield
                        k.op("pool", lambda e: e.tensor_scalar(out=t["Dm"][:], in0=t["X"][:], scalar1=-1.0, scalar2=0.0, op0=ALU.mult, op1=ALU.min),
                             reads=[t["X"]], writes=[t["Dm"]])
                        yield
                        k.op("act", lambda e: e.activation(out=t["Dm"][:], in_=t["Dm"][:], func=AF.Exp), reads=[t["Dm"]], writes=[t["Dm"]])
                        yield
                        k.op("pool", lambda e: e.tensor_tensor(out=t["DT"][:], in0=t["DT"][:], in1=M[:, m_dt_incl, :], op=ALU.mult), reads=[t["DT"], M], writes=[t["DT"]])
                        yield
                        k.op("pool", lambda e: e.tensor_tensor(out=t["Dm"][:], in0=t["Dm"][:], in1=M[:, m_d_strict, :], op=ALU.mult), reads=[t["Dm"], M], writes=[t["Dm"]])
                        yield
                        yield
                        k.op("act", lambda e: e.activation(out=t["E"][:], in_=t["gcb"][:], func=AF.Exp), reads=[t["gcb"]], writes=[t["E"]])
                        yield
                        k.op("act", lambda e: e.activation(out=sc[:, 1:2], in_=sc[:, 0:1], func=AF.Exp), reads=[sc], writes=[sc])
                        yield
                        k.op("dve", lambda e: e.tensor_tensor(out=sc[:, 1:2], in0=sc[:, 1:2], in1=bcol, op=ALU.mult), reads=[sc, bt], writes=[sc])
                        yield
                        k.op("dve", lambda e: e.tensor_tensor(out=sc[:, 2:3], in0=t["gcb"][:, last:last + 1], in1=sc[:, 0:1], op=ALU.subtract),
                             reads=[t["gcb"], sc], writes=[sc])
                        yield
                        k.op("act", lambda e: e.activation(out=sc[:, 2:3], in_=sc[:, 2:3], func=AF.Exp), reads=[sc], writes=[sc])
                        yield
                        yield
                        k.op("dve", lambda e: e.scalar_tensor_tensor(out=t["N"][:], in0=p1[:, 0:128], scalar=bcol, in1=t["Dm"][:], op0=ALU.mult, op1=ALU.mult),
                             reads=[p1, bt, t["Dm"]], writes=[t["N"]])
                        yield
                        k.op("pool", lambda e: e.tensor_scalar(out=t["N"][:], in0=t["N"][:], scalar1=-1.0, scalar2=None, op0=ALU.mult), reads=[t["N"]], writes=[t["N"]])
                        yield
                        k.op("dve", lambda e: e.tensor_tensor(out=t["AT"][:], in0=p1[:, 128:256], in1=t["DT"][:], op=ALU.mult), reads=[p1, t["DT"]], writes=[t["AT"]])
                        yield
                        yield
                        k.op("dve", lambda e: e.tensor_scalar(out=t["kbg"][:], in0=p1[:, 256:384], scalar1=sc[:, 1:2], scalar2=None, op0=ALU.mult), reads=[p1, sc], writes=[t["kbg"]])
                        yield
                        k.op("dve", lambda e: e.tensor_scalar(out=t["kdec"][:], in0=p1[:, 256:384], scalar1=sc[:, 2:3], scalar2=None, op0=ALU.mult), reads=[p1, sc], writes=[t["kdec"]])
                        yield
                        k.op("dve", lambda e: e.tensor_scalar(out=t["vb"][:], in0=p1[:, 384:512], scalar1=bcol, scalar2=None, op0=ALU.mult), reads=[p1, bt], writes=[t["vb"]])
                        yield
                        k.op("pool", lambda e: e.tensor_tensor(out=t["qdec"][:], in0=qT_c, in1=t["E"][:], op=ALU.mult), reads=[acc.sub(0), t["E"]], writes=[t["qdec"]])
                        yield
                        if GSTOP <= 4:
                            continue
                        yield
                        p3 = k.psum()
                        k.tr(p3, p3[:, 0:128], t["N"][:], self.ident, reads=[t["N"], M])
                        yield
                        k.op("act", lambda e: e.activation(out=t["NT"][:], in_=p3[:, 0:128], func=AF.Copy), reads=[p3], writes=[t["NT"]])
                        yield
                        yield
                        k.op("dve", lambda e: e.tensor_tensor(out=t["Wm"][:], in0=t["NT"][:], in1=self.ident, op=ALU.add), reads=[t["NT"], M], writes=[t["Wm"]])
                        yield
                        Bc, BcT, Wc = t["NT"], t["N"], t["Wm"]
                        nxt = [(t["Bm"], t["BmT"], t["W2"]), (t["B2"], t["B2T"], t["Wm"])]
                        for lev in range(1, 7):
                            Bn, BnT, Wn = nxt[(lev - 1) % 2]
                            pq = k.psum()
                            k.mm(pq, pq[:, 0:128], Bc[:], BcT[:], True, True, reads=[Bc, BcT])
                            yield
                            if lev < 6:
                                k.mm(pq, pq[:, 128:256], BcT[:], Bc[:], True, True, reads=[Bc, BcT])
                                yield
                            k.op("act", lambda e, pq=pq, BnT=BnT: e.activation(out=BnT[:], in_=pq[:, 0:128], func=AF.Copy), reads=[pq], writes=[BnT])
                            yield
                            if lev < 6:
                                k.op("dve", lambda e, pq=pq, Bn=Bn: e.tensor_copy(out=Bn[:], in_=pq[:, 128:256]), reads=[pq], writes=[Bn])
                                yield
                            yield
                            pw = k.psum()
                            k.mm(pw, pw[:, 0:128], BnT[:], Wc[:], True, True, reads=[BnT, Wc])
                            yield
                            k.op("dve", lambda e, pw=pw, Wn=Wn, Wc=Wc: e.tensor_tensor(out=Wn[:], in0=pw[:, 0:128], in1=Wc[:], op=ALU.add), reads=[pw, Wc], writes=[Wn])
                            yield
                            Bc, BcT, Wc = Bn, BnT, Wn
                            yield
                        TT = Wc
                        yield
                        p4 = k.psum()
                        k.mm(p4, p4[:, 0:128], TT[:], t["vb"][:], True, True, reads=[TT, t["vb"]])
                        yield
                        k.mm(p4, p4[:, 128:256], t["kbg"][:], TT[:], True, True, reads=[TT, t["kbg"]])
                        yield
                        k.op("act", lambda e: e.activation(out=t["u"][:], in_=p4[:, 0:128], func=AF.Copy), reads=[p4], writes=[t["u"]])
                        yield
                        k.op("dve", lambda e: e.tensor_copy(out=t["wkT"][:], in_=p4[:, 128:256]), reads=[p4], writes=[t["wkT"]])
                        yield
                        if GSTOP <= 5:
                            continue
                        yield
                        p5 = k.psum()
                        k.mm(p5, p5[:, 0:128], t["wkT"][:], S[:], True, True, reads=[t["wkT"], S])
                        yield
                        k.op("dve", lambda e: e.tensor_tensor(out=t["vnew"][:], in0=t["u"][:], in1=p5[:, 0:128], op=ALU.subtract), reads=[p5, t["u"]], writes=[t["vnew"]])
                        yield
                        k.mm(p5, p5[:, 128:256], t["qdec"][:], t["Sb"][:], True, False, reads=[t["qdec"], t["Sb"]])
                        yield
                        k.mm(p5, p5[:, 128:256], t["AT"][:], t["vnew"][:], False, True, reads=[t["AT"], t["vnew"]])
                        yield
                        k.mm(p5, p5[:, 256:384], t["kdec"][:], t["vnew"][:], True, True, reads=[t["kdec"], t["vnew"]])
                        yield
                        k.op("act", lambda e, ci=ci: e.activation(out=osum[:, ci, :], in_=p5[:, 128:256], func=AF.Copy), reads=[p5], writes=[osum.sub(ci)])
                        yield
                        k.op("dve", lambda e: e.scalar_tensor_tensor(out=S[:], in0=S[:], scalar=t["E"][:, last:last + 1], in1=p5[:, 256:384], op0=ALU.mult, op1=ALU.add),
                             reads=[p5, S, t["E"]], writes=[S])
                        yield
                        k.op("act", lambda e: e.activation(out=t["Sb"][:], in_=S[:], func=AF.Copy), reads=[S], writes=[t["Sb"]])
                        yield
                    if not smp:
                        k.dma("sp", self.O["n_st"][seq_i, l, d, h], S[:], reads=[S], writes=[])
                        yield
                chains = []
                for d in range(2):
                    chains.append([chain(d, tqs[d], scs[d], Ss[d], osums[d]), [4 * d, 4 * d + 1, 4 * d + 2, 4 * d + 3], 0])
                save_rot, save_next = k.ps_rot, k.psnext
                while chains:
                    for cobj in list(chains):
                        k.ps_rot, k.psnext = cobj[1], cobj[2]
                        try:
                            next(cobj[0])
                        except StopIteration:
                            chains.remove(cobj)
                        cobj[2] = k.psnext
                k.ps_rot, k.psnext = save_rot, save_next
                if GSTOP <= 6:
                    continue
                for ci in range(nch):
                    a0 = s0 + ci * 128
                    ps = k.psum()
                    for kc in range(8):
                        k.mm(ps, ps[:, 0:128], hT[:, kc, a0:a0 + 128], w[:, kc, 384:512], kc == 0, kc == 7, reads=[w, hT.sub((a0 // 512) * 512)])
                    t = tqs[ci % 2]
                    sc = scs[ci % 2]
                    k.op("act", lambda e, ps=ps: e.activation(out=t["gz"][:], in_=ps[:, 0:128], func=AF.Silu), reads=[ps], writes=[t["gz"]])
                    k.op("pool", lambda e, ci=ci: e.tensor_tensor(out=osum[:, ci, :], in0=osum[:, ci, :], in1=osums[1][:, ci, :], op=ALU.add),
                         reads=[osum.sub(ci), osums[1].sub(ci)], writes=[osum.sub(ci)])
                    k.op("act", lambda e, ci=ci: e.activation(out=t["on"][:], in_=osum[:, ci, :], func=AF.Square, accum_out=sc[:, 3:4]), reads=[osum.sub(ci)], writes=[t["on"], sc])
                    k.op("act", lambda e: e.activation(out=sc[:, 4:5], in_=sc[:, 3:4], func=AF.Sqrt, bias=self.epsc[:, 0:1], scale=1.0 / 128), reads=[sc, self.epsc], writes=[sc])
                    k.op("dve", lambda e: e.reciprocal(out=sc[:, 4:5], in_=sc[:, 4:5]), reads=[sc], writes=[sc])
                    k.op("dve", lambda e, ci=ci: e.scalar_tensor_tensor(out=t["on"][:], in0=osum[:, ci, :], scalar=sc[:, 4:5], in1=self.rowv(l, "gnorm"), op0=ALU.mult, op1=ALU.mult),
                         reads=[osum.sub(ci), sc, self.rowbc], writes=[t["on"]])
                    k.op("pool", lambda e: e.tensor_tensor(out=t["on"][:], in0=t["on"][:], in1=t["gz"][:], op=ALU.mult), reads=[t["on"], t["gz"]], writes=[t["on"]])
                    p6 = k.psum()
                    k.tr(p6, p6[:, 0:128], t["on"][:], self.ident, reads=[t["on"], M])
                    k.op("act", lambda e, p6=p6, ci=ci, h=h: e.activation(out=mixT[:, h, ci * 128:(ci + 1) * 128], in_=p6[:, 0:128], func=AF.Copy), reads=[p6], writes=[mixT.sub(h)])
            if GSTOP > 6:
                for (g0, n) in groups:
                    self.apply_out(l, path, s0 + g0, n, (lambda c: mixT[:, c, g0:g0 + n], [mixT]), 4, wout, 0)
        k.end_phase()


_CACHE = {}


def _get_program():
    if "nc" not in _CACHE:
        b = Builder()
        _CACHE["nc"] = b.build()
        _CACHE["b"] = b
    return _CACHE["nc"], _CACHE["b"]


def make_in_maps(inp):
    f = lambda a: np.ascontiguousarray(np.asarray(a, dtype=np.float32))
    hc = host_consts()
    shared = {}
    for nm in ["w_ada", "ffn1_w1", "ffn1_w2", "w_in", "mla_w_qb", "mla_w_kvb", "w_out", "ffn2_w1", "ffn2_w2"]:
        shared[nm] = f(inp[nm])
    shared["b_ada"] = f(inp["b_ada"]).reshape(L, 72, 128)
    shared["convw"] = f(inp["gdn_conv_w"]).reshape(L * 5 * 12, 128)
    rowv = np.zeros((L, 304), np.float32)
    rowv[:, 0:8] = f(inp["swa_sink"])
    rowv[:, 8:16] = f(inp["gdn_a_log"]).reshape(L, 8)
    rowv[:, 16:24] = f(inp["gdn_dt_bias"]).reshape(L, 8)
    rowv[:, 24:152] = f(inp["gdn_norm"])
    shared["rowv"] = rowv.reshape(1, L * 304)
    shared.update(hc)
    maps = []
    for c in range(8):
        b = c // 4
        m = dict(shared)
        m["xp"] = f(inp["x_prompt"][2 * c:2 * c + 2]).reshape(NPT, D)
        m["xs"] = f(inp["x_sample"][b])
        m["c_ckv"] = f(inp["cache_mla_ckv"][b])
        m["c_kr"] = f(inp["cache_mla_krope"][b])
        m["c_sk"] = f(inp["cache_swa_k"][b]).reshape(L, PAST, 128)
        m["c_sv"] = f(inp["cache_swa_v"][b]).reshape(L, PAST, 128)
        m["c_st"] = f(inp["state_gdn"][b])
        vecA = np.zeros((82, 128), np.float32)
        vecA[0:16] = f(inp["norm_ffn1"]).reshape(16, 128)
        vecA[16:32] = f(inp["norm_mix"]).reshape(16, 128)
        vecA[32:48] = f(inp["norm_ffn2"]).reshape(16, 128)
        vecA[48:56] = f(inp["final_norm"]).reshape(8, 128)
        vecA[56:64] = f(inp["c_ctx"]).reshape(8, 128)
        vecA[64:72] = f(inp["c"][b]).reshape(8, 128)
        vecA[72:78] = f(inp["mla_q_norm"]).reshape(6, 128)
        vecA[78:82] = f(inp["mla_kv_norm"]).reshape(4, 128)
        m["vecA"] = vecA
        maps.append(m)
    return maps


def kernel(**inp):
    nc, b = _get_program()
    maps = make_in_maps(inp)
    res = run_bass_kernel_spmd(nc, maps, core_ids=list(range(8)))
    R = res.results
    yp = np.concatenate([R[c]["yp"].reshape(2, 256, D) for c in range(8)], 0)
    ys = np.stack([R[0]["ys"], R[4]["ys"]], 0)
    n_ckv = np.concatenate([R[c]["n_ckv"] for c in range(8)], 0)
    n_kr = np.concatenate([R[c]["n_kr"] for c in range(8)], 0)
    n_sk = np.concatenate([R[c]["n_sk"] for c in range(8)], 0).reshape(16, L, 256, 2, 64)
    n_sv = np.concatenate([R[c]["n_sv"] for c in range(8)], 0).reshape(16, L, 256, 2, 64)
    n_st = np.concatenate([R[c]["n_st"] for c in range(8)], 0)
    return (yp.astype(np.float32), ys.astype(np.float32), n_ckv.astype(np.float32), n_kr.astype(np.float32),
            n_sk.astype(np.float32), n_sv.astype(np.float32), n_st.astype(np.float32))
```

```python
import contextlib
import numpy as np
import concourse.bass as bass
import concourse.mybir as mybir
from concourse.bass_utils import run_bass_kernel_spmd

F32 = mybir.dt.float32
BF16 = mybir.dt.bfloat16
AF = mybir.ActivationFunctionType
ALU = mybir.AluOpType

D = 1024
DFF = 2816
L = 2
NPT = 512
NST = 2048
PAST = 512
EPS = 1e-6
DEBUG = []
STOP = None
SKIP = None
GSTOP = 99
FUSE_WAIT = True


class Buf:
    __slots__ = ("name", "w", "r", "parent", "kids")

    def __init__(self, name, parent=None):
        self.name = name
        self.w = None
        self.r = []
        self.parent = parent
        self.kids = {}

    def sub(self, key):
        if key not in self.kids:
            self.kids[key] = Buf(f"{self.name}.{key}", self)
        return self.kids[key]

    def family(self):
        out = [self]
        p = self.parent
        while p is not None:
            out.append(p)
            p = p.parent
        st = list(self.kids.values())
        while st:
            q = st.pop()
            out.append(q)
            st.extend(q.kids.values())
        return out


class T:
    def __init__(self, t, name):
        self.t = t
        self.b = Buf(name)

    def __getitem__(self, key):
        return self.t[key]

    def sub(self, key):
        return self.b.sub(key)


def _b(x):
    return x.b if isinstance(x, T) else x


class K:
    SAME_ENGINE_SYNC = True

    def __init__(self, n_dma_sems=64):
        self.nc = bass.Bass("TRN2", target_bir_lowering=False)
        nc = self.nc
        self.es = contextlib.ExitStack()
        self.engs = {"pe": nc.tensor, "act": nc.scalar, "dve": nc.vector, "pool": nc.gpsimd, "sp": nc.sync}
        self.sems = {}
        self.cnt = {}
        for e in self.engs:
            self.sems[e] = self.es.enter_context(nc.semaphore(f"prog_{e}"))
            self.cnt[e] = 0
        self.dsems = [self.es.enter_context(nc.semaphore(f"dma_{i}")) for i in range(n_dma_sems)]
        self.dval = [0] * n_dma_sems
        self.dnext = 0
        self.dnext2 = 0
        self.semobj = dict(self.sems)
        for i, s in enumerate(self.dsems):
            self.semobj[("d", i)] = s
        self.waited = {}
        self.n_wait = 0
        self.n_ins = 0
        self.phase_es = None
        self.psums = []
        self.psnext = 0
        self.ps_rot = list(range(8))

    def sb(self, name, shape, dtype=F32, glob=False):
        es = self.es if (glob or self.phase_es is None) else self.phase_es
        self.n_alloc = getattr(self, "n_alloc", 0) + 1
        name = f"sb{self.n_alloc}_{name}"
        t = es.enter_context(self.nc.sbuf_tensor(name, list(shape), dtype))
        return T(t, name)

    def dram(self, name, shape, dtype=F32, kind="Internal"):
        return T(self.nc.dram_tensor(name, list(shape), dtype, kind=kind), name)

    def init_psum(self):
        for i in range(8):
            t = self.es.enter_context(self.nc.psum_tensor(f"psb{i}", [128, 512], F32))
            self.psums.append(T(t, f"psb{i}"))

    def psum(self):
        rot = self.ps_rot
        self.psnext = (self.psnext + 1) % len(rot)
        return self.psums[rot[self.psnext]]

    def psum_hold(self, n):
        held = self.ps_rot[-n:]
        self.ps_rot = self.ps_rot[:-n]
        self.psnext = 0
        return [self.psums[i] for i in held]

    def psum_release(self, banks):
        self.ps_rot = self.ps_rot + [self.psums.index(b) for b in banks]

    def begin_phase(self):
        assert self.phase_es is None
        self.phase_es = contextlib.ExitStack()

    def end_phase(self):
        self.barrier()
        self.phase_es.close()
        self.phase_es = None

    def _deps(self, reads, writes, e=None):
        deps = {}

        def add(d):
            if d is not None and deps.get(d[0], 0) < d[1]:
                deps[d[0]] = d[1]
        for b in reads:
            bb = _b(b)
            for f in bb.family():
                add(f.w)
            if bb.name.startswith("psb"):
                for f in bb.family():
                    for r in f.r:
                        if r[0] != e:
                            add(r)
        for b in writes:
            for f in _b(b).family():
                add(f.w)
                for r in f.r:
                    add(r)
        return deps

    def _emit_waits(self, e, deps, keep_one=False):
        eng = self.engs[e]
        need = []
        for kk, v in deps.items():
            if kk == e and (e == "pe" or not self.SAME_ENGINE_SYNC):
                continue
            if self.waited.get((e, kk), 0) >= v:
                continue
            need.append((kk, v))
            self.waited[(e, kk)] = v
        fused = None
        if keep_one and FUSE_WAIT and need:
            fused = need.pop()
        for kk, v in need:
            eng.wait_ge(self.semobj[kk], v)
            self.n_wait += 1
        return fused

    def _record(self, tag, reads, writes):
        for b in reads:
            _b(b).r.append(tag)
        for b in writes:
            bb = _b(b)
            bb.w = tag
            bb.r = []

    def op(self, e, fn, reads=(), writes=()):
        deps = self._deps(reads, writes, e)
        fused = self._emit_waits(e, deps, keep_one=True)
        ins = fn(self.engs[e])
        if fused is not None:
            ins._wait_ge(self.semobj[fused[0]], fused[1])
        self.cnt[e] += 1
        ins.then_inc(self.sems[e], 1)
        self._record((e, self.cnt[e]), reads, writes)
        self.n_ins += 1
        return ins

    def dma(self, q, out, in_, reads=(), writes=(), **kw):
        deps = self._deps(reads, writes)
        half = len(self.dsems) // 2
        if q == "sp":
            i = self.dnext % half
            self.dnext += 1
        else:
            i = half + self.dnext2 % half
            self.dnext2 += 1
        key = ("d", i)
        if self.dval[i] > 0:
            deps[key] = max(deps.get(key, 0), self.dval[i])
        self._emit_waits(q, deps)
        ins = self.engs[q].dma_start(out=out, in_=in_, **kw)
        self.dval[i] += 16
        ins.then_inc(self.dsems[i], 16)
        self._record((key, self.dval[i]), reads, writes)
        self.n_ins += 1
        return ins

    def barrier(self):
        alld = {e: self.cnt[e] for e in self.engs if self.cnt[e] > 0}
        for i, v in enumerate(self.dval):
            if v > 0:
                alld[("d", i)] = v
        for e in self.engs:
            self._emit_waits(e, dict(alld))

    def mm(self, ps, out, lhsT, rhs, start, stop, reads):
        self.op("pe", lambda e: e.matmul(out, lhsT=lhsT, rhs=rhs, start=start, stop=stop), reads=reads, writes=[ps])

    def tr(self, ps, out, in_, ident, reads):
        self.op("pe", lambda e: e.transpose(out, in_, ident), reads=reads, writes=[ps])


def _rope_tables(n_tokens, dim):
    rows = n_tokens // 64
    row = np.repeat(np.arange(rows, dtype=np.float32), 64)
    col = np.tile(np.arange(64, dtype=np.float32), rows)
    axis_dim = dim // 2
    inv = (1.0 / (10000.0 ** (np.arange(0, axis_dim, 2, dtype=np.float32) / axis_dim))).astype(np.float32)
    ang_r = row[:, None] * inv[None, :]
    ang_c = col[:, None] * inv[None, :]
    ang = np.concatenate([ang_r, ang_r, ang_c, ang_c], axis=-1)
    return np.cos(ang).astype(np.float32), np.sin(ang).astype(np.float32)


def host_consts():
    c = {}
    p = np.arange(128)[:, None]
    f = np.arange(128)[None, :]
    masks = np.zeros((128, 5, 128), np.float32)
    masks[:, 0, :] = (p == f)
    masks[:, 1, :] = (p <= f)
    masks[:, 2, :] = (p < f)
    masks[:, 3, :] = (p >= f)
    masks[:, 4, :] = (p > f)
    amasks = np.zeros((128, 5, 128), np.float32)
    for i in range(1, 5):
        amasks[:, i, :] = (masks[:, i, :] - 1.0) * 1.0e9
    c["amasks"] = amasks
    c["masks"] = masks
    cm, sm = _rope_tables(NST, 32)
    cosm = np.ones((96, NST), np.float32)
    sinm = np.zeros((96, NST), np.float32)
    cosm[64:96] = cm.T
    sinm[64:96] = sm.T
    cs, ss = _rope_tables(NST, 64)
    c["rope_m"] = np.stack([cosm, sinm], 1).copy()
    c["rope_s"] = np.stack([cs.T, ss.T], 1).copy()
    return c


def n_is_k(ap, self):
    return ap.tensor.name == self.sm_k.t.name


class Builder:
    def __init__(self):
        self.k = K()
        k = self.k
        self.nc = k.nc
        self.ins = {}
        self.outs = {}
        k.init_psum()

    def din(self, name, shape):
        t = self.k.dram(name, shape, F32, kind="ExternalInput")
        self.ins[name] = t
        return t

    def dout(self, name, shape):
        t = self.k.dram(name, shape, F32, kind="ExternalOutput")
        self.outs[name] = t
        return t

    def build(self):
        k = self.k
        W = {}
        for nm, shp in [("xp", [NPT, D]), ("xs", [NST, D]), ("c_ckv", [L, PAST, 256]), ("c_kr", [L, PAST, 32]),
                        ("c_sk", [L, PAST, 128]), ("c_sv", [L, PAST, 128]), ("c_st", [L, 2, 4, 128, 128]),
                        ("vecA", [82, 128]), ("b_ada", [L, 72, 128]), ("convw", [120, 128]), ("rowv", [1, 304 * L]),
                        ("w_ada", [L, D, 9 * D]), ("ffn1_w1", [L, D, 2 * DFF]), ("ffn1_w2", [L, DFF, D]),
                        ("w_in", [L, D, 3504]), ("mla_w_qb", [L, 384, 768]), ("mla_w_kvb", [L, 256, 1024]),
                        ("w_out", [L, 1536, D]), ("ffn2_w1", [L, D, 2 * DFF]), ("ffn2_w2", [L, DFF, D]),
                        ("masks", [128, 5, 128]), ("amasks", [128, 5, 128]), ("rope_m", [96, 2, NST]), ("rope_s", [64, 2, NST])]:
            W[nm] = self.din(nm, shp)
        self.W = W
        O = {}
        O["yp"] = self.dout("yp", [NPT, D])
        O["ys"] = self.dout("ys", [NST, D])
        O["n_ckv"] = self.dout("n_ckv", [2, L, 256, 256])
        O["n_kr"] = self.dout("n_kr", [2, L, 256, 32])
        O["n_sk"] = self.dout("n_sk", [2, L, 256, 128])
        O["n_sv"] = self.dout("n_sv", [2, L, 256, 128])
        O["n_st"] = self.dout("n_st", [2, L, 2, 4, 128, 128])
        self.O = O
        self.dbg = {}

        self.xT = {"p": k.dram("xTp", [128, 8, NPT]), "s": k.dram("xTs", [128, 8, NST])}
        self.ntok = {"p": NPT, "s": NST}
        self.blocks = [("p", 0)] + [("s", i * 512) for i in range(4)]

        self.masks = k.sb("masks", [128, 5, 128], F32, glob=True)
        k.dma("sp", self.masks[:], W["masks"][:], writes=[self.masks])
        self.ident = self.masks.t[:, 0, :]
        self.ones_f = k.sb("ones_f", [128, 128], F32, glob=True)
        self.ones_b = k.sb("ones_b", [128, 128], BF16, glob=True)
        k.op("dve", lambda e: e.memset(self.ones_f[:], 1.0), writes=[self.ones_f])
        k.op("dve", lambda e: e.memset(self.ones_b[:], 1.0), writes=[self.ones_b])
        self.epsc = k.sb("epsc", [128, 1], F32, glob=True)
        k.op("dve", lambda e: e.memset(self.epsc[:], EPS), writes=[self.epsc])
        self.masks_b = k.sb("masks_b", [128, 5, 128], BF16, glob=True)
        k.op("dve", lambda e: e.tensor_copy(out=self.masks_b[:], in_=self.masks[:]), reads=[self.masks], writes=[self.masks_b])
        self.vecT = k.sb("vecT", [128, 82], F32, glob=True)
        self.badaT = k.sb("badaT", [128, L, 72], F32, glob=True)
        self.convT = k.sb("convT", [128, 120], F32, glob=True)
        self.rowbc = k.sb("rowbc", [128, 304 * L], F32, glob=True)
        self.modT = k.sb("modT", [128, L, 72, 2], F32, glob=True)
        self.modA = k.sb("modA", [128, L, 2, 3, 8], F32, glob=True)
        self.modG = k.sb("modG", [128, L, 2, 3, 8], F32, glob=True)
        self.hT = {"p": k.sb("hTp", [128, 8, NPT], BF16, glob=True), "s": k.sb("hTs", [128, 8, NST], BF16, glob=True)}
        stages = [("setup", self.setup_small), ("adaln", self.adaln), ("load", self.load_inputs)]
        for l in range(L):
            stages.append((f"ffn1_{l}", lambda l=l: self.ffn(l, 0)))
            for path in ("p", "s"):
                stages.append((f"mla_{l}{path}", lambda l=l, path=path: self.mla(l, path)))
                stages.append((f"swa_{l}{path}", lambda l=l, path=path: self.swa(l, path)))
                stages.append((f"gdn_{l}{path}", lambda l=l, path=path: self.gdn(l, path)))
            stages.append((f"ffn2_{l}", lambda l=l: self.ffn(l, 2)))
        stages.append(("final", self.final))
        for nm, fn in stages:
            if SKIP and nm in SKIP:
                continue
            fn()
            if STOP and nm == STOP:
                break
        if DEBUG:
            k.barrier()
            for nm in DEBUG:
                if nm in ("xTp", "xTs"):
                    src = self.xT[nm[-1]]
                    o = self.dout("dbg_" + nm, [128, 8, self.ntok[nm[-1]]])
                    for t0 in range(0, self.ntok[nm[-1]], 512):
                        k.dma("sp", o[:, :, t0:t0 + 512], src[:, :, t0:t0 + 512], reads=[src], writes=[o])
                elif nm in ("hTp", "hTs"):
                    src = self.hT[nm[-1]]
                    o = T(self.nc.dram_tensor("dbg_" + nm, [128, 8, self.ntok[nm[-1]]], BF16, kind="ExternalOutput"), "dbg_" + nm)
                    k.dma("sp", o[:, :, :], src[:, :, :], reads=[src], writes=[o])
                elif nm == "modT":
                    o = self.dout("dbg_modT", [128, L * 72 * 2])
                    k.dma("sp", o[:, :], self.modT[:, :, :, :].rearrange("p l c r -> p (l c r)"), reads=[self.modT], writes=[o])
        k.barrier()
        return self.nc

    def setup_small(self):
        k = self.k
        W = self.W
        k.begin_phase()
        stage = k.sb("stage", [128, 128], F32)
        for (src, nrow, dst) in [(W["vecA"][:, :], 82, self.vecT[:, :]), (W["b_ada"][0], 72, self.badaT[:, 0, :]),
                                 (W["b_ada"][1], 72, self.badaT[:, 1, :]), (W["convw"][:, :], 120, self.convT[:, :])]:
            k.dma("sp", stage[0:nrow, :], src, writes=[stage])
            ps = k.psum()
            k.tr(ps, ps[:, 0:nrow], stage[0:nrow, :], self.ident[0:nrow, 0:nrow], reads=[stage, self.masks])
            k.op("dve", lambda e, dst=dst, ps=ps, nrow=nrow: e.tensor_copy(out=dst, in_=ps[:, 0:nrow]), reads=[ps],
                 writes=[self.vecT, self.badaT, self.convT])
        row = k.sb("rowstage", [1, 304 * L], F32)
        k.dma("sp", row[:], W["rowv"][:, :], writes=[row])
        for i in range(0, 304 * L, 512):
            n = min(512, 304 * L - i)
            ps = k.psum()
            k.mm(ps, ps[:, 0:n], self.ones_f[0:1, :], row[0:1, i:i + n], True, True, reads=[self.ones_f, row])
            k.op("dve", lambda e, ps=ps, i=i, n=n: e.tensor_copy(out=self.rowbc[:, i:i + n], in_=ps[:, 0:n]), reads=[ps], writes=[self.rowbc])
        k.end_phase()

    def rowv(self, l, what):
        o = {"sink": (0, 8), "alog": (8, 16), "dtb": (16, 24), "gnorm": (24, 152)}[what]
        return self.rowbc.t[:, l * 304 + o[0]: l * 304 + o[1]]

    def adaln(self):
        k = self.k
        W = self.W
        k.begin_phase()
        condT = k.sb("condT", [128, 8, 2], BF16)
        tmp = k.sb("condtmp", [128, 16], F32)
        k.op("act", lambda e: e.activation(out=tmp[:], in_=self.vecT[:, 56:72], func=AF.Silu), reads=[self.vecT], writes=[tmp])
        for r in range(2):
            k.op("dve", lambda e, r=r: e.tensor_copy(out=condT[:, :, r], in_=tmp[:, r * 8:(r + 1) * 8]), reads=[tmp], writes=[condT])
        wb = [k.sb(f"wada{i}", [128, 8, 512], BF16) for i in range(2)]
        for l in range(L):
            psm = k.psum()
            for g in range(18):
                w = wb[g % 2]
                k.dma("pool", w[:], W["w_ada"][l].rearrange("(kc p) n -> p kc n", p=128)[:, :, g * 512:(g + 1) * 512], writes=[w])
                for j in range(4):
                    cc = g * 4 + j
                    for kc in range(8):
                        k.mm(psm, psm[:, cc * 2:cc * 2 + 2], w[:, kc, j * 128:(j + 1) * 128], condT[:, kc, :], kc == 0, kc == 7,
                             reads=[w, condT])
            for r in range(2):
                k.op("dve", lambda e, l=l, r=r, psm=psm: e.tensor_tensor(
                    out=self.modT[:, l, :, r], in0=psm[:, 0:144].rearrange("p (c r) -> p c r", r=2)[:, :, r],
                    in1=self.badaT[:, l, :], op=ALU.add), reads=[psm, self.badaT], writes=[self.modT])
        for l in range(L):
            for r in range(2):
                for s in range(3):
                    gain = self.vecT[:, [0, 16, 32][s] + l * 8:[0, 16, 32][s] + l * 8 + 8]
                    sc = self.modT[:, l, (3 * s + 1) * 8:(3 * s + 2) * 8, r]
                    k.op("dve", lambda e, l=l, r=r, s=s, gain=gain, sc=sc: e.scalar_tensor_tensor(
                        out=self.modA[:, l, r, s, :], in0=sc, scalar=1.0, in1=gain, op0=ALU.add, op1=ALU.mult),
                        reads=[self.modT, self.vecT], writes=[self.modA])
                    gt = self.modT[:, l, (3 * s + 2) * 8:(3 * s + 3) * 8, r]
                    k.op("dve", lambda e, l=l, r=r, s=s, gt=gt: e.tensor_scalar(
                        out=self.modG[:, l, r, s, :], in0=gt, scalar1=(1.0 if s == 1 else 0.5), scalar2=None, op0=ALU.mult),
                        reads=[self.modT], writes=[self.modG])
        k.end_phase()

    def mod_shift(self, l, r, s, kc):
        c = (3 * s) * 8 + kc
        return self.modT.t[:, l, c:c + 1, r]

    def load_inputs(self):
        k = self.k
        k.begin_phase()
        xin = [k.sb(f"xin{i}", [128, D], F32) for i in range(2)]
        xblk = [k.sb(f"xblk{i}", [128, 8, 512], F32) for i in range(2)]
        bi = 0
        for (path, t0) in self.blocks:
            src = self.W["xp"] if path == "p" else self.W["xs"]
            xb = xblk[bi % 2]
            for tt in range(4):
                xi = xin[tt % 2]
                k.dma("sp", xi[:], src[t0 + tt * 128:t0 + (tt + 1) * 128, :], writes=[xi])
                for half in range(2):
                    ps = k.psum()
                    for j in range(4):
                        kc = half * 4 + j
                        k.tr(ps, ps[:, j * 128:(j + 1) * 128], xi[:, kc * 128:(kc + 1) * 128], self.ident, reads=[xi, self.masks])
                    eng = "dve" if half == 0 else "act"
                    if eng == "dve":
                        k.op("dve", lambda e, xb=xb, ps=ps, half=half, tt=tt: e.tensor_copy(
                            out=xb[:, half * 4:half * 4 + 4, tt * 128:(tt + 1) * 128],
                            in_=ps[:, :].rearrange("p (j t) -> p j t", j=4)), reads=[ps], writes=[xb])
                    else:
                        k.op("act", lambda e, xb=xb, ps=ps, half=half, tt=tt: e.activation(
                            out=xb[:, half * 4:half * 4 + 4, tt * 128:(tt + 1) * 128],
                            in_=ps[:, :].rearrange("p (j t) -> p j t", j=4), func=AF.Copy), reads=[ps], writes=[xb])
            k.dma("sp", self.xT[path][:, :, t0:t0 + 512], xb[:], reads=[xb], writes=[self.xT[path].sub(t0)])
            bi += 1
        k.end_phase()

    def rmsnorm(self, x, n, Acol, Bcol, out, sq, rstd, nfeat_chunks=8, nfeat=D, tmp=None):
        k = self.k
        xs, xr = x
        outf, outw = out
        for kc in range(nfeat_chunks):
            k.op("act", lambda e, kc=kc: e.activation(out=sq[:, kc, 0:n], in_=xs(kc), func=AF.Square), reads=xr, writes=[sq])
        ps = k.psum()
        for kc in range(nfeat_chunks):
            k.mm(ps, ps[:, 0:n], self.ones_b[:, :], sq[:, kc, 0:n], kc == 0, kc == nfeat_chunks - 1, reads=[self.ones_b, sq])
        k.op("act", lambda e: e.activation(out=rstd[:, 0:n], in_=ps[:, 0:n], func=AF.Sqrt, bias=self.epsc[:, 0:1], scale=1.0 / nfeat),
             reads=[ps, self.epsc], writes=[rstd])
        k.op("dve", lambda e: e.reciprocal(out=rstd[:, 0:n], in_=rstd[:, 0:n]), reads=[rstd], writes=[rstd])
        for kc in range(nfeat_chunks):
            if Bcol is None:
                k.op("dve", lambda e, kc=kc: e.scalar_tensor_tensor(out=outf(kc), in0=xs(kc), scalar=Acol(kc), in1=rstd[:, 0:n],
                                                                     op0=ALU.mult, op1=ALU.mult), reads=list(xr) + [rstd], writes=outw)
            else:
                k.op("dve", lambda e, kc=kc: e.scalar_tensor_tensor(out=tmp[:, kc, 0:n], in0=xs(kc), scalar=Acol(kc), in1=rstd[:, 0:n],
                                                                     op0=ALU.mult, op1=ALU.mult), reads=list(xr) + [rstd], writes=[tmp])
                k.op("act", lambda e, kc=kc: e.activation(out=outf(kc), in_=tmp[:, kc, 0:n], func=AF.Identity, bias=Bcol(kc), scale=1.0),
                     reads=[tmp], writes=outw)

    def ffn(self, l, s):
        k = self.k
        W = self.W
        w1 = W["ffn1_w1" if s == 0 else "ffn2_w1"][l].rearrange("(kc p) n -> p kc n", p=128)
        w2 = W["ffn1_w2" if s == 0 else "ffn2_w2"][l].rearrange("(j p) n -> p j n", p=128)
        k.begin_phase()
        xb = k.sb("f_x", [128, 8, 512], F32)
        hb = k.sb("f_h", [128, 8, 512], BF16)
        tmp = k.sb("f_tmp", [128, 8, 512], F32)
        act = k.sb("f_act", [128, 22, 512], BF16)
        sq = k.sb("f_sq", [128, 8, 512], BF16)
        rstd = k.sb("f_rstd", [128, 512], F32)
        sg = [k.sb(f"f_sg{i}", [128, 512], F32) for i in range(2)]
        w1b = [k.sb(f"f_w1{i}", [128, 8, 512], BF16) for i in range(2)]
        w2b = [k.sb(f"f_w2{i}", [128, 22, 512], BF16) for i in range(2)]
        nw1 = 0
        nw2 = 0
        w1c = k.dram(f"w1c_{l}_{s}", [128, 11, 8, 512], BF16)
        w2c = k.dram(f"w2c_{l}_{s}", [128, 2, 22, 512], BF16)
        for bi, (path, t0) in enumerate(self.blocks):
            r = 0 if path == "p" else 1
            xsub = self.xT[path].sub(t0)
            k.dma("sp", xb[:], self.xT[path][:, :, t0:t0 + 512], reads=[xsub], writes=[xb])
            self.rmsnorm((lambda kc: xb[:, kc, :], [xb]), 512,
                         lambda kc: self.modA[:, l, r, s, kc:kc + 1], lambda kc: self.mod_shift(l, r, s, kc),
                         (lambda kc: hb[:, kc, :], [hb]), sq, rstd, tmp=tmp)
            for jg in range(11):
                wt = w1b[nw1 % 2]
                nw1 += 1
                if bi == 0:
                    k.dma("pool", wt[:, :, 0:256], w1[:, :, jg * 256:(jg + 1) * 256], writes=[wt])
                    k.dma("pool", wt[:, :, 256:512], w1[:, :, DFF + jg * 256:DFF + (jg + 1) * 256], writes=[wt])
                    k.dma("sp", w1c[:, jg], wt[:], reads=[wt], writes=[w1c.sub(jg)])
                else:
                    k.dma("sp", wt[:], w1c[:, jg], reads=[w1c.sub(jg)], writes=[wt])
                for jj in range(2):
                    j = jg * 2 + jj
                    pg = k.psum()
                    pu = k.psum()
                    for kc in range(8):
                        k.mm(pg, pg[:, :], wt[:, kc, jj * 128:(jj + 1) * 128], hb[:, kc, :], kc == 0, kc == 7, reads=[wt, hb])
                    for kc in range(8):
                        k.mm(pu, pu[:, :], wt[:, kc, 256 + jj * 128:256 + (jj + 1) * 128], hb[:, kc, :], kc == 0, kc == 7, reads=[wt, hb])
                    sgt = sg[j % 2]
                    k.op("act", lambda e, sgt=sgt, pg=pg: e.activation(out=sgt[:], in_=pg[:, :], func=AF.Silu), reads=[pg], writes=[sgt])
                    k.op("dve", lambda e, sgt=sgt, pu=pu, j=j: e.tensor_tensor(out=act[:, j, :], in0=pu[:, :], in1=sgt[:], op=ALU.mult),
                         reads=[pu, sgt], writes=[act.sub(j)])
            for half in range(2):
                wt = w2b[nw2 % 2]
                nw2 += 1
                if bi == 0:
                    for q4 in range(2):
                        k.dma("pool", wt[:, q4 * 11:(q4 + 1) * 11, :], w2[:, q4 * 11:(q4 + 1) * 11, half * 512:(half + 1) * 512], writes=[wt])
                    k.dma("sp", w2c[:, half], wt[:], reads=[wt], writes=[w2c.sub(half)])
                else:
                    k.dma("sp", wt[:], w2c[:, half], reads=[w2c.sub(half)], writes=[wt])
                for f4 in range(4):
                    fc = half * 4 + f4
                    po = k.psum()
                    for j in range(22):
                        k.mm(po, po[:, :], wt[:, j, f4 * 128:(f4 + 1) * 128], act[:, j, :], j == 0, j == 21, reads=[wt, act.sub(j)])
                    k.op("dve", lambda e, po=po, fc=fc: e.scalar_tensor_tensor(
                        out=xb[:, fc, :], in0=po[:, :], scalar=self.modG[:, l, r, s, fc:fc + 1], in1=xb[:, fc, :],
                        op0=ALU.mult, op1=ALU.add), reads=[po, xb, self.modG], writes=[xb])
            k.dma("sp", self.xT[path][:, :, t0:t0 + 512], xb[:], reads=[xb], writes=[xsub])
            if s == 0:
                hT = self.hT[path]
                self.rmsnorm((lambda kc: xb[:, kc, :], [xb]), 512,
                             lambda kc: self.modA[:, l, r, 1, kc:kc + 1], lambda kc: self.mod_shift(l, r, 1, kc),
                             (lambda kc: hT[:, kc, t0:t0 + 512], [hT.sub(t0)]), sq, rstd, tmp=tmp)
        k.end_phase()

    def final(self):
        k = self.k
        k.begin_phase()
        xb = k.sb("o_x", [128, 8, 512], F32)
        yb = k.sb("o_y", [128, 8, 512], F32)
        sq = k.sb("o_sq", [128, 8, 512], BF16)
        rstd = k.sb("o_rstd", [128, 512], F32)
        yo = [k.sb(f"o_yo{i}", [128, D], F32) for i in range(2)]
        n = 0
        for (path, t0) in self.blocks:
            k.dma("sp", xb[:], self.xT[path][:, :, t0:t0 + 512], reads=[self.xT[path].sub(t0)], writes=[xb])
            self.rmsnorm((lambda kc: xb[:, kc, :], [xb]), 512, lambda kc: self.vecT[:, 48 + kc:49 + kc], None,
                         (lambda kc: yb[:, kc, :], [yb]), sq, rstd)
            dst = self.O["yp"] if path == "p" else self.O["ys"]
            for tt in range(4):
                y = yo[n % 2]
                n += 1
                for half in range(2):
                    ps = k.psum()
                    for j in range(4):
                        kc = half * 4 + j
                        k.tr(ps, ps[:, j * 128:(j + 1) * 128], yb[:, kc, tt * 128:(tt + 1) * 128], self.ident, reads=[yb, self.masks])
                    if half == 0:
                        k.op("dve", lambda e, y=y, ps=ps: e.tensor_copy(out=y[:, 0:512], in_=ps[:, :]), reads=[ps], writes=[y])
                    else:
                        k.op("act", lambda e, y=y, ps=ps: e.activation(out=y[:, 512:1024], in_=ps[:, :], func=AF.Copy), reads=[ps], writes=[y])
                k.dma("sp", dst[t0 + tt * 128:t0 + (tt + 1) * 128, :], y[:], reads=[y], writes=[dst])
        k.end_phase()

    ao_k = 128

    def apply_out(self, l, path, t0, n, mixT, nchunks, wout_t, chunk0):
        k = self.k
        r = 0 if path == "p" else 1
        mixf, mixr = mixT
        xb = self.ao_x
        xsub = self.xT[path].sub((t0 // 512) * 512)
        k.dma("sp", xb[:, :, 0:n], self.xT[path][:, :, t0:t0 + n], reads=[xsub], writes=[xb])
        for fc in range(8):
            po = k.psum()
            for c in range(nchunks):
                k.mm(po, po[:, 0:n], wout_t[0:self.ao_k, chunk0 + c, fc * 128:(fc + 1) * 128], mixf(c), c == 0, c == nchunks - 1,
                     reads=[wout_t] + list(mixr))
            k.op("dve", lambda e, po=po, fc=fc: e.scalar_tensor_tensor(
                out=xb[:, fc, 0:n], in0=po[:, 0:n], scalar=self.modG[:, l, r, 1, fc:fc + 1], in1=xb[:, fc, 0:n],
                op0=ALU.mult, op1=ALU.add), reads=[po, xb, self.modG], writes=[xb])
        k.dma("sp", self.xT[path][:, :, t0:t0 + n], xb[:, :, 0:n], reads=[xb], writes=[xsub])

    def store_tok(self, srcf, srcr, nfeat, ntok, dst_fn, f0=0):
        k = self.k
        for tt in range(ntok // 128):
            ps = k.psum()
            k.tr(ps, ps[:, 0:nfeat], srcf(tt), self.ident[0:nfeat, 0:nfeat], reads=list(srcr) + [self.masks])
            st = self.st_buf[self.st_n % 2]
            self.st_n += 1
            k.op("act", lambda e, st=st, ps=ps: e.activation(out=st[:, 0:nfeat - f0], in_=ps[:, f0:nfeat], func=AF.Copy), reads=[ps], writes=[st])
            k.dma("sp", dst_fn(tt), st[:, 0:nfeat - f0], reads=[st], writes=[])

    def mixer(self, l, path):
        self.mla(l, path)
        self.swa(l, path)
        self.gdn(l, path)

    def seqs(self, path):
        return [(0, 256), (256, 256)] if path == "p" else [(0, NST)]

    def softmax_bound(self, q2max, k2max, scale, negc):
        k = self.k
        t = self.sm_t
        k.op("dve", lambda e: e.tensor_tensor(out=t[0:1, 0:1], in0=q2max, in1=k2max, op=ALU.add), reads=[self.sm_q, self.sm_k], writes=[t])
        k.op("dve", lambda e: e.tensor_scalar(out=t[0:1, 0:1], in0=t[0:1, 0:1], scalar1=-0.5 * scale, scalar2=None, op0=ALU.mult),
             reads=[t], writes=[t])
        ps = k.psum()
        k.mm(ps, ps[:, 0:1], self.ones_f[0:1, :], t[0:1, 0:1], True, True, reads=[self.ones_f, t])
        k.op("dve", lambda e: e.tensor_copy(out=negc[:, 0:1], in_=ps[:, 0:1]), reads=[ps], writes=[negc])

    def sq_max(self, src_ap, nrow, n, dst, first, reads, dstT=None):
        k = self.k
        sqt = self.sm_sq
        k.op("pool", lambda e: e.tensor_tensor(out=sqt[0:nrow, 0:n], in0=src_ap, in1=src_ap, op=ALU.mult), reads=reads, writes=[sqt])
        ps = k.psum()
        k.mm(ps, ps[0:1, 0:n], self.ones_b[0:nrow, 0:1], sqt[0:nrow, 0:n], True, True, reads=[self.ones_b, sqt])
        tm = self.sm_m
        if dstT is None:
            dstT = self.sm_k if n_is_k(dst, self) else self.sm_q
        if first:
            k.op("dve", lambda e: e.reduce_max(out=dst, in_=ps[0:1, 0:n], axis=mybir.AxisListType.X), reads=[ps], writes=[dstT])
        else:
            k.op("dve", lambda e: e.reduce_max(out=tm[0:1, 0:1], in_=ps[0:1, 0:n], axis=mybir.AxisListType.X), reads=[ps], writes=[tm])
            k.op("dve", lambda e: e.tensor_tensor(out=dst, in0=dst, in1=tm[0:1, 0:1], op=ALU.max), reads=[tm, dstT], writes=[dstT])

    def mla(self, l, path):
        k = self.k
        W = self.W
        smp = path == "s"
        hT = self.hT[path]
        scale = 96 ** -0.5
        k.begin_phase()
        self.ao_x = k.sb("ao_x", [128, 8, 512], F32)
        self.st_buf = [k.sb(f"st{i}", [128, 256], F32) for i in range(2)]
        self.st_n = 0
        self.sm_t = k.sb("sm_t", [1, 1], F32)
        self.sm_q = k.sb("sm_q", [1, 1], F32)
        self.sm_k = k.sb("sm_k", [1, 8], F32)
        self.sm_m = k.sb("sm_m", [1, 1], F32)
        self.sm_sq = k.sb("sm_sq", [96, 512], BF16)
        negc = k.sb("negc", [128, 1], F32)
        win = k.sb("m_win", [128, 8, 672], BF16)
        k.dma("pool", win[:], W["w_in"][l].rearrange("(kc p) n -> p kc n", p=128)[:, :, 0:672], writes=[win])
        wqb = k.sb("m_wqb", [128, 3, 768], BF16)
        k.dma("pool", wqb[:], W["mla_w_qb"][l].rearrange("(kc p) n -> p kc n", p=128), writes=[wqb])
        wkvb = k.sb("m_wkvb", [128, 2, 1024], BF16)
        k.dma("pool", wkvb[:], W["mla_w_kvb"][l].rearrange("(kc p) n -> p kc n", p=128), writes=[wkvb])
        wout = k.sb("m_wout", [64, 4, D], BF16)
        wkr = k.sb("m_wkr", [128, 8, 96], BF16)
        k.op("pool", lambda e: e.memset(wkr[:], 0.0), writes=[wkr])
        k.op("pool", lambda e: e.tensor_copy(out=wkr[:, :, 64:96], in_=win[:, :, 640:672]), reads=[win], writes=[wkr])
        if smp:
            rope = k.sb("m_rope", [96, 2, NST], F32)
            k.dma("sp", rope[:], W["rope_m"][:, :, :], writes=[rope])
            wkr_r = k.sb("m_wkr_r", [128, 8, 96], BF16)
            k.op("pool", lambda e: e.memset(wkr_r[:], 0.0), writes=[wkr_r])
            wqb_r = k.sb("m_wqb_r", [128, 3, 768], BF16)
            k.op("pool", lambda e: e.memset(wqb_r[:], 0.0), writes=[wqb_r])
            for a in range(2):
                b0 = 64 + a * 16
                k.op("pool", lambda e, b0=b0: e.tensor_scalar(out=wkr_r[:, :, b0:b0 + 8], in0=wkr[:, :, b0 + 8:b0 + 16], scalar1=-1.0,
                                                               scalar2=None, op0=ALU.mult), reads=[wkr], writes=[wkr_r])
                k.op("pool", lambda e, b0=b0: e.tensor_copy(out=wkr_r[:, :, b0 + 8:b0 + 16], in_=wkr[:, :, b0:b0 + 8]), reads=[wkr], writes=[wkr_r])
                for h in range(8):
                    c0 = h * 96 + b0
                    k.op("pool", lambda e, c0=c0: e.tensor_scalar(out=wqb_r[:, :, c0:c0 + 8], in0=wqb[:, :, c0 + 8:c0 + 16], scalar1=-1.0,
                                                                   scalar2=None, op0=ALU.mult), reads=[wqb], writes=[wqb_r])
                    k.op("pool", lambda e, c0=c0: e.tensor_copy(out=wqb_r[:, :, c0 + 8:c0 + 16], in_=wqb[:, :, c0:c0 + 8]), reads=[wqb], writes=[wqb_r])
        sq = k.sb("m_sq", [128, 3, 512], BF16)
        rstd = k.sb("m_rstd", [128, 512], F32)
        cf = k.sb("m_cf", [128, 3, 512], F32)
        krf = k.sb("m_krf", [96, 512], F32)
        krb = k.sb("m_krb", [96, 512], F32)
        qtmp, qtmp2 = krf, krb
        qTs = [k.sb(f"m_qT{i}", [96, 512], BF16) for i in range(4)]
        negcs = [k.sb(f"m_negc{i}", [128, 1], F32) for i in range(4)]
        smqs = [k.sb(f"m_smq{i}", [1, 1], F32) for i in range(4)]
        smts = [k.sb(f"m_smt{i}", [1, 1], F32) for i in range(4)]
        qT = qTs[0]
        PT = [k.sb(f"m_PT{i}", [128, 512], BF16) for i in range(3)]
        mixH = k.sb("m_mixH", [64, 4, 512], BF16)
        rrow = k.sb("m_rrow", [65, 512], F32)
        rbc = k.sb("m_rbc", [64, 512], F32)
        for (s0, T_) in self.seqs(path):
            nk = T_ + (PAST if smp else 0)
            nkt = nk // 128
            cqn = k.sb(f"m_cqn{s0}", [128, 3, T_], BF16)
            ckvn = k.sb(f"m_ckvn{s0}", [128, 2, nk], BF16)
            Kt = k.sb(f"m_Kt{s0}", [96, 4, nk], BF16)
            Vt = k.sb(f"m_Vt{s0}", [128, nkt, 4, 65], BF16)
            krall = k.sb(f"m_krall{s0}", [96, nk], BF16)
            k.op("pool", lambda e: e.memset(Vt[:], 1.0), writes=[Vt])
            groups = [(g0, min(512, T_ - g0)) for g0 in range(0, T_, 512)]
            seq_i = s0 // 256
            for (g0, n) in groups:
                a0 = s0 + g0
                for (c0, nch, gcol, dst, nfeat) in [(0, 3, 72, cqn, 384), (384, 2, 78, ckvn, 256)]:
                    pss = []
                    for c in range(nch):
                        ps = k.psum()
                        pss.append(ps)
                        for kc in range(8):
                            k.mm(ps, ps[:, 0:n], win[:, kc, c0 + c * 128:c0 + (c + 1) * 128], hT[:, kc, a0:a0 + n], kc == 0, kc == 7,
                                 reads=[win, hT.sub((a0 // 512) * 512)])
                    for c in range(nch):
                        k.op("act", lambda e, c=c, ps=pss[c]: e.activation(out=cf[:, c, 0:n], in_=ps[:, 0:n], func=AF.Copy), reads=[pss[c]], writes=[cf])
                    gcl = gcol + l * nch
                    self.rmsnorm((lambda c: cf[:, c, 0:n], [cf]), n, lambda c: self.vecT[:, gcl + c:gcl + c + 1], None,
                                 (lambda c: cf[:, c, 0:n], [cf]), sq, rstd, nfeat_chunks=nch, nfeat=nfeat)
                    for c in range(nch):
                        k.op("pool", lambda e, c=c, dst=dst: e.tensor_copy(out=dst[:, c, g0:g0 + n], in_=cf[:, c, 0:n]), reads=[cf], writes=[dst])
                    if dst is ckvn and not smp:
                        for c in range(2):
                            self.store_tok(lambda tt, c=c: cf[:, c, tt * 128:(tt + 1) * 128], [cf], 128, n,
                                           lambda tt, c=c: self.O["n_ckv"][seq_i, l, g0 + tt * 128:g0 + (tt + 1) * 128, c * 128:(c + 1) * 128])
                pa = k.psum()
                for kc in range(8):
                    k.mm(pa, pa[0:96, 0:n], wkr[:, kc, :], hT[:, kc, a0:a0 + n], kc == 0, kc == 7, reads=[wkr, hT.sub((a0 // 512) * 512)])
                if smp:
                    pb = k.psum()
                    for kc in range(8):
                        k.mm(pb, pb[0:96, 0:n], wkr_r[:, kc, :], hT[:, kc, a0:a0 + n], kc == 0, kc == 7, reads=[wkr_r, hT.sub((a0 // 512) * 512)])
                    k.op("dve", lambda e: e.tensor_tensor(out=krf[64:96, 0:n], in0=pa[64:96, 0:n], in1=rope[64:96, 0, g0:g0 + n], op=ALU.mult),
                         reads=[pa, rope], writes=[krf])
                    k.op("dve", lambda e: e.tensor_tensor(out=krb[64:96, 0:n], in0=pb[64:96, 0:n], in1=rope[64:96, 1, g0:g0 + n], op=ALU.mult),
                         reads=[pb, rope], writes=[krb])
                    k.op("pool", lambda e: e.tensor_tensor(out=krall[64:96, g0:g0 + n], in0=krf[64:96, 0:n], in1=krb[64:96, 0:n], op=ALU.add),
                         reads=[krf, krb], writes=[krall])
                else:
                    k.op("dve", lambda e: e.tensor_copy(out=krf[0:96, 0:n], in_=pa[0:96, 0:n]), reads=[pa], writes=[krf])
                    k.op("pool", lambda e: e.tensor_copy(out=krall[64:96, g0:g0 + n], in_=krf[64:96, 0:n]), reads=[krf], writes=[krall])
                    self.store_tok(lambda tt: krf[0:96, tt * 128:(tt + 1) * 128], [krf], 96, n,
                                   lambda tt: self.O["n_kr"][seq_i, l, g0 + tt * 128:g0 + (tt + 1) * 128, :], f0=64)
            if smp:
                kin = k.sb("m_kin", [128, 96], F32)
                k.op("pool", lambda e: e.memset(kin[:], 0.0), writes=[kin])
                for i in range(4):
                    ci = cf.t[:, i % 2, 0:256]
                    k.dma("sp", ci, W["c_ckv"][l, i * 128:(i + 1) * 128, :], writes=[cf])
                    ps = k.psum()
                    for c in range(2):
                        k.tr(ps, ps[:, c * 128:(c + 1) * 128], ci[:, c * 128:(c + 1) * 128], self.ident, reads=[cf, self.masks])
                    k.op("act", lambda e, ps=ps, i=i: e.activation(out=ckvn[:, :, T_ + i * 128:T_ + (i + 1) * 128],
                                                                   in_=ps[:, 0:256].rearrange("p (c t) -> p c t", c=2), func=AF.Copy),
                         reads=[ps], writes=[ckvn])
                    k.dma("sp", kin[:, 64:96], W["c_kr"][l, i * 128:(i + 1) * 128, :], writes=[kin])
                    ps2 = k.psum()
                    k.tr(ps2, ps2[0:96, 0:128], kin[:, :], self.ident, reads=[kin, self.masks])
                    k.op("dve", lambda e, ps2=ps2, i=i: e.tensor_copy(out=krall[64:96, T_ + i * 128:T_ + (i + 1) * 128], in_=ps2[64:96, 0:128]),
                         reads=[ps2], writes=[krall])
            kgroups = [(g0, min(512, nk - g0)) for g0 in range(0, nk, 512)]
            for hg in range(2):
                k.dma("pool", wout[:], W["w_out"][l][hg * 256:(hg + 1) * 256, :].rearrange("(h p) n -> p h n", p=64), writes=[wout])
                for (g0, n) in kgroups:
                    for hh in range(4):
                        h = hg * 4 + hh
                        ps = k.psum()
                        for c in range(2):
                            k.mm(ps, ps[0:64, 0:n], wkvb[:, c, h * 128:h * 128 + 64], ckvn[:, c, g0:g0 + n], c == 0, c == 1, reads=[wkvb, ckvn])
                        eng = "act" if hh % 2 == 0 else "dve"
                        if eng == "act":
                            k.op("act", lambda e, ps=ps, hh=hh: e.activation(out=Kt[0:64, hh, g0:g0 + n], in_=ps[0:64, 0:n], func=AF.Copy),
                                 reads=[ps], writes=[Kt.sub(hh)])
                        else:
                            k.op("dve", lambda e, ps=ps, hh=hh: e.tensor_copy(out=Kt[0:64, hh, g0:g0 + n], in_=ps[0:64, 0:n]), reads=[ps], writes=[Kt.sub(hh)])
                        if hg == 0:
                            pass
                        k.op("pool", lambda e, hh=hh: e.tensor_copy(out=Kt[64:96, hh, g0:g0 + n], in_=krall[64:96, g0:g0 + n]),
                             reads=[krall], writes=[Kt.sub(hh)])
                        self.sq_max(Kt[0:96, hh, g0:g0 + n], 96, n, self.sm_k[0:1, hh:hh + 1], g0 == 0, [Kt.sub(hh)])
                    for kt in range(g0 // 128, (g0 + n) // 128):
                        ps = k.psum()
                        for c in range(2):
                            k.mm(ps, ps[:, 0:256].rearrange("p (h d) -> p h d", h=4),
                                 ckvn[:, c, kt * 128:(kt + 1) * 128],
                                 wkvb[:, c, hg * 512:(hg + 1) * 512].rearrange("p (h d) -> p h d", h=4)[:, :, 64:128], c == 0, c == 1,
                                 reads=[wkvb, ckvn])
                        k.op("act", lambda e, ps=ps, kt=kt: e.activation(out=Vt[:, kt, :, 0:64], in_=ps[:, 0:256].rearrange("p (h d) -> p h d", h=4),
                                                                         func=AF.Copy), reads=[ps], writes=[Vt])
                npt = 0
                for (g0, n) in groups:
                    nqt = n // 128
                    for hh in range(4):
                        h = hg * 4 + hh
                        pa = k.psum()
                        for c in range(3):
                            k.mm(pa, pa[0:96, 0:n], wqb[:, c, h * 96:(h + 1) * 96], cqn[:, c, g0:g0 + n], c == 0, c == 2, reads=[wqb, cqn])
                        if smp:
                            pb = k.psum()
                            for c in range(3):
                                k.mm(pb, pb[0:96, 0:n], wqb_r[:, c, h * 96:(h + 1) * 96], cqn[:, c, g0:g0 + n], c == 0, c == 2, reads=[wqb_r, cqn])
                            k.op("dve", lambda e: e.tensor_tensor(out=qtmp[:, 0:n], in0=pa[0:96, 0:n], in1=rope[:, 0, g0:g0 + n], op=ALU.mult),
                                 reads=[pa, rope], writes=[qtmp])
                            k.op("dve", lambda e: e.tensor_tensor(out=qtmp2[:, 0:n], in0=pb[0:96, 0:n], in1=rope[:, 1, g0:g0 + n], op=ALU.mult),
                                 reads=[pb, rope], writes=[qtmp2])
                            k.op("pool", lambda e: e.tensor_tensor(out=qTs[hh][:, 0:n], in0=qtmp[:, 0:n], in1=qtmp2[:, 0:n], op=ALU.add),
                                 reads=[qtmp, qtmp2], writes=[qTs[hh]])
                        else:
                            k.op("act", lambda e: e.activation(out=qTs[hh][:, 0:n], in_=pa[0:96, 0:n], func=AF.Copy), reads=[pa], writes=[qTs[hh]])
                        self.sm_q = smqs[hh]; self.sm_t = smts[hh]
                        self.sq_max(qTs[hh][0:96, 0:n], 96, n, self.sm_q[0:1, 0:1], True, [qTs[hh]], dstT=self.sm_q)
                        self.softmax_bound(self.sm_q[0:1, 0:1], self.sm_k[0:1, hh:hh + 1], scale, negcs[hh])
                    for hh in range(4):
                        h = hg * 4 + hh
                        qT = qTs[hh]
                        negc = negcs[hh]
                        pos = k.psum_hold(1)
                        po = pos[0]
                        def qk(kt):
                            ps = k.psum()
                            k.mm(ps, ps[:, 0:n], Kt[:, hh, kt * 128:(kt + 1) * 128], qT[:, 0:n], True, True, reads=[Kt.sub(hh), qT])
                            return ps
                        pend = qk(0)
                        for kt in range(nkt):
                            ps = pend
                            if kt + 1 < nkt:
                                pend = qk(kt + 1)
                            pt = PT[npt % 3]
                            npt += 1
                            k.op("act", lambda e, pt=pt, ps=ps: e.activation(out=pt[:, 0:n], in_=ps[:, 0:n], func=AF.Exp, bias=negc[:, 0:1], scale=scale),
                                 reads=[ps, negc], writes=[pt])
                            k.mm(po, po[0:65, 0:n], Vt[:, kt, hh, :], pt[:, 0:n], kt == 0, kt == nkt - 1, reads=[pt, Vt])
                        k.op("act", lambda e, po=po: e.activation(out=rrow[64:65, 0:n], in_=po[64:65, 0:n], func=AF.Copy), reads=[po], writes=[rrow])
                        pbc = k.psum()
                        k.mm(pbc, pbc[0:64, 0:n], self.ones_f[64:65, 0:64], rrow[64:65, 0:n], True, True, reads=[self.ones_f, rrow])
                        k.op("dve", lambda e, pbc=pbc: e.reciprocal(out=rbc[:, 0:n], in_=pbc[0:64, 0:n]), reads=[pbc], writes=[rbc])
                        k.op("dve", lambda e, po=po, hh=hh: e.tensor_tensor(out=mixH[:, hh, 0:n], in0=po[0:64, 0:n], in1=rbc[:, 0:n], op=ALU.mult),
                             reads=[po, rbc], writes=[mixH])
                        k.psum_release(pos)
                    self.ao_k = 64
                    self.apply_out(l, path, s0 + g0, n, (lambda c: mixH[:, c, 0:n], [mixH]), 4, wout, 0)
                    self.ao_k = 128
        k.end_phase()

    def swa(self, l, path):
        k = self.k
        W = self.W
        smp = path == "s"
        hT = self.hT[path]
        scale = 64 ** -0.5
        k.begin_phase()
        self.ao_x = k.sb("ao_x", [128, 8, 512], F32)
        self.st_buf = [k.sb(f"st{i}", [128, 256], F32) for i in range(2)]
        self.st_n = 0
        self.sm_t = k.sb("sm_t", [1, 1], F32)
        self.sm_q = k.sb("sm_q", [1, 1], F32)
        self.sm_k = k.sb("sm_k", [1, 8], F32)
        self.sm_m = k.sb("sm_m", [1, 1], F32)
        self.sm_sq = k.sb("sm_sq", [96, 512], BF16)
        negc = k.sb("negc", [128, 1], F32)
        psink = k.sb("psink", [128, 8], F32)
        wsw = k.sb("s_w", [128, 8, 768], BF16)
        k.dma("pool", wsw[:], W["w_in"][l].rearrange("(kc p) n -> p kc n", p=128)[:, :, 672:1440], writes=[wsw])
        wout = k.sb("s_wout", [64, 8, D], BF16)
        k.dma("pool", wout[:], W["w_out"][l][512:1024, :].rearrange("(h p) n -> p h n", p=64), writes=[wout])
        if smp:
            rope = k.sb("s_rope", [64, 2, NST], F32)
            k.dma("sp", rope[:], W["rope_s"][:, :, :], writes=[rope])
            wr = k.sb("s_wr", [128, 8, 640], BF16)
            wv = wsw[:, :, 0:640].rearrange("p k (h a z d) -> p k h a z d", h=10, a=2, z=2)
            wrv = wr[:, :, :].rearrange("p k (h a z d) -> p k h a z d", h=10, a=2, z=2)
            for kc in range(8):
                k.op("pool", lambda e, kc=kc: e.tensor_scalar(out=wrv[:, kc, :, :, 0, :], in0=wv[:, kc, :, :, 1, :], scalar1=-1.0, scalar2=None, op0=ALU.mult),
                     reads=[wsw], writes=[wr])
                k.op("pool", lambda e, kc=kc: e.tensor_copy(out=wrv[:, kc, :, :, 1, :], in_=wv[:, kc, :, :, 0, :]), reads=[wsw], writes=[wr])
        kf = k.sb("s_kf", [64, 512], F32)
        kb_ = k.sb("s_kb", [64, 512], F32)

        def proj_rope(col0, a0, n, g0, dst_ap, dst_w, fp_out=None):
            pa = k.psum()
            for kc in range(8):
                k.mm(pa, pa[0:64, 0:n], wsw[:, kc, col0:col0 + 64], hT[:, kc, a0:a0 + n], kc == 0, kc == 7, reads=[wsw, hT.sub((a0 // 512) * 512)])
            if smp:
                pb = k.psum()
                for kc in range(8):
                    k.mm(pb, pb[0:64, 0:n], wr[:, kc, col0:col0 + 64], hT[:, kc, a0:a0 + n], kc == 0, kc == 7, reads=[wr, hT.sub((a0 // 512) * 512)])
                k.op("dve", lambda e: e.tensor_tensor(out=kf[:, 0:n], in0=pa[0:64, 0:n], in1=rope[:, 0, g0:g0 + n], op=ALU.mult), reads=[pa, rope], writes=[kf])
                k.op("dve", lambda e: e.tensor_tensor(out=kb_[:, 0:n], in0=pb[0:64, 0:n], in1=rope[:, 1, g0:g0 + n], op=ALU.mult), reads=[pb, rope], writes=[kb_])
                k.op("pool", lambda e: e.tensor_tensor(out=dst_ap, in0=kf[:, 0:n], in1=kb_[:, 0:n], op=ALU.add), reads=[kf, kb_], writes=dst_w)
            else:
                if fp_out is not None:
                    k.op("dve", lambda e: e.tensor_copy(out=kf[:, 0:n], in_=pa[0:64, 0:n]), reads=[pa], writes=[kf])
                    k.op("pool", lambda e: e.tensor_copy(out=dst_ap, in_=kf[:, 0:n]), reads=[kf], writes=dst_w)
                else:
                    k.op("act", lambda e: e.activation(out=dst_ap, in_=pa[0:64, 0:n], func=AF.Copy), reads=[pa], writes=dst_w)

        qs = k.sb("s_q", [64, 4, 512], BF16)
        PT = [k.sb(f"s_PT{i}", [128, 4, 128], BF16) for i in range(3)]
        mixH = k.sb("s_mixH", [64, 8, 512], BF16)
        rrow = k.sb("s_rrow", [65, 512], F32)
        rbc = k.sb("s_rbc", [64, 512], F32)
        for (s0, T_) in self.seqs(path):
            nk = T_ + (PAST if smp else 0)
            nkt = nk // 128
            nlt = T_ // 128
            seq_i = s0 // 256
            Ks = k.sb(f"s_K{s0}", [64, 2, nk], BF16)
            Vs = k.sb(f"s_V{s0}", [128, nkt, 2, 65], BF16)
            k.op("pool", lambda e: e.memset(Vs[:], 1.0), writes=[Vs])
            vst = [k.sb(f"s_vst{s0}_{i}", [128, 128], F32) for i in range(2)]
            groups = [(g0, min(512, T_ - g0)) for g0 in range(0, T_, 512)]
            for (g0, n) in groups:
                a0 = s0 + g0
                for kvh in range(2):
                    proj_rope(512 + kvh * 64, a0, n, g0, Ks[:, kvh, g0:g0 + n], [Ks.sub(kvh)], fp_out=(None if smp else True))
                    if not smp:
                        self.store_tok(lambda tt: kf[0:64, tt * 128:(tt + 1) * 128], [kf], 64, n,
                                       lambda tt, kvh=kvh: self.O["n_sk"][seq_i, l, g0 + tt * 128:g0 + (tt + 1) * 128, kvh * 64:(kvh + 1) * 64])
                    self.sq_max(Ks[0:64, kvh, g0:g0 + n], 64, n, self.sm_k[0:1, kvh:kvh + 1], g0 == 0, [Ks.sub(kvh)])
                for tt in range(n // 128):
                    kt = (g0 // 128) + tt
                    ps = k.psum()
                    for kc in range(8):
                        k.mm(ps, ps[:, 0:128], hT[:, kc, a0 + tt * 128:a0 + (tt + 1) * 128], wsw[:, kc, 640:768], kc == 0, kc == 7,
                             reads=[wsw, hT.sub((a0 // 512) * 512)])
                    k.op("act", lambda e, ps=ps, kt=kt: e.activation(out=Vs[:, kt, :, 0:64], in_=ps[:, 0:128].rearrange("p (h d) -> p h d", h=2), func=AF.Copy),
                         reads=[ps], writes=[Vs])
                    if not smp:
                        vt = vst[tt % 2]
                        k.op("dve", lambda e, ps=ps, vt=vt: e.tensor_copy(out=vt[:], in_=ps[:, 0:128]), reads=[ps], writes=[vt])
                        k.dma("sp", self.O["n_sv"][seq_i, l, g0 + tt * 128:g0 + (tt + 1) * 128, :], vt[:], reads=[vt], writes=[])
            if smp:
                cin = [k.sb(f"s_cin{i}", [128, 128], F32) for i in range(2)]
                cvn = [k.sb(f"s_cvn{i}", [128, 128], F32) for i in range(2)]
                for i in range(4):
                    ci = cin[i % 2]
                    k.dma("sp", ci[:], W["c_sk"][l, i * 128:(i + 1) * 128, :], writes=[ci])
                    for kvh in range(2):
                        ps = k.psum()
                        k.tr(ps, ps[0:64, 0:128], ci[:, kvh * 64:(kvh + 1) * 64], self.ident, reads=[ci, self.masks])
                        k.op("act", lambda e, ps=ps, kvh=kvh, i=i: e.activation(out=Ks[:, kvh, T_ + i * 128:T_ + (i + 1) * 128], in_=ps[0:64, 0:128], func=AF.Copy),
                             reads=[ps], writes=[Ks.sub(kvh)])
                    cv = cvn[i % 2]
                    k.dma("sp", cv[:], W["c_sv"][l, i * 128:(i + 1) * 128, :], writes=[cv])
                    k.op("pool", lambda e, cv=cv, i=i: e.tensor_copy(out=Vs[:, nlt + i, :, 0:64], in_=cv[:, :].rearrange("p (h d) -> p h d", h=2)),
                         reads=[cv], writes=[Vs])
                for kvh in range(2):
                    for g0 in range(T_, nk, 512):
                        self.sq_max(Ks[0:64, kvh, g0:g0 + 512], 64, 512, self.sm_k[0:1, kvh:kvh + 1], False, [Ks.sub(kvh)])
            npt = 0
            for (g0, n) in groups:
                a0 = s0 + g0
                nqt = n // 128
                for kvh in range(2):
                    for g in range(4):
                        proj_rope((kvh * 4 + g) * 64, a0, n, g0, qs[:, g, 0:n], [qs])
                        self.sq_max(qs[0:64, g, 0:n], 64, n, self.sm_q[0:1, 0:1], g == 0, [qs])
                    self.softmax_bound(self.sm_q[0:1, 0:1], self.sm_k[0:1, kvh:kvh + 1], scale, negc)
                    k.op("act", lambda e, kvh=kvh: e.activation(out=psink[:, kvh * 4:(kvh + 1) * 4], in_=self.rowv(l, "sink")[:, kvh * 4:(kvh + 1) * 4],
                                                                func=AF.Exp, bias=negc[:, 0:1], scale=1.0), reads=[self.rowbc, negc], writes=[psink])
                    for qt in range(nqt):
                        gq = g0 // 128 + qt
                        if smp:
                            kts = [(t, t - gq) for t in (gq - 1, gq, gq + 1) if 0 <= t < nlt] + [(nlt + i, 0) for i in range(4)]
                        else:
                            kts = [(t, 0) for t in range(nlt)]
                        pos = k.psum_hold(1)
                        po = pos[0]
                        def qk(kt):
                            ps = k.psum()
                            k.mm(ps, ps[:, :].rearrange("p (g q) -> p g q", g=4), Ks[:, kvh, kt * 128:(kt + 1) * 128], qs[:, :, qt * 128:(qt + 1) * 128],
                                 True, True, reads=[Ks.sub(kvh), qs])
                            return ps
                        pend = qk(kts[0][0])
                        for ki, (kt, rel) in enumerate(kts):
                            ps = pend
                            if ki + 1 < len(kts):
                                pend = qk(kts[ki + 1][0])
                            pt = PT[npt % 3]
                            npt += 1
                            k.op("act", lambda e, pt=pt, ps=ps: e.activation(out=pt[:, :, :], in_=ps[:, :].rearrange("p (g q) -> p g q", g=4), func=AF.Exp,
                                                                           bias=negc[:, 0:1], scale=scale), reads=[ps, negc], writes=[pt])
                            if rel != 0:
                                mi = 3 if rel == -1 else 1
                                for g in range(4):
                                    k.op("pool" if g % 2 else "dve", lambda e, pt=pt, g=g, mi=mi: e.tensor_tensor(out=pt[:, g, :], in0=pt[:, g, :], in1=self.masks_b[:, mi, :], op=ALU.mult),
                                         reads=[pt, self.masks_b], writes=[pt])
                            k.mm(po, po[0:65, :].rearrange("p (g q) -> p g q", g=4), Vs[:, kt, kvh, :], pt[:, :, :], ki == 0, ki == len(kts) - 1, reads=[pt, Vs])
                        for g in range(4):
                            k.op("dve", lambda e, po=po, g=g, kvh=kvh: e.tensor_scalar(out=rrow[64:65, g * 128:(g + 1) * 128], in0=po[64:65, g * 128:(g + 1) * 128],
                                                                                         scalar1=psink[64:65, kvh * 4 + g:kvh * 4 + g + 1], scalar2=None, op0=ALU.add),
                                 reads=[po, psink], writes=[rrow])
                        pbc = k.psum()
                        k.mm(pbc, pbc[0:64, :], self.ones_f[64:65, 0:64], rrow[64:65, :], True, True, reads=[self.ones_f, rrow])
                        k.op("dve", lambda e, pbc=pbc: e.reciprocal(out=rbc[:, :], in_=pbc[0:64, :]), reads=[pbc], writes=[rbc])
                        k.op("dve", lambda e, po=po, kvh=kvh, qt=qt: e.tensor_tensor(
                            out=mixH[:, kvh * 4:(kvh + 1) * 4, qt * 128:(qt + 1) * 128], in0=po[0:64, :].rearrange("p (g q) -> p g q", g=4),
                            in1=rbc[:, :].rearrange("p (g q) -> p g q", g=4), op=ALU.mult), reads=[po, rbc], writes=[mixH])
                        k.psum_release(pos)
                self.ao_k = 64
                self.apply_out(l, path, s0 + g0, n, (lambda c: mixH[:, c, 0:n], [mixH]), 8, wout, 0)
                self.ao_k = 128
        k.end_phase()

    def gdn(self, l, path):
        k = self.k
        W = self.W
        smp = path == "s"
        hT = self.hT[path]
        M = self.masks
        k.begin_phase()
        self.ao_x = k.sb("ao_x", [128, 8, 512], F32)
        win_all = W["w_in"][l].rearrange("(kc p) n -> p kc n", p=128)
        AM = k.sb("g_am", [128, 5, 128], F32)
        k.dma("sp", AM[:], W["amasks"][:], writes=[AM])
        wg = k.sb("g_wg", [128, 8, 16], BF16)
        k.dma("pool", wg[:], win_all[:, :, 3488:3504], writes=[wg])
        wout = k.sb("g_wout", [128, 4, D], BF16)
        k.dma("pool", wout[:], W["w_out"][l].rearrange("(c p) n -> p c n", p=128)[:, 8:12, :], writes=[wout])
        wh = [k.sb(f"g_wh{i}", [128, 8, 512], BF16) for i in range(1)]
        nalog = k.sb("g_nalog", [128, 8], F32)
        k.op("act", lambda e: e.activation(out=nalog[:], in_=self.rowv(l, "alog"), func=AF.Exp), reads=[self.rowbc], writes=[nalog])
        k.op("dve", lambda e: e.tensor_scalar(out=nalog[:], in0=nalog[:], scalar1=-1.0, scalar2=None, op0=ALU.mult), reads=[nalog], writes=[nalog])
        nwh = 0
        for (s0, T_) in self.seqs(path):
            nch = T_ // 128
            seq_i = s0 // 256
            gg = k.sb(f"g_g{s0}", [128, nch, 8], F32)
            bt = k.sb(f"g_b{s0}", [128, nch, 8], F32)
            for ci in range(nch):
                a0 = s0 + ci * 128
                ps = k.psum()
                for kc in range(8):
                    k.mm(ps, ps[:, 0:16], hT[:, kc, a0:a0 + 128], wg[:, kc, :], kc == 0, kc == 7, reads=[wg, hT.sub((a0 // 512) * 512)])
                k.op("dve", lambda e, ps=ps, ci=ci: e.tensor_tensor(out=gg[:, ci, :], in0=ps[:, 0:8], in1=self.rowv(l, "dtb"), op=ALU.add),
                     reads=[ps, self.rowbc], writes=[gg])
                k.op("act", lambda e, ps=ps, ci=ci: e.activation(out=bt[:, ci, :], in_=ps[:, 8:16], func=AF.Sigmoid), reads=[ps], writes=[bt])
            nbt = k.sb(f"g_nb{s0}", [128, nch, 8], F32)
            k.op("pool", lambda e: e.tensor_scalar(out=nbt[:], in0=bt[:], scalar1=-1.0, scalar2=None, op0=ALU.mult), reads=[bt], writes=[nbt])
            k.op("dve", lambda e: e.tensor_scalar(out=gg[:], in0=gg[:], scalar1=60.0, scalar2=None, op0=ALU.min), reads=[gg], writes=[gg])
            k.op("act", lambda e: e.activation(out=gg[:], in_=gg[:], func=AF.Exp), reads=[gg], writes=[gg])
            k.op("act", lambda e: e.activation(out=gg[:], in_=gg[:], func=AF.Ln, bias=1.0, scale=1.0), reads=[gg], writes=[gg])
            for ci in range(nch):
                k.op("dve", lambda e, ci=ci: e.tensor_tensor(out=gg[:, ci, :], in0=gg[:, ci, :], in1=nalog[:], op=ALU.mult), reads=[gg, nalog], writes=[gg])
            if GSTOP <= 1:
                continue
            pre = k.sb(f"g_pre{s0}", [128, 3, T_ + 4], F32)
            acc = k.sb(f"g_acc{s0}", [128, 3, T_], F32)
            osum = k.sb(f"g_osum{s0}", [128, nch, 128], F32)
            mixT = k.sb(f"g_mixT{s0}", [128, 4, T_], BF16)
            sqf = k.sb(f"g_sqf{s0}", [128, 512], F32)
            rst = k.sb(f"g_rst{s0}", [128, 512], F32)
            names = ["gcb", "X", "DT", "Dm", "N", "NT", "Bm", "BmT", "B2", "B2T", "Wm", "W2", "kbg", "vb",
                     "u", "wkT", "E", "gones", "gz", "on"]
            names_b = ["AT", "kdec", "qdec", "vnew", "Sb"]
            tqs = [{nm: k.sb(f"g_{nm}{s0}_{d}", [128, 128], F32) for nm in names} for d in range(2)]
            for d in range(2):
                tqs[d].update({nm: k.sb(f"g_{nm}{s0}_{d}", [128, 128], BF16) for nm in names_b})
            accb = k.sb(f"g_accb{s0}", [128, 2, T_], BF16)
            scs = [k.sb(f"g_sc{s0}_{d}", [128, 8], F32) for d in range(2)]
            Ss = [k.sb(f"g_S{s0}_{d}", [128, 128], F32) for d in range(2)]
            osums = [osum, k.sb(f"g_osumb{s0}", [128, nch, 128], F32)]
            tq, sc = tqs[0], scs[0]
            k.op("pool", lambda e: e.memset(pre[:], 0.0), writes=[pre])
            groups = [(g0, min(512, T_ - g0)) for g0 in range(0, T_, 512)]
            for h in range(4):
                w = wh[nwh % len(wh)]
                nwh += 1
                for wi, c0 in enumerate([1440, 1952, 2464, 2976]):
                    k.dma("pool", w[:, :, wi * 128:(wi + 1) * 128], win_all[:, :, c0 + h * 128:c0 + (h + 1) * 128], writes=[w])
                for (g0, n) in groups:
                    a0 = s0 + g0
                    for wi in range(3):
                        ps = k.psum()
                        for kc in range(8):
                            k.mm(ps, ps[:, 0:n], w[:, kc, wi * 128:(wi + 1) * 128], hT[:, kc, a0:a0 + n], kc == 0, kc == 7, reads=[w, hT.sub((a0 // 512) * 512)])
                        k.op("act", lambda e, ps=ps, wi=wi: e.activation(out=pre[:, wi, 2 + g0:2 + g0 + n], in_=ps[:, 0:n], func=AF.Copy), reads=[ps], writes=[pre.sub(wi)])
                for wi in range(3):
                    eng = "dve"
                    cch = wi * 4 + h
                    for tap in range(5):
                        wcol = self.convT[:, l * 60 + tap * 12 + cch:l * 60 + tap * 12 + cch + 1]
                        if tap == 0:
                            k.op(eng, lambda e, wi=wi, wcol=wcol: e.tensor_scalar(out=acc[:, wi, :], in0=pre[:, wi, 0:T_], scalar1=wcol, scalar2=None, op0=ALU.mult),
                                 reads=[pre.sub(wi), self.convT], writes=[acc.sub(wi)])
                        else:
                            k.op(eng, lambda e, wi=wi, wcol=wcol, tap=tap: e.scalar_tensor_tensor(out=acc[:, wi, :], in0=pre[:, wi, tap:tap + T_], scalar=wcol,
                                                                                                 in1=acc[:, wi, :], op0=ALU.mult, op1=ALU.add),
                                 reads=[pre.sub(wi), self.convT, acc.sub(wi)], writes=[acc.sub(wi)])
                    k.op("act", lambda e, wi=wi: e.activation(out=acc[:, wi, :], in_=acc[:, wi, :], func=AF.Silu), reads=[acc.sub(wi)], writes=[acc.sub(wi)])
                if GSTOP <= 2:
                    continue
                for wi in range(2):
                    for (g0, n) in groups:
                        k.op("act", lambda e, wi=wi: e.activation(out=sqf[:, 0:n], in_=acc[:, wi, g0:g0 + n], func=AF.Square), reads=[acc.sub(wi)], writes=[sqf])
                        ps = k.psum()
                        k.mm(ps, ps[:, 0:n], self.ones_f[:, :], sqf[:, 0:n], True, True, reads=[self.ones_f, sqf])
                        k.op("act", lambda e, ps=ps: e.activation(out=rst[:, 0:n], in_=ps[:, 0:n], func=AF.Sqrt, bias=self.epsc[:, 0:1], scale=1.0),
                             reads=[ps, self.epsc], writes=[rst])
                        k.op("dve", lambda e: e.reciprocal(out=rst[:, 0:n], in_=rst[:, 0:n]), reads=[rst], writes=[rst])
                        if wi == 0:
                            k.op("dve", lambda e, wi=wi: e.scalar_tensor_tensor(out=acc[:, wi, g0:g0 + n], in0=acc[:, wi, g0:g0 + n], scalar=128 ** -0.5, in1=rst[:, 0:n],
                                                                                 op0=ALU.mult, op1=ALU.mult), reads=[acc.sub(wi), rst], writes=[acc.sub(wi)])
                        else:
                            k.op("pool", lambda e, wi=wi: e.tensor_tensor(out=acc[:, wi, g0:g0 + n], in0=acc[:, wi, g0:g0 + n], in1=rst[:, 0:n], op=ALU.mult),
                                 reads=[acc.sub(wi), rst], writes=[acc.sub(wi)])
                k.op("act", lambda e: e.activation(out=accb[:, 0, :], in_=acc[:, 0, :], func=AF.Copy), reads=[acc.sub(0)], writes=[accb])
                k.op("pool", lambda e: e.tensor_copy(out=accb[:, 1, :], in_=acc[:, 1, :]), reads=[acc.sub(1)], writes=[accb])
                if GSTOP <= 3:
                    continue
                def chain(d, tq, sc, S, osum):
                    col = d * 4 + h
                    if smp:
                        k.dma("sp", S[:], W["c_st"][l, d, h], writes=[S])
                        yield
                    else:
                        k.op("pool", lambda e: e.memset(S[:], 0.0), writes=[S])
                        yield
                    k.op("act", lambda e: e.activation(out=tq["Sb"][:], in_=S[:], func=AF.Copy), reads=[S], writes=[tq["Sb"]])
                    yield
                    order = range(nch) if d == 0 else range(nch - 1, -1, -1)
                    mcs = 1 if d == 0 else 3
                    m_dt_incl = 1 if d == 0 else 3
                    m_d_strict = 4 if d == 0 else 2
                    last = 127 if d == 0 else 0
                    for ci in order:
                        c0 = ci * 128
                        kT_c = acc[:, 1, c0:c0 + 128]
                        qT_c = acc[:, 0, c0:c0 + 128]
                        vT_c = acc[:, 2, c0:c0 + 128]
                        gcol = gg[:, ci, col:col + 1]
                        bcol = bt[:, ci, col:col + 1]
                        rd = [acc.sub(0), acc.sub(1), acc.sub(2)]
                        t = tq
                        yield
                        p1 = k.psum()
                        k.mm(p1, p1[:, 0:128], kT_c, kT_c, True, True, reads=rd)
                        yield
                        k.mm(p1, p1[:, 128:256], accb[:, 1, c0:c0 + 128], accb[:, 0, c0:c0 + 128], True, True, reads=[accb])
                        yield
                        k.tr(p1, p1[:, 256:384], kT_c, self.ident, reads=rd + [M])
                        yield
                        k.tr(p1, p1[:, 384:512], vT_c, self.ident, reads=rd + [M])
                        yield
                        yield
                        k.op("dve", lambda e: e.tensor_scalar(out=t["gones"][:], in0=self.ones_f[:], scalar1=gcol, scalar2=None, op0=ALU.mult),
                             reads=[gg, self.ones_f], writes=[t["gones"]])
                        yield
                        p2 = k.psum()
                        k.mm(p2, p2[:, 0:128], t["gones"][:], M[:, mcs, :], True, True, reads=[t["gones"], M])
                        yield
                        k.mm(p2, p2[:, 128:129], M[:, mcs, :], gcol, True, True, reads=[M, gg])
                        yield
                        k.op("act", lambda e: e.activation(out=t["gcb"][:], in_=p2[:, 0:128], func=AF.Copy), reads=[p2], writes=[t["gcb"]])
                        yield
                        k.op("dve", lambda e: e.tensor_copy(out=sc[:, 0:1], in_=p2[:, 128:129]), reads=[p2], writes=[sc])
                        yield
                        yield
                        k.op("dve", lambda e: e.tensor_scalar(out=t["X"][:], in0=t["gcb"][:], scalar1=sc[:, 0:1], scalar2=None, op0=ALU.subtract),
                             reads=[t["gcb"], sc], writes=[t["X"]])
                        yield
                        k.op("dve", lambda e: e.tensor_tensor(out=t["DT"][:], in0=t["X"][:], in1=AM[:, m_dt_incl, :], op=ALU.add), reads=[t["X"], AM], writes=[t["DT"]])
                        yield
                        k.op("act", lambda e: e.activation(out=t["DT"][:], in_=t["DT"][:], func=AF.Exp), reads=[t["DT"]], writes=[t["DT"]])
                        yield
                        k.op("dve", lambda e: e.scalar_tensor_tensor(out=t["Dm"][:], in0=t["X"][:], scalar=-1.0, in1=AM[:, m_d_strict, :], op0=ALU.mult, op1=ALU.add),
                             reads=[t["X"], AM], writes=[t["Dm"]])
                        yield
                        k.op("act", lambda e: e.activation(out=t["Dm"][:], in_=t["Dm"][:], func=AF.Exp), reads=[t["Dm"]], writes=[t["Dm"]])
                        yield
                        yield
                        k.op("act", lambda e: e.activation(out=t["E"][:], in_=t["gcb"][:], func=AF.Exp), reads=[t["gcb"]], writes=[t["E"]])
                        yield
                        k.op("act", lambda e: e.activation(out=sc[:, 1:2], in_=sc[:, 0:1], func=AF.Exp), reads=[sc], writes=[sc])
                        yield
                        k.op("dve", lambda e: e.tensor_tensor(out=sc[:, 1:2], in0=sc[:, 1:2], in1=bcol, op=ALU.mult), reads=[sc, bt], writes=[sc])
                        yield
                        k.op("dve", lambda e: e.tensor_tensor(out=sc[:, 2:3], in0=t["gcb"][:, last:last + 1], in1=sc[:, 0:1], op=ALU.subtract),
                             reads=[t["gcb"], sc], writes=[sc])
                        yield
                        k.op("act", lambda e: e.activation(out=sc[:, 2:3], in_=sc[:, 2:3], func=AF.Exp), reads=[sc], writes=[sc])
                        yield
                        yield
                        k.op("dve", lambda e: e.scalar_tensor_tensor(out=t["N"][:], in0=p1[:, 0:128], scalar=nbt[:, ci, col:col + 1], in1=t["Dm"][:], op0=ALU.mult, op1=ALU.mult),
                             reads=[p1, nbt, t["Dm"]], writes=[t["N"]])
                        yield
                        k.op("dve", lambda e: e.tensor_tensor(out=t["AT"][:], in0=p1[:, 128:256], in1=t["DT"][:], op=ALU.mult), reads=[p1, t["DT"]], writes=[t["AT"]])
                        yield
                        yield
                        k.op("dve", lambda e: e.tensor_scalar(out=t["kbg"][:], in0=p1[:, 256:384], scalar1=sc[:, 1:2], scalar2=None, op0=ALU.mult), reads=[p1, sc], writes=[t["kbg"]])
                        yield
                        k.op("dve", lambda e: e.tensor_scalar(out=t["kdec"][:], in0=p1[:, 256:384], scalar1=sc[:, 2:3], scalar2=None, op0=ALU.mult), reads=[p1, sc], writes=[t["kdec"]])
                        yield
                        k.op("dve", lambda e: e.tensor_scalar(out=t["vb"][:], in0=p1[:, 384:512], scalar1=bcol, scalar2=None, op0=ALU.mult), reads=[p1, bt], writes=[t["vb"]])
                        yield
                        k.op("pool", lambda e: e.tensor_tensor(out=t["qdec"][:], in0=qT_c, in1=t["E"][:], op=ALU.mult), reads=[acc.sub(0), t["E"]], writes=[t["qdec"]])
                        yield
                        if GSTOP <= 4:
                            continue
                        yield
                        p3 = k.psum()
                        k.tr(p3, p3[:, 0:128], t["N"][:], self.ident, reads=[t["N"], M])
                        yield
                        k.op("act", lambda e: e.activation(out=t["NT"][:], in_=p3[:, 0:128], func=AF.Copy), reads=[p3], writes=[t["NT"]])
                        yield
                        yield
                        k.op("dve", lambda e: e.tensor_tensor(out=t["Wm"][:], in0=t["NT"][:], in1=self.ident, op=ALU.add), reads=[t["NT"], M], writes=[t["Wm"]])
                        yield
                        Bc, BcT, Wc = t["NT"], t["N"], t["Wm"]
                        nxt = [(t["Bm"], t["BmT"], t["W2"]), (t["B2"], t["B2T"], t["Wm"])]
                        for lev in range(1, 7):
                            Bn, BnT, Wn = nxt[(lev - 1) % 2]
                            pq = k.psum()
                            k.mm(pq, pq[:, 0:128], Bc[:], BcT[:], True, True, reads=[Bc, BcT])
                            yield
                            if lev < 6:
                                k.mm(pq, pq[:, 128:256], BcT[:], Bc[:], True, True, reads=[Bc, BcT])
                                yield
                            k.op("act", lambda e, pq=pq, BnT=BnT: e.activation(out=BnT[:], in_=pq[:, 0:128], func=AF.Copy), reads=[pq], writes=[BnT])
                            yield
                            if lev < 6:
                                k.op("dve", lambda e, pq=pq, Bn=Bn: e.tensor_copy(out=Bn[:], in_=pq[:, 128:256]), reads=[pq], writes=[Bn])
                                yield
                            yield
                            pw = k.psum()
                            k.mm(pw, pw[:, 0:128], BnT[:], Wc[:], True, True, reads=[BnT, Wc])
                            yield
                            k.op("dve", lambda e, pw=pw, Wn=Wn, Wc=Wc: e.tensor_tensor(out=Wn[:], in0=pw[:, 0:128], in1=Wc[:], op=ALU.add), reads=[pw, Wc], writes=[Wn])
                            yield
                            Bc, BcT, Wc = Bn, BnT, Wn
                            yield
                        TT = Wc
                        yield
                        p4 = k.psum()
                        k.mm(p4, p4[:, 0:128], TT[:], t["vb"][:], True, True, reads=[TT, t["vb"]])
                        yield
                        k.mm(p4, p4[:, 128:256], t["kbg"][:], TT[:], True, True, reads=[TT, t["kbg"]])
                        yield
                        k.op("act", lambda e: e.activation(out=t["u"][:], in_=p4[:, 0:128], func=AF.Copy), reads=[p4], writes=[t["u"]])
                        yield
                        k.op("dve", lambda e: e.tensor_copy(out=t["wkT"][:], in_=p4[:, 128:256]), reads=[p4], writes=[t["wkT"]])
                        yield
                        if GSTOP <= 5:
                            continue
                        yield
                        p5 = k.psum()
                        k.mm(p5, p5[:, 0:128], t["wkT"][:], S[:], True, True, reads=[t["wkT"], S])
                        yield
                        k.op("dve", lambda e: e.tensor_tensor(out=t["vnew"][:], in0=t["u"][:], in1=p5[:, 0:128], op=ALU.subtract), reads=[p5, t["u"]], writes=[t["vnew"]])
                        yield
                        k.mm(p5, p5[:, 128:256], t["qdec"][:], t["Sb"][:], True, False, reads=[t["qdec"], t["Sb"]])
                        yield
                        k.mm(p5, p5[:, 128:256], t["AT"][:], t["vnew"][:], False, True, reads=[t["AT"], t["vnew"]])
                        yield
                        k.mm(p5, p5[:, 256:384], t["kdec"][:], t["vnew"][:], True, True, reads=[t["kdec"], t["vnew"]])
                        yield
                        k.op("act", lambda e, ci=ci: e.activation(out=osum[:, ci, :], in_=p5[:, 128:256], func=AF.Copy), reads=[p5], writes=[osum.sub(ci)])
                        yield
                        k.op("dve", lambda e: e.scalar_tensor_tensor(out=S[:], in0=S[:], scalar=t["E"][:, last:last + 1], in1=p5[:, 256:384], op0=ALU.mult, op1=ALU.add),
                             reads=[p5, S, t["E"]], writes=[S])
                        yield
                        k.op("act", lambda e: e.activation(out=t["Sb"][:], in_=S[:], func=AF.Copy), reads=[S], writes=[t["Sb"]])
                        yield
                    if not smp:
                        k.dma("sp", self.O["n_st"][seq_i, l, d, h], S[:], reads=[S], writes=[])
                        yield
                chains = []
                for d in range(2):
                    chains.append([chain(d, tqs[d], scs[d], Ss[d], osums[d]), [4 * d, 4 * d + 1, 4 * d + 2, 4 * d + 3], 0])
                save_rot, save_next = k.ps_rot, k.psnext
                while chains:
                    for cobj in list(chains):
                        k.ps_rot, k.psnext = cobj[1], cobj[2]
                        try:
                            next(cobj[0])
                        except StopIteration:
                            chains.remove(cobj)
                        cobj[2] = k.psnext
                k.ps_rot, k.psnext = save_rot, save_next
                if GSTOP <= 6:
                    continue
                for ci in range(nch):
                    a0 = s0 + ci * 128
                    ps = k.psum()
                    for kc in range(8):
                        k.mm(ps, ps[:, 0:128], hT[:, kc, a0:a0 + 128], w[:, kc, 384:512], kc == 0, kc == 7, reads=[w, hT.sub((a0 // 512) * 512)])
                    t = tqs[ci % 2]
                    sc = scs[ci % 2]
                    k.op("act", lambda e, ps=ps: e.activation(out=t["gz"][:], in_=ps[:, 0:128], func=AF.Silu), reads=[ps], writes=[t["gz"]])
                    k.op("pool", lambda e, ci=ci: e.tensor_tensor(out=osum[:, ci, :], in0=osum[:, ci, :], in1=osums[1][:, ci, :], op=ALU.add),
                         reads=[osum.sub(ci), osums[1].sub(ci)], writes=[osum.sub(ci)])
                    k.op("act", lambda e, ci=ci: e.activation(out=t["on"][:], in_=osum[:, ci, :], func=AF.Square, accum_out=sc[:, 3:4]), reads=[osum.sub(ci)], writes=[t["on"], sc])
                    k.op("act", lambda e: e.activation(out=sc[:, 4:5], in_=sc[:, 3:4], func=AF.Sqrt, bias=self.epsc[:, 0:1], scale=1.0 / 128), reads=[sc, self.epsc], writes=[sc])
                    k.op("dve", lambda e: e.reciprocal(out=sc[:, 4:5], in_=sc[:, 4:5]), reads=[sc], writes=[sc])
                    k.op("dve", lambda e, ci=ci: e.scalar_tensor_tensor(out=t["on"][:], in0=osum[:, ci, :], scalar=sc[:, 4:5], in1=self.rowv(l, "gnorm"), op0=ALU.mult, op1=ALU.mult),
                         reads=[osum.sub(ci), sc, self.rowbc], writes=[t["on"]])
                    k.op("pool", lambda e: e.tensor_tensor(out=t["on"][:], in0=t["on"][:], in1=t["gz"][:], op=ALU.mult), reads=[t["on"], t["gz"]], writes=[t["on"]])
                    p6 = k.psum()
                    k.tr(p6, p6[:, 0:128], t["on"][:], self.ident, reads=[t["on"], M])
                    k.op("act", lambda e, p6=p6, ci=ci, h=h: e.activation(out=mixT[:, h, ci * 128:(ci + 1) * 128], in_=p6[:, 0:128], func=AF.Copy), reads=[p6], writes=[mixT.sub(h)])
            if GSTOP > 6:
                for (g0, n) in groups:
                    self.apply_out(l, path, s0 + g0, n, (lambda c: mixT[:, c, g0:g0 + n], [mixT]), 4, wout, 0)
        k.end_phase()


_CACHE = {}


def _get_program():
    if "nc" not in _CACHE:
        b = Builder()
        _CACHE["nc"] = b.build()
        _CACHE["b"] = b
    return _CACHE["nc"], _CACHE["b"]


def make_in_maps(inp):
    f = lambda a: np.ascontiguousarray(np.asarray(a, dtype=np.float32))
    hc = host_consts()
    shared = {}
    for nm in ["w_ada", "ffn1_w1", "ffn1_w2", "w_in", "mla_w_qb", "mla_w_kvb", "w_out", "ffn2_w1", "ffn2_w2"]:
        shared[nm] = f(inp[nm])
    shared["b_ada"] = f(inp["b_ada"]).reshape(L, 72, 128)
    shared["convw"] = f(inp["gdn_conv_w"]).reshape(L * 5 * 12, 128)
    rowv = np.zeros((L, 304), np.float32)
    rowv[:, 0:8] = f(inp["swa_sink"])
    rowv[:, 8:16] = f(inp["gdn_a_log"]).reshape(L, 8)
    rowv[:, 16:24] = f(inp["gdn_dt_bias"]).reshape(L, 8)
    rowv[:, 24:152] = f(inp["gdn_norm"])
    shared["rowv"] = rowv.reshape(1, L * 304)
    shared.update(hc)
    maps = []
    for c in range(8):
        b = c // 4
        m = dict(shared)
        m["xp"] = f(inp["x_prompt"][2 * c:2 * c + 2]).reshape(NPT, D)
        m["xs"] = f(inp["x_sample"][b])
        m["c_ckv"] = f(inp["cache_mla_ckv"][b])
        m["c_kr"] = f(inp["cache_mla_krope"][b])
        m["c_sk"] = f(inp["cache_swa_k"][b]).reshape(L, PAST, 128)
        m["c_sv"] = f(inp["cache_swa_v"][b]).reshape(L, PAST, 128)
        m["c_st"] = f(inp["state_gdn"][b])
        vecA = np.zeros((82, 128), np.float32)
        vecA[0:16] = f(inp["norm_ffn1"]).reshape(16, 128)
        vecA[16:32] = f(inp["norm_mix"]).reshape(16, 128)
        vecA[32:48] = f(inp["norm_ffn2"]).reshape(16, 128)
        vecA[48:56] = f(inp["final_norm"]).reshape(8, 128)
        vecA[56:64] = f(inp["c_ctx"]).reshape(8, 128)
        vecA[64:72] = f(inp["c"][b]).reshape(8, 128)
        vecA[72:78] = f(inp["mla_q_norm"]).reshape(6, 128)
        vecA[78:82] = f(inp["mla_kv_norm"]).reshape(4, 128)
        m["vecA"] = vecA
        maps.append(m)
    return maps


def kernel(**inp):
    nc, b = _get_program()
    maps = make_in_maps(inp)
    res = run_bass_kernel_spmd(nc, maps, core_ids=list(range(8)))
    R = res.results
    yp = np.concatenate([R[c]["yp"].reshape(2, 256, D) for c in range(8)], 0)
    ys = np.stack([R[0]["ys"], R[4]["ys"]], 0)
    n_ckv = np.concatenate([R[c]["n_ckv"] for c in range(8)], 0)
    n_kr = np.concatenate([R[c]["n_kr"] for c in range(8)], 0)
    n_sk = np.concatenate([R[c]["n_sk"] for c in range(8)], 0).reshape(16, L, 256, 2, 64)
    n_sv = np.concatenate([R[c]["n_sv"] for c in range(8)], 0).reshape(16, L, 256, 2, 64)
    n_st = np.concatenate([R[c]["n_st"] for c in range(8)], 0)
    return (yp.astype(np.float32), ys.astype(np.float32), n_ckv.astype(np.float32), n_kr.astype(np.float32),
            n_sk.astype(np.float32), n_sv.astype(np.float32), n_st.astype(np.float32))
```

```python
import contextlib
import numpy as np
import concourse.bass as bass
import concourse.mybir as mybir
from concourse.bass_utils import run_bass_kernel_spmd

F32 = mybir.dt.float32
BF16 = mybir.dt.bfloat16
AF = mybir.ActivationFunctionType
ALU = mybir.AluOpType

D = 1024
DFF = 2816
L = 2
NPT = 512
NST = 2048
PAST = 512
EPS = 1e-6
DEBUG = []
STOP = None
SKIP = None
GSTOP = 99
FUSE_WAIT = True


class Buf:
    __slots__ = ("name", "w", "r", "parent", "kids")

    def __init__(self, name, parent=None):
        self.name = name
        self.w = None
        self.r = []
        self.parent = parent
        self.kids = {}

    def sub(self, key):
        if key not in self.kids:
            self.kids[key] = Buf(f"{self.name}.{key}", self)
        return self.kids[key]

    def family(self):
        out = [self]
        p = self.parent
        while p is not None:
            out.append(p)
            p = p.parent
        st = list(self.kids.values())
        while st:
            q = st.pop()
            out.append(q)
            st.extend(q.kids.values())
        return out


class T:
    def __init__(self, t, name):
        self.t = t
        self.b = Buf(name)

    def __getitem__(self, key):
        return self.t[key]

    def sub(self, key):
        return self.b.sub(key)


def _b(x):
    return x.b if isinstance(x, T) else x


class K:
    SAME_ENGINE_SYNC = True

    def __init__(self, n_dma_sems=64):
        self.nc = bass.Bass("TRN2", target_bir_lowering=False)
        nc = self.nc
        self.es = contextlib.ExitStack()
        self.engs = {"pe": nc.tensor, "act": nc.scalar, "dve": nc.vector, "pool": nc.gpsimd, "sp": nc.sync}
        self.sems = {}
        self.cnt = {}
        for e in self.engs:
            self.sems[e] = self.es.enter_context(nc.semaphore(f"prog_{e}"))
            self.cnt[e] = 0
        self.dsems = [self.es.enter_context(nc.semaphore(f"dma_{i}")) for i in range(n_dma_sems)]
        self.dval = [0] * n_dma_sems
        self.dnext = 0
        self.dnext2 = 0
        self.semobj = dict(self.sems)
        for i, s in enumerate(self.dsems):
            self.semobj[("d", i)] = s
        self.waited = {}
        self.n_wait = 0
        self.n_ins = 0
        self.phase_es = None
        self.psums = []
        self.psnext = 0
        self.ps_rot = list(range(8))

    def sb(self, name, shape, dtype=F32, glob=False):
        es = self.es if (glob or self.phase_es is None) else self.phase_es
        self.n_alloc = getattr(self, "n_alloc", 0) + 1
        name = f"sb{self.n_alloc}_{name}"
        t = es.enter_context(self.nc.sbuf_tensor(name, list(shape), dtype))
        return T(t, name)

    def dram(self, name, shape, dtype=F32, kind="Internal"):
        return T(self.nc.dram_tensor(name, list(shape), dtype, kind=kind), name)

    def init_psum(self):
        for i in range(8):
            t = self.es.enter_context(self.nc.psum_tensor(f"psb{i}", [128, 512], F32))
            self.psums.append(T(t, f"psb{i}"))

    def psum(self):
        rot = self.ps_rot
        self.psnext = (self.psnext + 1) % len(rot)
        return self.psums[rot[self.psnext]]

    def psum_hold(self, n):
        held = self.ps_rot[-n:]
        self.ps_rot = self.ps_rot[:-n]
        self.psnext = 0
        return [self.psums[i] for i in held]

    def psum_release(self, banks):
        self.ps_rot = self.ps_rot + [self.psums.index(b) for b in banks]

    def begin_phase(self):
        assert self.phase_es is None
        self.phase_es = contextlib.ExitStack()

    def end_phase(self):
        self.barrier()
        self.phase_es.close()
        self.phase_es = None

    def _deps(self, reads, writes, e=None):
        deps = {}

        def add(d):
            if d is not None and deps.get(d[0], 0) < d[1]:
                deps[d[0]] = d[1]
        for b in reads:
            bb = _b(b)
            for f in bb.family():
                add(f.w)
            if bb.name.startswith("psb"):
                for f in bb.family():
                    for r in f.r:
                        if r[0] != e:
                            add(r)
        for b in writes:
            for f in _b(b).family():
                add(f.w)
                for r in f.r:
                    add(r)
        return deps

    def _emit_waits(self, e, deps, keep_one=False):
        eng = self.engs[e]
        need = []
        for kk, v in deps.items():
            if kk == e and (e == "pe" or not self.SAME_ENGINE_SYNC):
                continue
            if self.waited.get((e, kk), 0) >= v:
                continue
            need.append((kk, v))
            self.waited[(e, kk)] = v
        fused = None
        if keep_one and FUSE_WAIT and need:
            fused = need.pop()
        for kk, v in need:
            eng.wait_ge(self.semobj[kk], v)
            self.n_wait += 1
        return fused

    def _record(self, tag, reads, writes):
        for b in reads:
            _b(b).r.append(tag)
        for b in writes:
            bb = _b(b)
            bb.w = tag
            bb.r = []

    def op(self, e, fn, reads=(), writes=()):
        deps = self._deps(reads, writes, e)
        fused = self._emit_waits(e, deps, keep_one=True)
        ins = fn(self.engs[e])
        if fused is not None:
            ins._wait_ge(self.semobj[fused[0]], fused[1])
        self.cnt[e] += 1
        ins.then_inc(self.sems[e], 1)
        self._record((e, self.cnt[e]), reads, writes)
        self.n_ins += 1
        return ins

    def dma(self, q, out, in_, reads=(), writes=(), **kw):
        deps = self._deps(reads, writes)
        half = len(self.dsems) // 2
        if q == "sp":
            i = self.dnext % half
            self.dnext += 1
        else:
            i = half + self.dnext2 % half
            self.dnext2 += 1
        key = ("d", i)
        if self.dval[i] > 0:
            deps[key] = max(deps.get(key, 0), self.dval[i])
        self._emit_waits(q, deps)
        ins = self.engs[q].dma_start(out=out, in_=in_, **kw)
        self.dval[i] += 16
        ins.then_inc(self.dsems[i], 16)
        self._record((key, self.dval[i]), reads, writes)
        self.n_ins += 1
        return ins

    def barrier(self):
        alld = {e: self.cnt[e] for e in self.engs if self.cnt[e] > 0}
        for i, v in enumerate(self.dval):
            if v > 0:
                alld[("d", i)] = v
        for e in self.engs:
            self._emit_waits(e, dict(alld))

    def mm(self, ps, out, lhsT, rhs, start, stop, reads):
        self.op("pe", lambda e: e.matmul(out, lhsT=lhsT, rhs=rhs, start=start, stop=stop), reads=reads, writes=[ps])

    def tr(self, ps, out, in_, ident, reads):
        self.op("pe", lambda e: e.transpose(out, in_, ident), reads=reads, writes=[ps])


def _rope_tables(n_tokens, dim):
    rows = n_tokens // 64
    row = np.repeat(np.arange(rows, dtype=np.float32), 64)
    col = np.tile(np.arange(64, dtype=np.float32), rows)
    axis_dim = dim // 2
    inv = (1.0 / (10000.0 ** (np.arange(0, axis_dim, 2, dtype=np.float32) / axis_dim))).astype(np.float32)
    ang_r = row[:, None] * inv[None, :]
    ang_c = col[:, None] * inv[None, :]
    ang = np.concatenate([ang_r, ang_r, ang_c, ang_c], axis=-1)
    return np.cos(ang).astype(np.float32), np.sin(ang).astype(np.float32)


def host_consts():
    c = {}
    p = np.arange(128)[:, None]
    f = np.arange(128)[None, :]
    masks = np.zeros((128, 5, 128), np.float32)
    masks[:, 0, :] = (p == f)
    masks[:, 1, :] = (p <= f)
    masks[:, 2, :] = (p < f)
    masks[:, 3, :] = (p >= f)
    masks[:, 4, :] = (p > f)
    amasks = np.zeros((128, 5, 128), np.float32)
    for i in range(1, 5):
        amasks[:, i, :] = (masks[:, i, :] - 1.0) * 1.0e9
    c["amasks"] = amasks
    c["masks"] = masks
    cm, sm = _rope_tables(NST, 32)
    cosm = np.ones((96, NST), np.float32)
    sinm = np.zeros((96, NST), np.float32)
    cosm[64:96] = cm.T
    sinm[64:96] = sm.T
    cs, ss = _rope_tables(NST, 64)
    c["rope_m"] = np.stack([cosm, sinm], 1).copy()
    c["rope_s"] = np.stack([cs.T, ss.T], 1).copy()
    return c


def n_is_k(ap, self):
    return ap.tensor.name == self.sm_k.t.name


class Builder:
    def __init__(self):
        self.k = K()
        k = self.k
        self.nc = k.nc
        self.ins = {}
        self.outs = {}
        k.init_psum()

    def din(self, name, shape):
        t = self.k.dram(name, shape, F32, kind="ExternalInput")
        self.ins[name] = t
        return t

    def dout(self, name, shape):
        t = self.k.dram(name, shape, F32, kind="ExternalOutput")
        self.outs[name] = t
        return t

    def build(self):
        k = self.k
        W = {}
        for nm, shp in [("xp", [NPT, D]), ("xs", [NST, D]), ("c_ckv", [L, PAST, 256]), ("c_kr", [L, PAST, 32]),
                        ("c_sk", [L, PAST, 128]), ("c_sv", [L, PAST, 128]), ("c_st", [L, 2, 4, 128, 128]),
                        ("vecA", [82, 128]), ("b_ada", [L, 72, 128]), ("convw", [120, 128]), ("rowv", [1, 304 * L]),
                        ("w_ada", [L, D, 9 * D]), ("ffn1_w1", [L, D, 2 * DFF]), ("ffn1_w2", [L, DFF, D]),
                        ("w_in", [L, D, 3504]), ("mla_w_qb", [L, 384, 768]), ("mla_w_kvb", [L, 256, 1024]),
                        ("w_out", [L, 1536, D]), ("ffn2_w1", [L, D, 2 * DFF]), ("ffn2_w2", [L, DFF, D]),
                        ("masks", [128, 5, 128]), ("amasks", [128, 5, 128]), ("rope_m", [96, 2, NST]), ("rope_s", [64, 2, NST])]:
            W[nm] = self.din(nm, shp)
        self.W = W
        O = {}
        O["yp"] = self.dout("yp", [NPT, D])
        O["ys"] = self.dout("ys", [NST, D])
        O["n_ckv"] = self.dout("n_ckv", [2, L, 256, 256])
        O["n_kr"] = self.dout("n_kr", [2, L, 256, 32])
        O["n_sk"] = self.dout("n_sk", [2, L, 256, 128])
        O["n_sv"] = self.dout("n_sv", [2, L, 256, 128])
        O["n_st"] = self.dout("n_st", [2, L, 2, 4, 128, 128])
        self.O = O
        self.dbg = {}

        self.xT = {"p": k.dram("xTp", [128, 8, NPT]), "s": k.dram("xTs", [128, 8, NST])}
        self.ntok = {"p": NPT, "s": NST}
        self.blocks = [("p", 0)] + [("s", i * 512) for i in range(4)]

        self.masks = k.sb("masks", [128, 5, 128], F32, glob=True)
        k.dma("sp", self.masks[:], W["masks"][:], writes=[self.masks])
        self.ident = self.masks.t[:, 0, :]
        self.ones_f = k.sb("ones_f", [128, 128], F32, glob=True)
        self.ones_b = k.sb("ones_b", [128, 128], BF16, glob=True)
        k.op("dve", lambda e: e.memset(self.ones_f[:], 1.0), writes=[self.ones_f])
        k.op("dve", lambda e: e.memset(self.ones_b[:], 1.0), writes=[self.ones_b])
        self.epsc = k.sb("epsc", [128, 1], F32, glob=True)
        k.op("dve", lambda e: e.memset(self.epsc[:], EPS), writes=[self.epsc])
        self.masks_b = k.sb("masks_b", [128, 5, 128], BF16, glob=True)
        k.op("dve", lambda e: e.tensor_copy(out=self.masks_b[:], in_=self.masks[:]), reads=[self.masks], writes=[self.masks_b])
        self.vecT = k.sb("vecT", [128, 82], F32, glob=True)
        self.badaT = k.sb("badaT", [128, L, 72], F32, glob=True)
        self.convT = k.sb("convT", [128, 120], F32, glob=True)
        self.rowbc = k.sb("rowbc", [128, 304 * L], F32, glob=True)
        self.modT = k.sb("modT", [128, L, 72, 2], F32, glob=True)
        self.modA = k.sb("modA", [128, L, 2, 3, 8], F32, glob=True)
        self.modG = k.sb("modG", [128, L, 2, 3, 8], F32, glob=True)
        self.hT = {"p": k.sb("hTp", [128, 8, NPT], BF16, glob=True), "s": k.sb("hTs", [128, 8, NST], BF16, glob=True)}
        stages = [("setup", self.setup_small), ("adaln", self.adaln), ("load", self.load_inputs)]
        for l in range(L):
            stages.append((f"ffn1_{l}", lambda l=l: self.ffn(l, 0)))
            for path in ("p", "s"):
                stages.append((f"mla_{l}{path}", lambda l=l, path=path: self.mla(l, path)))
                stages.append((f"swa_{l}{path}", lambda l=l, path=path: self.swa(l, path)))
                stages.append((f"gdn_{l}{path}", lambda l=l, path=path: self.gdn(l, path)))
            stages.append((f"ffn2_{l}", lambda l=l: self.ffn(l, 2)))
        stages.append(("final", self.final))
        for nm, fn in stages:
            if SKIP and nm in SKIP:
                continue
            fn()
            if STOP and nm == STOP:
                break
        if DEBUG:
            k.barrier()
            for nm in DEBUG:
                if nm in ("xTp", "xTs"):
                    src = self.xT[nm[-1]]
                    o = self.dout("dbg_" + nm, [128, 8, self.ntok[nm[-1]]])
                    for t0 in range(0, self.ntok[nm[-1]], 512):
                        k.dma("sp", o[:, :, t0:t0 + 512], src[:, :, t0:t0 + 512], reads=[src], writes=[o])
                elif nm in ("hTp", "hTs"):
                    src = self.hT[nm[-1]]
                    o = T(self.nc.dram_tensor("dbg_" + nm, [128, 8, self.ntok[nm[-1]]], BF16, kind="ExternalOutput"), "dbg_" + nm)
                    k.dma("sp", o[:, :, :], src[:, :, :], reads=[src], writes=[o])
                elif nm == "modT":
                    o = self.dout("dbg_modT", [128, L * 72 * 2])
                    k.dma("sp", o[:, :], self.modT[:, :, :, :].rearrange("p l c r -> p (l c r)"), reads=[self.modT], writes=[o])
        k.barrier()
        return self.nc

    def setup_small(self):
        k = self.k
        W = self.W
        k.begin_phase()
        stage = k.sb("stage", [128, 128], F32)
        for (src, nrow, dst) in [(W["vecA"][:, :], 82, self.vecT[:, :]), (W["b_ada"][0], 72, self.badaT[:, 0, :]),
                                 (W["b_ada"][1], 72, self.badaT[:, 1, :]), (W["convw"][:, :], 120, self.convT[:, :])]:
            k.dma("sp", stage[0:nrow, :], src, writes=[stage])
            ps = k.psum()
            k.tr(ps, ps[:, 0:nrow], stage[0:nrow, :], self.ident[0:nrow, 0:nrow], reads=[stage, self.masks])
            k.op("dve", lambda e, dst=dst, ps=ps, nrow=nrow: e.tensor_copy(out=dst, in_=ps[:, 0:nrow]), reads=[ps],
                 writes=[self.vecT, self.badaT, self.convT])
        row = k.sb("rowstage", [1, 304 * L], F32)
        k.dma("sp", row[:], W["rowv"][:, :], writes=[row])
        for i in range(0, 304 * L, 512):
            n = min(512, 304 * L - i)
            ps = k.psum()
            k.mm(ps, ps[:, 0:n], self.ones_f[0:1, :], row[0:1, i:i + n], True, True, reads=[self.ones_f, row])
            k.op("dve", lambda e, ps=ps, i=i, n=n: e.tensor_copy(out=self.rowbc[:, i:i + n], in_=ps[:, 0:n]), reads=[ps], writes=[self.rowbc])
        k.end_phase()

    def rowv(self, l, what):
        o = {"sink": (0, 8), "alog": (8, 16), "dtb": (16, 24), "gnorm": (24, 152)}[what]
        return self.rowbc.t[:, l * 304 + o[0]: l * 304 + o[1]]

    def adaln(self):
        k = self.k
        W = self.W
        k.begin_phase()
        condT = k.sb("condT", [128, 8, 2], BF16)
        tmp = k.sb("condtmp", [128, 16], F32)
        k.op("act", lambda e: e.activation(out=tmp[:], in_=self.vecT[:, 56:72], func=AF.Silu), reads=[self.vecT], writes=[tmp])
        for r in range(2):
            k.op("dve", lambda e, r=r: e.tensor_copy(out=condT[:, :, r], in_=tmp[:, r * 8:(r + 1) * 8]), reads=[tmp], writes=[condT])
        wb = [k.sb(f"wada{i}", [128, 8, 512], BF16) for i in range(2)]
        for l in range(L):
            psm = k.psum()
            for g in range(18):
                w = wb[g % 2]
                k.dma("pool", w[:], W["w_ada"][l].rearrange("(kc p) n -> p kc n", p=128)[:, :, g * 512:(g + 1) * 512], writes=[w])
                for j in range(4):
                    cc = g * 4 + j
                    for kc in range(8):
                        k.mm(psm, psm[:, cc * 2:cc * 2 + 2], w[:, kc, j * 128:(j + 1) * 128], condT[:, kc, :], kc == 0, kc == 7,
                             reads=[w, condT])
            for r in range(2):
                k.op("dve", lambda e, l=l, r=r, psm=psm: e.tensor_tensor(
                    out=self.modT[:, l, :, r], in0=psm[:, 0:144].rearrange("p (c r) -> p c r", r=2)[:, :, r],
                    in1=self.badaT[:, l, :], op=ALU.add), reads=[psm, self.badaT], writes=[self.modT])
        for l in range(L):
            for r in range(2):
                for s in range(3):
                    gain = self.vecT[:, [0, 16, 32][s] + l * 8:[0, 16, 32][s] + l * 8 + 8]
                    sc = self.modT[:, l, (3 * s + 1) * 8:(3 * s + 2) * 8, r]
                    k.op("dve", lambda e, l=l, r=r, s=s, gain=gain, sc=sc: e.scalar_tensor_tensor(
                        out=self.modA[:, l, r, s, :], in0=sc, scalar=1.0, in1=gain, op0=ALU.add, op1=ALU.mult),
                        reads=[self.modT, self.vecT], writes=[self.modA])
                    gt = self.modT[:, l, (3 * s + 2) * 8:(3 * s + 3) * 8, r]
                    k.op("dve", lambda e, l=l, r=r, s=s, gt=gt: e.tensor_scalar(
                        out=self.modG[:, l, r, s, :], in0=gt, scalar1=(1.0 if s == 1 else 0.5), scalar2=None, op0=ALU.mult),
                        reads=[self.modT], writes=[self.modG])
        k.end_phase()

    def mod_shift(self, l, r, s, kc):
        c = (3 * s) * 8 + kc
        return self.modT.t[:, l, c:c + 1, r]

    def load_inputs(self):
        k = self.k
        k.begin_phase()
        xin = [k.sb(f"xin{i}", [128, D], F32) for i in range(2)]
        xblk = [k.sb(f"xblk{i}", [128, 8, 512], F32) for i in range(2)]
        bi = 0
        for (path, t0) in self.blocks:
            src = self.W["xp"] if path == "p" else self.W["xs"]
            xb = xblk[bi % 2]
            for tt in range(4):
                xi = xin[tt % 2]
                k.dma("sp", xi[:], src[t0 + tt * 128:t0 + (tt + 1) * 128, :], writes=[xi])
                for half in range(2):
                    ps = k.psum()
                    for j in range(4):
                        kc = half * 4 + j
                        k.tr(ps, ps[:, j * 128:(j + 1) * 128], xi[:, kc * 128:(kc + 1) * 128], self.ident, reads=[xi, self.masks])
                    eng = "dve" if half == 0 else "act"
                    if eng == "dve":
                        k.op("dve", lambda e, xb=xb, ps=ps, half=half, tt=tt: e.tensor_copy(
                            out=xb[:, half * 4:half * 4 + 4, tt * 128:(tt + 1) * 128],
                            in_=ps[:, :].rearrange("p (j t) -> p j t", j=4)), reads=[ps], writes=[xb])
                    else:
                        k.op("act", lambda e, xb=xb, ps=ps, half=half, tt=tt: e.activation(
                            out=xb[:, half * 4:half * 4 + 4, tt * 128:(tt + 1) * 128],
                            in_=ps[:, :].rearrange("p (j t) -> p j t", j=4), func=AF.Copy), reads=[ps], writes=[xb])
            k.dma("sp", self.xT[path][:, :, t0:t0 + 512], xb[:], reads=[xb], writes=[self.xT[path].sub(t0)])
            bi += 1
        k.end_phase()

    def rmsnorm(self, x, n, Acol, Bcol, out, sq, rstd, nfeat_chunks=8, nfeat=D, tmp=None):
        k = self.k
        xs, xr = x
        outf, outw = out
        for kc in range(nfeat_chunks):
            k.op("act", lambda e, kc=kc: e.activation(out=sq[:, kc, 0:n], in_=xs(kc), func=AF.Square), reads=xr, writes=[sq])
        ps = k.psum()
        for kc in range(nfeat_chunks):
            k.mm(ps, ps[:, 0:n], self.ones_b[:, :], sq[:, kc, 0:n], kc == 0, kc == nfeat_chunks - 1, reads=[self.ones_b, sq])
        k.op("act", lambda e: e.activation(out=rstd[:, 0:n], in_=ps[:, 0:n], func=AF.Sqrt, bias=self.epsc[:, 0:1], scale=1.0 / nfeat),
             reads=[ps, self.epsc], writes=[rstd])
        k.op("dve", lambda e: e.reciprocal(out=rstd[:, 0:n], in_=rstd[:, 0:n]), reads=[rstd], writes=[rstd])
        for kc in range(nfeat_chunks):
            if Bcol is None:
                k.op("dve", lambda e, kc=kc: e.scalar_tensor_tensor(out=outf(kc), in0=xs(kc), scalar=Acol(kc), in1=rstd[:, 0:n],
                                                                     op0=ALU.mult, op1=ALU.mult), reads=list(xr) + [rstd], writes=outw)
            else:
                k.op("dve", lambda e, kc=kc: e.scalar_tensor_tensor(out=tmp[:, kc, 0:n], in0=xs(kc), scalar=Acol(kc), in1=rstd[:, 0:n],
                                                                     op0=ALU.mult, op1=ALU.mult), reads=list(xr) + [rstd], writes=[tmp])
                k.op("act", lambda e, kc=kc: e.activation(out=outf(kc), in_=tmp[:, kc, 0:n], func=AF.Identity, bias=Bcol(kc), scale=1.0),
                     reads=[tmp], writes=outw)

    def ffn(self, l, s):
        k = self.k
        W = self.W
        w1 = W["ffn1_w1" if s == 0 else "ffn2_w1"][l].rearrange("(kc p) n -> p kc n", p=128)
        w2 = W["ffn1_w2" if s == 0 else "ffn2_w2"][l].rearrange("(j p) n -> p j n", p=128)
        k.begin_phase()
        xb = k.sb("f_x", [128, 8, 512], F32)
        hb = k.sb("f_h", [128, 8, 512], BF16)
        tmp = k.sb("f_tmp", [128, 8, 512], F32)
        act = k.sb("f_act", [128, 22, 512], BF16)
        sq = k.sb("f_sq", [128, 8, 512], BF16)
        rstd = k.sb("f_rstd", [128, 512], F32)
        sg = [k.sb(f"f_sg{i}", [128, 512], F32) for i in range(2)]
        w1b = [k.sb(f"f_w1{i}", [128, 8, 512], BF16) for i in range(2)]
        w2b = [k.sb(f"f_w2{i}", [128, 22, 512], BF16) for i in range(2)]
        nw1 = 0
        nw2 = 0
        w1c = k.dram(f"w1c_{l}_{s}", [128, 11, 8, 512], BF16)
        w2c = k.dram(f"w2c_{l}_{s}", [128, 2, 22, 512], BF16)
        for bi, (path, t0) in enumerate(self.blocks):
            r = 0 if path == "p" else 1
            xsub = self.xT[path].sub(t0)
            k.dma("sp", xb[:], self.xT[path][:, :, t0:t0 + 512], reads=[xsub], writes=[xb])
            self.rmsnorm((lambda kc: xb[:, kc, :], [xb]), 512,
                         lambda kc: self.modA[:, l, r, s, kc:kc + 1], lambda kc: self.mod_shift(l, r, s, kc),
                         (lambda kc: hb[:, kc, :], [hb]), sq, rstd, tmp=tmp)
            for jg in range(11):
                wt = w1b[nw1 % 2]
                nw1 += 1
                if bi == 0:
                    k.dma("pool", wt[:, :, 0:256], w1[:, :, jg * 256:(jg + 1) * 256], writes=[wt])
                    k.dma("pool", wt[:, :, 256:512], w1[:, :, DFF + jg * 256:DFF + (jg + 1) * 256], writes=[wt])
                    k.dma("sp", w1c[:, jg], wt[:], reads=[wt], writes=[w1c.sub(jg)])
                else:
                    k.dma("sp", wt[:], w1c[:, jg], reads=[w1c.sub(jg)], writes=[wt])
                for jj in range(2):
                    j = jg * 2 + jj
                    pg = k.psum()
                    pu = k.psum()
                    for kc in range(8):
                        k.mm(pg, pg[:, :], wt[:, kc, jj * 128:(jj + 1) * 128], hb[:, kc, :], kc == 0, kc == 7, reads=[wt, hb])
                    for kc in range(8):
                        k.mm(pu, pu[:, :], wt[:, kc, 256 + jj * 128:256 + (jj + 1) * 128], hb[:, kc, :], kc == 0, kc == 7, reads=[wt, hb])
                    sgt = sg[j % 2]
                    k.op("act", lambda e, sgt=sgt, pg=pg: e.activation(out=sgt[:], in_=pg[:, :], func=AF.Silu), reads=[pg], writes=[sgt])
                    k.op("dve", lambda e, sgt=sgt, pu=pu, j=j: e.tensor_tensor(out=act[:, j, :], in0=pu[:, :], in1=sgt[:], op=ALU.mult),
                         reads=[pu, sgt], writes=[act.sub(j)])
            for half in range(2):
                wt = w2b[nw2 % 2]
                nw2 += 1
                if bi == 0:
                    for q4 in range(2):
                        k.dma("pool", wt[:, q4 * 11:(q4 + 1) * 11, :], w2[:, q4 * 11:(q4 + 1) * 11, half * 512:(half + 1) * 512], writes=[wt])
                    k.dma("sp", w2c[:, half], wt[:], reads=[wt], writes=[w2c.sub(half)])
                else:
                    k.dma("sp", wt[:], w2c[:, half], reads=[w2c.sub(half)], writes=[wt])
                for f4 in range(4):
                    fc = half * 4 + f4
                    po = k.psum()
                    for j in range(22):
                        k.mm(po, po[:, :], wt[:, j, f4 * 128:(f4 + 1) * 128], act[:, j, :], j == 0, j == 21, reads=[wt, act.sub(j)])
                    k.op("dve", lambda e, po=po, fc=fc: e.scalar_tensor_tensor(
                        out=xb[:, fc, :], in0=po[:, :], scalar=self.modG[:, l, r, s, fc:fc + 1], in1=xb[:, fc, :],
                        op0=ALU.mult, op1=ALU.add), reads=[po, xb, self.modG], writes=[xb])
            k.dma("sp", self.xT[path][:, :, t0:t0 + 512], xb[:], reads=[xb], writes=[xsub])
            if s == 0:
                hT = self.hT[path]
                self.rmsnorm((lambda kc: xb[:, kc, :], [xb]), 512,
                             lambda kc: self.modA[:, l, r, 1, kc:kc + 1], lambda kc: self.mod_shift(l, r, 1, kc),
                             (lambda kc: hT[:, kc, t0:t0 + 512], [hT.sub(t0)]), sq, rstd, tmp=tmp)
        k.end_phase()

    def final(self):
        k = self.k
        k.begin_phase()
        xb = k.sb("o_x", [128, 8, 512], F32)
        yb = k.sb("o_y", [128, 8, 512], F32)
        sq = k.sb("o_sq", [128, 8, 512], BF16)
        rstd = k.sb("o_rstd", [128, 512], F32)
        yo = [k.sb(f"o_yo{i}", [128, D], F32) for i in range(2)]
        n = 0
        for (path, t0) in self.blocks:
            k.dma("sp", xb[:], self.xT[path][:, :, t0:t0 + 512], reads=[self.xT[path].sub(t0)], writes=[xb])
            self.rmsnorm((lambda kc: xb[:, kc, :], [xb]), 512, lambda kc: self.vecT[:, 48 + kc:49 + kc], None,
                         (lambda kc: yb[:, kc, :], [yb]), sq, rstd)
            dst = self.O["yp"] if path == "p" else self.O["ys"]
            for tt in range(4):
                y = yo[n % 2]
                n += 1
                for half in range(2):
                    ps = k.psum()
                    for j in range(4):
                        kc = half * 4 + j
                        k.tr(ps, ps[:, j * 128:(j + 1) * 128], yb[:, kc, tt * 128:(tt + 1) * 128], self.ident, reads=[yb, self.masks])
                    if half == 0:
                        k.op("dve", lambda e, y=y, ps=ps: e.tensor_copy(out=y[:, 0:512], in_=ps[:, :]), reads=[ps], writes=[y])
                    else:
                        k.op("act", lambda e, y=y, ps=ps: e.activation(out=y[:, 512:1024], in_=ps[:, :], func=AF.Copy), reads=[ps], writes=[y])
                k.dma("sp", dst[t0 + tt * 128:t0 + (tt + 1) * 128, :], y[:], reads=[y], writes=[dst])
        k.end_phase()

    ao_k = 128

    def apply_out(self, l, path, t0, n, mixT, nchunks, wout_t, chunk0):
        k = self.k
        r = 0 if path == "p" else 1
        mixf, mixr = mixT
        xb = self.ao_x
        xsub = self.xT[path].sub((t0 // 512) * 512)
        k.dma("sp", xb[:, :, 0:n], self.xT[path][:, :, t0:t0 + n], reads=[xsub], writes=[xb])
        for fc in range(8):
            po = k.psum()
            for c in range(nchunks):
                k.mm(po, po[:, 0:n], wout_t[0:self.ao_k, chunk0 + c, fc * 128:(fc + 1) * 128], mixf(c), c == 0, c == nchunks - 1,
                     reads=[wout_t] + list(mixr))
            k.op("dve", lambda e, po=po, fc=fc: e.scalar_tensor_tensor(
                out=xb[:, fc, 0:n], in0=po[:, 0:n], scalar=self.modG[:, l, r, 1, fc:fc + 1], in1=xb[:, fc, 0:n],
                op0=ALU.mult, op1=ALU.add), reads=[po, xb, self.modG], writes=[xb])
        k.dma("sp", self.xT[path][:, :, t0:t0 + n], xb[:, :, 0:n], reads=[xb], writes=[xsub])

    def store_tok(self, srcf, srcr, nfeat, ntok, dst_fn, f0=0):
        k = self.k
        for tt in range(ntok // 128):
            ps = k.psum()
            k.tr(ps, ps[:, 0:nfeat], srcf(tt), self.ident[0:nfeat, 0:nfeat], reads=list(srcr) + [self.masks])
            st = self.st_buf[self.st_n % 2]
            self.st_n += 1
            k.op("act", lambda e, st=st, ps=ps: e.activation(out=st[:, 0:nfeat - f0], in_=ps[:, f0:nfeat], func=AF.Copy), reads=[ps], writes=[st])
            k.dma("sp", dst_fn(tt), st[:, 0:nfeat - f0], reads=[st], writes=[])

    def mixer(self, l, path):
        self.mla(l, path)
        self.swa(l, path)
        self.gdn(l, path)

    def seqs(self, path):
        return [(0, 256), (256, 256)] if path == "p" else [(0, NST)]

    def softmax_bound(self, q2max, k2max, scale, negc):
        k = self.k
        t = self.sm_t
        k.op("dve", lambda e: e.tensor_tensor(out=t[0:1, 0:1], in0=q2max, in1=k2max, op=ALU.add), reads=[self.sm_q, self.sm_k], writes=[t])
        k.op("dve", lambda e: e.tensor_scalar(out=t[0:1, 0:1], in0=t[0:1, 0:1], scalar1=-0.5 * scale, scalar2=None, op0=ALU.mult),
             reads=[t], writes=[t])
        ps = k.psum()
        k.mm(ps, ps[:, 0:1], self.ones_f[0:1, :], t[0:1, 0:1], True, True, reads=[self.ones_f, t])
        k.op("dve", lambda e: e.tensor_copy(out=negc[:, 0:1], in_=ps[:, 0:1]), reads=[ps], writes=[negc])

    def sq_max(self, src_ap, nrow, n, dst, first, reads, dstT=None):
        k = self.k
        sqt = self.sm_sq
        k.op("pool", lambda e: e.tensor_tensor(out=sqt[0:nrow, 0:n], in0=src_ap, in1=src_ap, op=ALU.mult), reads=reads, writes=[sqt])
        ps = k.psum()
        k.mm(ps, ps[0:1, 0:n], self.ones_b[0:nrow, 0:1], sqt[0:nrow, 0:n], True, True, reads=[self.ones_b, sqt])
        tm = self.sm_m
        if dstT is None:
            dstT = self.sm_k if n_is_k(dst, self) else self.sm_q
        if first:
            k.op("dve", lambda e: e.reduce_max(out=dst, in_=ps[0:1, 0:n], axis=mybir.AxisListType.X), reads=[ps], writes=[dstT])
        else:
            k.op("dve", lambda e: e.reduce_max(out=tm[0:1, 0:1], in_=ps[0:1, 0:n], axis=mybir.AxisListType.X), reads=[ps], writes=[tm])
            k.op("dve", lambda e: e.tensor_tensor(out=dst, in0=dst, in1=tm[0:1, 0:1], op=ALU.max), reads=[tm, dstT], writes=[dstT])

    def mla(self, l, path):
        k = self.k
        W = self.W
        smp = path == "s"
        hT = self.hT[path]
        scale = 96 ** -0.5
        k.begin_phase()
        self.ao_x = k.sb("ao_x", [128, 8, 512], F32)
        self.st_buf = [k.sb(f"st{i}", [128, 256], F32) for i in range(2)]
        self.st_n = 0
        self.sm_t = k.sb("sm_t", [1, 1], F32)
        self.sm_q = k.sb("sm_q", [1, 1], F32)
        self.sm_k = k.sb("sm_k", [1, 8], F32)
        self.sm_m = k.sb("sm_m", [1, 1], F32)
        self.sm_sq = k.sb("sm_sq", [96, 512], BF16)
        negc = k.sb("negc", [128, 1], F32)
        win = k.sb("m_win", [128, 8, 672], BF16)
        k.dma("pool", win[:], W["w_in"][l].rearrange("(kc p) n -> p kc n", p=128)[:, :, 0:672], writes=[win])
        wqb = k.sb("m_wqb", [128, 3, 768], BF16)
        k.dma("pool", wqb[:], W["mla_w_qb"][l].rearrange("(kc p) n -> p kc n", p=128), writes=[wqb])
        wkvb = k.sb("m_wkvb", [128, 2, 1024], BF16)
        k.dma("pool", wkvb[:], W["mla_w_kvb"][l].rearrange("(kc p) n -> p kc n", p=128), writes=[wkvb])
        wout = k.sb("m_wout", [64, 4, D], BF16)
        wkr = k.sb("m_wkr", [128, 8, 96], BF16)
        k.op("pool", lambda e: e.memset(wkr[:], 0.0), writes=[wkr])
        k.op("pool", lambda e: e.tensor_copy(out=wkr[:, :, 64:96], in_=win[:, :, 640:672]), reads=[win], writes=[wkr])
        if smp:
            rope = k.sb("m_rope", [96, 2, NST], F32)
            k.dma("sp", rope[:], W["rope_m"][:, :, :], writes=[rope])
            wkr_r = k.sb("m_wkr_r", [128, 8, 96], BF16)
            k.op("pool", lambda e: e.memset(wkr_r[:], 0.0), writes=[wkr_r])
            wqb_r = k.sb("m_wqb_r", [128, 3, 768], BF16)
            k.op("pool", lambda e: e.memset(wqb_r[:], 0.0), writes=[wqb_r])
            for a in range(2):
                b0 = 64 + a * 16
                k.op("pool", lambda e, b0=b0: e.tensor_scalar(out=wkr_r[:, :, b0:b0 + 8], in0=wkr[:, :, b0 + 8:b0 + 16], scalar1=-1.0,
                                                               scalar2=None, op0=ALU.mult), reads=[wkr], writes=[wkr_r])
                k.op("pool", lambda e, b0=b0: e.tensor_copy(out=wkr_r[:, :, b0 + 8:b0 + 16], in_=wkr[:, :, b0:b0 + 8]), reads=[wkr], writes=[wkr_r])
                for h in range(8):
                    c0 = h * 96 + b0
                    k.op("pool", lambda e, c0=c0: e.tensor_scalar(out=wqb_r[:, :, c0:c0 + 8], in0=wqb[:, :, c0 + 8:c0 + 16], scalar1=-1.0,
                                                                   scalar2=None, op0=ALU.mult), reads=[wqb], writes=[wqb_r])
                    k.op("pool", lambda e, c0=c0: e.tensor_copy(out=wqb_r[:, :, c0 + 8:c0 + 16], in_=wqb[:, :, c0:c0 + 8]), reads=[wqb], writes=[wqb_r])
        sq = k.sb("m_sq", [128, 3, 512], BF16)
        rstd = k.sb("m_rstd", [128, 512], F32)
        cf = k.sb("m_cf", [128, 3, 512], F32)
        krf = k.sb("m_krf", [96, 512], F32)
        krb = k.sb("m_krb", [96, 512], F32)
        qtmp, qtmp2 = krf, krb
        qTs = [k.sb(f"m_qT{i}", [96, 512], BF16) for i in range(4)]
        negcs = [k.sb(f"m_negc{i}", [128, 1], F32) for i in range(4)]
        smqs = [k.sb(f"m_smq{i}", [1, 1], F32) for i in range(4)]
        smts = [k.sb(f"m_smt{i}", [1, 1], F32) for i in range(4)]
        qT = qTs[0]
        PT = [k.sb(f"m_PT{i}", [128, 512], BF16) for i in range(3)]
        mixH = k.sb("m_mixH", [64, 4, 512], BF16)
        rrow = k.sb("m_rrow", [65, 512], F32)
        rbc = k.sb("m_rbc", [64, 512], F32)
        for (s0, T_) in self.seqs(path):
            nk = T_ + (PAST if smp else 0)
            nkt = nk // 128
            cqn = k.sb(f"m_cqn{s0}", [128, 3, T_], BF16)
            ckvn = k.sb(f"m_ckvn{s0}", [128, 2, nk], BF16)
            Kt = k.sb(f"m_Kt{s0}", [96, 4, nk], BF16)
            Vt = k.sb(f"m_Vt{s0}", [128, nkt, 4, 65], BF16)
            krall = k.sb(f"m_krall{s0}", [96, nk], BF16)
            k.op("pool", lambda e: e.memset(Vt[:], 1.0), writes=[Vt])
            groups = [(g0, min(512, T_ - g0)) for g0 in range(0, T_, 512)]
            seq_i = s0 // 256
            for (g0, n) in groups:
                a0 = s0 + g0
                for (c0, nch, gcol, dst, nfeat) in [(0, 3, 72, cqn, 384), (384, 2, 78, ckvn, 256)]:
                    pss = []
                    for c in range(nch):
                        ps = k.psum()
                        pss.append(ps)
                        for kc in range(8):
                            k.mm(ps, ps[:, 0:n], win[:, kc, c0 + c * 128:c0 + (c + 1) * 128], hT[:, kc, a0:a0 + n], kc == 0, kc == 7,
                                 reads=[win, hT.sub((a0 // 512) * 512)])
                    for c in range(nch):
                        k.op("act", lambda e, c=c, ps=pss[c]: e.activation(out=cf[:, c, 0:n], in_=ps[:, 0:n], func=AF.Copy), reads=[pss[c]], writes=[cf])
                    gcl = gcol + l * nch
                    self.rmsnorm((lambda c: cf[:, c, 0:n], [cf]), n, lambda c: self.vecT[:, gcl + c:gcl + c + 1], None,
                                 (lambda c: cf[:, c, 0:n], [cf]), sq, rstd, nfeat_chunks=nch, nfeat=nfeat)
                    for c in range(nch):
                        k.op("pool", lambda e, c=c, dst=dst: e.tensor_copy(out=dst[:, c, g0:g0 + n], in_=cf[:, c, 0:n]), reads=[cf], writes=[dst])
                    if dst is ckvn and not smp:
                        for c in range(2):
                            self.store_tok(lambda tt, c=c: cf[:, c, tt * 128:(tt + 1) * 128], [cf], 128, n,
                                           lambda tt, c=c: self.O["n_ckv"][seq_i, l, g0 + tt * 128:g0 + (tt + 1) * 128, c * 128:(c + 1) * 128])
                pa = k.psum()
                for kc in range(8):
                    k.mm(pa, pa[0:96, 0:n], wkr[:, kc, :], hT[:, kc, a0:a0 + n], kc == 0, kc == 7, reads=[wkr, hT.sub((a0 // 512) * 512)])
                if smp:
                    pb = k.psum()
                    for kc in range(8):
                        k.mm(pb, pb[0:96, 0:n], wkr_r[:, kc, :], hT[:, kc, a0:a0 + n], kc == 0, kc == 7, reads=[wkr_r, hT.sub((a0 // 512) * 512)])
                    k.op("dve", lambda e: e.tensor_tensor(out=krf[64:96, 0:n], in0=pa[64:96, 0:n], in1=rope[64:96, 0, g0:g0 + n], op=ALU.mult),
                         reads=[pa, rope], writes=[krf])
                    k.op("dve", lambda e: e.tensor_tensor(out=krb[64:96, 0:n], in0=pb[64:96, 0:n], in1=rope[64:96, 1, g0:g0 + n], op=ALU.mult),
                         reads=[pb, rope], writes=[krb])
                    k.op("pool", lambda e: e.tensor_tensor(out=krall[64:96, g0:g0 + n], in0=krf[64:96, 0:n], in1=krb[64:96, 0:n], op=ALU.add),
                         reads=[krf, krb], writes=[krall])
                else:
                    k.op("dve", lambda e: e.tensor_copy(out=krf[0:96, 0:n], in_=pa[0:96, 0:n]), reads=[pa], writes=[krf])
                    k.op("pool", lambda e: e.tensor_copy(out=krall[64:96, g0:g0 + n], in_=krf[64:96, 0:n]), reads=[krf], writes=[krall])
                    self.store_tok(lambda tt: krf[0:96, tt * 128:(tt + 1) * 128], [krf], 96, n,
                                   lambda tt: self.O["n_kr"][seq_i, l, g0 + tt * 128:g0 + (tt + 1) * 128, :], f0=64)
            if smp:
                kin = k.sb("m_kin", [128, 96], F32)
                k.op("pool", lambda e: e.memset(kin[:], 0.0), writes=[kin])
                for i in range(4):
                    ci = cf.t[:, i % 2, 0:256]
                    k.dma("sp", ci, W["c_ckv"][l, i * 128:(i + 1) * 128, :], writes=[cf])
                    ps = k.psum()
                    for c in range(2):
                        k.tr(ps, ps[:, c * 128:(c + 1) * 128], ci[:, c * 128:(c + 1) * 128], self.ident, reads=[cf, self.masks])
                    k.op("act", lambda e, ps=ps, i=i: e.activation(out=ckvn[:, :, T_ + i * 128:T_ + (i + 1) * 128],
                                                                   in_=ps[:, 0:256].rearrange("p (c t) -> p c t", c=2), func=AF.Copy),
                         reads=[ps], writes=[ckvn])
                    k.dma("sp", kin[:, 64:96], W["c_kr"][l, i * 128:(i + 1) * 128, :], writes=[kin])
                    ps2 = k.psum()
                    k.tr(ps2, ps2[0:96, 0:128], kin[:, :], self.ident, reads=[kin, self.masks])
                    k.op("dve", lambda e, ps2=ps2, i=i: e.tensor_copy(out=krall[64:96, T_ + i * 128:T_ + (i + 1) * 128], in_=ps2[64:96, 0:128]),
                         reads=[ps2], writes=[krall])
            kgroups = [(g0, min(512, nk - g0)) for g0 in range(0, nk, 512)]
            for hg in range(2):
                k.dma("pool", wout[:], W["w_out"][l][hg * 256:(hg + 1) * 256, :].rearrange("(h p) n -> p h n", p=64), writes=[wout])
                for (g0, n) in kgroups:
                    for hh in range(4):
                        h = hg * 4 + hh
                        ps = k.psum()
                        for c in range(2):
                            k.mm(ps, ps[0:64, 0:n], wkvb[:, c, h * 128:h * 128 + 64], ckvn[:, c, g0:g0 + n], c == 0, c == 1, reads=[wkvb, ckvn])
                        eng = "act" if hh % 2 == 0 else "dve"
                        if eng == "act":
                            k.op("act", lambda e, ps=ps, hh=hh: e.activation(out=Kt[0:64, hh, g0:g0 + n], in_=ps[0:64, 0:n], func=AF.Copy),
                                 reads=[ps], writes=[Kt.sub(hh)])
                        else:
                            k.op("dve", lambda e, ps=ps, hh=hh: e.tensor_copy(out=Kt[0:64, hh, g0:g0 + n], in_=ps[0:64, 0:n]), reads=[ps], writes=[Kt.sub(hh)])
                        if hg == 0:
                            pass
                        k.op("pool", lambda e, hh=hh: e.tensor_copy(out=Kt[64:96, hh, g0:g0 + n], in_=krall[64:96, g0:g0 + n]),
                             reads=[krall], writes=[Kt.sub(hh)])
                        self.sq_max(Kt[0:96, hh, g0:g0 + n], 96, n, self.sm_k[0:1, hh:hh + 1], g0 == 0, [Kt.sub(hh)])
                    for kt in range(g0 // 128, (g0 + n) // 128):
                        ps = k.psum()
                        for c in range(2):
                            k.mm(ps, ps[:, 0:256].rearrange("p (h d) -> p h d", h=4),
                                 ckvn[:, c, kt * 128:(kt + 1) * 128],
                                 wkvb[:, c, hg * 512:(hg + 1) * 512].rearrange("p (h d) -> p h d", h=4)[:, :, 64:128], c == 0, c == 1,
                                 reads=[wkvb, ckvn])
                        k.op("act", lambda e, ps=ps, kt=kt: e.activation(out=Vt[:, kt, :, 0:64], in_=ps[:, 0:256].rearrange("p (h d) -> p h d", h=4),
                                                                         func=AF.Copy), reads=[ps], writes=[Vt])
                npt = 0
                for (g0, n) in groups:
                    nqt = n // 128
                    for hh in range(4):
                        h = hg * 4 + hh
                        pa = k.psum()
                        for c in range(3):
                            k.mm(pa, pa[0:96, 0:n], wqb[:, c, h * 96:(h + 1) * 96], cqn[:, c, g0:g0 + n], c == 0, c == 2, reads=[wqb, cqn])
                        if smp:
                            pb = k.psum()
                            for c in range(3):
                                k.mm(pb, pb[0:96, 0:n], wqb_r[:, c, h * 96:(h + 1) * 96], cqn[:, c, g0:g0 + n], c == 0, c == 2, reads=[wqb_r, cqn])
                            k.op("dve", lambda e: e.tensor_tensor(out=qtmp[:, 0:n], in0=pa[0:96, 0:n], in1=rope[:, 0, g0:g0 + n], op=ALU.mult),
                                 reads=[pa, rope], writes=[qtmp])
                            k.op("dve", lambda e: e.tensor_tensor(out=qtmp2[:, 0:n], in0=pb[0:96, 0:n], in1=rope[:, 1, g0:g0 + n], op=ALU.mult),
                                 reads=[pb, rope], writes=[qtmp2])
                            k.op("pool", lambda e: e.tensor_tensor(out=qTs[hh][:, 0:n], in0=qtmp[:, 0:n], in1=qtmp2[:, 0:n], op=ALU.add),
                                 reads=[qtmp, qtmp2], writes=[qTs[hh]])
                        else:
                            k.op("act", lambda e: e.activation(out=qTs[hh][:, 0:n], in_=pa[0:96, 0:n], func=AF.Copy), reads=[pa], writes=[qTs[hh]])
                        self.sm_q = smqs[hh]; self.sm_t = smts[hh]
                        self.sq_max(qTs[hh][0:96, 0:n], 96, n, self.sm_q[0:1, 0:1], True, [qTs[hh]], dstT=self.sm_q)
                        self.softmax_bound(self.sm_q[0:1, 0:1], self.sm_k[0:1, hh:hh + 1], scale, negcs[hh])
                    for hh in range(4):
                        h = hg * 4 + hh
                        qT = qTs[hh]
                        negc = negcs[hh]
                        pos = k.psum_hold(1)
                        po = pos[0]
                        def qk(kt):
                            ps = k.psum()
                            k.mm(ps, ps[:, 0:n], Kt[:, hh, kt * 128:(kt + 1) * 128], qT[:, 0:n], True, True, reads=[Kt.sub(hh), qT])
                            return ps
                        pend = qk(0)
                        for kt in range(nkt):
                            ps = pend
                            if kt + 1 < nkt:
                                pend = qk(kt + 1)
                            pt = PT[npt % 3]
                            npt += 1
                            k.op("act", lambda e, pt=pt, ps=ps: e.activation(out=pt[:, 0:n], in_=ps[:, 0:n], func=AF.Exp, bias=negc[:, 0:1], scale=scale),
                                 reads=[ps, negc], writes=[pt])
                            k.mm(po, po[0:65, 0:n], Vt[:, kt, hh, :], pt[:, 0:n], kt == 0, kt == nkt - 1, reads=[pt, Vt])
                        k.op("act", lambda e, po=po: e.activation(out=rrow[64:65, 0:n], in_=po[64:65, 0:n], func=AF.Copy), reads=[po], writes=[rrow])
                        pbc = k.psum()
                        k.mm(pbc, pbc[0:64, 0:n], self.ones_f[64:65, 0:64], rrow[64:65, 0:n], True, True, reads=[self.ones_f, rrow])
                        k.op("dve", lambda e, pbc=pbc: e.reciprocal(out=rbc[:, 0:n], in_=pbc[0:64, 0:n]), reads=[pbc], writes=[rbc])
                        k.op("dve", lambda e, po=po, hh=hh: e.tensor_tensor(out=mixH[:, hh, 0:n], in0=po[0:64, 0:n], in1=rbc[:, 0:n], op=ALU.mult),
                             reads=[po, rbc], writes=[mixH])
                        k.psum_release(pos)
                    self.ao_k = 64
                    self.apply_out(l, path, s0 + g0, n, (lambda c: mixH[:, c, 0:n], [mixH]), 4, wout, 0)
                    self.ao_k = 128
        k.end_phase()

    def swa(self, l, path):
        k = self.k
        W = self.W
        smp = path == "s"
        hT = self.hT[path]
        scale = 64 ** -0.5
        k.begin_phase()
        self.ao_x = k.sb("ao_x", [128, 8, 512], F32)
        self.st_buf = [k.sb(f"st{i}", [128, 256], F32) for i in range(2)]
        self.st_n = 0
        self.sm_t = k.sb("sm_t", [1, 1], F32)
        self.sm_q = k.sb("sm_q", [1, 1], F32)
        self.sm_k = k.sb("sm_k", [1, 8], F32)
        self.sm_m = k.sb("sm_m", [1, 1], F32)
        self.sm_sq = k.sb("sm_sq", [96, 512], BF16)
        negc = k.sb("negc", [128, 1], F32)
        psink = k.sb("psink", [128, 8], F32)
        wsw = k.sb("s_w", [128, 8, 768], BF16)
        k.dma("pool", wsw[:], W["w_in"][l].rearrange("(kc p) n -> p kc n", p=128)[:, :, 672:1440], writes=[wsw])
        wout = k.sb("s_wout", [64, 8, D], BF16)
        k.dma("pool", wout[:], W["w_out"][l][512:1024, :].rearrange("(h p) n -> p h n", p=64), writes=[wout])
        if smp:
            rope = k.sb("s_rope", [64, 2, NST], F32)
            k.dma("sp", rope[:], W["rope_s"][:, :, :], writes=[rope])
            wr = k.sb("s_wr", [128, 8, 640], BF16)
            wv = wsw[:, :, 0:640].rearrange("p k (h a z d) -> p k h a z d", h=10, a=2, z=2)
            wrv = wr[:, :, :].rearrange("p k (h a z d) -> p k h a z d", h=10, a=2, z=2)
            for kc in range(8):
                k.op("pool", lambda e, kc=kc: e.tensor_scalar(out=wrv[:, kc, :, :, 0, :], in0=wv[:, kc, :, :, 1, :], scalar1=-1.0, scalar2=None, op0=ALU.mult),
                     reads=[wsw], writes=[wr])
                k.op("pool", lambda e, kc=kc: e.tensor_copy(out=wrv[:, kc, :, :, 1, :], in_=wv[:, kc, :, :, 0, :]), reads=[wsw], writes=[wr])
        kf = k.sb("s_kf", [64, 512], F32)
        kb_ = k.sb("s_kb", [64, 512], F32)

        def proj_rope(col0, a0, n, g0, dst_ap, dst_w, fp_out=None):
            pa = k.psum()
            for kc in range(8):
                k.mm(pa, pa[0:64, 0:n], wsw[:, kc, col0:col0 + 64], hT[:, kc, a0:a0 + n], kc == 0, kc == 7, reads=[wsw, hT.sub((a0 // 512) * 512)])
            if smp:
                pb = k.psum()
                for kc in range(8):
                    k.mm(pb, pb[0:64, 0:n], wr[:, kc, col0:col0 + 64], hT[:, kc, a0:a0 + n], kc == 0, kc == 7, reads=[wr, hT.sub((a0 // 512) * 512)])
                k.op("dve", lambda e: e.tensor_tensor(out=kf[:, 0:n], in0=pa[0:64, 0:n], in1=rope[:, 0, g0:g0 + n], op=ALU.mult), reads=[pa, rope], writes=[kf])
                k.op("dve", lambda e: e.tensor_tensor(out=kb_[:, 0:n], in0=pb[0:64, 0:n], in1=rope[:, 1, g0:g0 + n], op=ALU.mult), reads=[pb, rope], writes=[kb_])
                k.op("pool", lambda e: e.tensor_tensor(out=dst_ap, in0=kf[:, 0:n], in1=kb_[:, 0:n], op=ALU.add), reads=[kf, kb_], writes=dst_w)
            else:
                if fp_out is not None:
                    k.op("dve", lambda e: e.tensor_copy(out=kf[:, 0:n], in_=pa[0:64, 0:n]), reads=[pa], writes=[kf])
                    k.op("pool", lambda e: e.tensor_copy(out=dst_ap, in_=kf[:, 0:n]), reads=[kf], writes=dst_w)
                else:
                    k.op("act", lambda e: e.activation(out=dst_ap, in_=pa[0:64, 0:n], func=AF.Copy), reads=[pa], writes=dst_w)

        qs = k.sb("s_q", [64, 4, 512], BF16)
        PT = [k.sb(f"s_PT{i}", [128, 4, 128], BF16) for i in range(3)]
        mixH = k.sb("s_mixH", [64, 8, 512], BF16)
        rrow = k.sb("s_rrow", [65, 512], F32)
        rbc = k.sb("s_rbc", [64, 512], F32)
        for (s0, T_) in self.seqs(path):
            nk = T_ + (PAST if smp else 0)
            nkt = nk // 128
            nlt = T_ // 128
            seq_i = s0 // 256
            Ks = k.sb(f"s_K{s0}", [64, 2, nk], BF16)
            Vs = k.sb(f"s_V{s0}", [128, nkt, 2, 65], BF16)
            k.op("pool", lambda e: e.memset(Vs[:], 1.0), writes=[Vs])
            vst = [k.sb(f"s_vst{s0}_{i}", [128, 128], F32) for i in range(2)]
            groups = [(g0, min(512, T_ - g0)) for g0 in range(0, T_, 512)]
            for (g0, n) in groups:
                a0 = s0 + g0
                for kvh in range(2):
                    proj_rope(512 + kvh * 64, a0, n, g0, Ks[:, kvh, g0:g0 + n], [Ks.sub(kvh)], fp_out=(None if smp else True))
                    if not smp:
                        self.store_tok(lambda tt: kf[0:64, tt * 128:(tt + 1) * 128], [kf], 64, n,
                                       lambda tt, kvh=kvh: self.O["n_sk"][seq_i, l, g0 + tt * 128:g0 + (tt + 1) * 128, kvh * 64:(kvh + 1) * 64])
                    self.sq_max(Ks[0:64, kvh, g0:g0 + n], 64, n, self.sm_k[0:1, kvh:kvh + 1], g0 == 0, [Ks.sub(kvh)])
                for tt in range(n // 128):
                    kt = (g0 // 128) + tt
                    ps = k.psum()
                    for kc in range(8):
                        k.mm(ps, ps[:, 0:128], hT[:, kc, a0 + tt * 128:a0 + (tt + 1) * 128], wsw[:, kc, 640:768], kc == 0, kc == 7,
                             reads=[wsw, hT.sub((a0 // 512) * 512)])
                    k.op("act", lambda e, ps=ps, kt=kt: e.activation(out=Vs[:, kt, :, 0:64], in_=ps[:, 0:128].rearrange("p (h d) -> p h d", h=2), func=AF.Copy),
                         reads=[ps], writes=[Vs])
                    if not smp:
                        vt = vst[tt % 2]
                        k.op("dve", lambda e, ps=ps, vt=vt: e.tensor_copy(out=vt[:], in_=ps[:, 0:128]), reads=[ps], writes=[vt])
                        k.dma("sp", self.O["n_sv"][seq_i, l, g0 + tt * 128:g0 + (tt + 1) * 128, :], vt[:], reads=[vt], writes=[])
            if smp:
                cin = [k.sb(f"s_cin{i}", [128, 128], F32) for i in range(2)]
                cvn = [k.sb(f"s_cvn{i}", [128, 128], F32) for i in range(2)]
                for i in range(4):
                    ci = cin[i % 2]
                    k.dma("sp", ci[:], W["c_sk"][l, i * 128:(i + 1) * 128, :], writes=[ci])
                    for kvh in range(2):
                        ps = k.psum()
                        k.tr(ps, ps[0:64, 0:128], ci[:, kvh * 64:(kvh + 1) * 64], self.ident, reads=[ci, self.masks])
                        k.op("act", lambda e, ps=ps, kvh=kvh, i=i: e.activation(out=Ks[:, kvh, T_ + i * 128:T_ + (i + 1) * 128], in_=ps[0:64, 0:128], func=AF.Copy),
                             reads=[ps], writes=[Ks.sub(kvh)])
                    cv = cvn[i % 2]
                    k.dma("sp", cv[:], W["c_sv"][l, i * 128:(i + 1) * 128, :], writes=[cv])
                    k.op("pool", lambda e, cv=cv, i=i: e.tensor_copy(out=Vs[:, nlt + i, :, 0:64], in_=cv[:, :].rearrange("p (h d) -> p h d", h=2)),
                         reads=[cv], writes=[Vs])
                for kvh in range(2):
                    for g0 in range(T_, nk, 512):
                        self.sq_max(Ks[0:64, kvh, g0:g0 + 512], 64, 512, self.sm_k[0:1, kvh:kvh + 1], False, [Ks.sub(kvh)])
            npt = 0
            for (g0, n) in groups:
                a0 = s0 + g0
                nqt = n // 128
                for kvh in range(2):
                    for g in range(4):
                        proj_rope((kvh * 4 + g) * 64, a0, n, g0, qs[:, g, 0:n], [qs])
                        self.sq_max(qs[0:64, g, 0:n], 64, n, self.sm_q[0:1, 0:1], g == 0, [qs])
                    self.softmax_bound(self.sm_q[0:1, 0:1], self.sm_k[0:1, kvh:kvh + 1], scale, negc)
                    k.op("act", lambda e, kvh=kvh: e.activation(out=psink[:, kvh * 4:(kvh + 1) * 4], in_=self.rowv(l, "sink")[:, kvh * 4:(kvh + 1) * 4],
                                                                func=AF.Exp, bias=negc[:, 0:1], scale=1.0), reads=[self.rowbc, negc], writes=[psink])
                    for qt in range(nqt):
                        gq = g0 // 128 + qt
                        if smp:
                            kts = [(t, t - gq) for t in (gq - 1, gq, gq + 1) if 0 <= t < nlt] + [(nlt + i, 0) for i in range(4)]
                        else:
                            kts = [(t, 0) for t in range(nlt)]
                        pos = k.psum_hold(1)
                        po = pos[0]
                        def qk(kt):
                            ps = k.psum()
                            k.mm(ps, ps[:, :].rearrange("p (g q) -> p g q", g=4), Ks[:, kvh, kt * 128:(kt + 1) * 128], qs[:, :, qt * 128:(qt + 1) * 128],
                                 True, True, reads=[Ks.sub(kvh), qs])
                            return ps
                        pend = qk(kts[0][0])
                        for ki, (kt, rel) in enumerate(kts):
                            ps = pend
                            if ki + 1 < len(kts):
                                pend = qk(kts[ki + 1][0])
                            pt = PT[npt % 3]
                            npt += 1
                            k.op("act", lambda e, pt=pt, ps=ps: e.activation(out=pt[:, :, :], in_=ps[:, :].rearrange("p (g q) -> p g q", g=4), func=AF.Exp,
                                                                           bias=negc[:, 0:1], scale=scale), reads=[ps, negc], writes=[pt])
                            if rel != 0:
                                mi = 3 if rel == -1 else 1
                                for g in range(4):
                                    k.op("pool" if g % 2 else "dve", lambda e, pt=pt, g=g, mi=mi: e.tensor_tensor(out=pt[:, g, :], in0=pt[:, g, :], in1=self.masks_b[:, mi, :], op=ALU.mult),
                                         reads=[pt, self.masks_b], writes=[pt])
                            k.mm(po, po[0:65, :].rearrange("p (g q) -> p g q", g=4), Vs[:, kt, kvh, :], pt[:, :, :], ki == 0, ki == len(kts) - 1, reads=[pt, Vs])
                        for g in range(4):
                            k.op("dve", lambda e, po=po, g=g, kvh=kvh: e.tensor_scalar(out=rrow[64:65, g * 128:(g + 1) * 128], in0=po[64:65, g * 128:(g + 1) * 128],
                                                                                         scalar1=psink[64:65, kvh * 4 + g:kvh * 4 + g + 1], scalar2=None, op0=ALU.add),
                                 reads=[po, psink], writes=[rrow])
                        pbc = k.psum()
                        k.mm(pbc, pbc[0:64, :], self.ones_f[64:65, 0:64], rrow[64:65, :], True, True, reads=[self.ones_f, rrow])
                        k.op("dve", lambda e, pbc=pbc: e.reciprocal(out=rbc[:, :], in_=pbc[0:64, :]), reads=[pbc], writes=[rbc])
                        k.op("dve", lambda e, po=po, kvh=kvh, qt=qt: e.tensor_tensor(
                            out=mixH[:, kvh * 4:(kvh + 1) * 4, qt * 128:(qt + 1) * 128], in0=po[0:64, :].rearrange("p (g q) -> p g q", g=4),
                            in1=rbc[:, :].rearrange("p (g q) -> p g q", g=4), op=ALU.mult), reads=[po, rbc], writes=[mixH])
                        k.psum_release(pos)
                self.ao_k = 64
                self.apply_out(l, path, s0 + g0, n, (lambda c: mixH[:, c, 0:n], [mixH]), 8, wout, 0)
                self.ao_k = 128
        k.end_phase()

    def gdn(self, l, path):
        k = self.k
        W = self.W
        smp = path == "s"
        hT = self.hT[path]
        M = self.masks
        k.begin_phase()
        self.ao_x = k.sb("ao_x", [128, 8, 512], F32)
        win_all = W["w_in"][l].rearrange("(kc p) n -> p kc n", p=128)
        AM = k.sb("g_am", [128, 5, 128], F32)
        k.dma("sp", AM[:], W["amasks"][:], writes=[AM])
        wg = k.sb("g_wg", [128, 8, 16], BF16)
        k.dma("pool", wg[:], win_all[:, :, 3488:3504], writes=[wg])
        wout = k.sb("g_wout", [128, 4, D], BF16)
        k.dma("pool", wout[:], W["w_out"][l].rearrange("(c p) n -> p c n", p=128)[:, 8:12, :], writes=[wout])
        wh = [k.sb(f"g_wh{i}", [128, 8, 512], BF16) for i in range(1)]
        nalog = k.sb("g_nalog", [128, 8], F32)
        k.op("act", lambda e: e.activation(out=nalog[:], in_=self.rowv(l, "alog"), func=AF.Exp), reads=[self.rowbc], writes=[nalog])
        k.op("dve", lambda e: e.tensor_scalar(out=nalog[:], in0=nalog[:], scalar1=-1.0, scalar2=None, op0=ALU.mult), reads=[nalog], writes=[nalog])
        nwh = 0
        for (s0, T_) in self.seqs(path):
            nch = T_ // 128
            seq_i = s0 // 256
            gg = k.sb(f"g_g{s0}", [128, nch, 8], F32)
            bt = k.sb(f"g_b{s0}", [128, nch, 8], F32)
            for ci in range(nch):
                a0 = s0 + ci * 128
                ps = k.psum()
                for kc in range(8):
                    k.mm(ps, ps[:, 0:16], hT[:, kc, a0:a0 + 128], wg[:, kc, :], kc == 0, kc == 7, reads=[wg, hT.sub((a0 // 512) * 512)])
                k.op("dve", lambda e, ps=ps, ci=ci: e.tensor_tensor(out=gg[:, ci, :], in0=ps[:, 0:8], in1=self.rowv(l, "dtb"), op=ALU.add),
                     reads=[ps, self.rowbc], writes=[gg])
                k.op("act", lambda e, ps=ps, ci=ci: e.activation(out=bt[:, ci, :], in_=ps[:, 8:16], func=AF.Sigmoid), reads=[ps], writes=[bt])
            nbt = k.sb(f"g_nb{s0}", [128, nch, 8], F32)
            k.op("pool", lambda e: e.tensor_scalar(out=nbt[:], in0=bt[:], scalar1=-1.0, scalar2=None, op0=ALU.mult), reads=[bt], writes=[nbt])
            k.op("dve", lambda e: e.tensor_scalar(out=gg[:], in0=gg[:], scalar1=60.0, scalar2=None, op0=ALU.min), reads=[gg], writes=[gg])
            k.op("act", lambda e: e.activation(out=gg[:], in_=gg[:], func=AF.Exp), reads=[gg], writes=[gg])
            k.op("act", lambda e: e.activation(out=gg[:], in_=gg[:], func=AF.Ln, bias=1.0, scale=1.0), reads=[gg], writes=[gg])
            for ci in range(nch):
                k.op("dve", lambda e, ci=ci: e.tensor_tensor(out=gg[:, ci, :], in0=gg[:, ci, :], in1=nalog[:], op=ALU.mult), reads=[gg, nalog], writes=[gg])
            if GSTOP <= 1:
                continue
            pre = k.sb(f"g_pre{s0}", [128, 3, T_ + 4], F32)
            acc = k.sb(f"g_acc{s0}", [128, 3, T_], F32)
            osum = k.sb(f"g_osum{s0}", [128, nch, 128], F32)
            mixT = k.sb(f"g_mixT{s0}", [128, 4, T_], BF16)
            sqf = k.sb(f"g_sqf{s0}", [128, 512], F32)
            rst = k.sb(f"g_rst{s0}", [128, 512], F32)
            names = ["gcb", "X", "DT", "Dm", "N", "NT", "Bm", "BmT", "B2", "B2T", "Wm", "W2", "kbg", "vb",
                     "u", "wkT", "E", "gones", "gz", "on"]
            names_b = ["AT", "kdec", "qdec", "vnew", "Sb"]
            tqs = [{nm: k.sb(f"g_{nm}{s0}_{d}", [128, 128], F32) for nm in names} for d in range(2)]
            for d in range(2):
                tqs[d].update({nm: k.sb(f"g_{nm}{s0}_{d}", [128, 128], BF16) for nm in names_b})
            accb = k.sb(f"g_accb{s0}", [128, 2, T_], BF16)
            scs = [k.sb(f"g_sc{s0}_{d}", [128, 8], F32) for d in range(2)]
            Ss = [k.sb(f"g_S{s0}_{d}", [128, 128], F32) for d in range(2)]
            osums = [osum, k.sb(f"g_osumb{s0}", [128, nch, 128], F32)]
            tq, sc = tqs[0], scs[0]
            k.op("pool", lambda e: e.memset(pre[:], 0.0), writes=[pre])
            groups = [(g0, min(512, T_ - g0)) for g0 in range(0, T_, 512)]
            for h in range(4):
                w = wh[nwh % len(wh)]
                nwh += 1
                for wi, c0 in enumerate([1440, 1952, 2464, 2976]):
                    k.dma("pool", w[:, :, wi * 128:(wi + 1) * 128], win_all[:, :, c0 + h * 128:c0 + (h + 1) * 128], writes=[w])
                for (g0, n) in groups:
                    a0 = s0 + g0
                    for wi in range(3):
                        ps = k.psum()
                        for kc in range(8):
                            k.mm(ps, ps[:, 0:n], w[:, kc, wi * 128:(wi + 1) * 128], hT[:, kc, a0:a0 + n], kc == 0, kc == 7, reads=[w, hT.sub((a0 // 512) * 512)])
                        k.op("act", lambda e, ps=ps, wi=wi: e.activation(out=pre[:, wi, 2 + g0:2 + g0 + n], in_=ps[:, 0:n], func=AF.Copy), reads=[ps], writes=[pre.sub(wi)])
                for wi in range(3):
                    eng = "dve"
                    cch = wi * 4 + h
                    for tap in range(5):
                        wcol = self.convT[:, l * 60 + tap * 12 + cch:l * 60 + tap * 12 + cch + 1]
                        if tap == 0:
                            k.op(eng, lambda e, wi=wi, wcol=wcol: e.tensor_scalar(out=acc[:, wi, :], in0=pre[:, wi, 0:T_], scalar1=wcol, scalar2=None, op0=ALU.mult),
                                 reads=[pre.sub(wi), self.convT], writes=[acc.sub(wi)])
                        else:
                            k.op(eng, lambda e, wi=wi, wcol=wcol, tap=tap: e.scalar_tensor_tensor(out=acc[:, wi, :], in0=pre[:, wi, tap:tap + T_], scalar=wcol,
                                                                                                 in1=acc[:, wi, :], op0=ALU.mult, op1=ALU.add),
                                 reads=[pre.sub(wi), self.convT, acc.sub(wi)], writes=[acc.sub(wi)])
                    k.op("act", lambda e, wi=wi: e.activation(out=acc[:, wi, :], in_=acc[:, wi, :], func=AF.Silu), reads=[acc.sub(wi)], writes=[acc.sub(wi)])
                if GSTOP <= 2:
                    continue
                for wi in range(2):
                    for (g0, n) in groups:
                        k.op("act", lambda e, wi=wi: e.activation(out=sqf[:, 0:n], in_=acc[:, wi, g0:g0 + n], func=AF.Square), reads=[acc.sub(wi)], writes=[sqf])
                        ps = k.psum()
                        k.mm(ps, ps[:, 0:n], self.ones_f[:, :], sqf[:, 0:n], True, True, reads=[self.ones_f, sqf])
                        k.op("act", lambda e, ps=ps: e.activation(out=rst[:, 0:n], in_=ps[:, 0:n], func=AF.Sqrt, bias=self.epsc[:, 0:1], scale=1.0),
                             reads=[ps, self.epsc], writes=[rst])
                        k.op("dve", lambda e: e.reciprocal(out=rst[:, 0:n], in_=rst[:, 0:n]), reads=[rst], writes=[rst])
                        if wi == 0:
                            k.op("dve", lambda e, wi=wi: e.scalar_tensor_tensor(out=acc[:, wi, g0:g0 + n], in0=acc[:, wi, g0:g0 + n], scalar=128 ** -0.5, in1=rst[:, 0:n],
                                                                                 op0=ALU.mult, op1=ALU.mult), reads=[acc.sub(wi), rst], writes=[acc.sub(wi)])
                        else:
                            k.op("pool", lambda e, wi=wi: e.tensor_tensor(out=acc[:, wi, g0:g0 + n], in0=acc[:, wi, g0:g0 + n], in1=rst[:, 0:n], op=ALU.mult),
                                 reads=[acc.sub(wi), rst], writes=[acc.sub(wi)])
                k.op("act", lambda e: e.activation(out=accb[:, 0, :], in_=acc[:, 0, :], func=AF.Copy), reads=[acc.sub(0)], writes=[accb])
                k.op("pool", lambda e: e.tensor_copy(out=accb[:, 1, :], in_=acc[:, 1, :]), reads=[acc.sub(1)], writes=[accb])
                if GSTOP <= 3:
                    continue
                def chain(d, tq, sc, S, osum):
                    col = d * 4 + h
                    if smp:
                        k.dma("sp", S[:], W["c_st"][l, d, h], writes=[S])
                        yield
                    else:
                        k.op("pool", lambda e: e.memset(S[:], 0.0), writes=[S])
                        yield
                    k.op("act", lambda e: e.activation(out=tq["Sb"][:], in_=S[:], func=AF.Copy), reads=[S], writes=[tq["Sb"]])
                    yield
                    order = range(nch) if d == 0 else range(nch - 1, -1, -1)
                    mcs = 1 if d == 0 else 3
                    m_dt_incl = 1 if d == 0 else 3
                    m_d_strict = 4 if d == 0 else 2
                    last = 127 if d == 0 else 0
                    for ci in order:
                        c0 = ci * 128
                        kT_c = acc[:, 1, c0:c0 + 128]
                        qT_c = acc[:, 0, c0:c0 + 128]
                        vT_c = acc[:, 2, c0:c0 + 128]
                        gcol = gg[:, ci, col:col + 1]
                        bcol = bt[:, ci, col:col + 1]
                        rd = [acc.sub(0), acc.sub(1), acc.sub(2)]
                        t = tq
                        yield
                        p1 = k.psum()
                        k.mm(p1, p1[:, 0:128], kT_c, kT_c, True, True, reads=rd)
                        yield
                        k.mm(p1, p1[:, 128:256], accb[:, 1, c0:c0 + 128], accb[:, 0, c0:c0 + 128], True, True, reads=[accb])
                        yield
                        k.tr(p1, p1[:, 256:384], kT_c, self.ident, reads=rd + [M])
                        yield
                        k.tr(p1, p1[:, 384:512], vT_c, self.ident, reads=rd + [M])
                        yield
                        yield
                        k.op("dve", lambda e: e.tensor_scalar(out=t["gones"][:], in0=self.ones_f[:], scalar1=gcol, scalar2=None, op0=ALU.mult),
                             reads=[gg, self.ones_f], writes=[t["gones"]])
                        yield
                        p2 = k.psum()
                        k.mm(p2, p2[:, 0:128], t["gones"][:], M[:, mcs, :], True, True, reads=[t["gones"], M])
                        yield
                        k.mm(p2, p2[:, 128:129], M[:, mcs, :], gcol, True, True, reads=[M, gg])
                        yield
                        k.op("dve", lambda e: e.tensor_copy(out=sc[:, 0:1], in_=p2[:, 128:129]), reads=[p2], writes=[sc])
                        yield
                        yield
                        k.op("dve", lambda e: e.tensor_scalar(out=t["X"][:], in0=p2[:, 0:128], scalar1=sc[:, 0:1], scalar2=None, op0=ALU.subtract),
                             reads=[p2, sc], writes=[t["X"]])
                        yield
                        k.op("dve", lambda e: e.tensor_tensor(out=t["DT"][:], in0=t["X"][:], in1=AM[:, m_dt_incl, :], op=ALU.add), reads=[t["X"], AM], writes=[t["DT"]])
                        yield
                        k.op("act", lambda e: e.activation(out=t["DT"][:], in_=t["DT"][:], func=AF.Exp), reads=[t["DT"]], writes=[t["DT"]])
                        yield
                        k.op("dve", lambda e: e.scalar_tensor_tensor(out=t["Dm"][:], in0=t["X"][:], scalar=-1.0, in1=AM[:, m_d_strict, :], op0=ALU.mult, op1=ALU.add),
                             reads=[t["X"], AM], writes=[t["Dm"]])
                        yield
                        k.op("act", lambda e: e.activation(out=t["Dm"][:], in_=t["Dm"][:], func=AF.Exp), reads=[t["Dm"]], writes=[t["Dm"]])
                        yield
                        yield
                        k.op("act", lambda e: e.activation(out=t["E"][:], in_=p2[:, 0:128], func=AF.Exp), reads=[p2], writes=[t["E"]])
                        yield
                        k.op("act", lambda e: e.activation(out=sc[:, 1:2], in_=sc[:, 0:1], func=AF.Exp), reads=[sc], writes=[sc])
                        yield
                        k.op("dve", lambda e: e.tensor_tensor(out=sc[:, 1:2], in0=sc[:, 1:2], in1=bcol, op=ALU.mult), reads=[sc, bt], writes=[sc])
                        yield
                        k.op("dve", lambda e: e.tensor_tensor(out=sc[:, 2:3], in0=p2[:, last:last + 1], in1=sc[:, 0:1], op=ALU.subtract),
                             reads=[p2, sc], writes=[sc])
                        yield
                        k.op("act", lambda e: e.activation(out=sc[:, 2:3], in_=sc[:, 2:3], func=AF.Exp), reads=[sc], writes=[sc])
                        yield
                        yield
                        k.op("dve", lambda e: e.scalar_tensor_tensor(out=t["N"][:], in0=p1[:, 0:128], scalar=nbt[:, ci, col:col + 1], in1=t["Dm"][:], op0=ALU.mult, op1=ALU.mult),
                             reads=[p1, nbt, t["Dm"]], writes=[t["N"]])
                        yield
                        k.op("dve", lambda e: e.tensor_tensor(out=t["AT"][:], in0=p1[:, 128:256], in1=t["DT"][:], op=ALU.mult), reads=[p1, t["DT"]], writes=[t["AT"]])
                        yield
                        yield
                        k.op("dve", lambda e: e.tensor_scalar(out=t["kbg"][:], in0=p1[:, 256:384], scalar1=sc[:, 1:2], scalar2=None, op0=ALU.mult), reads=[p1, sc], writes=[t["kbg"]])
                        yield
                        k.op("dve", lambda e: e.tensor_scalar(out=t["kdec"][:], in0=p1[:, 256:384], scalar1=sc[:, 2:3], scalar2=None, op0=ALU.mult), reads=[p1, sc], writes=[t["kdec"]])
                        yield
                        k.op("dve", lambda e: e.tensor_scalar(out=t["vb"][:], in0=p1[:, 384:512], scalar1=bcol, scalar2=None, op0=ALU.mult), reads=[p1, bt], writes=[t["vb"]])
                        yield
                        k.op("pool", lambda e: e.tensor_tensor(out=t["qdec"][:], in0=qT_c, in1=t["E"][:], op=ALU.mult), reads=[acc.sub(0), t["E"]], writes=[t["qdec"]])
                        yield
                        if GSTOP <= 4:
                            continue
                        yield
                        p3 = k.psum()
                        k.tr(p3, p3[:, 0:128], t["N"][:], self.ident, reads=[t["N"], M])
                        yield
                        k.op("act", lambda e: e.activation(out=t["NT"][:], in_=p3[:, 0:128], func=AF.Copy), reads=[p3], writes=[t["NT"]])
                        yield
                        yield
                        k.op("dve", lambda e: e.tensor_tensor(out=t["Wm"][:], in0=p3[:, 0:128], in1=self.ident, op=ALU.add), reads=[p3, M], writes=[t["Wm"]])
                        yield
                        Bc, BcT, Wc = t["NT"], t["N"], t["Wm"]
                        nxt = [(t["Bm"], t["BmT"], t["W2"]), (t["B2"], t["B2T"], t["Wm"])]
                        for lev in range(1, 7):
                            Bn, BnT, Wn = nxt[(lev - 1) % 2]
                            pq = k.psum()
                            k.mm(pq, pq[:, 0:128], Bc[:], BcT[:], True, True, reads=[Bc, BcT])
                            yield
                            if lev < 6:
                                pq2 = k.psum()
                                k.mm(pq2, pq2[:, 0:128], BcT[:], Bc[:], True, True, reads=[Bc, BcT])
                                yield
                            k.op("act", lambda e, pq=pq, BnT=BnT: e.activation(out=BnT[:], in_=pq[:, 0:128], func=AF.Copy), reads=[pq], writes=[BnT])
                            yield
                            if lev < 6:
                                k.op("dve", lambda e, pq2=pq2, Bn=Bn: e.tensor_copy(out=Bn[:], in_=pq2[:, 0:128]), reads=[pq2], writes=[Bn])
                                yield
                            yield
                            pw = k.psum()
                            k.mm(pw, pw[:, 0:128], BnT[:], Wc[:], True, True, reads=[BnT, Wc])
                            yield
                            k.op("dve", lambda e, pw=pw, Wn=Wn, Wc=Wc: e.tensor_tensor(out=Wn[:], in0=pw[:, 0:128], in1=Wc[:], op=ALU.add), reads=[pw, Wc], writes=[Wn])
                            yield
                            Bc, BcT, Wc = Bn, BnT, Wn
                            yield
                        TT = Wc
                        yield
                        p4 = k.psum()
                        k.mm(p4, p4[:, 0:128], TT[:], t["vb"][:], True, True, reads=[TT, t["vb"]])
                        yield
                        k.mm(p4, p4[:, 128:256], t["kbg"][:], TT[:], True, True, reads=[TT, t["kbg"]])
                        yield
                        k.op("act", lambda e: e.activation(out=t["u"][:], in_=p4[:, 0:128], func=AF.Copy), reads=[p4], writes=[t["u"]])
                        yield
                        k.op("dve", lambda e: e.tensor_copy(out=t["wkT"][:], in_=p4[:, 128:256]), reads=[p4], writes=[t["wkT"]])
                        yield
                        if GSTOP <= 5:
                            continue
                        yield
                        p5 = k.psum()
                        k.mm(p5, p5[:, 0:128], t["wkT"][:], S[:], True, True, reads=[t["wkT"], S])
                        yield
                        k.op("dve", lambda e: e.tensor_tensor(out=t["vnew"][:], in0=t["u"][:], in1=p5[:, 0:128], op=ALU.subtract), reads=[p5, t["u"]], writes=[t["vnew"]])
                        yield
                        k.mm(p5, p5[:, 128:256], t["qdec"][:], t["Sb"][:], True, False, reads=[t["qdec"], t["Sb"]])
                        yield
                        k.mm(p5, p5[:, 128:256], t["AT"][:], t["vnew"][:], False, True, reads=[t["AT"], t["vnew"]])
                        yield
                        k.mm(p5, p5[:, 256:384], t["kdec"][:], t["vnew"][:], True, True, reads=[t["kdec"], t["vnew"]])
                        yield
                        k.op("act", lambda e, ci=ci: e.activation(out=osum[:, ci, :], in_=p5[:, 128:256], func=AF.Copy), reads=[p5], writes=[osum.sub(ci)])
                        yield
                        k.op("dve", lambda e: e.scalar_tensor_tensor(out=S[:], in0=S[:], scalar=t["E"][:, last:last + 1], in1=p5[:, 256:384], op0=ALU.mult, op1=ALU.add),
                             reads=[p5, S, t["E"]], writes=[S])
                        yield
                        k.op("act", lambda e: e.activation(out=t["Sb"][:], in_=S[:], func=AF.Copy), reads=[S], writes=[t["Sb"]])
                        yield
                    if not smp:
                        k.dma("sp", self.O["n_st"][seq_i, l, d, h], S[:], reads=[S], writes=[])
                        yield
                chains = []
                for d in range(2):
                    chains.append([chain(d, tqs[d], scs[d], Ss[d], osums[d]), [4 * d, 4 * d + 1, 4 * d + 2, 4 * d + 3], 0])
                save_rot, save_next = k.ps_rot, k.psnext
                while chains:
                    for cobj in list(chains):
                        k.ps_rot, k.psnext = cobj[1], cobj[2]
                        try:
                            next(cobj[0])
                        except StopIteration:
                            chains.remove(cobj)
                        cobj[2] = k.psnext
                k.ps_rot, k.psnext = save_rot, save_next
                if GSTOP <= 6:
                    continue
                for ci in range(nch):
                    a0 = s0 + ci * 128
                    ps = k.psum()
                    for kc in range(8):
                        k.mm(ps, ps[:, 0:128], hT[:, kc, a0:a0 + 128], w[:, kc, 384:512], kc == 0, kc == 7, reads=[w, hT.sub((a0 // 512) * 512)])
                    t = tqs[ci % 2]
                    sc = scs[ci % 2]
                    k.op("act", lambda e, ps=ps: e.activation(out=t["gz"][:], in_=ps[:, 0:128], func=AF.Silu), reads=[ps], writes=[t["gz"]])
                    k.op("pool", lambda e, ci=ci: e.tensor_tensor(out=osum[:, ci, :], in0=osum[:, ci, :], in1=osums[1][:, ci, :], op=ALU.add),
                         reads=[osum.sub(ci), osums[1].sub(ci)], writes=[osum.sub(ci)])
                    k.op("act", lambda e, ci=ci: e.activation(out=t["on"][:], in_=osum[:, ci, :], func=AF.Square, accum_out=sc[:, 3:4]), reads=[osum.sub(ci)], writes=[t["on"], sc])
                    k.op("act", lambda e: e.activation(out=sc[:, 4:5], in_=sc[:, 3:4], func=AF.Sqrt, bias=self.epsc[:, 0:1], scale=1.0 / 128), reads=[sc, self.epsc], writes=[sc])
                    k.op("dve", lambda e: e.reciprocal(out=sc[:, 4:5], in_=sc[:, 4:5]), reads=[sc], writes=[sc])
                    k.op("dve", lambda e, ci=ci: e.scalar_tensor_tensor(out=t["on"][:], in0=osum[:, ci, :], scalar=sc[:, 4:5], in1=self.rowv(l, "gnorm"), op0=ALU.mult, op1=ALU.mult),
                         reads=[osum.sub(ci), sc, self.rowbc], writes=[t["on"]])
                    k.op("pool", lambda e: e.tensor_tensor(out=t["on"][:], in0=t["on"][:], in1=t["gz"][:], op=ALU.mult), reads=[t["on"], t["gz"]], writes=[t["on"]])
                    p6 = k.psum()
                    k.tr(p6, p6[:, 0:128], t["on"][:], self.ident, reads=[t["on"], M])
                    k.op("act", lambda e, p6=p6, ci=ci, h=h: e.activation(out=mixT[:, h, ci * 128:(ci + 1) * 128], in_=p6[:, 0:128], func=AF.Copy), reads=[p6], writes=[mixT.sub(h)])
            if GSTOP > 6:
                for (g0, n) in groups:
                    self.apply_out(l, path, s0 + g0, n, (lambda c: mixT[:, c, g0:g0 + n], [mixT]), 4, wout, 0)
        k.end_phase()


_CACHE = {}


def _get_program():
    if "nc" not in _CACHE:
        b = Builder()
        _CACHE["nc"] = b.build()
        _CACHE["b"] = b
    return _CACHE["nc"], _CACHE["b"]


def make_in_maps(inp):
    f = lambda a: np.ascontiguousarray(np.asarray(a, dtype=np.float32))
    hc = host_consts()
    shared = {}
    for nm in ["w_ada", "ffn1_w1", "ffn1_w2", "w_in", "mla_w_qb", "mla_w_kvb", "w_out", "ffn2_w1", "ffn2_w2"]:
        shared[nm] = f(inp[nm])
    shared["b_ada"] = f(inp["b_ada"]).reshape(L, 72, 128)
    shared["convw"] = f(inp["gdn_conv_w"]).reshape(L * 5 * 12, 128)
    rowv = np.zeros((L, 304), np.float32)
    rowv[:, 0:8] = f(inp["swa_sink"])
    rowv[:, 8:16] = f(inp["gdn_a_log"]).reshape(L, 8)
    rowv[:, 16:24] = f(inp["gdn_dt_bias"]).reshape(L, 8)
    rowv[:, 24:152] = f(inp["gdn_norm"])
    shared["rowv"] = rowv.reshape(1, L * 304)
    shared.update(hc)
    maps = []
    for c in range(8):
        b = c // 4
        m = dict(shared)
        m["xp"] = f(inp["x_prompt"][2 * c:2 * c + 2]).reshape(NPT, D)
        m["xs"] = f(inp["x_sample"][b])
        m["c_ckv"] = f(inp["cache_mla_ckv"][b])
        m["c_kr"] = f(inp["cache_mla_krope"][b])
        m["c_sk"] = f(inp["cache_swa_k"][b]).reshape(L, PAST, 128)
        m["c_sv"] = f(inp["cache_swa_v"][b]).reshape(L, PAST, 128)
        m["c_st"] = f(inp["state_gdn"][b])
        vecA = np.zeros((82, 128), np.float32)
        vecA[0:16] = f(inp["norm_ffn1"]).reshape(16, 128)
        vecA[16:32] = f(inp["norm_mix"]).reshape(16, 128)
        vecA[32:48] = f(inp["norm_ffn2"]).reshape(16, 128)
        vecA[48:56] = f(inp["final_norm"]).reshape(8, 128)
        vecA[56:64] = f(inp["c_ctx"]).reshape(8, 128)
        vecA[64:72] = f(inp["c"][b]).reshape(8, 128)
        vecA[72:78] = f(inp["mla_q_norm"]).reshape(6, 128)
        vecA[78:82] = f(inp["mla_kv_norm"]).reshape(4, 128)
        m["vecA"] = vecA
        maps.append(m)
    return maps


def kernel(**inp):
    nc, b = _get_program()
    maps = make_in_maps(inp)
    res = run_bass_kernel_spmd(nc, maps, core_ids=list(range(8)))
    R = res.results
    yp = np.concatenate([R[c]["yp"].reshape(2, 256, D) for c in range(8)], 0)
    ys = np.stack([R[0]["ys"], R[4]["ys"]], 0)
    n_ckv = np.concatenate([R[c]["n_ckv"] for c in range(8)], 0)
    n_kr = np.concatenate([R[c]["n_kr"] for c in range(8)], 0)
    n_sk = np.concatenate([R[c]["n_sk"] for c in range(8)], 0).reshape(16, L, 256, 2, 64)
    n_sv = np.concatenate([R[c]["n_sv"] for c in range(8)], 0).reshape(16, L, 256, 2, 64)
    n_st = np.concatenate([R[c]["n_st"] for c in range(8)], 0)
    return (yp.astype(np.float32), ys.astype(np.float32), n_ckv.astype(np.float32), n_kr.astype(np.float32),
            n_sk.astype(np.float32), n_sv.astype(np.float32), n_st.astype(np.float32))
```

```python
import contextlib
import numpy as np
import concourse.bass as bass
import concourse.mybir as mybir
from concourse.bass_utils import run_bass_kernel_spmd

F32 = mybir.dt.float32
BF16 = mybir.dt.bfloat16
AF = mybir.ActivationFunctionType
ALU = mybir.AluOpType

D = 1024
DFF = 2816
L = 2
NPT = 512
NST = 2048
PAST = 512
EPS = 1e-6
DEBUG = []
STOP = None
SKIP = None
GSTOP = 99
FUSE_WAIT = True


class Buf:
    __slots__ = ("name", "w", "r", "parent", "kids")

    def __init__(self, name, parent=None):
        self.name = name
        self.w = None
        self.r = []
        self.parent = parent
        self.kids = {}

    def sub(self, key):
        if key not in self.kids:
            self.kids[key] = Buf(f"{self.name}.{key}", self)
        return self.kids[key]

    def family(self):
        out = [self]
        p = self.parent
        while p is not None:
            out.append(p)
            p = p.parent
        st = list(self.kids.values())
        while st:
            q = st.pop()
            out.append(q)
            st.extend(q.kids.values())
        return out


class T:
    def __init__(self, t, name):
        self.t = t
        self.b = Buf(name)

    def __getitem__(self, key):
        return self.t[key]

    def sub(self, key):
        return self.b.sub(key)


def _b(x):
    return x.b if isinstance(x, T) else x


class K:
    SAME_ENGINE_SYNC = True

    def __init__(self, n_dma_sems=64):
        self.nc = bass.Bass("TRN2", target_bir_lowering=False)
        nc = self.nc
        self.es = contextlib.ExitStack()
        self.engs = {"pe": nc.tensor, "act": nc.scalar, "dve": nc.vector, "pool": nc.gpsimd, "sp": nc.sync}
        self.sems = {}
        self.cnt = {}
        for e in self.engs:
            self.sems[e] = self.es.enter_context(nc.semaphore(f"prog_{e}"))
            self.cnt[e] = 0
        self.dsems = [self.es.enter_context(nc.semaphore(f"dma_{i}")) for i in range(n_dma_sems)]
        self.dval = [0] * n_dma_sems
        self.dnext = 0
        self.dnext2 = 0
        self.semobj = dict(self.sems)
        for i, s in enumerate(self.dsems):
            self.semobj[("d", i)] = s
        self.waited = {}
        self.n_wait = 0
        self.n_ins = 0
        self.phase_es = None
        self.psums = []
        self.psnext = 0
        self.ps_rot = list(range(8))

    def sb(self, name, shape, dtype=F32, glob=False):
        es = self.es if (glob or self.phase_es is None) else self.phase_es
        self.n_alloc = getattr(self, "n_alloc", 0) + 1
        name = f"sb{self.n_alloc}_{name}"
        t = es.enter_context(self.nc.sbuf_tensor(name, list(shape), dtype))
        return T(t, name)

    def dram(self, name, shape, dtype=F32, kind="Internal"):
        return T(self.nc.dram_tensor(name, list(shape), dtype, kind=kind), name)

    def init_psum(self):
        for i in range(8):
            t = self.es.enter_context(self.nc.psum_tensor(f"psb{i}", [128, 512], F32))
            self.psums.append(T(t, f"psb{i}"))

    def psum(self):
        rot = self.ps_rot
        self.psnext = (self.psnext + 1) % len(rot)
        return self.psums[rot[self.psnext]]

    def psum_hold(self, n):
        held = self.ps_rot[-n:]
        self.ps_rot = self.ps_rot[:-n]
        self.psnext = 0
        return [self.psums[i] for i in held]

    def psum_release(self, banks):
        self.ps_rot = self.ps_rot + [self.psums.index(b) for b in banks]

    def begin_phase(self):
        assert self.phase_es is None
        self.phase_es = contextlib.ExitStack()

    def end_phase(self):
        self.barrier()
        self.phase_es.close()
        self.phase_es = None

    def _deps(self, reads, writes, e=None):
        deps = {}

        def add(d):
            if d is not None and deps.get(d[0], 0) < d[1]:
                deps[d[0]] = d[1]
        for b in reads:
            bb = _b(b)
            for f in bb.family():
                add(f.w)
            if bb.name.startswith("psb"):
                for f in bb.family():
                    for r in f.r:
                        if r[0] != e:
                            add(r)
        for b in writes:
            for f in _b(b).family():
                add(f.w)
                for r in f.r:
                    add(r)
        return deps

    def _emit_waits(self, e, deps, keep_one=False):
        eng = self.engs[e]
        need = []
        for kk, v in deps.items():
            if kk == e and (e == "pe" or not self.SAME_ENGINE_SYNC):
                continue
            if self.waited.get((e, kk), 0) >= v:
                continue
            need.append((kk, v))
            self.waited[(e, kk)] = v
        fused = None
        if keep_one and FUSE_WAIT and need:
            fused = need.pop()
        for kk, v in need:
            eng.wait_ge(self.semobj[kk], v)
            self.n_wait += 1
        return fused

    def _record(self, tag, reads, writes):
        for b in reads:
            _b(b).r.append(tag)
        for b in writes:
            bb = _b(b)
            bb.w = tag
            bb.r = []

    def op(self, e, fn, reads=(), writes=()):
        deps = self._deps(reads, writes, e)
        fused = self._emit_waits(e, deps, keep_one=True)
        ins = fn(self.engs[e])
        if fused is not None:
            ins._wait_ge(self.semobj[fused[0]], fused[1])
        self.cnt[e] += 1
        ins.then_inc(self.sems[e], 1)
        self._record((e, self.cnt[e]), reads, writes)
        self.n_ins += 1
        return ins

    def dma(self, q, out, in_, reads=(), writes=(), **kw):
        deps = self._deps(reads, writes)
        half = len(self.dsems) // 2
        if q == "sp":
            i = self.dnext % half
            self.dnext += 1
        else:
            i = half + self.dnext2 % half
            self.dnext2 += 1
        key = ("d", i)
        if self.dval[i] > 0:
            deps[key] = max(deps.get(key, 0), self.dval[i])
        self._emit_waits(q, deps)
        ins = self.engs[q].dma_start(out=out, in_=in_, **kw)
        self.dval[i] += 16
        ins.then_inc(self.dsems[i], 16)
        self._record((key, self.dval[i]), reads, writes)
        self.n_ins += 1
        return ins

    def barrier(self):
        alld = {e: self.cnt[e] for e in self.engs if self.cnt[e] > 0}
        for i, v in enumerate(self.dval):
            if v > 0:
                alld[("d", i)] = v
        for e in self.engs:
            self._emit_waits(e, dict(alld))

    def mm(self, ps, out, lhsT, rhs, start, stop, reads):
        self.op("pe", lambda e: e.matmul(out, lhsT=lhsT, rhs=rhs, start=start, stop=stop), reads=reads, writes=[ps])

    def tr(self, ps, out, in_, ident, reads):
        self.op("pe", lambda e: e.transpose(out, in_, ident), reads=reads, writes=[ps])


def _rope_tables(n_tokens, dim):
    rows = n_tokens // 64
    row = np.repeat(np.arange(rows, dtype=np.float32), 64)
    col = np.tile(np.arange(64, dtype=np.float32), rows)
    axis_dim = dim // 2
    inv = (1.0 / (10000.0 ** (np.arange(0, axis_dim, 2, dtype=np.float32) / axis_dim))).astype(np.float32)
    ang_r = row[:, None] * inv[None, :]
    ang_c = col[:, None] * inv[None, :]
    ang = np.concatenate([ang_r, ang_r, ang_c, ang_c], axis=-1)
    return np.cos(ang).astype(np.float32), np.sin(ang).astype(np.float32)


def host_consts():
    c = {}
    p = np.arange(128)[:, None]
    f = np.arange(128)[None, :]
    masks = np.zeros((128, 5, 128), np.float32)
    masks[:, 0, :] = (p == f)
    masks[:, 1, :] = (p <= f)
    masks[:, 2, :] = (p < f)
    masks[:, 3, :] = (p >= f)
    masks[:, 4, :] = (p > f)
    amasks = np.zeros((128, 5, 128), np.float32)
    for i in range(1, 5):
        amasks[:, i, :] = (masks[:, i, :] - 1.0) * 1.0e9
    c["amasks"] = amasks
    c["masks"] = masks
    cm, sm = _rope_tables(NST, 32)
    cosm = np.ones((96, NST), np.float32)
    sinm = np.zeros((96, NST), np.float32)
    cosm[64:96] = cm.T
    sinm[64:96] = sm.T
    cs, ss = _rope_tables(NST, 64)
    c["rope_m"] = np.stack([cosm, sinm], 1).copy()
    c["rope_s"] = np.stack([cs.T, ss.T], 1).copy()
    return c


def n_is_k(ap, self):
    return ap.tensor.name == self.sm_k.t.name


class Builder:
    def __init__(self):
        self.k = K()
        k = self.k
        self.nc = k.nc
        self.ins = {}
        self.outs = {}
        k.init_psum()

    def din(self, name, shape):
        t = self.k.dram(name, shape, F32, kind="ExternalInput")
        self.ins[name] = t
        return t

    def dout(self, name, shape):
        t = self.k.dram(name, shape, F32, kind="ExternalOutput")
        self.outs[name] = t
        return t

    def build(self):
        k = self.k
        W = {}
        for nm, shp in [("xp", [NPT, D]), ("xs", [NST, D]), ("c_ckv", [L, PAST, 256]), ("c_kr", [L, PAST, 32]),
                        ("c_sk", [L, PAST, 128]), ("c_sv", [L, PAST, 128]), ("c_st", [L, 2, 4, 128, 128]),
                        ("vecA", [82, 128]), ("b_ada", [L, 72, 128]), ("convw", [120, 128]), ("rowv", [1, 304 * L]),
                        ("w_ada", [L, D, 9 * D]), ("ffn1_w1", [L, D, 2 * DFF]), ("ffn1_w2", [L, DFF, D]),
                        ("w_in", [L, D, 3504]), ("mla_w_qb", [L, 384, 768]), ("mla_w_kvb", [L, 256, 1024]),
                        ("w_out", [L, 1536, D]), ("ffn2_w1", [L, D, 2 * DFF]), ("ffn2_w2", [L, DFF, D]),
                        ("masks", [128, 5, 128]), ("amasks", [128, 5, 128]), ("rope_m", [96, 2, NST]), ("rope_s", [64, 2, NST])]:
            W[nm] = self.din(nm, shp)
        self.W = W
        O = {}
        O["yp"] = self.dout("yp", [NPT, D])
        O["ys"] = self.dout("ys", [NST, D])
        O["n_ckv"] = self.dout("n_ckv", [2, L, 256, 256])
        O["n_kr"] = self.dout("n_kr", [2, L, 256, 32])
        O["n_sk"] = self.dout("n_sk", [2, L, 256, 128])
        O["n_sv"] = self.dout("n_sv", [2, L, 256, 128])
        O["n_st"] = self.dout("n_st", [2, L, 2, 4, 128, 128])
        self.O = O
        self.dbg = {}

        self.xT = {"p": k.dram("xTp", [128, 8, NPT]), "s": k.dram("xTs", [128, 8, NST])}
        self.ntok = {"p": NPT, "s": NST}
        self.blocks = [("p", 0)] + [("s", i * 512) for i in range(4)]

        self.masks = k.sb("masks", [128, 5, 128], F32, glob=True)
        k.dma("sp", self.masks[:], W["masks"][:], writes=[self.masks])
        self.ident = self.masks.t[:, 0, :]
        self.ones_f = k.sb("ones_f", [128, 128], F32, glob=True)
        self.ones_b = k.sb("ones_b", [128, 128], BF16, glob=True)
        k.op("dve", lambda e: e.memset(self.ones_f[:], 1.0), writes=[self.ones_f])
        k.op("dve", lambda e: e.memset(self.ones_b[:], 1.0), writes=[self.ones_b])
        self.epsc = k.sb("epsc", [128, 1], F32, glob=True)
        k.op("dve", lambda e: e.memset(self.epsc[:], EPS), writes=[self.epsc])
        self.masks_b = k.sb("masks_b", [128, 5, 128], BF16, glob=True)
        k.op("dve", lambda e: e.tensor_copy(out=self.masks_b[:], in_=self.masks[:]), reads=[self.masks], writes=[self.masks_b])
        self.vecT = k.sb("vecT", [128, 82], F32, glob=True)
        self.badaT = k.sb("badaT", [128, L, 72], F32, glob=True)
        self.convT = k.sb("convT", [128, 120], F32, glob=True)
        self.rowbc = k.sb("rowbc", [128, 304 * L], F32, glob=True)
        self.modT = k.sb("modT", [128, L, 72, 2], F32, glob=True)
        self.modA = k.sb("modA", [128, L, 2, 3, 8], F32, glob=True)
        self.modG = k.sb("modG", [128, L, 2, 3, 8], F32, glob=True)
        self.hT = {"p": k.sb("hTp", [128, 8, NPT], BF16, glob=True), "s": k.sb("hTs", [128, 8, NST], BF16, glob=True)}
        stages = [("setup", self.setup_small), ("adaln", self.adaln), ("load", self.load_inputs)]
        for l in range(L):
            stages.append((f"ffn1_{l}", lambda l=l: self.ffn(l, 0)))
            for path in ("p", "s"):
                stages.append((f"mla_{l}{path}", lambda l=l, path=path: self.mla(l, path)))
                stages.append((f"swa_{l}{path}", lambda l=l, path=path: self.swa(l, path)))
                stages.append((f"gdn_{l}{path}", lambda l=l, path=path: self.gdn(l, path)))
            stages.append((f"ffn2_{l}", lambda l=l: self.ffn(l, 2)))
        stages.append(("final", self.final))
        for nm, fn in stages:
            if SKIP and nm in SKIP:
                continue
            fn()
            if STOP and nm == STOP:
                break
        if DEBUG:
            k.barrier()
            for nm in DEBUG:
                if nm in ("xTp", "xTs"):
                    src = self.xT[nm[-1]]
                    o = self.dout("dbg_" + nm, [128, 8, self.ntok[nm[-1]]])
                    for t0 in range(0, self.ntok[nm[-1]], 512):
                        k.dma("sp", o[:, :, t0:t0 + 512], src[:, :, t0:t0 + 512], reads=[src], writes=[o])
                elif nm in ("hTp", "hTs"):
                    src = self.hT[nm[-1]]
                    o = T(self.nc.dram_tensor("dbg_" + nm, [128, 8, self.ntok[nm[-1]]], BF16, kind="ExternalOutput"), "dbg_" + nm)
                    k.dma("sp", o[:, :, :], src[:, :, :], reads=[src], writes=[o])
                elif nm == "modT":
                    o = self.dout("dbg_modT", [128, L * 72 * 2])
                    k.dma("sp", o[:, :], self.modT[:, :, :, :].rearrange("p l c r -> p (l c r)"), reads=[self.modT], writes=[o])
        k.barrier()
        return self.nc

    def setup_small(self):
        k = self.k
        W = self.W
        k.begin_phase()
        stage = k.sb("stage", [128, 128], F32)
        for (src, nrow, dst) in [(W["vecA"][:, :], 82, self.vecT[:, :]), (W["b_ada"][0], 72, self.badaT[:, 0, :]),
                                 (W["b_ada"][1], 72, self.badaT[:, 1, :]), (W["convw"][:, :], 120, self.convT[:, :])]:
            k.dma("sp", stage[0:nrow, :], src, writes=[stage])
            ps = k.psum()
            k.tr(ps, ps[:, 0:nrow], stage[0:nrow, :], self.ident[0:nrow, 0:nrow], reads=[stage, self.masks])
            k.op("dve", lambda e, dst=dst, ps=ps, nrow=nrow: e.tensor_copy(out=dst, in_=ps[:, 0:nrow]), reads=[ps],
                 writes=[self.vecT, self.badaT, self.convT])
        row = k.sb("rowstage", [1, 304 * L], F32)
        k.dma("sp", row[:], W["rowv"][:, :], writes=[row])
        for i in range(0, 304 * L, 512):
            n = min(512, 304 * L - i)
            ps = k.psum()
            k.mm(ps, ps[:, 0:n], self.ones_f[0:1, :], row[0:1, i:i + n], True, True, reads=[self.ones_f, row])
            k.op("dve", lambda e, ps=ps, i=i, n=n: e.tensor_copy(out=self.rowbc[:, i:i + n], in_=ps[:, 0:n]), reads=[ps], writes=[self.rowbc])
        k.end_phase()

    def rowv(self, l, what):
        o = {"sink": (0, 8), "alog": (8, 16), "dtb": (16, 24), "gnorm": (24, 152)}[what]
        return self.rowbc.t[:, l * 304 + o[0]: l * 304 + o[1]]

    def adaln(self):
        k = self.k
        W = self.W
        k.begin_phase()
        condT = k.sb("condT", [128, 8, 2], BF16)
        tmp = k.sb("condtmp", [128, 16], F32)
        k.op("act", lambda e: e.activation(out=tmp[:], in_=self.vecT[:, 56:72], func=AF.Silu), reads=[self.vecT], writes=[tmp])
        for r in range(2):
            k.op("dve", lambda e, r=r: e.tensor_copy(out=condT[:, :, r], in_=tmp[:, r * 8:(r + 1) * 8]), reads=[tmp], writes=[condT])
        wb = [k.sb(f"wada{i}", [128, 8, 512], BF16) for i in range(2)]
        for l in range(L):
            psm = k.psum()
            for g in range(18):
                w = wb[g % 2]
                k.dma("pool", w[:], W["w_ada"][l].rearrange("(kc p) n -> p kc n", p=128)[:, :, g * 512:(g + 1) * 512], writes=[w])
                for j in range(4):
                    cc = g * 4 + j
                    for kc in range(8):
                        k.mm(psm, psm[:, cc * 2:cc * 2 + 2], w[:, kc, j * 128:(j + 1) * 128], condT[:, kc, :], kc == 0, kc == 7,
                             reads=[w, condT])
            for r in range(2):
                k.op("dve", lambda e, l=l, r=r, psm=psm: e.tensor_tensor(
                    out=self.modT[:, l, :, r], in0=psm[:, 0:144].rearrange("p (c r) -> p c r", r=2)[:, :, r],
                    in1=self.badaT[:, l, :], op=ALU.add), reads=[psm, self.badaT], writes=[self.modT])
        for l in range(L):
            for r in range(2):
                for s in range(3):
                    gain = self.vecT[:, [0, 16, 32][s] + l * 8:[0, 16, 32][s] + l * 8 + 8]
                    sc = self.modT[:, l, (3 * s + 1) * 8:(3 * s + 2) * 8, r]
                    k.op("dve", lambda e, l=l, r=r, s=s, gain=gain, sc=sc: e.scalar_tensor_tensor(
                        out=self.modA[:, l, r, s, :], in0=sc, scalar=1.0, in1=gain, op0=ALU.add, op1=ALU.mult),
                        reads=[self.modT, self.vecT], writes=[self.modA])
                    gt = self.modT[:, l, (3 * s + 2) * 8:(3 * s + 3) * 8, r]
                    k.op("dve", lambda e, l=l, r=r, s=s, gt=gt: e.tensor_scalar(
                        out=self.modG[:, l, r, s, :], in0=gt, scalar1=(1.0 if s == 1 else 0.5), scalar2=None, op0=ALU.mult),
                        reads=[self.modT], writes=[self.modG])
        k.end_phase()

    def mod_shift(self, l, r, s, kc):
        c = (3 * s) * 8 + kc
        return self.modT.t[:, l, c:c + 1, r]

    def load_inputs(self):
        k = self.k
        k.begin_phase()
        xin = [k.sb(f"xin{i}", [128, D], F32) for i in range(2)]
        xblk = [k.sb(f"xblk{i}", [128, 8, 512], F32) for i in range(2)]
        bi = 0
        for (path, t0) in self.blocks:
            src = self.W["xp"] if path == "p" else self.W["xs"]
            xb = xblk[bi % 2]
            for tt in range(4):
                xi = xin[tt % 2]
                k.dma("sp", xi[:], src[t0 + tt * 128:t0 + (tt + 1) * 128, :], writes=[xi])
                for half in range(2):
                    ps = k.psum()
                    for j in range(4):
                        kc = half * 4 + j
                        k.tr(ps, ps[:, j * 128:(j + 1) * 128], xi[:, kc * 128:(kc + 1) * 128], self.ident, reads=[xi, self.masks])
                    eng = "dve" if half == 0 else "act"
                    if eng == "dve":
                        k.op("dve", lambda e, xb=xb, ps=ps, half=half, tt=tt: e.tensor_copy(
                            out=xb[:, half * 4:half * 4 + 4, tt * 128:(tt + 1) * 128],
                            in_=ps[:, :].rearrange("p (j t) -> p j t", j=4)), reads=[ps], writes=[xb])
                    else:
                        k.op("act", lambda e, xb=xb, ps=ps, half=half, tt=tt: e.activation(
                            out=xb[:, half * 4:half * 4 + 4, tt * 128:(tt + 1) * 128],
                            in_=ps[:, :].rearrange("p (j t) -> p j t", j=4), func=AF.Copy), reads=[ps], writes=[xb])
            k.dma("sp", self.xT[path][:, :, t0:t0 + 512], xb[:], reads=[xb], writes=[self.xT[path].sub(t0)])
            bi += 1
        k.end_phase()

    def rmsnorm(self, x, n, Acol, Bcol, out, sq, rstd, nfeat_chunks=8, nfeat=D, tmp=None):
        k = self.k
        xs, xr = x
        outf, outw = out
        for kc in range(nfeat_chunks):
            k.op("act", lambda e, kc=kc: e.activation(out=sq[:, kc, 0:n], in_=xs(kc), func=AF.Square), reads=xr, writes=[sq])
        ps = k.psum()
        for kc in range(nfeat_chunks):
            k.mm(ps, ps[:, 0:n], self.ones_b[:, :], sq[:, kc, 0:n], kc == 0, kc == nfeat_chunks - 1, reads=[self.ones_b, sq])
        k.op("act", lambda e: e.activation(out=rstd[:, 0:n], in_=ps[:, 0:n], func=AF.Sqrt, bias=self.epsc[:, 0:1], scale=1.0 / nfeat),
             reads=[ps, self.epsc], writes=[rstd])
        k.op("dve", lambda e: e.reciprocal(out=rstd[:, 0:n], in_=rstd[:, 0:n]), reads=[rstd], writes=[rstd])
        for kc in range(nfeat_chunks):
            if Bcol is None:
                k.op("dve", lambda e, kc=kc: e.scalar_tensor_tensor(out=outf(kc), in0=xs(kc), scalar=Acol(kc), in1=rstd[:, 0:n],
                                                                     op0=ALU.mult, op1=ALU.mult), reads=list(xr) + [rstd], writes=outw)
            else:
                k.op("dve", lambda e, kc=kc: e.scalar_tensor_tensor(out=tmp[:, kc, 0:n], in0=xs(kc), scalar=Acol(kc), in1=rstd[:, 0:n],
                                                                     op0=ALU.mult, op1=ALU.mult), reads=list(xr) + [rstd], writes=[tmp])
                k.op("act", lambda e, kc=kc: e.activation(out=outf(kc), in_=tmp[:, kc, 0:n], func=AF.Identity, bias=Bcol(kc), scale=1.0),
                     reads=[tmp], writes=outw)

    def ffn(self, l, s):
        k = self.k
        W = self.W
        w1 = W["ffn1_w1" if s == 0 else "ffn2_w1"][l].rearrange("(kc p) n -> p kc n", p=128)
        w2 = W["ffn1_w2" if s == 0 else "ffn2_w2"][l].rearrange("(j p) n -> p j n", p=128)
        k.begin_phase()
        xb = k.sb("f_x", [128, 8, 512], F32)
        hb = k.sb("f_h", [128, 8, 512], BF16)
        tmp = k.sb("f_tmp", [128, 8, 512], F32)
        act = k.sb("f_act", [128, 22, 512], BF16)
        sq = k.sb("f_sq", [128, 8, 512], BF16)
        rstd = k.sb("f_rstd", [128, 512], F32)
        sg = [k.sb(f"f_sg{i}", [128, 512], F32) for i in range(2)]
        w1b = [k.sb(f"f_w1{i}", [128, 8, 512], BF16) for i in range(2)]
        w2b = [k.sb(f"f_w2{i}", [128, 22, 512], BF16) for i in range(2)]
        nw1 = 0
        nw2 = 0
        w1c = k.dram(f"w1c_{l}_{s}", [128, 11, 8, 512], BF16)
        w2c = k.dram(f"w2c_{l}_{s}", [128, 2, 22, 512], BF16)
        for bi, (path, t0) in enumerate(self.blocks):
            r = 0 if path == "p" else 1
            xsub = self.xT[path].sub(t0)
            k.dma("sp", xb[:], self.xT[path][:, :, t0:t0 + 512], reads=[xsub], writes=[xb])
            self.rmsnorm((lambda kc: xb[:, kc, :], [xb]), 512,
                         lambda kc: self.modA[:, l, r, s, kc:kc + 1], lambda kc: self.mod_shift(l, r, s, kc),
                         (lambda kc: hb[:, kc, :], [hb]), sq, rstd, tmp=tmp)
            for jg in range(11):
                wt = w1b[nw1 % 2]
                nw1 += 1
                if bi == 0:
                    k.dma("pool", wt[:, :, 0:256], w1[:, :, jg * 256:(jg + 1) * 256], writes=[wt])
                    k.dma("pool", wt[:, :, 256:512], w1[:, :, DFF + jg * 256:DFF + (jg + 1) * 256], writes=[wt])
                    k.dma("sp", w1c[:, jg], wt[:], reads=[wt], writes=[w1c.sub(jg)])
                else:
                    k.dma("sp", wt[:], w1c[:, jg], reads=[w1c.sub(jg)], writes=[wt])
                for jj in range(2):
                    j = jg * 2 + jj
                    pg = k.psum()
                    pu = k.psum()
                    for kc in range(8):
                        k.mm(pg, pg[:, :], wt[:, kc, jj * 128:(jj + 1) * 128], hb[:, kc, :], kc == 0, kc == 7, reads=[wt, hb])
                    for kc in range(8):
                        k.mm(pu, pu[:, :], wt[:, kc, 256 + jj * 128:256 + (jj + 1) * 128], hb[:, kc, :], kc == 0, kc == 7, reads=[wt, hb])
                    sgt = sg[j % 2]
                    k.op("act", lambda e, sgt=sgt, pg=pg: e.activation(out=sgt[:], in_=pg[:, :], func=AF.Silu), reads=[pg], writes=[sgt])
                    k.op("dve", lambda e, sgt=sgt, pu=pu, j=j: e.tensor_tensor(out=act[:, j, :], in0=pu[:, :], in1=sgt[:], op=ALU.mult),
                         reads=[pu, sgt], writes=[act.sub(j)])
            for half in range(2):
                wt = w2b[nw2 % 2]
                nw2 += 1
                if bi == 0:
                    for q4 in range(2):
                        k.dma("pool", wt[:, q4 * 11:(q4 + 1) * 11, :], w2[:, q4 * 11:(q4 + 1) * 11, half * 512:(half + 1) * 512], writes=[wt])
                    k.dma("sp", w2c[:, half], wt[:], reads=[wt], writes=[w2c.sub(half)])
                else:
                    k.dma("sp", wt[:], w2c[:, half], reads=[w2c.sub(half)], writes=[wt])
                for f4 in range(4):
                    fc = half * 4 + f4
                    po = k.psum()
                    for j in range(22):
                        k.mm(po, po[:, :], wt[:, j, f4 * 128:(f4 + 1) * 128], act[:, j, :], j == 0, j == 21, reads=[wt, act.sub(j)])
                    k.op("dve", lambda e, po=po, fc=fc: e.scalar_tensor_tensor(
                        out=xb[:, fc, :], in0=po[:, :], scalar=self.modG[:, l, r, s, fc:fc + 1], in1=xb[:, fc, :],
                        op0=ALU.mult, op1=ALU.add), reads=[po, xb, self.modG], writes=[xb])
            k.dma("sp", self.xT[path][:, :, t0:t0 + 512], xb[:], reads=[xb], writes=[xsub])
            if s == 0:
                hT = self.hT[path]
                self.rmsnorm((lambda kc: xb[:, kc, :], [xb]), 512,
                             lambda kc: self.modA[:, l, r, 1, kc:kc + 1], lambda kc: self.mod_shift(l, r, 1, kc),
                             (lambda kc: hT[:, kc, t0:t0 + 512], [hT.sub(t0)]), sq, rstd, tmp=tmp)
        k.end_phase()

    def final(self):
        k = self.k
        k.begin_phase()
        xb = k.sb("o_x", [128, 8, 512], F32)
        yb = k.sb("o_y", [128, 8, 512], F32)
        sq = k.sb("o_sq", [128, 8, 512], BF16)
        rstd = k.sb("o_rstd", [128, 512], F32)
        yo = [k.sb(f"o_yo{i}", [128, D], F32) for i in range(2)]
        n = 0
        for (path, t0) in self.blocks:
            k.dma("sp", xb[:], self.xT[path][:, :, t0:t0 + 512], reads=[self.xT[path].sub(t0)], writes=[xb])
            self.rmsnorm((lambda kc: xb[:, kc, :], [xb]), 512, lambda kc: self.vecT[:, 48 + kc:49 + kc], None,
                         (lambda kc: yb[:, kc, :], [yb]), sq, rstd)
            dst = self.O["yp"] if path == "p" else self.O["ys"]
            for tt in range(4):
                y = yo[n % 2]
                n += 1
                for half in range(2):
                    ps = k.psum()
                    for j in range(4):
                        kc = half * 4 + j
                        k.tr(ps, ps[:, j * 128:(j + 1) * 128], yb[:, kc, tt * 128:(tt + 1) * 128], self.ident, reads=[yb, self.masks])
                    if half == 0:
                        k.op("dve", lambda e, y=y, ps=ps: e.tensor_copy(out=y[:, 0:512], in_=ps[:, :]), reads=[ps], writes=[y])
                    else:
                        k.op("act", lambda e, y=y, ps=ps: e.activation(out=y[:, 512:1024], in_=ps[:, :], func=AF.Copy), reads=[ps], writes=[y])
                k.dma("sp", dst[t0 + tt * 128:t0 + (tt + 1) * 128, :], y[:], reads=[y], writes=[dst])
        k.end_phase()

    ao_k = 128

    def apply_out(self, l, path, t0, n, mixT, nchunks, wout_t, chunk0):
        k = self.k
        r = 0 if path == "p" else 1
        mixf, mixr = mixT
        xb = self.ao_x
        xsub = self.xT[path].sub((t0 // 512) * 512)
        k.dma("sp", xb[:, :, 0:n], self.xT[path][:, :, t0:t0 + n], reads=[xsub], writes=[xb])
        for fc in range(8):
            po = k.psum()
            for c in range(nchunks):
                k.mm(po, po[:, 0:n], wout_t[0:self.ao_k, chunk0 + c, fc * 128:(fc + 1) * 128], mixf(c), c == 0, c == nchunks - 1,
                     reads=[wout_t] + list(mixr))
            k.op("dve", lambda e, po=po, fc=fc: e.scalar_tensor_tensor(
                out=xb[:, fc, 0:n], in0=po[:, 0:n], scalar=self.modG[:, l, r, 1, fc:fc + 1], in1=xb[:, fc, 0:n],
                op0=ALU.mult, op1=ALU.add), reads=[po, xb, self.modG], writes=[xb])
        k.dma("sp", self.xT[path][:, :, t0:t0 + n], xb[:, :, 0:n], reads=[xb], writes=[xsub])

    def store_tok(self, srcf, srcr, nfeat, ntok, dst_fn, f0=0):
        k = self.k
        for tt in range(ntok // 128):
            ps = k.psum()
            k.tr(ps, ps[:, 0:nfeat], srcf(tt), self.ident[0:nfeat, 0:nfeat], reads=list(srcr) + [self.masks])
            st = self.st_buf[self.st_n % 2]
            self.st_n += 1
            k.op("act", lambda e, st=st, ps=ps: e.activation(out=st[:, 0:nfeat - f0], in_=ps[:, f0:nfeat], func=AF.Copy), reads=[ps], writes=[st])
            k.dma("sp", dst_fn(tt), st[:, 0:nfeat - f0], reads=[st], writes=[])

    def mixer(self, l, path):
        self.mla(l, path)
        self.swa(l, path)
        self.gdn(l, path)

    def seqs(self, path):
        return [(0, 256), (256, 256)] if path == "p" else [(0, NST)]

    def softmax_bound(self, q2max, k2max, scale, negc):
        k = self.k
        t = self.sm_t
        k.op("dve", lambda e: e.tensor_tensor(out=t[0:1, 0:1], in0=q2max, in1=k2max, op=ALU.add), reads=[self.sm_q, self.sm_k], writes=[t])
        k.op("dve", lambda e: e.tensor_scalar(out=t[0:1, 0:1], in0=t[0:1, 0:1], scalar1=-0.5 * scale, scalar2=None, op0=ALU.mult),
             reads=[t], writes=[t])
        ps = k.psum()
        k.mm(ps, ps[:, 0:1], self.ones_f[0:1, :], t[0:1, 0:1], True, True, reads=[self.ones_f, t])
        k.op("dve", lambda e: e.tensor_copy(out=negc[:, 0:1], in_=ps[:, 0:1]), reads=[ps], writes=[negc])

    def sq_max(self, src_ap, nrow, n, dst, first, reads, dstT=None):
        k = self.k
        sqt = self.sm_sq
        k.op("pool", lambda e: e.tensor_tensor(out=sqt[0:nrow, 0:n], in0=src_ap, in1=src_ap, op=ALU.mult), reads=reads, writes=[sqt])
        ps = k.psum()
        k.mm(ps, ps[0:1, 0:n], self.ones_b[0:nrow, 0:1], sqt[0:nrow, 0:n], True, True, reads=[self.ones_b, sqt])
        tm = self.sm_m
        if dstT is None:
            dstT = self.sm_k if n_is_k(dst, self) else self.sm_q
        if first:
            k.op("dve", lambda e: e.reduce_max(out=dst, in_=ps[0:1, 0:n], axis=mybir.AxisListType.X), reads=[ps], writes=[dstT])
        else:
            k.op("dve", lambda e: e.reduce_max(out=tm[0:1, 0:1], in_=ps[0:1, 0:n], axis=mybir.AxisListType.X), reads=[ps], writes=[tm])
            k.op("dve", lambda e: e.tensor_tensor(out=dst, in0=dst, in1=tm[0:1, 0:1], op=ALU.max), reads=[tm, dstT], writes=[dstT])

    def mla(self, l, path):
        k = self.k
        W = self.W
        smp = path == "s"
        hT = self.hT[path]
        scale = 96 ** -0.5
        k.begin_phase()
        self.ao_x = k.sb("ao_x", [128, 8, 512], F32)
        self.st_buf = [k.sb(f"st{i}", [128, 256], F32) for i in range(2)]
        self.st_n = 0
        self.sm_t = k.sb("sm_t", [1, 1], F32)
        self.sm_q = k.sb("sm_q", [1, 1], F32)
        self.sm_k = k.sb("sm_k", [1, 8], F32)
        self.sm_m = k.sb("sm_m", [1, 1], F32)
        self.sm_sq = k.sb("sm_sq", [96, 512], BF16)
        negc = k.sb("negc", [128, 1], F32)
        win = k.sb("m_win", [128, 8, 672], BF16)
        k.dma("pool", win[:], W["w_in"][l].rearrange("(kc p) n -> p kc n", p=128)[:, :, 0:672], writes=[win])
        wqb = k.sb("m_wqb", [128, 3, 768], BF16)
        k.dma("pool", wqb[:], W["mla_w_qb"][l].rearrange("(kc p) n -> p kc n", p=128), writes=[wqb])
        wkvb = k.sb("m_wkvb", [128, 2, 1024], BF16)
        k.dma("pool", wkvb[:], W["mla_w_kvb"][l].rearrange("(kc p) n -> p kc n", p=128), writes=[wkvb])
        wout = k.sb("m_wout", [64, 4, D], BF16)
        wkr = k.sb("m_wkr", [128, 8, 96], BF16)
        k.op("pool", lambda e: e.memset(wkr[:], 0.0), writes=[wkr])
        k.op("pool", lambda e: e.tensor_copy(out=wkr[:, :, 64:96], in_=win[:, :, 640:672]), reads=[win], writes=[wkr])
        if smp:
            rope = k.sb("m_rope", [96, 2, NST], F32)
            k.dma("sp", rope[:], W["rope_m"][:, :, :], writes=[rope])
            wkr_r = k.sb("m_wkr_r", [128, 8, 96], BF16)
            k.op("pool", lambda e: e.memset(wkr_r[:], 0.0), writes=[wkr_r])
            wqb_r = k.sb("m_wqb_r", [128, 3, 768], BF16)
            k.op("pool", lambda e: e.memset(wqb_r[:], 0.0), writes=[wqb_r])
            for a in range(2):
                b0 = 64 + a * 16
                k.op("pool", lambda e, b0=b0: e.tensor_scalar(out=wkr_r[:, :, b0:b0 + 8], in0=wkr[:, :, b0 + 8:b0 + 16], scalar1=-1.0,
                                                               scalar2=None, op0=ALU.mult), reads=[wkr], writes=[wkr_r])
                k.op("pool", lambda e, b0=b0: e.tensor_copy(out=wkr_r[:, :, b0 + 8:b0 + 16], in_=wkr[:, :, b0:b0 + 8]), reads=[wkr], writes=[wkr_r])
                for h in range(8):
                    c0 = h * 96 + b0
                    k.op("pool", lambda e, c0=c0: e.tensor_scalar(out=wqb_r[:, :, c0:c0 + 8], in0=wqb[:, :, c0 + 8:c0 + 16], scalar1=-1.0,
                                                                   scalar2=None, op0=ALU.mult), reads=[wqb], writes=[wqb_r])
                    k.op("pool", lambda e, c0=c0: e.tensor_copy(out=wqb_r[:, :, c0 + 8:c0 + 16], in_=wqb[:, :, c0:c0 + 8]), reads=[wqb], writes=[wqb_r])
        sq = k.sb("m_sq", [128, 3, 512], BF16)
        rstd = k.sb("m_rstd", [128, 512], F32)
        cf = k.sb("m_cf", [128, 3, 512], F32)
        krf = k.sb("m_krf", [96, 512], F32)
        krb = k.sb("m_krb", [96, 512], F32)
        qtmp, qtmp2 = krf, krb
        qTs = [k.sb(f"m_qT{i}", [96, 512], BF16) for i in range(4)]
        negcs = [k.sb(f"m_negc{i}", [128, 1], F32) for i in range(4)]
        smqs = [k.sb(f"m_smq{i}", [1, 1], F32) for i in range(4)]
        smts = [k.sb(f"m_smt{i}", [1, 1], F32) for i in range(4)]
        qT = qTs[0]
        PT = [k.sb(f"m_PT{i}", [128, 512], BF16) for i in range(3)]
        mixH = k.sb("m_mixH", [64, 4, 512], BF16)
        rrow = k.sb("m_rrow", [65, 512], F32)
        rbc = k.sb("m_rbc", [64, 512], F32)
        for (s0, T_) in self.seqs(path):
            nk = T_ + (PAST if smp else 0)
            nkt = nk // 128
            cqn = k.sb(f"m_cqn{s0}", [128, 3, T_], BF16)
            ckvn = k.sb(f"m_ckvn{s0}", [128, 2, nk], BF16)
            Kt = k.sb(f"m_Kt{s0}", [96, 4, nk], BF16)
            Vt = k.sb(f"m_Vt{s0}", [128, nkt, 4, 65], BF16)
            krall = k.sb(f"m_krall{s0}", [96, nk], BF16)
            k.op("pool", lambda e: e.memset(Vt[:], 1.0), writes=[Vt])
            groups = [(g0, min(512, T_ - g0)) for g0 in range(0, T_, 512)]
            seq_i = s0 // 256
            for (g0, n) in groups:
                a0 = s0 + g0
                for (c0, nch, gcol, dst, nfeat) in [(0, 3, 72, cqn, 384), (384, 2, 78, ckvn, 256)]:
                    pss = []
                    for c in range(nch):
                        ps = k.psum()
                        pss.append(ps)
                        for kc in range(8):
                            k.mm(ps, ps[:, 0:n], win[:, kc, c0 + c * 128:c0 + (c + 1) * 128], hT[:, kc, a0:a0 + n], kc == 0, kc == 7,
                                 reads=[win, hT.sub((a0 // 512) * 512)])
                    for c in range(nch):
                        k.op("act", lambda e, c=c, ps=pss[c]: e.activation(out=cf[:, c, 0:n], in_=ps[:, 0:n], func=AF.Copy), reads=[pss[c]], writes=[cf])
                    gcl = gcol + l * nch
                    self.rmsnorm((lambda c: cf[:, c, 0:n], [cf]), n, lambda c: self.vecT[:, gcl + c:gcl + c + 1], None,
                                 (lambda c: cf[:, c, 0:n], [cf]), sq, rstd, nfeat_chunks=nch, nfeat=nfeat)
                    for c in range(nch):
                        k.op("pool", lambda e, c=c, dst=dst: e.tensor_copy(out=dst[:, c, g0:g0 + n], in_=cf[:, c, 0:n]), reads=[cf], writes=[dst])
                    if dst is ckvn and not smp:
                        for c in range(2):
                            self.store_tok(lambda tt, c=c: cf[:, c, tt * 128:(tt + 1) * 128], [cf], 128, n,
                                           lambda tt, c=c: self.O["n_ckv"][seq_i, l, g0 + tt * 128:g0 + (tt + 1) * 128, c * 128:(c + 1) * 128])
                pa = k.psum()
                for kc in range(8):
                    k.mm(pa, pa[0:96, 0:n], wkr[:, kc, :], hT[:, kc, a0:a0 + n], kc == 0, kc == 7, reads=[wkr, hT.sub((a0 // 512) * 512)])
                if smp:
                    pb = k.psum()
                    for kc in range(8):
                        k.mm(pb, pb[0:96, 0:n], wkr_r[:, kc, :], hT[:, kc, a0:a0 + n], kc == 0, kc == 7, reads=[wkr_r, hT.sub((a0 // 512) * 512)])
                    k.op("dve", lambda e: e.tensor_tensor(out=krf[64:96, 0:n], in0=pa[64:96, 0:n], in1=rope[64:96, 0, g0:g0 + n], op=ALU.mult),
                         reads=[pa, rope], writes=[krf])
                    k.op("dve", lambda e: e.tensor_tensor(out=krb[64:96, 0:n], in0=pb[64:96, 0:n], in1=rope[64:96, 1, g0:g0 + n], op=ALU.mult),
                         reads=[pb, rope], writes=[krb])
                    k.op("pool", lambda e: e.tensor_tensor(out=krall[64:96, g0:g0 + n], in0=krf[64:96, 0:n], in1=krb[64:96, 0:n], op=ALU.add),
                         reads=[krf, krb], writes=[krall])
                else:
                    k.op("dve", lambda e: e.tensor_copy(out=krf[0:96, 0:n], in_=pa[0:96, 0:n]), reads=[pa], writes=[krf])
                    k.op("pool", lambda e: e.tensor_copy(out=krall[64:96, g0:g0 + n], in_=krf[64:96, 0:n]), reads=[krf], writes=[krall])
                    self.store_tok(lambda tt: krf[0:96, tt * 128:(tt + 1) * 128], [krf], 96, n,
                                   lambda tt: self.O["n_kr"][seq_i, l, g0 + tt * 128:g0 + (tt + 1) * 128, :], f0=64)
            if smp:
                kin = k.sb("m_kin", [128, 96], F32)
                k.op("pool", lambda e: e.memset(kin[:], 0.0), writes=[kin])
                for i in range(4):
                    ci = cf.t[:, i % 2, 0:256]
                    k.dma("sp", ci, W["c_ckv"][l, i * 128:(i + 1) * 128, :], writes=[cf])
                    ps = k.psum()
                    for c in range(2):
                        k.tr(ps, ps[:, c * 128:(c + 1) * 128], ci[:, c * 128:(c + 1) * 128], self.ident, reads=[cf, self.masks])
                    k.op("act", lambda e, ps=ps, i=i: e.activation(out=ckvn[:, :, T_ + i * 128:T_ + (i + 1) * 128],
                                                                   in_=ps[:, 0:256].rearrange("p (c t) -> p c t", c=2), func=AF.Copy),
                         reads=[ps], writes=[ckvn])
                    k.dma("sp", kin[:, 64:96], W["c_kr"][l, i * 128:(i + 1) * 128, :], writes=[kin])
                    ps2 = k.psum()
                    k.tr(ps2, ps2[0:96, 0:128], kin[:, :], self.ident, reads=[kin, self.masks])
                    k.op("dve", lambda e, ps2=ps2, i=i: e.tensor_copy(out=krall[64:96, T_ + i * 128:T_ + (i + 1) * 128], in_=ps2[64:96, 0:128]),
                         reads=[ps2], writes=[krall])
            kgroups = [(g0, min(512, nk - g0)) for g0 in range(0, nk, 512)]
            for hg in range(2):
                k.dma("pool", wout[:], W["w_out"][l][hg * 256:(hg + 1) * 256, :].rearrange("(h p) n -> p h n", p=64), writes=[wout])
                for (g0, n) in kgroups:
                    for hh in range(4):
                        h = hg * 4 + hh
                        ps = k.psum()
                        for c in range(2):
                            k.mm(ps, ps[0:64, 0:n], wkvb[:, c, h * 128:h * 128 + 64], ckvn[:, c, g0:g0 + n], c == 0, c == 1, reads=[wkvb, ckvn])
                        eng = "act" if hh % 2 == 0 else "dve"
                        if eng == "act":
                            k.op("act", lambda e, ps=ps, hh=hh: e.activation(out=Kt[0:64, hh, g0:g0 + n], in_=ps[0:64, 0:n], func=AF.Copy),
                                 reads=[ps], writes=[Kt.sub(hh)])
                        else:
                            k.op("dve", lambda e, ps=ps, hh=hh: e.tensor_copy(out=Kt[0:64, hh, g0:g0 + n], in_=ps[0:64, 0:n]), reads=[ps], writes=[Kt.sub(hh)])
                        if hg == 0:
                            pass
                        k.op("pool", lambda e, hh=hh: e.tensor_copy(out=Kt[64:96, hh, g0:g0 + n], in_=krall[64:96, g0:g0 + n]),
                             reads=[krall], writes=[Kt.sub(hh)])
                        self.sq_max(Kt[0:96, hh, g0:g0 + n], 96, n, self.sm_k[0:1, hh:hh + 1], g0 == 0, [Kt.sub(hh)])
                    for kt in range(g0 // 128, (g0 + n) // 128):
                        ps = k.psum()
                        for c in range(2):
                            k.mm(ps, ps[:, 0:256].rearrange("p (h d) -> p h d", h=4),
                                 ckvn[:, c, kt * 128:(kt + 1) * 128],
                                 wkvb[:, c, hg * 512:(hg + 1) * 512].rearrange("p (h d) -> p h d", h=4)[:, :, 64:128], c == 0, c == 1,
                                 reads=[wkvb, ckvn])
                        k.op("act", lambda e, ps=ps, kt=kt: e.activation(out=Vt[:, kt, :, 0:64], in_=ps[:, 0:256].rearrange("p (h d) -> p h d", h=4),
                                                                         func=AF.Copy), reads=[ps], writes=[Vt])
                npt = 0
                for (g0, n) in groups:
                    nqt = n // 128
                    for hh in range(4):
                        h = hg * 4 + hh
                        pa = k.psum()
                        for c in range(3):
                            k.mm(pa, pa[0:96, 0:n], wqb[:, c, h * 96:(h + 1) * 96], cqn[:, c, g0:g0 + n], c == 0, c == 2, reads=[wqb, cqn])
                        if smp:
                            pb = k.psum()
                            for c in range(3):
                                k.mm(pb, pb[0:96, 0:n], wqb_r[:, c, h * 96:(h + 1) * 96], cqn[:, c, g0:g0 + n], c == 0, c == 2, reads=[wqb_r, cqn])
                            k.op("dve", lambda e: e.tensor_tensor(out=qtmp[:, 0:n], in0=pa[0:96, 0:n], in1=rope[:, 0, g0:g0 + n], op=ALU.mult),
                                 reads=[pa, rope], writes=[qtmp])
                            k.op("dve", lambda e: e.tensor_tensor(out=qtmp2[:, 0:n], in0=pb[0:96, 0:n], in1=rope[:, 1, g0:g0 + n], op=ALU.mult),
                                 reads=[pb, rope], writes=[qtmp2])
                            k.op("pool", lambda e: e.tensor_tensor(out=qTs[hh][:, 0:n], in0=qtmp[:, 0:n], in1=qtmp2[:, 0:n], op=ALU.add),
                                 reads=[qtmp, qtmp2], writes=[qTs[hh]])
                        else:
                            k.op("act", lambda e: e.activation(out=qTs[hh][:, 0:n], in_=pa[0:96, 0:n], func=AF.Copy), reads=[pa], writes=[qTs[hh]])
                        self.sm_q = smqs[hh]; self.sm_t = smts[hh]
                        self.sq_max(qTs[hh][0:96, 0:n], 96, n, self.sm_q[0:1, 0:1], True, [qTs[hh]], dstT=self.sm_q)
                        self.softmax_bound(self.sm_q[0:1, 0:1], self.sm_k[0:1, hh:hh + 1], scale, negcs[hh])
                    for hh in range(4):
                        h = hg * 4 + hh
                        qT = qTs[hh]
                        negc = negcs[hh]
                        pos = k.psum_hold(1)
                        po = pos[0]
                        def qk(kt):
                            ps = k.psum()
                            k.mm(ps, ps[:, 0:n], Kt[:, hh, kt * 128:(kt + 1) * 128], qT[:, 0:n], True, True, reads=[Kt.sub(hh), qT])
                            return ps
                        pend = qk(0)
                        for kt in range(nkt):
                            ps = pend
                            if kt + 1 < nkt:
                                pend = qk(kt + 1)
                            pt = PT[npt % 3]
                            npt += 1
                            k.op("act", lambda e, pt=pt, ps=ps: e.activation(out=pt[:, 0:n], in_=ps[:, 0:n], func=AF.Exp, bias=negc[:, 0:1], scale=scale),
                                 reads=[ps, negc], writes=[pt])
                            k.mm(po, po[0:65, 0:n], Vt[:, kt, hh, :], pt[:, 0:n], kt == 0, kt == nkt - 1, reads=[pt, Vt])
                        k.op("act", lambda e, po=po: e.activation(out=rrow[64:65, 0:n], in_=po[64:65, 0:n], func=AF.Copy), reads=[po], writes=[rrow])
                        pbc = k.psum()
                        k.mm(pbc, pbc[0:64, 0:n], self.ones_f[64:65, 0:64], rrow[64:65, 0:n], True, True, reads=[self.ones_f, rrow])
                        k.op("dve", lambda e, pbc=pbc: e.reciprocal(out=rbc[:, 0:n], in_=pbc[0:64, 0:n]), reads=[pbc], writes=[rbc])
                        k.op("dve", lambda e, po=po, hh=hh: e.tensor_tensor(out=mixH[:, hh, 0:n], in0=po[0:64, 0:n], in1=rbc[:, 0:n], op=ALU.mult),
                             reads=[po, rbc], writes=[mixH])
                        k.psum_release(pos)
                    self.ao_k = 64
                    self.apply_out(l, path, s0 + g0, n, (lambda c: mixH[:, c, 0:n], [mixH]), 4, wout, 0)
                    self.ao_k = 128
        k.end_phase()

    def swa(self, l, path):
        k = self.k
        W = self.W
        smp = path == "s"
        hT = self.hT[path]
        scale = 64 ** -0.5
        k.begin_phase()
        self.ao_x = k.sb("ao_x", [128, 8, 512], F32)
        self.st_buf = [k.sb(f"st{i}", [128, 256], F32) for i in range(2)]
        self.st_n = 0
        self.sm_t = k.sb("sm_t", [1, 1], F32)
        self.sm_q = k.sb("sm_q", [1, 1], F32)
        self.sm_k = k.sb("sm_k", [1, 8], F32)
        self.sm_m = k.sb("sm_m", [1, 1], F32)
        self.sm_sq = k.sb("sm_sq", [96, 512], BF16)
        negc = k.sb("negc", [128, 1], F32)
        psink = k.sb("psink", [128, 8], F32)
        wsw = k.sb("s_w", [128, 8, 768], BF16)
        k.dma("pool", wsw[:], W["w_in"][l].rearrange("(kc p) n -> p kc n", p=128)[:, :, 672:1440], writes=[wsw])
        wout = k.sb("s_wout", [64, 8, D], BF16)
        k.dma("pool", wout[:], W["w_out"][l][512:1024, :].rearrange("(h p) n -> p h n", p=64), writes=[wout])
        if smp:
            rope = k.sb("s_rope", [64, 2, NST], F32)
            k.dma("sp", rope[:], W["rope_s"][:, :, :], writes=[rope])
            wr = k.sb("s_wr", [128, 8, 640], BF16)
            wv = wsw[:, :, 0:640].rearrange("p k (h a z d) -> p k h a z d", h=10, a=2, z=2)
            wrv = wr[:, :, :].rearrange("p k (h a z d) -> p k h a z d", h=10, a=2, z=2)
            for kc in range(8):
                k.op("pool", lambda e, kc=kc: e.tensor_scalar(out=wrv[:, kc, :, :, 0, :], in0=wv[:, kc, :, :, 1, :], scalar1=-1.0, scalar2=None, op0=ALU.mult),
                     reads=[wsw], writes=[wr])
                k.op("pool", lambda e, kc=kc: e.tensor_copy(out=wrv[:, kc, :, :, 1, :], in_=wv[:, kc, :, :, 0, :]), reads=[wsw], writes=[wr])
        kf = k.sb("s_kf", [64, 512], F32)
        kb_ = k.sb("s_kb", [64, 512], F32)

        def proj_rope(col0, a0, n, g0, dst_ap, dst_w, fp_out=None):
            pa = k.psum()
            for kc in range(8):
                k.mm(pa, pa[0:64, 0:n], wsw[:, kc, col0:col0 + 64], hT[:, kc, a0:a0 + n], kc == 0, kc == 7, reads=[wsw, hT.sub((a0 // 512) * 512)])
            if smp:
                pb = k.psum()
                for kc in range(8):
                    k.mm(pb, pb[0:64, 0:n], wr[:, kc, col0:col0 + 64], hT[:, kc, a0:a0 + n], kc == 0, kc == 7, reads=[wr, hT.sub((a0 // 512) * 512)])
                k.op("dve", lambda e: e.tensor_tensor(out=kf[:, 0:n], in0=pa[0:64, 0:n], in1=rope[:, 0, g0:g0 + n], op=ALU.mult), reads=[pa, rope], writes=[kf])
                k.op("dve", lambda e: e.tensor_tensor(out=kb_[:, 0:n], in0=pb[0:64, 0:n], in1=rope[:, 1, g0:g0 + n], op=ALU.mult), reads=[pb, rope], writes=[kb_])
                k.op("pool", lambda e: e.tensor_tensor(out=dst_ap, in0=kf[:, 0:n], in1=kb_[:, 0:n], op=ALU.add), reads=[kf, kb_], writes=dst_w)
            else:
                if fp_out is not None:
                    k.op("dve", lambda e: e.tensor_copy(out=kf[:, 0:n], in_=pa[0:64, 0:n]), reads=[pa], writes=[kf])
                    k.op("pool", lambda e: e.tensor_copy(out=dst_ap, in_=kf[:, 0:n]), reads=[kf], writes=dst_w)
                else:
                    k.op("act", lambda e: e.activation(out=dst_ap, in_=pa[0:64, 0:n], func=AF.Copy), reads=[pa], writes=dst_w)

        qs = k.sb("s_q", [64, 4, 512], BF16)
        PT = [k.sb(f"s_PT{i}", [128, 4, 128], BF16) for i in range(3)]
        mixH = k.sb("s_mixH", [64, 8, 512], BF16)
        rrow = k.sb("s_rrow", [65, 512], F32)
        rbc = k.sb("s_rbc", [64, 512], F32)
        for (s0, T_) in self.seqs(path):
            nk = T_ + (PAST if smp else 0)
            nkt = nk // 128
            nlt = T_ // 128
            seq_i = s0 // 256
            Ks = k.sb(f"s_K{s0}", [64, 2, nk], BF16)
            Vs = k.sb(f"s_V{s0}", [128, nkt, 2, 65], BF16)
            k.op("pool", lambda e: e.memset(Vs[:], 1.0), writes=[Vs])
            vst = [k.sb(f"s_vst{s0}_{i}", [128, 128], F32) for i in range(2)]
            groups = [(g0, min(512, T_ - g0)) for g0 in range(0, T_, 512)]
            for (g0, n) in groups:
                a0 = s0 + g0
                for kvh in range(2):
                    proj_rope(512 + kvh * 64, a0, n, g0, Ks[:, kvh, g0:g0 + n], [Ks.sub(kvh)], fp_out=(None if smp else True))
                    if not smp:
                        self.store_tok(lambda tt: kf[0:64, tt * 128:(tt + 1) * 128], [kf], 64, n,
                                       lambda tt, kvh=kvh: self.O["n_sk"][seq_i, l, g0 + tt * 128:g0 + (tt + 1) * 128, kvh * 64:(kvh + 1) * 64])
                    self.sq_max(Ks[0:64, kvh, g0:g0 + n], 64, n, self.sm_k[0:1, kvh:kvh + 1], g0 == 0, [Ks.sub(kvh)])
                for tt in range(n // 128):
                    kt = (g0 // 128) + tt
                    ps = k.psum()
                    for kc in range(8):
                        k.mm(ps, ps[:, 0:128], hT[:, kc, a0 + tt * 128:a0 + (tt + 1) * 128], wsw[:, kc, 640:768], kc == 0, kc == 7,
                             reads=[wsw, hT.sub((a0 // 512) * 512)])
                    k.op("act", lambda e, ps=ps, kt=kt: e.activation(out=Vs[:, kt, :, 0:64], in_=ps[:, 0:128].rearrange("p (h d) -> p h d", h=2), func=AF.Copy),
                         reads=[ps], writes=[Vs])
                    if not smp:
                        vt = vst[tt % 2]
                        k.op("dve", lambda e, ps=ps, vt=vt: e.tensor_copy(out=vt[:], in_=ps[:, 0:128]), reads=[ps], writes=[vt])
                        k.dma("sp", self.O["n_sv"][seq_i, l, g0 + tt * 128:g0 + (tt + 1) * 128, :], vt[:], reads=[vt], writes=[])
            if smp:
                cin = [k.sb(f"s_cin{i}", [128, 128], F32) for i in range(2)]
                cvn = [k.sb(f"s_cvn{i}", [128, 128], F32) for i in range(2)]
                for i in range(4):
                    ci = cin[i % 2]
                    k.dma("sp", ci[:], W["c_sk"][l, i * 128:(i + 1) * 128, :], writes=[ci])
                    for kvh in range(2):
                        ps = k.psum()
                        k.tr(ps, ps[0:64, 0:128], ci[:, kvh * 64:(kvh + 1) * 64], self.ident, reads=[ci, self.masks])
                        k.op("act", lambda e, ps=ps, kvh=kvh, i=i: e.activation(out=Ks[:, kvh, T_ + i * 128:T_ + (i + 1) * 128], in_=ps[0:64, 0:128], func=AF.Copy),
                             reads=[ps], writes=[Ks.sub(kvh)])
                    cv = cvn[i % 2]
                    k.dma("sp", cv[:], W["c_sv"][l, i * 128:(i + 1) * 128, :], writes=[cv])
                    k.op("pool", lambda e, cv=cv, i=i: e.tensor_copy(out=Vs[:, nlt + i, :, 0:64], in_=cv[:, :].rearrange("p (h d) -> p h d", h=2)),
                         reads=[cv], writes=[Vs])
                for kvh in range(2):
                    for g0 in range(T_, nk, 512):
                        self.sq_max(Ks[0:64, kvh, g0:g0 + 512], 64, 512, self.sm_k[0:1, kvh:kvh + 1], False, [Ks.sub(kvh)])
            npt = 0
            for (g0, n) in groups:
                a0 = s0 + g0
                nqt = n // 128
                for kvh in range(2):
                    for g in range(4):
                        proj_rope((kvh * 4 + g) * 64, a0, n, g0, qs[:, g, 0:n], [qs])
                        self.sq_max(qs[0:64, g, 0:n], 64, n, self.sm_q[0:1, 0:1], g == 0, [qs])
                    self.softmax_bound(self.sm_q[0:1, 0:1], self.sm_k[0:1, kvh:kvh + 1], scale, negc)
                    k.op("act", lambda e, kvh=kvh: e.activation(out=psink[:, kvh * 4:(kvh + 1) * 4], in_=self.rowv(l, "sink")[:, kvh * 4:(kvh + 1) * 4],
                                                                func=AF.Exp, bias=negc[:, 0:1], scale=1.0), reads=[self.rowbc, negc], writes=[psink])
                    for qt in range(nqt):
                        gq = g0 // 128 + qt
                        if smp:
                            kts = [(t, t - gq) for t in (gq - 1, gq, gq + 1) if 0 <= t < nlt] + [(nlt + i, 0) for i in range(4)]
                        else:
                            kts = [(t, 0) for t in range(nlt)]
                        pos = k.psum_hold(1)
                        po = pos[0]
                        def qk(kt):
                            ps = k.psum()
                            k.mm(ps, ps[:, :].rearrange("p (g q) -> p g q", g=4), Ks[:, kvh, kt * 128:(kt + 1) * 128], qs[:, :, qt * 128:(qt + 1) * 128],
                                 True, True, reads=[Ks.sub(kvh), qs])
                            return ps
                        pend = qk(kts[0][0])
                        for ki, (kt, rel) in enumerate(kts):
                            ps = pend
                            if ki + 1 < len(kts):
                                pend = qk(kts[ki + 1][0])
                            pt = PT[npt % 3]
                            npt += 1
                            k.op("act", lambda e, pt=pt, ps=ps: e.activation(out=pt[:, :, :], in_=ps[:, :].rearrange("p (g q) -> p g q", g=4), func=AF.Exp,
                                                                           bias=negc[:, 0:1], scale=scale), reads=[ps, negc], writes=[pt])
                            if rel != 0:
                                mi = 3 if rel == -1 else 1
                                for g in range(4):
                                    k.op("pool" if g % 2 else "dve", lambda e, pt=pt, g=g, mi=mi: e.tensor_tensor(out=pt[:, g, :], in0=pt[:, g, :], in1=self.masks_b[:, mi, :], op=ALU.mult),
                                         reads=[pt, self.masks_b], writes=[pt])
                            k.mm(po, po[0:65, :].rearrange("p (g q) -> p g q", g=4), Vs[:, kt, kvh, :], pt[:, :, :], ki == 0, ki == len(kts) - 1, reads=[pt, Vs])
                        for g in range(4):
                            k.op("dve", lambda e, po=po, g=g, kvh=kvh: e.tensor_scalar(out=rrow[64:65, g * 128:(g + 1) * 128], in0=po[64:65, g * 128:(g + 1) * 128],
                                                                                         scalar1=psink[64:65, kvh * 4 + g:kvh * 4 + g + 1], scalar2=None, op0=ALU.add),
                                 reads=[po, psink], writes=[rrow])
                        pbc = k.psum()
                        k.mm(pbc, pbc[0:64, :], self.ones_f[64:65, 0:64], rrow[64:65, :], True, True, reads=[self.ones_f, rrow])
                        k.op("dve", lambda e, pbc=pbc: e.reciprocal(out=rbc[:, :], in_=pbc[0:64, :]), reads=[pbc], writes=[rbc])
                        k.op("dve", lambda e, po=po, kvh=kvh, qt=qt: e.tensor_tensor(
                            out=mixH[:, kvh * 4:(kvh + 1) * 4, qt * 128:(qt + 1) * 128], in0=po[0:64, :].rearrange("p (g q) -> p g q", g=4),
                            in1=rbc[:, :].rearrange("p (g q) -> p g q", g=4), op=ALU.mult), reads=[po, rbc], writes=[mixH])
                        k.psum_release(pos)
                self.ao_k = 64
                self.apply_out(l, path, s0 + g0, n, (lambda c: mixH[:, c, 0:n], [mixH]), 8, wout, 0)
                self.ao_k = 128
        k.end_phase()

    def gdn(self, l, path):
        k = self.k
        W = self.W
        smp = path == "s"
        hT = self.hT[path]
        M = self.masks
        k.begin_phase()
        self.ao_x = k.sb("ao_x", [128, 8, 512], F32)
        win_all = W["w_in"][l].rearrange("(kc p) n -> p kc n", p=128)
        AM = k.sb("g_am", [128, 5, 128], F32)
        k.dma("sp", AM[:], W["amasks"][:], writes=[AM])
        wg = k.sb("g_wg", [128, 8, 16], BF16)
        k.dma("pool", wg[:], win_all[:, :, 3488:3504], writes=[wg])
        wout = k.sb("g_wout", [128, 4, D], BF16)
        k.dma("pool", wout[:], W["w_out"][l].rearrange("(c p) n -> p c n", p=128)[:, 8:12, :], writes=[wout])
        wh = [k.sb(f"g_wh{i}", [128, 8, 512], BF16) for i in range(1)]
        nalog = k.sb("g_nalog", [128, 8], F32)
        k.op("act", lambda e: e.activation(out=nalog[:], in_=self.rowv(l, "alog"), func=AF.Exp), reads=[self.rowbc], writes=[nalog])
        k.op("dve", lambda e: e.tensor_scalar(out=nalog[:], in0=nalog[:], scalar1=-1.0, scalar2=None, op0=ALU.mult), reads=[nalog], writes=[nalog])
        nwh = 0
        for (s0, T_) in self.seqs(path):
            nch = T_ // 128
            seq_i = s0 // 256
            gg = k.sb(f"g_g{s0}", [128, nch, 8], F32)
            bt = k.sb(f"g_b{s0}", [128, nch, 8], F32)
            for ci in range(nch):
                a0 = s0 + ci * 128
                ps = k.psum()
                for kc in range(8):
                    k.mm(ps, ps[:, 0:16], hT[:, kc, a0:a0 + 128], wg[:, kc, :], kc == 0, kc == 7, reads=[wg, hT.sub((a0 // 512) * 512)])
                k.op("dve", lambda e, ps=ps, ci=ci: e.tensor_tensor(out=gg[:, ci, :], in0=ps[:, 0:8], in1=self.rowv(l, "dtb"), op=ALU.add),
                     reads=[ps, self.rowbc], writes=[gg])
                k.op("act", lambda e, ps=ps, ci=ci: e.activation(out=bt[:, ci, :], in_=ps[:, 8:16], func=AF.Sigmoid), reads=[ps], writes=[bt])
            nbt = k.sb(f"g_nb{s0}", [128, nch, 8], F32)
            k.op("pool", lambda e: e.tensor_scalar(out=nbt[:], in0=bt[:], scalar1=-1.0, scalar2=None, op0=ALU.mult), reads=[bt], writes=[nbt])
            k.op("dve", lambda e: e.tensor_scalar(out=gg[:], in0=gg[:], scalar1=60.0, scalar2=None, op0=ALU.min), reads=[gg], writes=[gg])
            k.op("act", lambda e: e.activation(out=gg[:], in_=gg[:], func=AF.Exp), reads=[gg], writes=[gg])
            k.op("act", lambda e: e.activation(out=gg[:], in_=gg[:], func=AF.Ln, bias=1.0, scale=1.0), reads=[gg], writes=[gg])
            for ci in range(nch):
                k.op("dve", lambda e, ci=ci: e.tensor_tensor(out=gg[:, ci, :], in0=gg[:, ci, :], in1=nalog[:], op=ALU.mult), reads=[gg, nalog], writes=[gg])
            if GSTOP <= 1:
                continue
            pre = k.sb(f"g_pre{s0}", [128, 3, T_ + 4], F32)
            acc = k.sb(f"g_acc{s0}", [128, 3, T_], F32)
            osum = k.sb(f"g_osum{s0}", [128, nch, 128], F32)
            mixT = k.sb(f"g_mixT{s0}", [128, 4, T_], BF16)
            sqf = k.sb(f"g_sqf{s0}", [128, 512], F32)
            rst = k.sb(f"g_rst{s0}", [128, 512], F32)
            names = ["gcb", "X", "DT", "Dm", "N", "NT", "Bm", "BmT", "B2", "B2T", "Wm", "W2", "kbg", "vb",
                     "u", "wkT", "E", "gones", "gz", "on"]
            names_b = ["AT", "kdec", "qdec", "vnew", "Sb"]
            tqs = [{nm: k.sb(f"g_{nm}{s0}_{d}", [128, 128], F32) for nm in names} for d in range(2)]
            for d in range(2):
                tqs[d].update({nm: k.sb(f"g_{nm}{s0}_{d}", [128, 128], BF16) for nm in names_b})
            accb = k.sb(f"g_accb{s0}", [128, 2, T_], BF16)
            scs = [k.sb(f"g_sc{s0}_{d}", [128, 8], F32) for d in range(2)]
            Ss = [k.sb(f"g_S{s0}_{d}", [128, 128], F32) for d in range(2)]
            osums = [osum, k.sb(f"g_osumb{s0}", [128, nch, 128], F32)]
            tq, sc = tqs[0], scs[0]
            k.op("pool", lambda e: e.memset(pre[:], 0.0), writes=[pre])
            groups = [(g0, min(512, T_ - g0)) for g0 in range(0, T_, 512)]
            for h in range(4):
                w = wh[nwh % len(wh)]
                nwh += 1
                for wi, c0 in enumerate([1440, 1952, 2464, 2976]):
                    k.dma("pool", w[:, :, wi * 128:(wi + 1) * 128], win_all[:, :, c0 + h * 128:c0 + (h + 1) * 128], writes=[w])
                for (g0, n) in groups:
                    a0 = s0 + g0
                    for wi in range(3):
                        ps = k.psum()
                        for kc in range(8):
                            k.mm(ps, ps[:, 0:n], w[:, kc, wi * 128:(wi + 1) * 128], hT[:, kc, a0:a0 + n], kc == 0, kc == 7, reads=[w, hT.sub((a0 // 512) * 512)])
                        k.op("act", lambda e, ps=ps, wi=wi: e.activation(out=pre[:, wi, 2 + g0:2 + g0 + n], in_=ps[:, 0:n], func=AF.Copy), reads=[ps], writes=[pre.sub(wi)])
                for wi in range(3):
                    eng = "dve"
                    cch = wi * 4 + h
                    for tap in range(5):
                        wcol = self.convT[:, l * 60 + tap * 12 + cch:l * 60 + tap * 12 + cch + 1]
                        if tap == 0:
                            k.op(eng, lambda e, wi=wi, wcol=wcol: e.tensor_scalar(out=acc[:, wi, :], in0=pre[:, wi, 0:T_], scalar1=wcol, scalar2=None, op0=ALU.mult),
                                 reads=[pre.sub(wi), self.convT], writes=[acc.sub(wi)])
                        else:
                            k.op(eng, lambda e, wi=wi, wcol=wcol, tap=tap: e.scalar_tensor_tensor(out=acc[:, wi, :], in0=pre[:, wi, tap:tap + T_], scalar=wcol,
                                                                                                 in1=acc[:, wi, :], op0=ALU.mult, op1=ALU.add),
                                 reads=[pre.sub(wi), self.convT, acc.sub(wi)], writes=[acc.sub(wi)])
                    k.op("act", lambda e, wi=wi: e.activation(out=acc[:, wi, :], in_=acc[:, wi, :], func=AF.Silu), reads=[acc.sub(wi)], writes=[acc.sub(wi)])
                if GSTOP <= 2:
                    continue
                for wi in range(2):
                    for (g0, n) in groups:
                        k.op("act", lambda e, wi=wi: e.activation(out=sqf[:, 0:n], in_=acc[:, wi, g0:g0 + n], func=AF.Square), reads=[acc.sub(wi)], writes=[sqf])
                        ps = k.psum()
                        k.mm(ps, ps[:, 0:n], self.ones_f[:, :], sqf[:, 0:n], True, True, reads=[self.ones_f, sqf])
                        k.op("act", lambda e, ps=ps: e.activation(out=rst[:, 0:n], in_=ps[:, 0:n], func=AF.Sqrt, bias=self.epsc[:, 0:1], scale=1.0),
                             reads=[ps, self.epsc], writes=[rst])
                        k.op("dve", lambda e: e.reciprocal(out=rst[:, 0:n], in_=rst[:, 0:n]), reads=[rst], writes=[rst])
                        if wi == 0:
                            k.op("dve", lambda e, wi=wi: e.scalar_tensor_tensor(out=acc[:, wi, g0:g0 + n], in0=acc[:, wi, g0:g0 + n], scalar=128 ** -0.5, in1=rst[:, 0:n],
                                                                                 op0=ALU.mult, op1=ALU.mult), reads=[acc.sub(wi), rst], writes=[acc.sub(wi)])
                        else:
                            k.op("pool", lambda e, wi=wi: e.tensor_tensor(out=acc[:, wi, g0:g0 + n], in0=acc[:, wi, g0:g0 + n], in1=rst[:, 0:n], op=ALU.mult),
                                 reads=[acc.sub(wi), rst], writes=[acc.sub(wi)])
                k.op("act", lambda e: e.activation(out=accb[:, 0, :], in_=acc[:, 0, :], func=AF.Copy), reads=[acc.sub(0)], writes=[accb])
                k.op("pool", lambda e: e.tensor_copy(out=accb[:, 1, :], in_=acc[:, 1, :]), reads=[acc.sub(1)], writes=[accb])
                if GSTOP <= 3:
                    continue
                def chain(d, tq, sc, S, osum):
                    col = d * 4 + h
                    if smp:
                        k.dma("sp", S[:], W["c_st"][l, d, h], writes=[S])
                        yield
                    else:
                        k.op("pool", lambda e: e.memset(S[:], 0.0), writes=[S])
                        yield
                    k.op("act", lambda e: e.activation(out=tq["Sb"][:], in_=S[:], func=AF.Copy), reads=[S], writes=[tq["Sb"]])
                    yield
                    order = range(nch) if d == 0 else range(nch - 1, -1, -1)
                    mcs = 1 if d == 0 else 3
                    m_dt_incl = 1 if d == 0 else 3
                    m_d_strict = 4 if d == 0 else 2
                    last = 127 if d == 0 else 0
                    for ci in order:
                        c0 = ci * 128
                        kT_c = acc[:, 1, c0:c0 + 128]
                        qT_c = acc[:, 0, c0:c0 + 128]
                        vT_c = acc[:, 2, c0:c0 + 128]
                        gcol = gg[:, ci, col:col + 1]
                        bcol = bt[:, ci, col:col + 1]
                        rd = [acc.sub(0), acc.sub(1), acc.sub(2)]
                        t = tq
                        yield
                        p1 = k.psum()
                        k.mm(p1, p1[:, 0:128], kT_c, kT_c, True, True, reads=rd)
                        yield
                        k.mm(p1, p1[:, 128:256], accb[:, 1, c0:c0 + 128], accb[:, 0, c0:c0 + 128], True, True, reads=[accb])
                        yield
                        k.tr(p1, p1[:, 256:384], kT_c, self.ident, reads=rd + [M])
                        yield
                        k.tr(p1, p1[:, 384:512], vT_c, self.ident, reads=rd + [M])
                        yield
                        yield
                        k.op("dve", lambda e: e.tensor_scalar(out=t["gones"][:], in0=self.ones_f[:], scalar1=gcol, scalar2=None, op0=ALU.mult),
                             reads=[gg, self.ones_f], writes=[t["gones"]])
                        yield
                        p2 = k.psum()
                        k.mm(p2, p2[:, 0:128], t["gones"][:], M[:, mcs, :], True, True, reads=[t["gones"], M])
                        yield
                        k.mm(p2, p2[:, 128:129], M[:, mcs, :], gcol, True, True, reads=[M, gg])
                        yield
                        k.op("dve", lambda e: e.tensor_copy(out=sc[:, 0:1], in_=p2[:, 128:129]), reads=[p2], writes=[sc])
                        yield
                        yield
                        k.op("dve", lambda e: e.scalar_tensor_tensor(out=t["DT"][:], in0=p2[:, 0:128], scalar=sc[:, 0:1], in1=AM[:, m_dt_incl, :], op0=ALU.subtract, op1=ALU.add),
                             reads=[p2, sc, AM], writes=[t["DT"]])
                        yield
                        k.op("act", lambda e: e.activation(out=t["DT"][:], in_=t["DT"][:], func=AF.Exp), reads=[t["DT"]], writes=[t["DT"]])
                        yield
                        k.op("dve", lambda e: e.scalar_tensor_tensor(out=t["Dm"][:], in0=p2[:, 0:128], scalar=sc[:, 0:1], in1=AM[:, m_d_strict, :], op0=ALU.subtract, op1=ALU.subtract),
                             reads=[p2, sc, AM], writes=[t["Dm"]])
                        yield
                        k.op("act", lambda e: e.activation(out=t["Dm"][:], in_=t["Dm"][:], func=AF.Exp, scale=-1.0), reads=[t["Dm"]], writes=[t["Dm"]])
                        yield
                        yield
                        k.op("act", lambda e: e.activation(out=t["E"][:], in_=p2[:, 0:128], func=AF.Exp), reads=[p2], writes=[t["E"]])
                        yield
                        k.op("act", lambda e: e.activation(out=sc[:, 1:2], in_=sc[:, 0:1], func=AF.Exp), reads=[sc], writes=[sc])
                        yield
                        k.op("dve", lambda e: e.tensor_tensor(out=sc[:, 1:2], in0=sc[:, 1:2], in1=bcol, op=ALU.mult), reads=[sc, bt], writes=[sc])
                        yield
                        k.op("dve", lambda e: e.tensor_tensor(out=sc[:, 2:3], in0=p2[:, last:last + 1], in1=sc[:, 0:1], op=ALU.subtract),
                             reads=[p2, sc], writes=[sc])
                        yield
                        k.op("act", lambda e: e.activation(out=sc[:, 2:3], in_=sc[:, 2:3], func=AF.Exp), reads=[sc], writes=[sc])
                        yield
                        yield
                        k.op("dve", lambda e: e.scalar_tensor_tensor(out=t["N"][:], in0=p1[:, 0:128], scalar=nbt[:, ci, col:col + 1], in1=t["Dm"][:], op0=ALU.mult, op1=ALU.mult),
                             reads=[p1, nbt, t["Dm"]], writes=[t["N"]])
                        yield
                        k.op("dve", lambda e: e.tensor_tensor(out=t["AT"][:], in0=p1[:, 128:256], in1=t["DT"][:], op=ALU.mult), reads=[p1, t["DT"]], writes=[t["AT"]])
                        yield
                        yield
                        k.op("dve", lambda e: e.tensor_scalar(out=t["kbg"][:], in0=p1[:, 256:384], scalar1=sc[:, 1:2], scalar2=None, op0=ALU.mult), reads=[p1, sc], writes=[t["kbg"]])
                        yield
                        k.op("dve", lambda e: e.tensor_scalar(out=t["kdec"][:], in0=p1[:, 256:384], scalar1=sc[:, 2:3], scalar2=None, op0=ALU.mult), reads=[p1, sc], writes=[t["kdec"]])
                        yield
                        k.op("dve", lambda e: e.tensor_scalar(out=t["vb"][:], in0=p1[:, 384:512], scalar1=bcol, scalar2=None, op0=ALU.mult), reads=[p1, bt], writes=[t["vb"]])
                        yield
                        k.op("pool", lambda e: e.tensor_tensor(out=t["qdec"][:], in0=qT_c, in1=t["E"][:], op=ALU.mult), reads=[acc.sub(0), t["E"]], writes=[t["qdec"]])
                        yield
                        if GSTOP <= 4:
                            continue
                        yield
                        p3 = k.psum()
                        k.tr(p3, p3[:, 0:128], t["N"][:], self.ident, reads=[t["N"], M])
                        yield
                        k.op("act", lambda e: e.activation(out=t["NT"][:], in_=p3[:, 0:128], func=AF.Copy), reads=[p3], writes=[t["NT"]])
                        yield
                        yield
                        k.op("dve", lambda e: e.tensor_tensor(out=t["Wm"][:], in0=p3[:, 0:128], in1=self.ident, op=ALU.add), reads=[p3, M], writes=[t["Wm"]])
                        yield
                        Bc, BcT, Wc = t["NT"], t["N"], t["Wm"]
                        nxt = [(t["Bm"], t["BmT"], t["W2"]), (t["B2"], t["B2T"], t["Wm"])]
                        for lev in range(1, 7):
                            Bn, BnT, Wn = nxt[(lev - 1) % 2]
                            pq = k.psum()
                            k.mm(pq, pq[:, 0:128], Bc[:], BcT[:], True, True, reads=[Bc, BcT])
                            yield
                            if lev < 6:
                                pq2 = k.psum()
                                k.mm(pq2, pq2[:, 0:128], BcT[:], Bc[:], True, True, reads=[Bc, BcT])
                                yield
                            k.op("act", lambda e, pq=pq, BnT=BnT: e.activation(out=BnT[:], in_=pq[:, 0:128], func=AF.Copy), reads=[pq], writes=[BnT])
                            yield
                            if lev < 6:
                                k.op("dve", lambda e, pq2=pq2, Bn=Bn: e.tensor_copy(out=Bn[:], in_=pq2[:, 0:128]), reads=[pq2], writes=[Bn])
                                yield
                            yield
                            pw = k.psum()
                            k.mm(pw, pw[:, 0:128], BnT[:], Wc[:], True, True, reads=[BnT, Wc])
                            yield
                            k.op("dve", lambda e, pw=pw, Wn=Wn, Wc=Wc: e.tensor_tensor(out=Wn[:], in0=pw[:, 0:128], in1=Wc[:], op=ALU.add), reads=[pw, Wc], writes=[Wn])
                            yield
                            Bc, BcT, Wc = Bn, BnT, Wn
                            yield
                        TT = Wc
                        yield
                        p4 = k.psum()
                        k.mm(p4, p4[:, 0:128], TT[:], t["vb"][:], True, True, reads=[TT, t["vb"]])
                        yield
                        k.mm(p4, p4[:, 128:256], t["kbg"][:], TT[:], True, True, reads=[TT, t["kbg"]])
                        yield
                        k.op("act", lambda e: e.activation(out=t["u"][:], in_=p4[:, 0:128], func=AF.Copy), reads=[p4], writes=[t["u"]])
                        yield
                        k.op("dve", lambda e: e.tensor_copy(out=t["wkT"][:], in_=p4[:, 128:256]), reads=[p4], writes=[t["wkT"]])
                        yield
                        if GSTOP <= 5:
                            continue
                        yield
                        p5 = k.psum()
                        k.mm(p5, p5[:, 0:128], t["wkT"][:], S[:], True, True, reads=[t["wkT"], S])
                        yield
                        k.op("dve", lambda e: e.tensor_tensor(out=t["vnew"][:], in0=t["u"][:], in1=p5[:, 0:128], op=ALU.subtract), reads=[p5, t["u"]], writes=[t["vnew"]])
                        yield
                        k.mm(p5, p5[:, 128:256], t["qdec"][:], t["Sb"][:], True, False, reads=[t["qdec"], t["Sb"]])
                        yield
                        k.mm(p5, p5[:, 128:256], t["AT"][:], t["vnew"][:], False, True, reads=[t["AT"], t["vnew"]])
                        yield
                        k.mm(p5, p5[:, 256:384], t["kdec"][:], t["vnew"][:], True, True, reads=[t["kdec"], t["vnew"]])
                        yield
                        k.op("act", lambda e, ci=ci: e.activation(out=osum[:, ci, :], in_=p5[:, 128:256], func=AF.Copy), reads=[p5], writes=[osum.sub(ci)])
                        yield
                        k.op("dve", lambda e: e.scalar_tensor_tensor(out=S[:], in0=S[:], scalar=t["E"][:, last:last + 1], in1=p5[:, 256:384], op0=ALU.mult, op1=ALU.add),
                             reads=[p5, S, t["E"]], writes=[S])
                        yield
                        k.op("act", lambda e: e.activation(out=t["Sb"][:], in_=S[:], func=AF.Copy), reads=[S], writes=[t["Sb"]])
                        yield
                    if not smp:
                        k.dma("sp", self.O["n_st"][seq_i, l, d, h], S[:], reads=[S], writes=[])
                        yield
                chains = []
                for d in range(2):
                    chains.append([chain(d, tqs[d], scs[d], Ss[d], osums[d]), [4 * d, 4 * d + 1, 4 * d + 2, 4 * d + 3], 0])
                save_rot, save_next = k.ps_rot, k.psnext
                while chains:
                    for cobj in list(chains):
                        k.ps_rot, k.psnext = cobj[1], cobj[2]
                        try:
                            next(cobj[0])
                        except StopIteration:
                            chains.remove(cobj)
                        cobj[2] = k.psnext
                k.ps_rot, k.psnext = save_rot, save_next
                if GSTOP <= 6:
                    continue
                for ci in range(nch):
                    a0 = s0 + ci * 128
                    ps = k.psum()
                    for kc in range(8):
                        k.mm(ps, ps[:, 0:128], hT[:, kc, a0:a0 + 128], w[:, kc, 384:512], kc == 0, kc == 7, reads=[w, hT.sub((a0 // 512) * 512)])
                    t = tqs[ci % 2]
                    sc = scs[ci % 2]
                    k.op("act", lambda e, ps=ps: e.activation(out=t["gz"][:], in_=ps[:, 0:128], func=AF.Silu), reads=[ps], writes=[t["gz"]])
                    k.op("pool", lambda e, ci=ci: e.tensor_tensor(out=osum[:, ci, :], in0=osum[:, ci, :], in1=osums[1][:, ci, :], op=ALU.add),
                         reads=[osum.sub(ci), osums[1].sub(ci)], writes=[osum.sub(ci)])
                    k.op("act", lambda e, ci=ci: e.activation(out=t["on"][:], in_=osum[:, ci, :], func=AF.Square, accum_out=sc[:, 3:4]), reads=[osum.sub(ci)], writes=[t["on"], sc])
                    k.op("act", lambda e: e.activation(out=sc[:, 4:5], in_=sc[:, 3:4], func=AF.Sqrt, bias=self.epsc[:, 0:1], scale=1.0 / 128), reads=[sc, self.epsc], writes=[sc])
                    k.op("dve", lambda e: e.reciprocal(out=sc[:, 4:5], in_=sc[:, 4:5]), reads=[sc], writes=[sc])
                    k.op("dve", lambda e, ci=ci: e.scalar_tensor_tensor(out=t["on"][:], in0=osum[:, ci, :], scalar=sc[:, 4:5], in1=self.rowv(l, "gnorm"), op0=ALU.mult, op1=ALU.mult),
                         reads=[osum.sub(ci), sc, self.rowbc], writes=[t["on"]])
                    k.op("pool", lambda e: e.tensor_tensor(out=t["on"][:], in0=t["on"][:], in1=t["gz"][:], op=ALU.mult), reads=[t["on"], t["gz"]], writes=[t["on"]])
                    p6 = k.psum()
                    k.tr(p6, p6[:, 0:128], t["on"][:], self.ident, reads=[t["on"], M])
                    k.op("act", lambda e, p6=p6, ci=ci, h=h: e.activation(out=mixT[:, h, ci * 128:(ci + 1) * 128], in_=p6[:, 0:128], func=AF.Copy), reads=[p6], writes=[mixT.sub(h)])
            if GSTOP > 6:
                for (g0, n) in groups:
                    self.apply_out(l, path, s0 + g0, n, (lambda c: mixT[:, c, g0:g0 + n], [mixT]), 4, wout, 0)
        k.end_phase()


_CACHE = {}


def _get_program():
    if "nc" not in _CACHE:
        b = Builder()
        _CACHE["nc"] = b.build()
        _CACHE["b"] = b
    return _CACHE["nc"], _CACHE["b"]


def make_in_maps(inp):
    f = lambda a: np.ascontiguousarray(np.asarray(a, dtype=np.float32))
    hc = host_consts()
    shared = {}
    for nm in ["w_ada", "ffn1_w1", "ffn1_w2", "w_in", "mla_w_qb", "mla_w_kvb", "w_out", "ffn2_w1", "ffn2_w2"]:
        shared[nm] = f(inp[nm])
    shared["b_ada"] = f(inp["b_ada"]).reshape(L, 72, 128)
    shared["convw"] = f(inp["gdn_conv_w"]).reshape(L * 5 * 12, 128)
    rowv = np.zeros((L, 304), np.float32)
    rowv[:, 0:8] = f(inp["swa_sink"])
    rowv[:, 8:16] = f(inp["gdn_a_log"]).reshape(L, 8)
    rowv[:, 16:24] = f(inp["gdn_dt_bias"]).reshape(L, 8)
    rowv[:, 24:152] = f(inp["gdn_norm"])
    shared["rowv"] = rowv.reshape(1, L * 304)
    shared.update(hc)
    maps = []
    for c in range(8):
        b = c // 4
        m = dict(shared)
        m["xp"] = f(inp["x_prompt"][2 * c:2 * c + 2]).reshape(NPT, D)
        m["xs"] = f(inp["x_sample"][b])
        m["c_ckv"] = f(inp["cache_mla_ckv"][b])
        m["c_kr"] = f(inp["cache_mla_krope"][b])
        m["c_sk"] = f(inp["cache_swa_k"][b]).reshape(L, PAST, 128)
        m["c_sv"] = f(inp["cache_swa_v"][b]).reshape(L, PAST, 128)
        m["c_st"] = f(inp["state_gdn"][b])
        vecA = np.zeros((82, 128), np.float32)
        vecA[0:16] = f(inp["norm_ffn1"]).reshape(16, 128)
        vecA[16:32] = f(inp["norm_mix"]).reshape(16, 128)
        vecA[32:48] = f(inp["norm_ffn2"]).reshape(16, 128)
        vecA[48:56] = f(inp["final_norm"]).reshape(8, 128)
        vecA[56:64] = f(inp["c_ctx"]).reshape(8, 128)
        vecA[64:72] = f(inp["c"][b]).reshape(8, 128)
        vecA[72:78] = f(inp["mla_q_norm"]).reshape(6, 128)
        vecA[78:82] = f(inp["mla_kv_norm"]).reshape(4, 128)
        m["vecA"] = vecA
        maps.append(m)
    return maps


def kernel(**inp):
    nc, b = _get_program()
    maps = make_in_maps(inp)
    res = run_bass_kernel_spmd(nc, maps, core_ids=list(range(8)))
    R = res.results
    yp = np.concatenate([R[c]["yp"].reshape(2, 256, D) for c in range(8)], 0)
    ys = np.stack([R[0]["ys"], R[4]["ys"]], 0)
    n_ckv = np.concatenate([R[c]["n_ckv"] for c in range(8)], 0)
    n_kr = np.concatenate([R[c]["n_kr"] for c in range(8)], 0)
    n_sk = np.concatenate([R[c]["n_sk"] for c in range(8)], 0).reshape(16, L, 256, 2, 64)
    n_sv = np.concatenate([R[c]["n_sv"] for c in range(8)], 0).reshape(16, L, 256, 2, 64)
    n_st = np.concatenate([R[c]["n_st"] for c in range(8)], 0)
    return (yp.astype(np.float32), ys.astype(np.float32), n_ckv.astype(np.float32), n_kr.astype(np.float32),
            n_sk.astype(np.float32), n_sv.astype(np.float32), n_st.astype(np.float32))
```
